# Optimizing a Trainium2 kernel written in Bass

```python
import math
import jax, jax.numpy as jnp
from jax import lax
import numpy as np

D_MODEL = 2048
BATCH = 4
SEQ = 2048
DEPTH = 1
DEC_BATCH = 128
DEC_SEQ = 1
PAST_LEN = 8192
PAGE_SIZE = 128

N_META = 16
HEAD_DIM = 64
N_Q_HEADS = 16
N_KV_HEADS = 4
Q_PER_KV = N_Q_HEADS // N_KV_HEADS
ATTN_WIDTH = N_Q_HEADS * HEAD_DIM
KV_WIDTH = N_KV_HEADS * HEAD_DIM
WINDOW = 128
BLOCK = 128
ROPE_THETA = 10000.0
SSM_WIDTH = D_MODEL // 2
SSM_GROUP = 16
N_SSM_GROUPS = SSM_WIDTH // SSM_GROUP
SSM_STATE = 64
EPS = 1e-6

Q_END = ATTN_WIDTH
K_END = Q_END + KV_WIDTH
V_END = K_END + KV_WIDTH
ZA_END = V_END + ATTN_WIDTH
U_END = ZA_END + SSM_WIDTH
ZS_END = U_END + SSM_WIDTH
GA_END = ZS_END + D_MODEL
IN_WIDTH = GA_END + D_MODEL
IN_SPLITS = (Q_END, K_END, V_END, ZA_END, U_END, ZS_END, GA_END)

kernel_name = "hybrid_swa_sink_s5_meta_decode_step"


def rmsnorm(x, g):
    xf = x.astype(jnp.float32)
    r = lax.rsqrt(jnp.mean(xf * xf, axis=-1, keepdims=True) + EPS)
    return (xf * r * g.astype(jnp.float32)).astype(x.dtype)


def rope(x, pos):
    half = HEAD_DIM // 2
    inv_freq = ROPE_THETA ** (-jnp.arange(half, dtype=jnp.float32) / half)
    ang = pos.astype(jnp.float32)[:, None] * inv_freq[None, :]
    cos = jnp.cos(ang)[:, None, :]
    sin = jnp.sin(ang)[:, None, :]
    xf = x.astype(jnp.float32)
    x1, x2 = xf[..., :half], xf[..., half:]
    return jnp.concatenate([x1 * cos - x2 * sin, x2 * cos + x1 * sin], axis=-1).astype(x.dtype)


def branch_inputs(x, pos, norm_gain, w_in, q_norm_gain, k_norm_gain):
    b, t = x.shape[:2]
    xn = rmsnorm(x, norm_gain)
    h = jnp.einsum('btd,de->bte', xn, w_in)
    q, k, v, z_a, u, z_s, g_a, g_s = jnp.split(h, IN_SPLITS, axis=-1)
    q = rope(rmsnorm(q.reshape(b, t, N_Q_HEADS, HEAD_DIM), q_norm_gain), pos)
    k = rope(rmsnorm(k.reshape(b, t, N_KV_HEADS, HEAD_DIM), k_norm_gain), pos)
    v = v.reshape(b, t, N_KV_HEADS, HEAD_DIM)
    return q, k, v, z_a, u, z_s, g_a, g_s


def sink_attention(q, k, v, mask, sinks):
    s = jnp.einsum('bnqgrd,bnkgd->bngrqk', q, k).astype(jnp.float32) * (HEAD_DIM ** -0.5)
    s = jnp.where(mask[None, :, None, None], s, -jnp.inf)
    sink = sinks.astype(jnp.float32).reshape(1, 1, N_KV_HEADS, Q_PER_KV, 1, 1)
    m = jnp.maximum(jnp.max(s, axis=-1, keepdims=True), sink)
    p = jnp.exp(s - m)
    w = p / (jnp.sum(p, axis=-1, keepdims=True) + jnp.exp(sink - m))
    return jnp.einsum('bngrqk,bnkgd->bnqgrd', w.astype(v.dtype), v)


def prompt_attention(q, k, v, sinks):
    b, length = q.shape[:2]
    pad = BLOCK - N_META
    total = length + pad
    nb = total // BLOCK
    padw = ((0, 0), (pad, 0), (0, 0), (0, 0))
    qb = jnp.pad(q, padw).reshape(b, nb, BLOCK, N_KV_HEADS, Q_PER_KV, HEAD_DIM)

    def band(z):
        zb = jnp.pad(z, padw).reshape(b, nb, BLOCK, N_KV_HEADS, HEAD_DIM)
        prev = jnp.pad(zb[:, :-1], ((0, 0), (1, 0), (0, 0), (0, 0), (0, 0)))
        meta = jnp.broadcast_to(z[:, None, :N_META], (b, nb, N_META, N_KV_HEADS, HEAD_DIM))
        return jnp.concatenate([meta, prev, zb], axis=2)

    kb, vb = band(k), band(v)
    qpos = (jnp.arange(total) - pad).reshape(nb, BLOCK)
    kpos = jnp.arange(nb)[:, None] * BLOCK - BLOCK - pad + jnp.arange(2 * BLOCK)[None, :]
    diff = qpos[:, :, None] - kpos[:, None, :]
    win_ok = (kpos[:, None, :] >= 0) & (diff >= 0) & (diff < WINDOW)
    meta_ok = jnp.arange(N_META)[None, None, :] <= qpos[:, :, None] - WINDOW
    mask = jnp.concatenate([meta_ok, win_ok], axis=-1)
    o = sink_attention(qb, kb, vb, mask, sinks)
    return o.reshape(b, total, ATTN_WIDTH)[:, pad:]


def sample_attention(q, k, v, meta_k, meta_v, win_k, win_v, sinks):
    b, s = q.shape[:2]
    kc = jnp.concatenate([meta_k, win_k, k], axis=1)[:, None]
    vc = jnp.concatenate([meta_v, win_v, v], axis=1)[:, None]
    qpos = PAST_LEN + jnp.arange(s)
    buf_pos = PAST_LEN - WINDOW + jnp.arange(WINDOW)
    d_buf = qpos[:, None] - buf_pos[None, :]
    buf_ok = (buf_pos[None, :] >= 0) & (d_buf < WINDOW)
    d_new = qpos[:, None] - qpos[None, :]
    new_ok = (d_new >= 0) & (d_new < WINDOW)
    meta_ok = jnp.arange(N_META)[None, :] <= qpos[:, None] - WINDOW
    mask = jnp.concatenate([meta_ok, buf_ok, new_ok], axis=-1)[None]
    o = sink_attention(q.reshape(b, 1, s, N_KV_HEADS, Q_PER_KV, HEAD_DIM), kc, vc, mask, sinks)
    return o.reshape(b, s, ATTN_WIDTH)


def ssm_combine(left, right):
    a_l, b_l = left
    a_r, b_r = right
    return a_l * a_r, a_r * b_l + b_r


def s5_branch(u, h0, a_re, a_im, log_dt, b_re, b_im, c_re, c_im, d_skip, w_glu, b_glu):
    f32 = jnp.float32
    b, t = u.shape[:2]
    lam = lax.complex(a_re.astype(f32), a_im.astype(f32))
    dt = jnp.exp(log_dt.astype(f32))[:, None]
    lam_bar = jnp.exp(lam * dt)
    b_bar = ((lam_bar - 1.0) / lam)[..., None] * lax.complex(b_re.astype(f32), b_im.astype(f32))
    c = lax.complex(c_re.astype(f32), c_im.astype(f32))
    uf = u.astype(f32).reshape(b, t, N_SSM_GROUPS, SSM_GROUP)
    bu = jnp.einsum('gph,btgh->btgp', b_bar, uf.astype(jnp.complex64))
    a = jnp.broadcast_to(lam_bar, bu.shape)
    a_cum, h = lax.associative_scan(ssm_combine, (a, bu), axis=1)
    h = h + a_cum * h0[:, None]
    y = jnp.einsum('ghp,btgp->btgh', c, h).real + d_skip.astype(f32).reshape(N_SSM_GROUPS, SSM_GROUP) * uf
    y = jax.nn.gelu(y.reshape(b, t, SSM_WIDTH))
    y = y * jax.nn.sigmoid(y @ w_glu.astype(f32) + b_glu.astype(f32))
    return y.astype(u.dtype), h[:, -1]


def merge(x, o_attn, z_a, y_ssm, z_s, g_a, g_s, w_attn_out, w_ssm_out, w_out):
    br_a = (o_attn * jax.nn.silu(z_a)) @ w_attn_out
    br_s = (y_ssm * jax.nn.silu(z_s)) @ w_ssm_out
    return x + (jax.nn.sigmoid(g_a) * br_a + jax.nn.sigmoid(g_s) * br_s) @ w_out


def setup_inputs(seed: int = 0) -> dict:
    key = jax.random.key(seed)
    ks = jax.random.split(key, 32)
    f32 = jnp.float32
    nrm = lambda k, shape, scale: jax.random.normal(k, shape, f32) * scale
    n_idx = jnp.arange(SSM_STATE, dtype=f32)
    a_re = -0.5 + nrm(ks[0], (DEPTH, N_SSM_GROUPS, SSM_STATE), 0.01)
    a_im = math.pi * n_idx[None, None, :] + nrm(ks[1], (DEPTH, N_SSM_GROUPS, SSM_STATE), 0.01)
    log_dt = jax.random.uniform(ks[2], (DEPTH, N_SSM_GROUPS), f32, math.log(1e-3), math.log(1e-1))
    return {
        "x_prompt": nrm(ks[3], (BATCH, SEQ, D_MODEL), 1.0),
        "x_sample": nrm(ks[4], (DEC_BATCH, DEC_SEQ, D_MODEL), 1.0),
        "cache_win_k": nrm(ks[5], (DEPTH, DEC_BATCH, WINDOW, N_KV_HEADS, HEAD_DIM), 1.0),
        "cache_win_v": nrm(ks[6], (DEPTH, DEC_BATCH, WINDOW, N_KV_HEADS, HEAD_DIM), 1.0),
        "cache_meta_k": nrm(ks[7], (DEPTH, DEC_BATCH, N_META, N_KV_HEADS, HEAD_DIM), 1.0),
        "cache_meta_v": nrm(ks[8], (DEPTH, DEC_BATCH, N_META, N_KV_HEADS, HEAD_DIM), 1.0),
        "state_ssm_re": nrm(ks[9], (DEPTH, DEC_BATCH, N_SSM_GROUPS, SSM_STATE), 0.5),
        "state_ssm_im": nrm(ks[10], (DEPTH, DEC_BATCH, N_SSM_GROUPS, SSM_STATE), 0.5),
        "meta_tokens": nrm(ks[11], (N_META, D_MODEL), 1.0),
        "norm_gain": 1.0 + nrm(ks[12], (DEPTH, D_MODEL), 0.02),
        "w_in": nrm(ks[13], (DEPTH, D_MODEL, IN_WIDTH), D_MODEL ** -0.5),
        "q_norm_gain": 1.0 + nrm(ks[14], (DEPTH, HEAD_DIM), 0.02),
        "k_norm_gain": 1.0 + nrm(ks[15], (DEPTH, HEAD_DIM), 0.02),
        "sinks": nrm(ks[16], (DEPTH, N_Q_HEADS), 0.5),
        "a_re": a_re,
        "a_im": a_im,
        "log_dt": log_dt,
        "b_re": nrm(ks[17], (DEPTH, N_SSM_GROUPS, SSM_STATE, SSM_GROUP), (2.0 * SSM_GROUP) ** -0.5),
        "b_im": nrm(ks[18], (DEPTH, N_SSM_GROUPS, SSM_STATE, SSM_GROUP), (2.0 * SSM_GROUP) ** -0.5),
        "c_re": nrm(ks[19], (DEPTH, N_SSM_GROUPS, SSM_GROUP, SSM_STATE), (2.0 * SSM_STATE) ** -0.5),
        "c_im": nrm(ks[20], (DEPTH, N_SSM_GROUPS, SSM_GROUP, SSM_STATE), (2.0 * SSM_STATE) ** -0.5),
        "d_skip": nrm(ks[21], (DEPTH, SSM_WIDTH), 1.0),
        "w_glu": nrm(ks[22], (DEPTH, SSM_WIDTH, SSM_WIDTH), SSM_WIDTH ** -0.5),
        "b_glu": nrm(ks[23], (DEPTH, SSM_WIDTH), 0.01),
        "w_attn_out": nrm(ks[24], (DEPTH, ATTN_WIDTH, D_MODEL), ATTN_WIDTH ** -0.5),
        "w_ssm_out": nrm(ks[25], (DEPTH, SSM_WIDTH, D_MODEL), SSM_WIDTH ** -0.5),
        "w_out": nrm(ks[26], (DEPTH, D_MODEL, D_MODEL), D_MODEL ** -0.5),
    }


def reference(x_prompt, x_sample, cache_win_k, cache_win_v, cache_meta_k, cache_meta_v,
              state_ssm_re, state_ssm_im, meta_tokens, norm_gain, w_in, q_norm_gain, k_norm_gain,
              sinks, a_re, a_im, log_dt, b_re, b_im, c_re, c_im, d_skip, w_glu, b_glu,
              w_attn_out, w_ssm_out, w_out):
    b_p = x_prompt.shape[0]
    b_s, s_len = x_sample.shape[:2]
    meta = jnp.broadcast_to(meta_tokens.astype(x_prompt.dtype)[None], (b_p, N_META, D_MODEL))
    xp = jnp.concatenate([meta, x_prompt], axis=1)
    xs = x_sample
    pos_p = jnp.arange(xp.shape[1])
    pos_s = PAST_LEN + jnp.arange(s_len)
    p_win_k, p_win_v, p_meta_k, p_meta_v, p_re, p_im = [], [], [], [], [], []
    s_win_k, s_win_v, s_re, s_im = [], [], [], []
    for l in range(DEPTH):
        ssm_w = (a_re[l], a_im[l], log_dt[l], b_re[l], b_im[l], c_re[l], c_im[l], d_skip[l], w_glu[l], b_glu[l])
        out_w = (w_attn_out[l], w_ssm_out[l], w_out[l])
        q, k, v, z_a, u, z_s, g_a, g_s = branch_inputs(xp, pos_p, norm_gain[l], w_in[l], q_norm_gain[l], k_norm_gain[l])
        o_a = prompt_attention(q, k, v, sinks[l])
        h0 = jnp.zeros((b_p, N_SSM_GROUPS, SSM_STATE), jnp.complex64)
        y_s, h_last = s5_branch(u, h0, *ssm_w)
        xp = merge(xp, o_a, z_a, y_s, z_s, g_a, g_s, *out_w)
        p_win_k.append(k[:, -WINDOW:])
        p_win_v.append(v[:, -WINDOW:])
        p_meta_k.append(k[:, :N_META])
        p_meta_v.append(v[:, :N_META])
        p_re.append(h_last.real)
        p_im.append(h_last.imag)
        q, k, v, z_a, u, z_s, g_a, g_s = branch_inputs(xs, pos_s, norm_gain[l], w_in[l], q_norm_gain[l], k_norm_gain[l])
        o_a = sample_attention(q, k, v, cache_meta_k[l], cache_meta_v[l], cache_win_k[l], cache_win_v[l], sinks[l])
        h0 = lax.complex(state_ssm_re[l].astype(jnp.float32), state_ssm_im[l].astype(jnp.float32))
        y_s, h_last = s5_branch(u, h0, *ssm_w)
        xs = merge(xs, o_a, z_a, y_s, z_s, g_a, g_s, *out_w)
        s_win_k.append(jnp.concatenate([cache_win_k[l], k], axis=1)[:, -WINDOW:])
        s_win_v.append(jnp.concatenate([cache_win_v[l], v], axis=1)[:, -WINDOW:])
        s_re.append(h_last.real)
        s_im.append(h_last.imag)
    y_prompt = xp[:, N_META:]
    return (y_prompt, xs, jnp.stack(p_win_k), jnp.stack(p_win_v), jnp.stack(p_meta_k), jnp.stack(p_meta_v),
            jnp.stack(p_re), jnp.stack(p_im), jnp.stack(s_win_k), jnp.stack(s_win_v), jnp.stack(s_re), jnp.stack(s_im))
```

```python
import os
import numpy as np
import concourse.bass as bass
import concourse.mybir as mybir
from concourse.bass_utils import run_bass_kernel_spmd

F32 = mybir.dt.float32
BF16 = mybir.dt.bfloat16
ALU = mybir.AluOpType
AF = mybir.ActivationFunctionType
AX = mybir.AxisListType

D = 2048
ND = 16
NT = 1024
NS = 16
NTS = NT + NS
EPS = 1e-6
N_CORES = 8
KTRUNC = ''
KUM = 'ab'


class Sched:
    ENGS = ("pe", "act", "dve", "pool", "sp")

    def __init__(self, nc):
        self.nc = nc
        self.ops = []

    def add(self, eng, fn, reads=(), writes=(), dma=None, fence=False):
        self.ops.append(dict(eng=eng, fn=fn, reads=tuple(reads), writes=tuple(writes), dma=dma, fence=fence))
        return len(self.ops) - 1

    def barrier(self, mk):
        keys = set()
        for o in self.ops:
            keys.update(o["reads"])
            keys.update(o["writes"])
        keys = sorted(keys, key=str)
        self.nbar = getattr(self, "nbar", 0) + 1
        bk = "bar%d" % self.nbar
        self.add("act", mk("act"), reads=keys, writes=keys + [bk])
        for e in ("dve", "pool", "sp"):
            self.add(e, mk(e), reads=[bk], writes=["%s_%s" % (bk, e)], dma=("bar_sp" if e == "sp" else None))

    def mark(self, name):
        if not hasattr(self, "marks"):
            self.marks = {}
        self.marks[name] = len(self.ops)

    def emit(self):
        nc = self.nc
        if KTRUNC:
            self.ops = self.ops[:self.marks[KTRUNC]]
        ops = self.ops
        last_w = {}
        readers = {}
        deps = [set() for _ in ops]
        for i, o in enumerate(ops):
            for b in o["reads"]:
                if b in last_w:
                    deps[i].add(last_w[b])
            for b in o["writes"]:
                if b in last_w:
                    deps[i].add(last_w[b])
                for r in readers.get(b, ()):
                    if r != i:
                        deps[i].add(r)
            for b in o["reads"]:
                readers.setdefault(b, []).append(i)
            for b in o["writes"]:
                last_w[b] = i
                readers[b] = []
        needed = set()
        prev_on = {}
        for i, o in enumerate(ops):
            keep = set()
            for d in deps[i]:
                if ops[d]["dma"] is None and ops[d]["eng"] == "pe" and o["eng"] == "pe" and o["dma"] is None:
                    continue
                keep.add(d)
            if o.get("fence") and o["eng"] in prev_on:
                keep.add(prev_on[o["eng"]])
            if o["dma"] is None:
                prev_on[o["eng"]] = i
            deps[i] = keep
            needed |= keep
        dma_keys = sorted({o["dma"] for o in ops if o["dma"] is not None}, key=str)
        sem_ctx = []
        sems = {}
        for e in self.ENGS:
            cm = nc.semaphore("s_" + e)
            sems[("eng", e)] = cm.__enter__()
            sem_ctx.append(cm)
        for n, k in enumerate(dma_keys):
            cm = nc.semaphore("d%d" % n)
            sems[("dma", k)] = cm.__enter__()
            sem_ctx.append(cm)
        cnt = {k: 0 for k in sems}
        ticket = [None] * len(ops)
        for i, o in enumerate(ops):
            if o["dma"] is not None:
                k = ("dma", o["dma"])
                cnt[k] += 16
                ticket[i] = (k, cnt[k])
            elif i in needed:
                k = ("eng", o["eng"])
                cnt[k] += 1
                ticket[i] = (k, cnt[k])
        streams = {e: [] for e in self.ENGS}
        waited = {e: {} for e in self.ENGS}
        for i, o in enumerate(ops):
            e = o["eng"]
            w = {}
            for d in deps[i]:
                k, v = ticket[d]
                if waited[e].get(k, 0) >= v:
                    continue
                w[k] = max(w.get(k, 0), v)
            for k, v in w.items():
                waited[e][k] = v
            streams[e].append((i, w))
        final = {k: v for k, v in cnt.items() if k[0] == "dma" and v > 0}

        def run_stream(e, engobj):
            for i, w in streams[e]:
                for k, v in w.items():
                    engobj.wait_ge(sems[k], v)
                ins = ops[i]["fn"](engobj)
                if ticket[i] is not None:
                    k, v = ticket[i]
                    ins.then_inc(sems[k], 16 if k[0] == "dma" else 1)
            if e == "sp":
                for k, v in final.items():
                    engobj.wait_ge(sems[k], v)

        with nc.Block() as block:
            @block.sync
            def _(eng):
                run_stream("sp", eng)

            @block.tensor
            def _(eng):
                run_stream("pe", eng)

            @block.scalar
            def _(eng):
                run_stream("act", eng)

            @block.vector
            def _(eng):
                run_stream("dve", eng)

            @block.gpsimd
            def _(eng):
                run_stream("pool", eng)
        for cm in reversed(sem_ctx):
            cm.__exit__(None, None, None)


IN_SPECS = [
    ("x_own", [NT, D]), ("x_pre2", [NT, D]), ("x_pre1", [16, D]), ("x_kvx", [144, D]), ("x_smp", [NS, D]),
    ("w_in", [D, 8704]), ("w_glu", [1024, 1024]), ("w_ao", [1024, D]), ("w_so", [1024, D]), ("w_out", [D, D]),
    ("norm_gain", [1, D]), ("qk_gain", [1, 128]), ("sinks", [1, 16]), ("b_glu", [128, 8]), ("dsk", [128, 64]),
    ("rope_cc", [128, 11, 64]), ("rope_ss", [128, 11, 64]),
    ("ident", [128, 128]),
    ("cwk", [NS, 128, 256]), ("cwv", [NS, 128, 256]), ("cmk", [NS, 16, 256]), ("cmv", [NS, 16, 256]),
    ("a_re2", [128, 64]), ("a_im2", [128, 64]), ("logdt", [1, 64]),
    ("b_self", [128, 64, 16]), ("b_part", [128, 64, 16]), ("c_self", [128, 64, 16]), ("c_part", [128, 64, 16]),
    ("kvals", [1, 24]), ("krow", [1, 260]), ("blockmask", [128, 128]), ("swapm", [128, 128]),
    ("st_self", [NS, 64, 128]), ("st_part", [NS, 64, 128]),
    ("maskc", [128, 128]), ("maskp", [128, 128]), ("maskp0", [128, 128]),
]
OUT_SPECS = [
    ("y_own", [NT, D]), ("y_smp", [NS, D]),
    ("pwk", [128, 256]), ("pwv", [128, 256]), ("pmk", [16, 256]), ("pmv", [16, 256]),
    ("pssm", [64, 128]),
    ("swk", [NS, 128, 256]), ("swv", [NS, 128, 256]), ("sssm", [NS, 64, 128]),
]


def build_program():
    nc = bass.Bass("TRN2", target_bir_lowering=False)
    S = Sched(nc)
    din = {n: nc.dram_tensor(n, s, F32, kind="ExternalInput").ap() for n, s in IN_SPECS}
    dout = {n: nc.dram_tensor(n, s, F32, kind="ExternalOutput").ap() for n, s in OUT_SPECS}
    ctxs = []

    def sb(name, shape, dt):
        cm = nc.sbuf_tensor("sb_" + name, shape, dt)
        t = cm.__enter__()
        ctxs.append(cm)
        return t

    def psum(name, shape, dt):
        cm = nc.psum_tensor(name, shape, dt)
        t = cm.__enter__()
        ctxs.append(cm)
        return t

    KB = 1024
    ARENA = 150 * KB
    AR = sb("arena", [128, ARENA // 2], BF16)

    def carve(off, n, dt):
        assert off % 4 == 0
        if dt == BF16:
            assert off + 2 * n <= ARENA, (off, n)
            return AR[:, off // 2: off // 2 + n]
        assert off + 4 * n <= ARENA, (off, n)
        return AR[:, off // 2: off // 2 + 2 * n].bitcast(F32)

    PS = [psum("ps%d" % i, [128, 512], F32) for i in range(8)]
    PSB = [p.bitcast(BF16) for p in PS]
    identf = sb("identf", [128, 128], F32)
    identb = sb("identb", [128, 128], BF16)
    epsb = sb("epsb", [128, 1], F32)
    qkg_bc = sb("qkg_bc", [128, 128], F32)
    ropecc = sb("ropecc", [128, 11, 64], F32)
    ropess = sb("ropess", [128, 11, 64], F32)
    stat = sb("stat", [128, 64], F32)
    dummy = sb("dummyk", [128, 8], F32)
    xnT = sb("xnT", [128, ND, NTS], BF16)
    Um = sb("Um", [128, 64, 18], BF16)
    kvals = sb("kvals", [128, 24], F32)
    krow = sb("krow", [128, 260], F32)
    sgn = sb("sgn", [128, 2], F32)
    halfpi = sb("halfpi", [128, 1], F32)
    onec = sb("onec", [128, 1], F32)
    pp = sb("pp", [128, 16, 64], F32)
    mumag = sb("mumag", [128, 64], F32)
    G258 = sb("G258", [128, 64], F32)
    C258 = sb("C258", [128, 64], F32)
    S258 = sb("S258", [128, 64], F32)
    dsk = sb("dsk", [128, 64], F32)
    blockmask = sb("blockmask", [128, 128], F32)
    swapm = sb("swapm", [128, 128], F32)
    b_glu = sb("b_glu", [128, 8], F32)
    expsink = sb("expsink", [128, 16], F32)
    masks = sb("masks", [128, 3, 128], BF16)
    qs16 = sb("qs16", [NS, 1024], BF16)
    onesb = sb("onesb", [128, 64], BF16)

    u8own = carve(0, 8192, BF16).rearrange("p (g s h) -> p g s h", g=64, s=8)
    u8p2 = carve(16 * KB, 8192, BF16).rearrange("p (g s h) -> p g s h", g=64, s=8)
    wbuf = [carve((68 + 16 * i) * KB, ND * 512, BF16).rearrange("p (c n) -> p c n", c=ND) for i in range(2)]
    u8m = carve(34 * KB, 8192, BF16).rearrange("p (g s h) -> p g s h", g=64, s=8)
    gain_bc = carve(51 * KB, D, F32)
    xbuf = [carve((100 + 8 * i) * KB, D, F32) for i in range(2)]
    xnb = [carve((116 + 4 * i) * KB, D, BF16) for i in range(2)]
    tmpq = [carve((124 + 4 * i) * KB, 1024, F32) for i in range(3)]
    sqj = carve(128 * KB, D, F32)
    xnTt = [carve((136 + 4 * i) * KB, ND * 128, BF16).rearrange("p (c t) -> p c t", c=ND) for i in range(2)]
    kf = [carve((144 + i) * KB, 256, F32) for i in range(2)]
    vf = [carve((146 + i) * KB, 256, F32) for i in range(2)]
    utok1 = carve(148 * KB, 1024, BF16)

    cnt = {"x": 0, "ps": 0, "kv": 0}

    def dma(eng, out, in_, reads, writes, key):
        S.add(eng, lambda e: e.dma_start(out=out, in_=in_), reads=reads, writes=writes, dma=key)

    def mk_bar(e):
        col = {"act": 0, "dve": 1, "pool": 2, "sp": 3}[e]
        if e == "act":
            return lambda en: en.copy(out=dummy[0:1, col:col + 1], in_=dummy[0:1, 4:5])
        if e == "sp":
            return lambda en: en.dma_start(out=dummy[0:1, col:col + 1], in_=din["kvals"][0:1, 0:1])
        return lambda en: en.memset(dummy[0:1, col:col + 1], 0.0)

    S.add("pool", lambda e: e.memset(dummy[:], 0.0), [], ["dummy"])
    dma("sp", identf[:], din["ident"], [], ["identf"], "identf")
    S.add("dve", lambda e: e.tensor_copy(out=identb[:], in_=identf[:]), ["identf"], ["identb"])
    S.add("pool", lambda e: e.memset(epsb[:], EPS), [], ["epsb"])
    dma("sp", gain_bc, din["norm_gain"].partition_broadcast(128), [], ["gain_bc"], "gain_bc")
    dma("sp", qkg_bc[:], din["qk_gain"].partition_broadcast(128), [], ["qkg_bc"], "qkg_bc")
    dma("sp", ropecc[:], din["rope_cc"], [], ["ropecc"], "ropecc")
    dma("sp", ropess[:], din["rope_ss"], [], ["ropess"], "ropess")

    def load_w(slot, src, col0, ncols, nk):
        key = "wbuf%d" % slot
        v = src.rearrange("(c p) n -> p c n", p=128)
        step = 4
        for c0 in range(0, nk, step):
            c1 = min(nk, c0 + step)
            dma("pool", wbuf[slot][:, c0:c1, 0:ncols], v[:, c0:c1, col0:col0 + ncols], [], [key], key)

    def norm_A(src_rows, n):
        j = (cnt["x"] % 2) if (xbuf[0] is not xbuf[1]) else 0
        cnt["x"] += 1
        xb_, xn_ = xbuf[j], xnb[j]
        sq_, g_ = sqj, gain_bc
        kx, kn = "xbuf%d" % j, "xnb%d" % j
        dma("sp", xb_[0:n, :], src_rows, [], [kx], kx)
        S.add("act", lambda e: e.activation(out=sq_[0:n, :], in_=xb_[0:n, :], func=AF.Square), [kx], ["tq1", "tq2a", "tq2b"])
        S.add("dve", lambda e: e.tensor_reduce(out=stat[0:n, 0:1], in_=sq_[0:n, :], axis=AX.X, op=ALU.add), ["tq1", "tq2a", "tq2b"], ["stat0"])
        S.add("act", lambda e: e.activation(out=stat[0:n, 1:2], in_=stat[0:n, 0:1], func=AF.Sqrt, bias=epsb[0:n, 0:1], scale=1.0 / D),
              ["stat0", "epsb"], ["stat1"])
        S.add("dve", lambda e: e.reciprocal(out=stat[0:n, 2:3], in_=stat[0:n, 1:2]), ["stat1"], ["stat2"])
        S.add("dve", lambda e: e.scalar_tensor_tensor(out=xn_[0:n, :], in0=xb_[0:n, :], scalar=stat[0:n, 2:3], in1=g_[0:n, :],
                                                      op0=ALU.mult, op1=ALU.mult), [kx, "stat2", "gain_bc"], [kn])
        return xn_, kn, n

    def norm_B(h, dst_fn, dst_keys):
        xn_, kn, n = h
        for half in range(2):
            b = 4 + (cnt["ps"] % 2)
            cnt["ps"] += 1
            kb = "PS%d" % b
            pv = PSB[b][:, 0:1024].rearrange("p (c t) -> p c t", c=8)
            for c in range(8):
                dc = half * 8 + c
                S.add("pe", lambda e, c=c, dc=dc, pv=pv: e.transpose(out=pv[:, c, 0:n], in_=xn_[0:n, dc * 128:(dc + 1) * 128],
                                                                      identity=identb[0:n, 0:n]), [kn, "identb"], [kb])
            dst = dst_fn(half * 8)
            if half == 0:
                S.add("act", lambda e, pv=pv, dst=dst: e.copy(out=dst, in_=pv[:, :, 0:n]), [kb], dst_keys)
            else:
                S.add("dve", lambda e, pv=pv, dst=dst: e.tensor_copy(out=dst, in_=pv[:, :, 0:n]), [kb], dst_keys)

    def norm_tile(src_rows, n, dst_fn, dst_keys):
        norm_B(norm_A(src_rows, n), dst_fn, dst_keys)

    def headnorm_rope(src_ps, src_keys, n, NH, ti, goff, extra_scale, dst, dst_keys):
        W = NH * 64
        t0, t1, t2 = tmpq
        c0 = 0
        for ap_, w in src_ps:
            S.add("act", lambda e, ap_=ap_, c0=c0, w=w: e.copy(out=t0[0:n, c0:c0 + w], in_=ap_), src_keys, ["tq0"])
            S.add("act", lambda e, ap_=ap_, c0=c0, w=w: e.activation(out=t1[0:n, c0:c0 + w], in_=ap_, func=AF.Square), src_keys, ["tq1"])
            c0 += w
        v3 = lambda t: t[0:n, 0:W].rearrange("p (h d) -> p h d", d=64)
        S.add("dve", lambda e: e.tensor_reduce(out=stat[0:n, 8:8 + NH], in_=v3(t1), axis=AX.X, op=ALU.add), ["tq1"], ["stq"])
        S.add("act", lambda e: e.activation(out=stat[0:n, 24:24 + NH], in_=stat[0:n, 8:8 + NH], func=AF.Sqrt, bias=epsb[0:n, 0:1], scale=1.0 / 64),
              ["stq", "epsb"], ["stq2"])
        S.add("dve", lambda e: e.reciprocal(out=stat[0:n, 40:40 + NH], in_=stat[0:n, 24:24 + NH]), ["stq2"], ["stq3"])
        if extra_scale != 1.0:
            S.add("dve", lambda e: e.tensor_scalar(out=stat[0:n, 40:40 + NH], in0=stat[0:n, 40:40 + NH], scalar1=float(extra_scale), scalar2=0.0,
                                                   op0=ALU.mult, op1=ALU.add), ["stq3"], ["stq3"])
        gb = qkg_bc[0:n, goff:goff + 64].unsqueeze(1).to_broadcast([n, NH, 64])
        S.add("dve", lambda e: e.tensor_tensor(out=v3(t0), in0=v3(t0), in1=gb, op=ALU.mult), ["tq0", "qkg_bc"], ["tq0"])
        cc = ropecc[0:n, ti, :].unsqueeze(1).to_broadcast([n, NH, 64])
        S.add("dve", lambda e: e.tensor_tensor(out=v3(t1), in0=v3(t0), in1=cc, op=ALU.mult), ["tq0", "ropecc", "stq"], ["tq1"])
        s_lo = ropess[0:n, ti, 0:32].unsqueeze(1).to_broadcast([n, NH, 32])
        s_hi = ropess[0:n, ti, 32:64].unsqueeze(1).to_broadcast([n, NH, 32])
        S.add("pool", lambda e: e.tensor_tensor(out=v3(t2)[:, :, 0:32], in0=v3(t0)[:, :, 32:64], in1=s_lo, op=ALU.mult), ["tq0", "ropess"], ["tq2a"])
        S.add("pool", lambda e: e.tensor_tensor(out=v3(t2)[:, :, 32:64], in0=v3(t0)[:, :, 0:32], in1=s_hi, op=ALU.mult), ["tq0", "ropess"], ["tq2b"])
        S.add("dve", lambda e: e.tensor_tensor(out=v3(t1), in0=v3(t1), in1=v3(t2), op=ALU.add), ["tq1", "tq2a", "tq2b"], ["tq1"])
        rb = stat[0:n, 40:40 + NH].unsqueeze(2).to_broadcast([n, NH, 64])
        dv = dst.rearrange("p (h d) -> p h d", d=64)
        S.add("dve", lambda e: e.tensor_tensor(out=dv, in0=v3(t1), in1=rb, op=ALU.mult), ["tq1", "stq3"], dst_keys)

    def proj_tok(lhs_fn, lhs_keys, n, slot, ncols, bank):
        kb = "PS%d" % bank
        for dc in range(ND):
            S.add("pe", lambda e, dc=dc: e.matmul(PS[bank][0:n, 0:ncols], lhsT=lhs_fn(dc), rhs=wbuf[slot][:, dc, 0:ncols],
                                                  start=(dc == 0), stop=(dc == ND - 1)), lhs_keys + ["wbuf%d" % slot], [kb])
        return kb

    pass
    S.mark('p1a')
    S.add("pool", lambda e: e.memset(u8m, 0.0), [], ["u8m"])
    load_w(0, din["w_in"], 2560, 512, ND)
    load_w(1, din["w_in"], 3072, 512, ND)

    def u_tile(lhs_fn, lhs_keys, n, evac_fn):
        for blk in range(2):
            b = cnt["kv"] % 2
            cnt["kv"] += 1
            kb = proj_tok(lhs_fn, lhs_keys, n, blk, 512, b)
            evac_fn(blk, PS[b], kb)

    S.mark("u_pre1")
    xp2 = din["x_pre2"].rearrange("(c s) d -> c s d", s=8)
    tl = [("smp", 0)]
    for i in range(8):
        tl += [("own", i), ("pre", i)]

    def p1_src(kind, i):
        if kind == "smp":
            return din["x_smp"], NS
        if kind == "own":
            return din["x_own"][i * 128:(i + 1) * 128, :], 128
        return xp2[:, i, :], 128

    def p1_finish(kind, i, h):
        if kind == "smp":
            norm_B(h, lambda dc0: xnT[:, dc0:dc0 + 8, NT:NTS], ["xnT_s"])
        elif kind == "own":
            norm_B(h, lambda dc0, i=i: xnT[:, dc0:dc0 + 8, i * 128:(i + 1) * 128], ["xnT_%d" % i])
        else:
            jt = i % 2
            tt = xnTt[jt]
            norm_B(h, lambda dc0, tt=tt: tt[:, dc0:dc0 + 8, 0:128], ["xnTt%d" % jt])

            def ev_p2(blk, ps_, kb, i=i):
                S.add("act", lambda e: e.copy(out=u8p2[:, blk * 32:(blk + 1) * 32, i, :], in_=ps_[:, 0:512].rearrange("p (g h) -> p g h", h=16)),
                      [kb], ["u8p2_%d_%d" % (i, blk)])
            u_tile(lambda dc, tt=tt: tt[:, dc, 0:128], ["xnTt%d" % jt], 128, ev_p2)

    pend1 = None
    for (kind, i) in tl:
        src_, n_ = p1_src(kind, i)
        h_ = norm_A(src_, n_)
        if pend1 is not None:
            p1_finish(*pend1)
        pend1 = (kind, i, h_)
    p1_finish(*pend1)
    S.mark("u_pre2")
    OWNK = ["xnT_%d" % i for i in range(8)]
    for i in range(8):
        def ev_own(blk, ps_, kb, i=i):
            S.add("act", lambda e: e.copy(out=u8own[:, blk * 32:(blk + 1) * 32, i, :], in_=ps_[:, 0:512].rearrange("p (g h) -> p g h", h=16)),
                  [kb], ["u8own_%d_%d" % (i, blk)])
        u_tile(lambda dc, i=i: xnT[:, dc, i:NT:8], OWNK, 128, ev_own)

    S.mark("u_own")

    def ev_smp(blk, ps_, kb):
        S.add("act", lambda e: e.copy(out=u8m[0:NS, blk * 32:(blk + 1) * 32, 7, :], in_=ps_[0:NS, 0:512].rearrange("p (g h) -> p g h", h=16)),
              [kb, "u8m"], ["u8m_s%d" % blk])
    u_tile(lambda dc: xnT[:, dc, NT:NTS], ["xnT_s"], NS, ev_smp)
    j = cnt["x"] % 2
    tt = xnTt[j]
    norm_tile(din["x_pre1"], 16, lambda dc0, tt=tt: tt[:, dc0:dc0 + 8, 0:16], ["xnTt%d" % j])

    def ev_pre1(blk, ps_, kb):
        S.add("act", lambda e: e.copy(out=utok1[0:16, blk * 512:(blk + 1) * 512], in_=ps_[0:16, 0:512]), [kb], ["utok1_%d" % blk])
    u_tile(lambda dc, tt=tt: tt[:, dc, 0:16], ["xnTt%d" % j], 16, ev_pre1)
    for s_ in range(8):
        dma("sp", u8m[32:34, :, s_, :], utok1[s_:16:8, :].rearrange("p (g h) -> p g h", h=16), ["utok1_0", "utok1_1", "u8m"], ["u8m_p1"], "u8m_p1")

    S.mark("u_smp")
    U8K_P2 = ["u8p2_%d_%d" % (i, b) for i in range(8) for b in range(2)]
    U8K_OWN = ["u8own_%d_%d" % (i, b) for i in range(8) for b in range(2)]
    U8K_M = ["u8m", "u8m_p1", "u8m_s0", "u8m_s1"]
    for part in range(2):
        for q4 in range(4):
            bk = 6 + q4 % 2
            kb = "PS%d" % bk
            pv = PSB[bk][:, 0:1024].rearrange("p (g t) -> p g t", t=64)
            for gl in range(16):
                gg = q4 * 16 + gl
                fence = (part == 1 and q4 == 0 and gl == 0)
                if part == 0:
                    S.add("pe", lambda e, gl=gl, gg=gg, pv=pv: e.transpose(out=pv[:, gl, 0:2], in_=u8m[32:34, gg, :, :].rearrange("p s h -> p (s h)"),
                                                                            identity=identb[32:34, 32:34]), U8K_M + ["identb"], [kb])
                else:
                    S.add("pe", lambda e, gl=gl, gg=gg, pv=pv: e.transpose(out=pv[:, gl, 0:16], in_=u8m[0:NS, gg, :, :].rearrange("p s h -> p (s h)"),
                                                                            identity=identb[0:NS, 0:NS]), U8K_M + ["identb"], [kb], fence=fence)
            if part == 0:
                S.add("dve", lambda e, q4=q4, pv=pv: e.tensor_copy(out=Um[:, q4 * 16:(q4 + 1) * 16, 0:2], in_=pv[:, :, 0:2]), [kb], ["Um_%d" % q4])
            else:
                S.add("dve", lambda e, q4=q4, pv=pv: e.tensor_copy(out=Um[:, q4 * 16:(q4 + 1) * 16, 2:18], in_=pv[:, :, 0:16]), [kb], ["Umb_%d" % q4])
    UMK = ["Um_%d" % q for q in range(4)] + ["Umb_%d" % q for q in range(4)]

    S.mark('p1b')
    S.barrier(mk_bar)
    S.mark('bar1')
    MAGIC = 12582912.0
    TWO_PI = float(2.0 * np.pi)
    SC = 1036
    GB = 4
    Ere = carve(51 * KB, 1536, F32).rearrange("p (g k) -> p g k", k=24)
    Eim = carve(57 * KB, 1536, F32).rearrange("p (g k) -> p g k", k=24)
    bbs = carve(63 * KB, 1024, F32).rearrange("p (g h) -> p g h", h=16)
    bbp = carve(67 * KB, 1024, F32).rearrange("p (g h) -> p g h", h=16)
    cself = carve(71 * KB, 1024, F32).rearrange("p (g h) -> p g h", h=16)
    cpart = carve(75 * KB, 1024, F32).rearrange("p (g h) -> p g h", h=16)
    sc = [carve(79 * KB + i * 4352, SC, F32) for i in range(4)]
    cosT = carve(96 * KB, GB * 259, F32).rearrange("p (g k) -> p g k", k=259)
    sinT = carve(96 * KB + 4352, GB * 259, F32).rearrange("p (g k) -> p g k", k=259)
    o_ = 96 * KB + 2 * 4352
    A32 = carve(o_, GB * 128, F32).rearrange("p (g q) -> p g q", q=128); o_ += 2 * KB
    R32 = carve(o_, GB * 128, F32).rearrange("p (g q) -> p g q", q=128); o_ += 2 * KB
    Ab = carve(o_, 2 * GB * 128, BF16).rearrange("p (w g q) -> p w g q", w=2, q=128); o_ += 2 * KB
    Mab = carve(o_, 2 * GB * 128, BF16).rearrange("p (w g q) -> p w g q", w=2, q=128); o_ += 2 * KB
    Msb = carve(o_, 2 * GB * 128, BF16).rearrange("p (w g q) -> p w g q", w=2, q=128); o_ += 2 * KB
    Mib = carve(o_, GB * 128, BF16).rearrange("p (g q) -> p g q", q=128); o_ += KB
    mt = []
    for i in range(2):
        mt.append(carve(o_, GB * 128, F32).rearrange("p (g q) -> p g q", q=128)); o_ += 2 * KB
    NCH = 258
    NU = 274
    Ub = carve(o_, GB * NU, BF16).rearrange("p (g t) -> p g t", t=NU); o_ += 2304
    Xt = carve(o_, GB * NCH, F32).rearrange("p (g t) -> p g t", t=NCH); o_ += 4352
    Xt2 = [carve(o_ + 1056 * i, NCH, F32) for i in range(2)]; o_ += 2112
    Gs = carve(o_, GB * 259, F32).rearrange("p (g k) -> p g k", k=259); o_ += 4352
    Xsm = carve(o_, GB * NS, F32).rearrange("p (g b) -> p g b", b=NS); o_ += 256
    st_s = carve(o_, GB * 128, F32).rearrange("p (g q) -> p g q", q=128); o_ += 2 * KB
    st_p = carve(o_, GB * 128, F32).rearrange("p (g q) -> p g q", q=128); o_ += 2 * KB
    hs1 = carve(o_, GB * NS, F32).rearrange("p (g b) -> p g b", b=NS); o_ += 256
    hs2 = carve(o_, GB * NS, F32).rearrange("p (g b) -> p g b", b=NS); o_ += 256
    hso = carve(o_, GB * 128, F32).rearrange("p (g q) -> p g q", q=128); o_ += 2 * KB
    sE = carve(o_, 2 * GB * 24, F32).rearrange("p (w g k) -> p w g k", w=2, k=24); o_ += KB
    hfo = carve(o_, 128, F32); o_ += 512
    assert o_ <= ARENA, o_
    o2 = o_
    Pcs = carve(o2, 2 * GB * 128, BF16).rearrange("p (w g q) -> p w g q", w=2, q=128); o2 += 2 * KB
    y8 = carve(o2, 8 * 128, BF16).rearrange("p (t f) -> p t f", f=128); o2 += 2 * KB
    y8s = carve(o2, 128, BF16); o2 += 256
    assert o2 <= ARENA, o2
    yA = sb("yA", [128, GB * 128], F32)[:].rearrange("p (g q) -> p g q", q=128)
    yB = sb("yB", [128, GB * 128], F32)[:].rearrange("p (g q) -> p g q", q=128)
    b3 = o2
    ygb = carve(b3, GB * 128, BF16).rearrange("p (g q) -> p g q", q=128)
    ysA = carve(b3 + 1024, GB * NS, F32).rearrange("p (g b) -> p g b", b=NS)
    ysB = carve(b3 + 1280, GB * NS, F32).rearrange("p (g b) -> p g b", b=NS)
    ygs = carve(b3 + 1536, GB * NS, BF16).rearrange("p (g b) -> p g b", b=NS)
    hn1 = carve(b3 + 1664, GB * NS, F32).rearrange("p (g b) -> p g b", b=NS)
    hn2 = carve(b3 + 1920, GB * NS, F32).rearrange("p (g b) -> p g b", b=NS)
    Hins = carve(b3 + 2176, GB * NS, BF16).rearrange("p (g b) -> p g b", b=NS)
    assert b3 + 2304 <= ARENA, b3
    ST = carve(34 * KB, 8 * NTS, BF16).rearrange("p (c t) -> p c t", t=NTS)
    bself = carve(96 * KB, 1024, F32).rearrange("p (g h) -> p g h", h=16)
    bpart = carve(100 * KB, 1024, F32).rearrange("p (g h) -> p g h", h=16)
    a_re2 = carve(104 * KB, 64, F32)
    a_im2 = carve(104 * KB + 256, 64, F32)
    ldt = carve(104 * KB + 512, 64, F32)

    dma("sp", a_re2, din["a_re2"], [], ["a_re2"], "a_re2")
    dma("sp", a_im2, din["a_im2"], [], ["a_im2"], "a_im2")
    dma("sp", ldt, din["logdt"].partition_broadcast(128), [], ["ldt"], "ldt")
    dma("sp", bself, din["b_self"], [], ["bself"], "bself")
    dma("sp", bpart, din["b_part"], [], ["bpart"], "bpart")
    dma("sp", cself, din["c_self"], [], ["cself"], "cself")
    dma("sp", cpart, din["c_part"], [], ["cpart"], "cpart")
    dma("sp", kvals[:], din["kvals"].partition_broadcast(128), [], ["kvals"], "kvals")
    dma("sp", krow[:], din["krow"].partition_broadcast(128), [], ["krow"], "krow")
    dma("sp", blockmask[:], din["blockmask"], [], ["blockmask"], "blockmask")
    dma("sp", swapm[:], din["swapm"], [], ["swapm"], "swapm")
    dma("sp", dsk[:], din["dsk"], [], ["dsk"], "dsk")
    S.add("pool", lambda e: e.memset(sgn[0:64, 0:1], -1.0), [], ["sgn_a"])
    S.add("pool", lambda e: e.memset(sgn[64:128, 0:1], 1.0), [], ["sgn_b"])
    S.add("pool", lambda e: e.memset(sgn[0:64, 1:2], 1.0), [], ["sgn_c"])
    S.add("pool", lambda e: e.memset(sgn[64:128, 1:2], -1.0), [], ["sgn_d"])
    S.add("pool", lambda e: e.memset(halfpi[:], float(np.pi / 2)), [], ["halfpi"])
    S.add("pool", lambda e: e.memset(onec[:], 1.0), [], ["onec"])
    SGN = ["sgn_a", "sgn_b", "sgn_c", "sgn_d"]

    def sincos(ang, cos_out, sin_out, F, rk, wk_cos, wk_sin):
        tA, tB, tC = sc[1][:, 0:F], sc[2][:, 0:F], sc[3][:, 0:F]
        S.add("dve", lambda e: e.tensor_scalar(out=tA, in0=ang, scalar1=float(1.0 / TWO_PI), scalar2=MAGIC, op0=ALU.mult, op1=ALU.add), rk, ["sc1"])
        S.add("dve", lambda e: e.tensor_scalar(out=tA, in0=tA, scalar1=-MAGIC, scalar2=-TWO_PI, op0=ALU.add, op1=ALU.mult), ["sc1"], ["sc1"])
        S.add("dve", lambda e: e.tensor_tensor(out=tA, in0=tA, in1=ang, op=ALU.add), ["sc1"] + rk, ["sc1"])
        S.add("act", lambda e: e.activation(out=tB, in_=tA, func=AF.Sin, scale=0.5), ["sc1"], ["sc2"])
        S.add("act", lambda e: e.activation(out=tC, in_=tA, func=AF.Sin, scale=0.5, bias=halfpi[:, 0:1]), ["sc1", "halfpi"], ["sc3"])
        S.add("dve", lambda e: e.scalar_tensor_tensor(out=sin_out, in0=tB, scalar=2.0, in1=tC, op0=ALU.mult, op1=ALU.mult), ["sc2", "sc3"], wk_sin)
        S.add("act", lambda e: e.activation(out=tA, in_=tB, func=AF.Square, scale=float(np.sqrt(2.0))), ["sc2", "sc1"], ["sc1"])
        S.add("act", lambda e: e.activation(out=cos_out, in_=tA, func=AF.Identity, scale=-1.0, bias=onec[:, 0:1]), ["sc1", "onec"], wk_cos)

    PPK = lambda i: "pp%d" % i
    S.add("act", lambda e: e.activation(out=pp[:, 0, :], in_=ldt, func=AF.Exp), ["ldt"], [PPK(0)])
    S.add("dve", lambda e: e.tensor_tensor(out=pp[:, 1, :], in0=a_re2, in1=pp[:, 0, :], op=ALU.mult), ["a_re2", PPK(0)], [PPK(1)])
    S.add("dve", lambda e: e.tensor_tensor(out=pp[:, 2, :], in0=a_im2, in1=pp[:, 0, :], op=ALU.mult), ["a_im2", PPK(0)], [PPK(2)])
    kv_b = kvals[:].unsqueeze(1).to_broadcast([128, 32, 24])
    v24 = lambda t: t[:, 0:768].rearrange("p (g k) -> p g k", k=24)
    for hf in range(2):
        gh = slice(hf * 32, hf * 32 + 32)
        S.add("dve", lambda e, gh=gh: e.tensor_tensor(out=v24(sc[0]), in0=pp[:, 2, gh].unsqueeze(2).to_broadcast([128, 32, 24]), in1=kv_b, op=ALU.mult),
              [PPK(2), "kvals", "sc0"], ["sc0"])
        ere_h = Ere[:, gh, :].rearrange("p g k -> p (g k)")
        eim_h = Eim[:, gh, :].rearrange("p g k -> p (g k)")
        sincos(sc[0][:, 0:768], ere_h, eim_h, 768, ["sc0"], ["Ere"], ["Eim"])
        S.add("dve", lambda e, gh=gh: e.tensor_tensor(out=v24(sc[0]), in0=pp[:, 1, gh].unsqueeze(2).to_broadcast([128, 32, 24]), in1=kv_b, op=ALU.mult),
              [PPK(1), "kvals", "sc0"], ["sc0"])
        S.add("act", lambda e: e.activation(out=sc[0][:, 0:768], in_=sc[0][:, 0:768], func=AF.Exp), ["sc0"], ["sc0"])
        S.add("act", lambda e, gh=gh: e.copy(out=mumag[:, gh], in_=v24(sc[0])[:, :, 23]), ["sc0"], ["mumag"])
        S.add("dve", lambda e, ere_h=ere_h: e.tensor_tensor(out=ere_h, in0=ere_h, in1=sc[0][:, 0:768], op=ALU.mult), ["Ere", "sc0"], ["Ere"])
        S.add("dve", lambda e, eim_h=eim_h: e.tensor_tensor(out=eim_h, in0=eim_h, in1=sc[0][:, 0:768], op=ALU.mult), ["Eim", "sc0"], ["Eim"])
    S.add("dve", lambda e: e.tensor_scalar(out=pp[:, 6, :], in0=pp[:, 2, :], scalar1=float(8.0 / TWO_PI), scalar2=MAGIC, op0=ALU.mult, op1=ALU.add), [PPK(2)], [PPK(6)])
    S.add("dve", lambda e: e.tensor_scalar(out=pp[:, 6, :], in0=pp[:, 6, :], scalar1=-MAGIC, scalar2=-TWO_PI, op0=ALU.add, op1=ALU.mult), [PPK(6)], [PPK(6)])
    S.add("dve", lambda e: e.scalar_tensor_tensor(out=pp[:, 3, :], in0=pp[:, 2, :], scalar=8.0, in1=pp[:, 6, :], op0=ALU.mult, op1=ALU.add), [PPK(2), PPK(6)], [PPK(3)])
    e1r, e1i = Ere[:, :, 16], Eim[:, :, 16]
    S.add("dve", lambda e: e.tensor_scalar(out=pp[:, 7, :], in0=e1r, scalar1=-1.0, scalar2=0.0, op0=ALU.add, op1=ALU.add), ["Ere"], [PPK(7)])
    S.add("dve", lambda e: e.tensor_tensor(out=pp[:, 8, :], in0=a_re2, in1=a_re2, op=ALU.mult), ["a_re2"], [PPK(8)])
    S.add("dve", lambda e: e.tensor_tensor(out=pp[:, 9, :], in0=a_im2, in1=a_im2, op=ALU.mult), ["a_im2"], [PPK(9)])
    S.add("dve", lambda e: e.tensor_tensor(out=pp[:, 8, :], in0=pp[:, 8, :], in1=pp[:, 9, :], op=ALU.add), [PPK(8), PPK(9)], [PPK(8)])
    S.add("dve", lambda e: e.reciprocal(out=pp[:, 8, :], in_=pp[:, 8, :]), [PPK(8)], [PPK(8)])
    S.add("dve", lambda e: e.tensor_tensor(out=pp[:, 9, :], in0=pp[:, 7, :], in1=a_re2, op=ALU.mult), [PPK(7), "a_re2", PPK(8)], [PPK(9)])
    S.add("dve", lambda e: e.tensor_tensor(out=pp[:, 10, :], in0=e1i, in1=a_im2, op=ALU.mult), ["Eim", "a_im2"], [PPK(10)])
    S.add("dve", lambda e: e.tensor_tensor(out=pp[:, 9, :], in0=pp[:, 9, :], in1=pp[:, 10, :], op=ALU.add), [PPK(9), PPK(10)], [PPK(9)])
    S.add("dve", lambda e: e.tensor_tensor(out=pp[:, 4, :], in0=pp[:, 9, :], in1=pp[:, 8, :], op=ALU.mult), [PPK(9), PPK(8)], [PPK(4)])
    S.add("dve", lambda e: e.tensor_tensor(out=pp[:, 9, :], in0=e1i, in1=a_re2, op=ALU.mult), ["Eim", "a_re2", PPK(4)], [PPK(9)])
    S.add("dve", lambda e: e.tensor_tensor(out=pp[:, 10, :], in0=pp[:, 7, :], in1=a_im2, op=ALU.mult), [PPK(7), "a_im2", PPK(9)], [PPK(10)])
    S.add("dve", lambda e: e.tensor_tensor(out=pp[:, 9, :], in0=pp[:, 9, :], in1=pp[:, 10, :], op=ALU.subtract), [PPK(9), PPK(10)], [PPK(9)])
    S.add("dve", lambda e: e.tensor_tensor(out=pp[:, 9, :], in0=pp[:, 9, :], in1=pp[:, 8, :], op=ALU.mult), [PPK(9), PPK(8)], [PPK(9)])
    S.add("dve", lambda e: e.tensor_scalar(out=pp[:, 5, :], in0=pp[:, 9, :], scalar1=sgn[:, 0:1], scalar2=1.0, op0=ALU.mult, op1=ALU.mult), [PPK(9)] + SGN, [PPK(5)])
    wr_b = pp[:, 4, :].unsqueeze(2).to_broadcast([128, 64, 16])
    swi_b = pp[:, 5, :].unsqueeze(2).to_broadcast([128, 64, 16])
    v16 = lambda t: t[:, 0:1024].rearrange("p (g h) -> p g h", h=16)
    S.add("dve", lambda e: e.tensor_tensor(out=v16(sc[0]), in0=bself, in1=wr_b, op=ALU.mult), ["bself", PPK(4), "sc0"], ["sc0"])
    S.add("pool", lambda e: e.tensor_tensor(out=v16(sc[1]), in0=bpart, in1=swi_b, op=ALU.mult), ["bpart", PPK(5), "sc1"], ["sc1"])
    S.add("dve", lambda e: e.tensor_tensor(out=bbs, in0=v16(sc[0]), in1=v16(sc[1]), op=ALU.add), ["sc0", "sc1"], ["bbs"])
    S.add("dve", lambda e: e.tensor_tensor(out=v16(sc[0]), in0=bpart, in1=wr_b, op=ALU.mult), ["bpart", PPK(4), "sc0"], ["sc0"])
    S.add("pool", lambda e: e.tensor_tensor(out=v16(sc[1]), in0=bself, in1=swi_b, op=ALU.mult), ["bself", PPK(5), "sc1"], ["sc1"])
    S.add("dve", lambda e: e.tensor_tensor(out=bbp, in0=v16(sc[0]), in1=v16(sc[1]), op=ALU.subtract), ["sc0", "sc1"], ["bbp"])
    S.mark('params')
    S.barrier(mk_bar)
    S.mark('bar2')

    S.add("pool", lambda e: e.memset(Gs, 0.0), [], ["Gs"])

    def bc4(T, V):
        return (T.unsqueeze(3).to_broadcast([128, GB, 8, 16]), V.unsqueeze(2).to_broadcast([128, GB, 8, 16]))

    def v4(t):
        return t.rearrange("p g (s h) -> p g s h", h=16)

    def gen_mat(out_ap, T1, V1, T2, V2, op2, rk, wk, add_eng="dve"):
        a0, a1 = bc4(T1, V1)
        b0, b1 = bc4(T2, V2)
        S.add("dve", lambda e: e.tensor_tensor(out=v4(mt[0]), in0=a0, in1=a1, op=ALU.mult), rk, ["mt0"])
        S.add("pool", lambda e: e.tensor_tensor(out=v4(mt[1]), in0=b0, in1=b1, op=ALU.mult), rk, ["mt1"])
        S.add(add_eng, lambda e: e.tensor_tensor(out=out_ap, in0=mt[0], in1=mt[1], op=op2), ["mt0", "mt1"], wk)

    def ssm_frontA(gb):
        g8 = slice(gb * GB, gb * GB + GB)
        sEim, nsEre = sE[:, 0, :, :], sE[:, 1, :, :]
        EreB, EimB = Ere[:, g8, :], Eim[:, g8, :]
        UBK = ["Ub_%d" % g for g in range(GB)] + ["Ub_a%d" % g for g in range(GB)] + ["Ub_b%d" % g for g in range(GB)]
        GSK = ["Gs_%d" % g for g in range(GB)]
        S.add("dve", lambda e, g8=g8: e.tensor_scalar(out=sE[:, 0, :, :], in0=Eim[:, g8, :], scalar1=sgn[:, 0:1], scalar2=1.0, op0=ALU.mult, op1=ALU.mult), ["Eim"] + SGN, ["sE0"])
        S.add("dve", lambda e, g8=g8: e.tensor_scalar(out=sE[:, 1, :, :], in0=Ere[:, g8, :], scalar1=sgn[:, 1:2], scalar2=1.0, op0=ALU.mult, op1=ALU.mult), ["Ere"] + SGN, ["sE1"])
        sEim, nsEre = sE[:, 0, :, :], sE[:, 1, :, :]
        EreB, EimB = Ere[:, g8, :], Eim[:, g8, :]
        gen_mat(A32, EreB[:, :, 0:8], bbs[:, g8, :], sEim[:, :, 0:8], bbp[:, g8, :], ALU.add, ["Ere", "sE0", "bbs", "bbp"], ["A32"])
        S.add("act", lambda e: e.copy(out=Ab[:, 0, :, :], in_=A32), ["A32"], ["Ab0"])
        gen_mat(Ab[:, 1, :, :], nsEre[:, :, 0:8], bbp[:, g8, :], EimB[:, :, 0:8], bbs[:, g8, :], ALU.add, ["Eim", "sE1", "bbs", "bbp"], ["Ab1"])
        pv = PSB[4][:, 0:1024].rearrange("p (w g t) -> p w g t", w=2, g=GB)
        for w_ in range(2):
            for g in range(GB):
                S.add("pe", lambda e, w_=w_, g=g, pv=pv: e.transpose(out=pv[:, w_, g, :], in_=Ab[:, w_, g, :], identity=identb[:]), ["Ab%d" % w_, "identb"], ["PS4"])
        S.add("act", lambda e, pv=pv: e.copy(out=Msb, in_=pv), ["PS4"], ["Msb"])

    def ssm_sincos(gb):
        g8 = slice(gb * GB, gb * GB + GB)
        sEim, nsEre = sE[:, 0, :, :], sE[:, 1, :, :]
        EreB, EimB = Ere[:, g8, :], Eim[:, g8, :]
        UBK = ["Ub_%d" % g for g in range(GB)] + ["Ub_a%d" % g for g in range(GB)] + ["Ub_b%d" % g for g in range(GB)]
        GSK = ["Gs_%d" % g for g in range(GB)]
        S.add("pool", lambda e, g8=g8: e.tensor_tensor(out=sc[0][:, 0:GB * 259].rearrange("p (g k) -> p g k", k=259),
                                                       in0=pp[:, 3, g8].unsqueeze(2).to_broadcast([128, GB, 259]),
                                                       in1=krow[:, 0:259].unsqueeze(1).to_broadcast([128, GB, 259]), op=ALU.mult),
              [PPK(3), "krow", "sc0"], ["sc0"])
        sincos(sc[0][:, 0:GB * 259], cosT.rearrange("p g k -> p (g k)"), sinT.rearrange("p g k -> p (g k)"), GB * 259, ["sc0"], ["cosT"], ["sinT"])
        S.add("act", lambda e, gb=gb: e.copy(out=C258[:, gb * GB:gb * GB + GB], in_=cosT[:, :, 258]), ["cosT"], ["C258"])
        S.add("act", lambda e, gb=gb: e.copy(out=S258[:, gb * GB:gb * GB + GB], in_=sinT[:, :, 258]), ["sinT"], ["S258"])

    def ssm_frontB(gb):
        g8 = slice(gb * GB, gb * GB + GB)
        sEim, nsEre = sE[:, 0, :, :], sE[:, 1, :, :]
        EreB, EimB = Ere[:, g8, :], Eim[:, g8, :]
        UBK = ["Ub_%d" % g for g in range(GB)] + ["Ub_a%d" % g for g in range(GB)] + ["Ub_b%d" % g for g in range(GB)]
        GSK = ["Gs_%d" % g for g in range(GB)]
        gen_mat(R32, nsEre[:, :, 8:16], cself[:, g8, :], EimB[:, :, 8:16], cpart[:, g8, :], ALU.subtract, ["Eim", "sE1", "cself", "cpart"], ["R32"])
        gen_mat(Mab[:, 0, :, :], nsEre[:, :, 16:24], cself[:, g8, :], EimB[:, :, 16:24], cpart[:, g8, :], ALU.subtract, ["Eim", "sE1", "cself", "cpart"], ["Mab0"])
        gen_mat(Mab[:, 1, :, :], sEim[:, :, 16:24], cself[:, g8, :], EreB[:, :, 16:24], cpart[:, g8, :], ALU.subtract, ["Ere", "sE0", "cself", "cpart"], ["Mab1"])
        for g0 in range(0, GB, 2):
            bk = 6 + (g0 // 2) % 2
            kb = "PS%d" % bk
            pv2 = PSB[bk][:, 30:30 + 640].rearrange("p (g t) -> p g t", t=320)
            for jj in range(2):
                gg = gb * GB + g0 + jj
                S.add("pe", lambda e, jj=jj, gg=gg, pv2=pv2: e.transpose(out=pv2[:, jj, 2:130], in_=u8p2[:, gg, :, :].rearrange("p s h -> p (s h)"), identity=identb[:]),
                      U8K_P2 + ["identb"], [kb])
                S.add("pe", lambda e, jj=jj, gg=gg, pv2=pv2: e.transpose(out=pv2[:, jj, 130:258], in_=u8own[:, gg, :, :].rearrange("p s h -> p (s h)"), identity=identb[:]),
                      U8K_OWN + ["identb"], [kb])
                S.add("act", lambda e, jj=jj, gg=gg, g0=g0, pv2=pv2: e.copy(out=Ub[:, g0 + jj, 2:258], in_=pv2[:, jj, 2:258]), [kb], ["Ub_%d" % (g0 + jj)])
                S.add("act", lambda e, jj=jj, gg=gg, g0=g0: e.copy(out=Ub[:, g0 + jj, 0:2], in_=Um[:, gg, 0:2]), UMK, ["Ub_a%d" % (g0 + jj)])
                S.add("act", lambda e, jj=jj, gg=gg, g0=g0: e.copy(out=Ub[:, g0 + jj, 258:274], in_=Um[:, gg, 2:18]), UMK, ["Ub_b%d" % (g0 + jj)])
        UBK = ["Ub_%d" % g for g in range(GB)] + ["Ub_a%d" % g for g in range(GB)] + ["Ub_b%d" % g for g in range(GB)]
        dma("sp", st_s[0:NS], din["st_self"][:, g8, :], [], ["st_s"], "st_s")
        dma("sp", st_p[0:NS], din["st_part"][:, g8, :], [], ["st_p"], "st_p")
        for g in range(GB):
            S.add("pe", lambda e, g=g: e.transpose(out=PS[2][:, g * NS:(g + 1) * NS], in_=st_s[0:NS, g, :], identity=identf[0:NS, 0:NS]), ["st_s", "identf"], ["PS2"])
            S.add("pe", lambda e, g=g: e.transpose(out=PS[2][:, 128 + g * NS:128 + (g + 1) * NS], in_=st_p[0:NS, g, :], identity=identf[0:NS, 0:NS]), ["st_p", "identf"], ["PS2"])
        h0v = PS[2][:, 0:GB * NS].rearrange("p (g b) -> p g b", b=NS)
        h0s = PS[2][:, 128:128 + GB * NS].rearrange("p (g b) -> p g b", b=NS)
        S.add("dve", lambda e, g8=g8: e.tensor_tensor(out=hs1, in0=h0v, in1=Ere[:, g8, 16:17].to_broadcast([128, GB, NS]), op=ALU.mult), ["PS2", "Ere"], ["hs1"])
        S.add("dve", lambda e: e.tensor_tensor(out=hs2, in0=h0s, in1=sE[:, 0, :, 16:17].to_broadcast([128, GB, NS]), op=ALU.mult), ["PS2", "sE0"], ["hs2"])
        S.add("pool", lambda e: e.tensor_tensor(out=hs1, in0=hs1, in1=hs2, op=ALU.add), ["hs1", "hs2"], ["hs1"])
        S.add("dve", lambda e, g8=g8: e.tensor_tensor(out=hn1, in0=h0v, in1=Ere[:, g8, 8:9].to_broadcast([128, GB, NS]), op=ALU.mult), ["PS2", "Ere"], ["hn1k"])
        S.add("dve", lambda e: e.tensor_tensor(out=hn2, in0=h0s, in1=sE[:, 0, :, 8:9].to_broadcast([128, GB, NS]), op=ALU.mult), ["PS2", "sE0"], ["hn2k"])
        S.add("pool", lambda e: e.tensor_tensor(out=Hins, in0=hn1, in1=hn2, op=ALU.add), ["hn1k", "hn2k"], ["Hinsk"])

    def ssm_loop(gb):
        g8 = slice(gb * GB, gb * GB + GB)
        sEim, nsEre = sE[:, 0, :, :], sE[:, 1, :, :]
        EreB, EimB = Ere[:, g8, :], Eim[:, g8, :]
        UBK = ["Ub_%d" % g for g in range(GB)] + ["Ub_a%d" % g for g in range(GB)] + ["Ub_b%d" % g for g in range(GB)]
        GSK = ["Gs_%d" % g for g in range(GB)]
        for g in range(GB):
            gg = gb * GB + g
            pa, pb_ = (0, 1) if g % 2 == 0 else (6, 7)
            ka, kb_ = "PS%d" % pa, "PS%d" % pb_
            S.add("pe", lambda e, g=g, pa=pa: e.matmul(PS[pa][:, 0:NU], lhsT=Msb[:, 0, g, :], rhs=Ub[:, g, :], start=True, stop=True), ["Msb"] + UBK, [ka])
            S.add("pe", lambda e, g=g, pb_=pb_: e.matmul(PS[pb_][:, 0:NCH], lhsT=Msb[:, 1, g, :], rhs=Ub[:, g, 0:NCH], start=True, stop=True), ["Msb"] + UBK, [kb_])
            S.add("dve", lambda e, g=g, pa=pa: e.tensor_tensor(out=Xt[:, g, :], in0=PS[pa][:, 0:NCH], in1=cosT[:, g, 1:259], op=ALU.mult), [ka, "cosT"], ["Xt_%d" % g])
            S.add("act", lambda e, g=g, pa=pa: e.copy(out=Xsm[:, g, :], in_=PS[pa][:, NCH:NU]), [ka], ["Xsm_%d" % g])
            x2 = Xt2[g % 2]
            S.add("dve", lambda e, g=g, pb_=pb_, x2=x2: e.tensor_tensor(out=x2, in0=PS[pb_][:, 0:NCH], in1=sinT[:, g, 1:259], op=ALU.mult), [kb_, "sinT"], ["Xt2_%d" % (g % 2)])
            S.add("dve", lambda e, g=g, x2=x2: e.tensor_tensor(out=Xt[:, g, :], in0=Xt[:, g, :], in1=x2, op=ALU.add), ["Xt_%d" % g, "Xt2_%d" % (g % 2)], ["Xt_%d" % g])
            S.add("dve", lambda e, g=g, gg=gg: e.tensor_tensor_scan(out=Gs[:, g, 1:259], data0=mumag[:, gg:gg + 1].to_broadcast([128, NCH]), data1=Xt[:, g, :],
                                                                    initial=0.0, op0=ALU.mult, op1=ALU.add), ["Xt_%d" % g, "mumag", "Gs"], ["Gs_%d" % g])
            S.add("act", lambda e, g=g, gg=gg: e.copy(out=G258[:, gg:gg + 1], in_=Gs[:, g, 258:259]), ["Gs_%d" % g], ["G258"])
        S.add("dve", lambda e: e.tensor_tensor(out=hs1, in0=hs1, in1=Xsm, op=ALU.add), ["hs1"] + ["Xsm_%d" % g for g in range(GB)], ["hs1"])

    def ssm_y1(gb):
        g8 = slice(gb * GB, gb * GB + GB)
        sEim, nsEre = sE[:, 0, :, :], sE[:, 1, :, :]
        EreB, EimB = Ere[:, g8, :], Eim[:, g8, :]
        UBK = ["Ub_%d" % g for g in range(GB)] + ["Ub_a%d" % g for g in range(GB)] + ["Ub_b%d" % g for g in range(GB)]
        GSK = ["Gs_%d" % g for g in range(GB)]
        GSK = ["Gs_%d" % g for g in range(GB)]
        for g in range(GB):
            S.add("pe", lambda e, g=g: e.matmul(PS[5][:, g * 128:(g + 1) * 128], lhsT=A32[:, g, :], rhs=R32[:, g, :], start=True, stop=True), ["A32", "R32"], ["PS5"])
        S.add("dve", lambda e: e.tensor_tensor(out=Mib, in0=PS[5][:, 0:GB * 128].rearrange("p (g q) -> p g q", q=128),
                                               in1=blockmask[:].unsqueeze(1).to_broadcast([128, GB, 128]), op=ALU.mult), ["PS5", "blockmask"], ["Mib"])

    def ssm_y2a(gb):
        g8 = slice(gb * GB, gb * GB + GB)
        sEim, nsEre = sE[:, 0, :, :], sE[:, 1, :, :]
        EreB, EimB = Ere[:, g8, :], Eim[:, g8, :]
        UBK = ["Ub_%d" % g for g in range(GB)] + ["Ub_a%d" % g for g in range(GB)] + ["Ub_b%d" % g for g in range(GB)]
        GSK = ["Gs_%d" % g for g in range(GB)]
        S.add("dve", lambda e: e.tensor_tensor(out=Pcs[:, 0, :, :], in0=cosT[:, :, 130:258], in1=Gs[:, :, 130:258], op=ALU.mult), ["cosT"] + GSK, ["Pc0"])
        S.add("pool", lambda e: e.tensor_tensor(out=Pcs[:, 1, :, :], in0=sinT[:, :, 130:258], in1=Gs[:, :, 130:258], op=ALU.mult), ["sinT"] + GSK, ["Pc1"])
        for g in range(GB):
            osl = PS[5][:, g * 128:(g + 1) * 128]
            S.add("pe", lambda e, g=g, osl=osl: e.matmul(osl, lhsT=Mib[:, g, :], rhs=Ub[:, g, 130:258], start=True, stop=False), ["Mib"] + UBK, ["PS5"])
            S.add("pe", lambda e, g=g, osl=osl: e.matmul(osl, lhsT=Mab[:, 0, g, :], rhs=Pcs[:, 0, g, :], start=False, stop=False), ["Mab0", "Pc0"], ["PS5"])
            S.add("pe", lambda e, g=g, osl=osl: e.matmul(osl, lhsT=Mab[:, 1, g, :], rhs=Pcs[:, 1, g, :], start=False, stop=True), ["Mab1", "Pc1"], ["PS5"])
            oss = PS[4][:, g * NS:(g + 1) * NS]
            S.add("pe", lambda e, g=g, oss=oss: e.matmul(oss, lhsT=Mib[:, g, :], rhs=Ub[:, g, 258:274], start=True, stop=False), ["Mib"] + UBK, ["PS4"])
            S.add("pe", lambda e, g=g, oss=oss: e.matmul(oss, lhsT=Mab[:, 0, g, :], rhs=Hins[:, g, :], start=False, stop=True), ["Mab0", "Hinsk"], ["PS4"])


    def ssm_y2b(gb):
        g8 = slice(gb * GB, gb * GB + GB)
        sEim, nsEre = sE[:, 0, :, :], sE[:, 1, :, :]
        EreB, EimB = Ere[:, g8, :], Eim[:, g8, :]
        UBK = ["Ub_%d" % g for g in range(GB)] + ["Ub_a%d" % g for g in range(GB)] + ["Ub_b%d" % g for g in range(GB)]
        GSK = ["Gs_%d" % g for g in range(GB)]
        def gelu_chain(ps_view, u_view, ya, yb, yout, W, kin, kout, tag):
            kk = "ypo" if tag == "o" else "yps"
            dsk_b = dsk[:, g8].unsqueeze(2).to_broadcast([128, GB, W])
            S.add("dve", lambda e: e.tensor_tensor(out=ya, in0=u_view, in1=dsk_b, op=ALU.mult), UBK + ["dsk", kk], [kk])
            S.add("dve", lambda e: e.tensor_tensor(out=ya, in0=ya, in1=ps_view, op=ALU.add), [kk] + kin, [kk])
            S.add("act", lambda e: e.activation(out=yb, in_=ya, func=AF.Square), [kk], [kk])
            S.add("dve", lambda e: e.scalar_tensor_tensor(out=yb, in0=yb, scalar=0.044715, in1=ya, op0=ALU.mult, op1=ALU.mult), [kk], [kk])
            S.add("pool", lambda e: e.tensor_tensor(out=yb, in0=yb, in1=ya, op=ALU.add), [kk], [kk])
            S.add("act", lambda e: e.activation(out=yb, in_=yb, func=AF.Tanh, scale=0.7978845608028654), [kk], [kk])
            S.add("dve", lambda e: e.scalar_tensor_tensor(out=yout, in0=yb, scalar=1.0, in1=ya, op0=ALU.add, op1=ALU.mult), [kk] + kout, kout)
        gelu_chain(PS[5][:, 0:GB * 128].rearrange("p (g q) -> p g q", q=128), Ub[:, :, 130:258], yA, yB, ygb, 128, ["PS5"], ["ygbk"], "o")
        gelu_chain(PS[4][:, 0:GB * NS].rearrange("p (g b) -> p g b", b=NS), Ub[:, :, 258:274], ysA, ysB, ygs, NS, ["PS4"], ["ygsk"], "s")
        goff = (gb % 2) * GB
        pvy = PSB[6][:, 0:GB * 128].rearrange("p (g q) -> p g q", q=128)
        for g in range(GB):
            S.add("pe", lambda e, g=g: e.transpose(out=pvy[:, g, :], in_=ygb[:, g, :], identity=identb[:]), ["ygbk", "identb"], ["PS6"])
        S.add("act", lambda e, goff=goff: e.activation(out=y8.rearrange("p t (g h) -> p g t h", h=16)[:, goff:goff + GB, :, :],
                                                       in_=pvy.rearrange("p g (t h) -> p g t h", h=16), func=AF.Identity, scale=0.5), ["PS6"], ["y8_%d" % (gb % 2)])
        pvs = PSB[7][:, 0:GB * 128].rearrange("p (g q) -> p g q", q=128)
        for g in range(GB):
            S.add("pe", lambda e, g=g: e.transpose(out=pvs[0:NS, g, :], in_=ygs[:, g, :], identity=identb[:]), ["ygsk", "identb"], ["PS7"])
        S.add("act", lambda e, goff=goff: e.activation(out=y8s[0:NS, goff * 16:(goff + GB) * 16].rearrange("p (g h) -> p g h", h=16), in_=pvs[0:NS, :, 112:128],
                                                       func=AF.Identity, scale=0.5), ["PS7"], ["y8s_%d" % (gb % 2)])
        if gb % 2 == 1:
            fc = gb // 2
            pvt = PSB[0][:, 0:1024].rearrange("p (t c) -> p t c", c=128)
            for t in range(8):
                S.add("pe", lambda e, t=t: e.transpose(out=pvt[:, t, :], in_=y8[:, t, :], identity=identb[:]), ["y8_0", "y8_1", "identb"], ["PS0"])
            S.add("act", lambda e, fc=fc: e.copy(out=ST[:, fc, 0:NT].rearrange("p (c t) -> p t c", t=8), in_=pvt), ["PS0"], ["ST_%d" % fc])
            S.add("pe", lambda e: e.transpose(out=PSB[1][:, 0:NS], in_=y8s[0:NS, :], identity=identb[0:NS, 0:NS]), ["y8s_0", "y8s_1", "identb"], ["PS1"])
            S.add("dve", lambda e, fc=fc: e.tensor_copy(out=ST[:, fc, NT:NTS], in_=PSB[1][:, 0:NS]), ["PS1"], ["STs_%d" % fc])
        for g in range(GB):
            S.add("pe", lambda e, g=g: e.transpose(out=PS[3][0:NS, g * 128:(g + 1) * 128], in_=hs1[:, g, :], identity=identf[:]), ["hs1", "identf"], ["PS3"])
        S.add("act", lambda e: e.copy(out=hso[0:NS], in_=PS[3][0:NS, 0:GB * 128].rearrange("p (g q) -> p g q", q=128)), ["PS3"], ["hso"])
        dma("sp", dout["sssm"][:, g8, :], hso[0:NS], ["hso"], [], "hso")

    NBATCH = 64 // GB
    ssm_sincos(0)
    for gb in range(NBATCH):
        ssm_frontA(gb)
        ssm_frontB(gb)
        ssm_loop(gb)
        ssm_y1(gb)
        ssm_y2a(gb)
        if gb + 1 < NBATCH:
            ssm_sincos(gb + 1)
        ssm_y2b(gb)
        S.mark('batch%d' % gb)

    S.add("pe", lambda e: e.matmul(PS[2][:, 0:64], lhsT=swapm[:], rhs=G258[:], start=True, stop=True), ["swapm", "G258"], ["PS2"])
    S.add("dve", lambda e: e.tensor_tensor(out=pp[:, 11, :], in0=C258[:], in1=G258[:], op=ALU.mult), ["C258", "G258"], [PPK(11)])
    S.add("dve", lambda e: e.tensor_scalar(out=pp[:, 12, :], in0=S258[:], scalar1=sgn[:, 0:1], scalar2=1.0, op0=ALU.mult, op1=ALU.mult), ["S258"] + SGN, [PPK(12)])
    S.add("dve", lambda e: e.tensor_tensor(out=pp[:, 12, :], in0=pp[:, 12, :], in1=PS[2][:, 0:64], op=ALU.mult), [PPK(12), "PS2"], [PPK(12)])
    S.add("dve", lambda e: e.tensor_tensor(out=pp[:, 11, :], in0=pp[:, 11, :], in1=pp[:, 12, :], op=ALU.add), [PPK(11), PPK(12)], [PPK(11)])
    S.add("pe", lambda e: e.transpose(out=PS[3][0:64, 0:128], in_=pp[:, 11, :], identity=identf[:]), [PPK(11), "identf"], ["PS3"])
    S.add("act", lambda e: e.copy(out=hfo[0:64, :], in_=PS[3][0:64, 0:128]), ["PS3"], ["hfo"])
    dma("sp", dout["pssm"], hfo[0:64, :], ["hfo"], [], "hfo")

    S.mark('p2')
    S.barrier(mk_bar)
    NKC = 1168
    qT = carve(0, 8 * NTS, BF16).rearrange("p (h t) -> p h t", t=NTS)
    kT = carve(17 * KB, 4 * NKC, BF16).rearrange("p (g t) -> p g t", t=NKC)
    PT = [[carve(27 * KB + (3 * j + i) * KB, 512, BF16) for i in range(3)] for j in range(2)]
    AT = carve(51 * KB, 8 * NTS, BF16).rearrange("p (c t) -> p c t", t=NTS)
    Vaug = carve(100 * KB, 10 * 4 * 128, BF16).rearrange("p (b g q) -> p b g q", b=10, q=128)
    o3 = 100 * KB + 10 * KB
    tmpq = [carve(o3 + 4 * KB * i, 1024, F32) for i in range(3)]; o3 += 12 * KB
    sqj = carve(o3 - 8 * KB, D, F32)
    xbuf_off3 = o3
    xbuf = [carve(o3, D, F32)] * 2; o3 += 8 * KB
    xnb = [carve(o3, D, BF16)] * 2; o3 += 4 * KB
    xnTt = [carve(o3, ND * 128, BF16).rearrange("p (c t) -> p c t", c=ND)] * 2; o3 += 4 * KB
    kf = [carve(o3 + i * KB, 256, F32) for i in range(2)]; o3 += 2 * KB
    vf = [carve(o3 + i * KB, 256, F32) for i in range(2)]; o3 += 2 * KB
    kb16 = carve(o3, 256, BF16); o3 += 512
    qf = carve(o3, 512, F32); o3 += 2 * KB
    qb16 = carve(o3, 512, BF16); o3 += KB
    otmp = [carve(o3 + i * KB, 512, BF16) for i in range(2)]; o3 += 2 * KB
    dtmp2 = [carve(xbuf_off3 + 2 * KB * i, 512, F32) for i in range(2)]
    assert o3 <= ARENA, o3
    gain_bc = carve(0, D, F32)
    cnt["x"] = 0
    dma("sp", gain_bc, din["norm_gain"].partition_broadcast(128), [], ["gain_bc"], "gain_bc")
    dma("pool", masks[:, 0, :], din["maskc"], [], ["masks0"], "masks0")
    dma("pool", masks[:, 1, :], din["maskp"], [], ["masks1"], "masks1")
    dma("pool", masks[:, 2, :], din["maskp0"], [], ["masks2"], "masks2")
    dma("sp", expsink[:], din["sinks"].partition_broadcast(128), [], ["expsink"], "expsink")
    S.add("act", lambda e: e.activation(out=expsink[:], in_=expsink[:], func=AF.Exp), ["expsink"], ["expsink"])
    S.add("pool", lambda e: e.memset(onesb[:], 1.0), [], ["onesb"])
    S.add("pool", lambda e: e.memset(Vaug[:, :, :, 64:128], 1.0), [], ["Vones"])

    load_w(0, din["w_in"], 1024, 512, ND)

    def kv_proj(lhs_fn, lhs_keys, n):
        b = cnt["kv"] % 2
        cnt["kv"] += 1
        kb = proj_tok(lhs_fn, lhs_keys, n, 0, 512, b)
        return b, kb

    def kv_post(b, kb, n, ti, kcol0, vblk, out_k=None, out_v=None):
        kfj, vfj = kf[b], vf[b]
        S.add("act", lambda e: e.copy(out=vfj[0:n, :], in_=PS[b][0:n, 256:512]), [kb], ["vf%d" % b])
        headnorm_rope([(PS[b][0:n, 0:256], 256)], [kb], n, 4, ti, 64, 1.0, kfj[0:n, :], ["kf%d" % b])
        if out_k is not None:
            dma("sp", out_k, kfj[0:n, :], ["kf%d" % b], ["swk_new"], "kf%d" % b)
        if out_v is not None:
            dma("sp", out_v, vfj[0:n, :], ["vf%d" % b], ["swv_new"], "vf%d" % b)
        if vblk is not None:
            S.add("pool", lambda e: e.tensor_copy(out=Vaug[0:n, vblk, :, 0:64], in_=vfj[0:n, :].rearrange("p (g d) -> p g d", d=64)),
                  ["vf%d" % b, "Vones"], ["Vaug_%d" % vblk])
            S.add("act", lambda e: e.copy(out=kb16[0:n, :], in_=kfj[0:n, :]), ["kf%d" % b], ["kb16"])
            pvk = PSB[6][:, 0:512].rearrange("p (g t) -> p g t", t=128)
            for g in range(4):
                S.add("pe", lambda e, g=g: e.transpose(out=pvk[0:64, g, 0:n], in_=kb16[0:n, g * 64:(g + 1) * 64], identity=identb[0:n, 0:n]),
                      ["kb16", "identb"], ["PS6"])
            S.add("act", lambda e: e.copy(out=kT[0:64, :, kcol0:kcol0 + n], in_=pvk[0:64, :, 0:n]), ["PS6"], ["kT_%d" % vblk])

    def kv_tile(lhs_fn, lhs_keys, n, ti, kcol0, vblk, out_k=None, out_v=None):
        b, kb = kv_proj(lhs_fn, lhs_keys, n)
        kv_post(b, kb, n, ti, kcol0, vblk, out_k, out_v)

    for (r0, n, ti, kc0, vb, ok, ov) in ((0, 128, 8, 1024, 8, None, None), (128, 16, 9, 1152, 9, dout["pmk"], dout["pmv"])):
        tt = xnTt[0]
        norm_tile(din["x_kvx"][r0:r0 + n, :], n, lambda dc0, tt=tt, n=n: tt[:, dc0:dc0 + 8, 0:n], ["xnTt0"])
        kv_tile(lambda dc, tt=tt, n=n: tt[:, dc, 0:n], ["xnTt0"], n, ti, kc0, vb, ok, ov)
    S.add("pool", lambda e: e.memset(qT, 0.0), ["gain_bc"], ["qT", "gain_bc"])
    dma("sp", dout["swk"][:, 0:127, :], din["cwk"][:, 1:128, :], [], [], "swk")
    dma("sp", dout["swv"][:, 0:127, :], din["cwv"][:, 1:128, :], [], [], "swv")
    kspecs = [(lambda dc, i=i: xnT[:, dc, i * 128:(i + 1) * 128], ["xnT_%d" % i], 128, i, i * 128, i,
               dout["pwk"] if i == 7 else None, dout["pwv"] if i == 7 else None) for i in range(8)]
    kspecs.append((lambda dc: xnT[:, dc, NT:NTS], ["xnT_s"], NS, 10, 0, None, dout["swk"][:, 127, :], dout["swv"][:, 127, :]))
    pendk = kv_proj(kspecs[0][0], kspecs[0][1], kspecs[0][2])
    for ki, (lf, lk, n_, ti_, kc_, vb_, ok_, ov_) in enumerate(kspecs):
        curk = pendk
        if ki + 1 < len(kspecs):
            pendk = kv_proj(kspecs[ki + 1][0], kspecs[ki + 1][1], kspecs[ki + 1][2])
        kv_post(curk[0], curk[1], n_, ti_, kc_, vb_, ok_, ov_)
    KTK = ["kT_%d" % i for i in range(10)]
    VK = ["Vaug_%d" % i for i in range(10)] + ["Vones"]

    def q_proj(lhs_fn, lhs_keys, n, slot):
        b = cnt["kv"] % 2
        cnt["kv"] += 1
        kb = proj_tok(lhs_fn, lhs_keys, n, slot, 512, b)
        return b, kb

    def q_post(b, kb, n, ti, tokc0, smp_hoff):
        headnorm_rope([(PS[b][0:n, 0:512], 512)], [kb], n, 8, ti, 0, 0.125, qf[0:n, :], ["qf"])
        if smp_hoff is not None:
            S.add("act", lambda e: e.copy(out=qs16[0:n, smp_hoff * 64:(smp_hoff + 8) * 64], in_=qf[0:n, :]), ["qf"], ["qs16_%d" % smp_hoff])
            return
        S.add("act", lambda e: e.copy(out=qb16[0:n, :], in_=qf[0:n, :]), ["qf"], ["qb16"])
        pvq = PSB[7][:, 0:1024].rearrange("p (h t) -> p h t", t=128)
        for h in range(8):
            S.add("pe", lambda e, h=h: e.transpose(out=pvq[0:64, h, 0:n], in_=qb16[0:n, h * 64:(h + 1) * 64], identity=identb[0:n, 0:n]),
                  ["qb16", "identb"], ["PS7"])
        S.add("act", lambda e: e.copy(out=qT[0:64, :, tokc0:tokc0 + n], in_=pvq[0:64, :, 0:n]), ["PS7", "qT"], ["qT_%d" % (tokc0 // 128)])

    def att_setup(nb, g, hl0, pb):
        d_ = dict(nb=nb, g=g, hl0=hl0, pb=pb)
        d_["banks"] = (2, 3, 4, 5) if pb == 0 else (6, 7, 0, 1)
        d_["prev_cols"] = slice(1024, 1152) if nb == 0 else slice((nb - 1) * 128, nb * 128)
        d_["prev_blk"] = 8 if nb == 0 else nb - 1
        d_["mprev"] = 2 if nb == 0 else 1
        return d_

    def att_A(d_):
        nb, g, hl0, pb = d_["nb"], d_["g"], d_["hl0"], d_["pb"]
        PSm, PSp, PSc, PSo = d_["banks"]
        PTm, PTp, PTc = PT[pb]
        qk = ["qT_%d" % nb]
        cur_cols = slice(nb * 128, (nb + 1) * 128)
        prev_cols, mprev = d_["prev_cols"], d_["mprev"]
        for r in range(4):
            rs = slice(r * 128, (r + 1) * 128)
            qa = qT[0:64, hl0 + r, nb * 128:(nb + 1) * 128]
            S.add("pe", lambda e, rs=rs, qa=qa: e.matmul(PS[PSm][0:16, rs], lhsT=kT[0:64, g, 1152:1168], rhs=qa, start=True, stop=True), KTK + qk, ["PS%d" % PSm])
            S.add("pe", lambda e, rs=rs: e.matmul(PS[PSp][:, rs], lhsT=identb[:], rhs=masks[:, mprev, :], start=True, stop=False), ["identb", "masks1", "masks2"], ["PS%d" % PSp])
            S.add("pe", lambda e, rs=rs, qa=qa: e.matmul(PS[PSp][:, rs], lhsT=kT[0:64, g, prev_cols], rhs=qa, start=False, stop=True), KTK + qk, ["PS%d" % PSp])
            S.add("pe", lambda e, rs=rs: e.matmul(PS[PSc][:, rs], lhsT=identb[:], rhs=masks[:, 0, :], start=True, stop=False), ["identb", "masks0"], ["PS%d" % PSc])
            S.add("pe", lambda e, rs=rs, qa=qa: e.matmul(PS[PSc][:, rs], lhsT=kT[0:64, g, cur_cols], rhs=qa, start=False, stop=True), KTK + qk, ["PS%d" % PSc])
        km, kp, kc = "PTm%d" % pb, "PTp%d" % pb, "PTc%d" % pb
        S.add("act", lambda e: e.activation(out=PTm[0:16, :], in_=PS[PSm][0:16, :], func=AF.Exp), ["PS%d" % PSm], [km])
        S.add("act", lambda e: e.activation(out=PTp, in_=PS[PSp][:, :], func=AF.Exp), ["PS%d" % PSp], [kp])
        S.add("act", lambda e: e.activation(out=PTc, in_=PS[PSc][:, :], func=AF.Exp), ["PS%d" % PSc], [kc])

    def att_B(d_):
        nb, g, pb = d_["nb"], d_["g"], d_["pb"]
        PSm, PSp, PSc, PSo = d_["banks"]
        PTm, PTp, PTc = PT[pb]
        km, kp, kc = "PTm%d" % pb, "PTp%d" % pb, "PTc%d" % pb
        prev_blk = d_["prev_blk"]
        for r in range(4):
            rs = slice(r * 128, (r + 1) * 128)
            S.add("pe", lambda e, rs=rs: e.matmul(PS[PSo][:, rs], lhsT=Vaug[0:16, 9, g, :], rhs=PTm[0:16, rs], start=True, stop=False), VK + [km], ["PS%d" % PSo])
            S.add("pe", lambda e, rs=rs: e.matmul(PS[PSo][:, rs], lhsT=Vaug[:, prev_blk, g, :], rhs=PTp[:, rs], start=False, stop=False), VK + [kp], ["PS%d" % PSo])
            S.add("pe", lambda e, rs=rs: e.matmul(PS[PSo][:, rs], lhsT=Vaug[:, nb, g, :], rhs=PTc[:, rs], start=False, stop=True), VK + [kc], ["PS%d" % PSo])

    def att_C(d_):
        nb, g, pb = d_["nb"], d_["g"], d_["pb"]
        PSo = d_["banks"][3]
        dtm = dtmp2[pb]
        dv = dtm[64:128, :].rearrange("p (r q) -> p r q", q=128)
        kd = "dtmp%d" % pb
        S.add("dve", lambda e: e.tensor_tensor(out=dv, in0=PS[PSo][64:128, :].rearrange("p (r q) -> p r q", q=128),
                                               in1=expsink[64:128, 4 * g:4 * g + 4].unsqueeze(2).to_broadcast([64, 4, 128]), op=ALU.add), ["PS%d" % PSo, "expsink"], [kd, "xbuf0"])
        S.add("act", lambda e: e.activation(out=dtm[64:128, :], in_=dtm[64:128, :], func=AF.Ln), [kd], [kd])
        S.add("act", lambda e: e.activation(out=dtm[64:128, :], in_=dtm[64:128, :], func=AF.Exp, scale=-1.0), [kd], [kd])
        ot = otmp[pb]
        S.add("dve", lambda e: e.tensor_tensor(out=ot[0:64, :], in0=PS[PSo][0:64, :], in1=dtm[64:128, :], op=ALU.mult), ["PS%d" % PSo, kd], ["otmp%d" % pb])
        o3v = ot[0:64, :].rearrange("p (r q) -> p r q", q=128)
        dma("sp", AT[0:64, 2 * g:2 * g + 2, nb * 128:(nb + 1) * 128], o3v[:, 0:4:2, :], ["otmp%d" % pb], ["AT_%d_%d" % (g, nb)], "otmp%d" % pb)
        dma("sp", AT[64:128, 2 * g:2 * g + 2, nb * 128:(nb + 1) * 128], o3v[:, 1:4:2, :], ["otmp%d" % pb], ["ATb_%d_%d" % (g, nb)], "otmp%d" % pb)

    acnt = 0
    for qblk in range(2):
        load_w(1, din["w_in"], qblk * 512, 512, ND)
        qspecs = [(lambda dc, i=i: xnT[:, dc, i * 128:(i + 1) * 128], ["xnT_%d" % i], 128, i, i * 128, None) for i in range(8)]
        qspecs.append((lambda dc: xnT[:, dc, NT:NTS], ["xnT_s"], NS, 10, 0, qblk * 8))
        pend = q_proj(qspecs[0][0], qspecs[0][1], qspecs[0][2], 1)
        for qi, (lf, lk, n_, ti_, tc_, sh_) in enumerate(qspecs):
            cur = pend
            if qi + 1 < len(qspecs):
                pend = q_proj(qspecs[qi + 1][0], qspecs[qi + 1][1], qspecs[qi + 1][2], 1)
            q_post(cur[0], cur[1], n_, ti_, tc_, sh_)
        blocks = []
        for nb in range(8):
            for gl in range(2):
                blocks.append(att_setup(nb, qblk * 2 + gl, gl * 4, acnt % 2))
                acnt += 1
        att_A(blocks[0])
        for bi in range(len(blocks)):
            if bi + 1 < len(blocks):
                att_A(blocks[bi + 1])
            att_B(blocks[bi])
            att_C(blocks[bi])
    ATK = ["AT_%d_%d" % (g, nb) for g in range(4) for nb in range(8)] + ["ATb_%d_%d" % (g, nb) for g in range(4) for nb in range(8)]
    S.mark('p3a')

    S.barrier(mk_bar)
    Kc = carve(0, NS * 256, BF16).rearrange("p (b e) -> p b e", e=256)
    Vc = carve(8 * KB, NS * 256, BF16).rearrange("p (b e) -> p b e", e=256)
    Kmc = carve(16 * KB, NS * 256, BF16).rearrange("p (b e) -> p b e", e=256)
    Vmc = carve(24 * KB, NS * 256, BF16).rearrange("p (b e) -> p b e", e=256)
    KsT = carve(100 * KB, NS * 4 * 128, BF16).rearrange("p (b g t) -> p b g t", b=NS, t=128)
    KsT2 = carve(116 * KB, NS * 4 * 32, BF16).rearrange("p (b g t) -> p b g t", b=NS, t=32)
    o4 = 120 * KB
    qsT = carve(o4, 16 * NS, BF16).rearrange("p (h b) -> p h b", b=NS); o4 += 512
    PTs = carve(o4, 256, BF16); o4 += 512
    PTs2 = carve(o4, 256, BF16); o4 += 512
    dts = carve(o4, 256, F32); o4 += KB
    osb = carve(o4, 256, BF16).rearrange("p (h b) -> p h b", b=NS); o4 += 512
    load_w(0, din["w_in"], 1536, 512, ND)
    load_w(1, din["w_in"], 1536 + 512, 512, ND)
    dma("pool", Kc, din["cwk"].rearrange("b j e -> j b e"), [], ["Kc"], "Kc")
    dma("pool", Vc, din["cwv"].rearrange("b j e -> j b e"), [], ["Vc"], "Vc")
    dma("pool", Kmc[0:16], din["cmk"].rearrange("b j e -> j b e"), [], ["Kmc_a"], "Kmc")
    dma("pool", Vmc[0:16], din["cmv"].rearrange("b j e -> j b e"), [], ["Vmc_a"], "Vmc")
    dma("pool", Kmc[16:17], dout["swk"][:, 127:128, :].rearrange("b o e -> o b e"), ["swk_new"], ["Kmc_b"], "Kmc")
    dma("pool", Vmc[16:17], dout["swv"][:, 127:128, :].rearrange("b o e -> o b e"), ["swv_new"], ["Vmc_b"], "Vmc")
    pvq2 = PSB[7][:, 0:256].rearrange("p (h b) -> p h b", b=NS)
    for h in range(16):
        S.add("pe", lambda e, h=h: e.transpose(out=pvq2[0:64, h, :], in_=qs16[0:NS, h * 64:(h + 1) * 64], identity=identb[0:NS, 0:NS]),
              ["qs16_0", "qs16_8", "identb"], ["PS7"])
    S.add("act", lambda e: e.copy(out=qsT[0:64], in_=pvq2[0:64]), ["PS7"], ["qsT"])
    for b0 in range(0, NS, 2):
        bk = 4 + (b0 // 2) % 2
        pvk2 = PSB[bk][:, 0:1024].rearrange("p (b g t) -> p b g t", b=2, t=128)
        for bb in range(2):
            for g in range(4):
                S.add("pe", lambda e, bb=bb, g=g, b0=b0, pvk2=pvk2: e.transpose(out=pvk2[0:64, bb, g, :], in_=Kc[:, b0 + bb, g * 64:(g + 1) * 64], identity=identb[:]),
                      ["Kc", "identb"], ["PS%d" % bk])
        S.add("act" if (b0 // 2) % 2 == 0 else "dve",
              (lambda e, b0=b0, pvk2=pvk2: e.copy(out=KsT[0:64, b0:b0 + 2], in_=pvk2[0:64])) if (b0 // 2) % 2 == 0 else
              (lambda e, b0=b0, pvk2=pvk2: e.tensor_copy(out=KsT[0:64, b0:b0 + 2], in_=pvk2[0:64])), ["PS%d" % bk], ["KsT_%d" % b0])
    KSK = ["KsT_%d" % b0 for b0 in range(0, NS, 2)]
    for b0 in range(0, NS, 8):
        bk = 6 + (b0 // 8) % 2
        pvk3 = PSB[bk][:, 0:1024].rearrange("p (b g t) -> p b g t", b=8, t=32)
        for bb in range(8):
            for g in range(4):
                S.add("pe", lambda e, bb=bb, g=g, b0=b0, pvk3=pvk3: e.transpose(out=pvk3[0:64, bb, g, 0:17], in_=Kmc[0:17, b0 + bb, g * 64:(g + 1) * 64],
                                                                              identity=identb[0:17, 0:17]), ["Kmc_a", "Kmc_b", "identb"], ["PS%d" % bk])
        S.add("act", lambda e, b0=b0, pvk3=pvk3: e.copy(out=KsT2[0:64, b0:b0 + 8, :, 0:17], in_=pvk3[0:64, :, :, 0:17]), ["PS%d" % bk], ["KsT2_%d" % b0])
    KS2K = ["KsT2_0", "KsT2_8"]
    for b in range(NS):
        for g in range(4):
            cs = slice(b * 16 + g * 4, b * 16 + g * 4 + 4)
            S.add("pe", lambda e, b=b, g=g, cs=cs: e.matmul(PS[0][:, cs], lhsT=KsT[0:64, b, g, :], rhs=qsT[0:64, 4 * g:4 * g + 4, b], start=True, stop=True),
                  KSK + ["qsT"], ["PS0"])
            S.add("pe", lambda e, b=b, g=g, cs=cs: e.matmul(PS[1][0:17, cs], lhsT=KsT2[0:64, b, g, 0:17], rhs=qsT[0:64, 4 * g:4 * g + 4, b], start=True, stop=True),
                  KS2K + ["qsT"], ["PS1"])
    S.add("act", lambda e: e.activation(out=PTs, in_=PS[0][:, 0:256], func=AF.Exp), ["PS0"], ["PTs"])
    S.add("act", lambda e: e.activation(out=PTs2[0:17], in_=PS[1][0:17, 0:256], func=AF.Exp), ["PS1"], ["PTs2"])
    S.add("pool", lambda e: e.memset(PTs[0:1, :], 0.0), ["PTs"], ["PTs"])
    S.add("pe", lambda e: e.matmul(PS[2][0:64, 0:256], lhsT=onesb[:, 0:64], rhs=PTs, start=True, stop=False), ["onesb", "PTs"], ["PS2"])
    S.add("pe", lambda e: e.matmul(PS[2][0:64, 0:256], lhsT=onesb[0:17, 0:64], rhs=PTs2[0:17], start=False, stop=True), ["onesb", "PTs2"], ["PS2"])
    for b in range(NS):
        for g in range(4):
            cs = slice(b * 16 + g * 4, b * 16 + g * 4 + 4)
            S.add("pe", lambda e, b=b, g=g, cs=cs: e.matmul(PS[3][0:64, cs], lhsT=Vc[:, b, g * 64:(g + 1) * 64], rhs=PTs[:, cs], start=True, stop=False), ["Vc", "PTs"], ["PS3"])
            S.add("pe", lambda e, b=b, g=g, cs=cs: e.matmul(PS[3][0:64, cs], lhsT=Vmc[0:17, b, g * 64:(g + 1) * 64], rhs=PTs2[0:17, cs], start=False, stop=True),
                  ["Vmc_a", "Vmc_b", "PTs2"], ["PS3"])
    dv3 = dts[0:64].rearrange("p (b h) -> p b h", h=16)
    S.add("dve", lambda e: e.tensor_tensor(out=dv3, in0=PS[2][0:64, 0:256].rearrange("p (b h) -> p b h", h=16),
                                           in1=expsink[0:64, :].unsqueeze(1).to_broadcast([64, NS, 16]), op=ALU.add), ["PS2", "expsink"], ["dts"])
    S.add("dve", lambda e: e.reciprocal(out=dts[0:64], in_=dts[0:64]), ["dts"], ["dts"])
    S.add("dve", lambda e: e.tensor_tensor(out=osb[0:64].rearrange("p h b -> p b h"), in0=PS[3][0:64, 0:256].rearrange("p (b h) -> p b h", h=16), in1=dv3, op=ALU.mult),
          ["PS3", "dts"], ["osb"])
    dma("sp", AT[0:64, :, NT:NTS], osb[0:64, 0:16:2, :], ["osb"], ["AT_s0"], "osb")
    dma("sp", AT[64:128, :, NT:NTS], osb[0:64, 1:16:2, :], ["osb"], ["AT_s1"], "osb")
    ATK = ATK + ["AT_s0", "AT_s1"]
    S.mark('p3b')

    S.barrier(mk_bar)
    mT = carve(0, 16 * NTS, BF16).rearrange("p (c t) -> p c t", t=NTS)
    ST2 = carve(100 * KB, 8 * NTS, BF16).rearrange("p (c t) -> p c t", t=NTS)
    o5 = 117 * KB
    slt = [carve(o5 + 2 * KB * i, 512, F32) for i in range(2)]; o5 += 4 * KB
    bra = carve(o5, 2 * NTS, F32).rearrange("p (c t) -> p c t", t=NTS); o5 += 2 * NTS * 4
    m1 = carve(o5, 2 * NTS, F32).rearrange("p (c t) -> p c t", t=NTS); o5 += 2 * NTS * 4
    o5 = (o5 + 3) // 4 * 4
    xin = [carve(o5 + 2 * KB * i, 512, F32) for i in range(3)]; o5 += 6 * KB
    oti = [carve(o5 + 2 * KB * i, 512, F32) for i in range(3)]; o5 += 6 * KB
    assert o5 <= ARENA, o5
    dma("sp", b_glu[:], din["b_glu"], [], ["b_glu"], "b_glu")
    TB = ((0, 512), (512, 512), (NT, NS))
    XK = ["xnT_%d" % i for i in range(8)] + ["xnT_s"]
    STK = ["ST_%d" % i for i in range(8)] + ["STs_%d" % i for i in range(8)]
    c4 = {"ps": 0, "sl": 0}

    def feat_mm(slot, wc0, nk, rhs_fn, rk):
        for (t0, N) in TB:
            b = c4["ps"] % 4
            c4["ps"] += 1
            kb = "PS%d" % b
            for kc in range(nk):
                S.add("pe", lambda e, kc=kc, b=b, t0=t0, N=N: e.matmul(PS[b][:, 0:N], lhsT=wbuf[slot][:, kc, wc0:wc0 + 128], rhs=rhs_fn(kc, t0, N),
                                                                       start=(kc == 0), stop=(kc == nk - 1)), rk + ["wbuf%d" % slot], [kb])
            yield b, kb, t0, N

    def act_tmp(b, kb, N, func, bias=None):
        j = c4["sl"] % 2
        c4["sl"] += 1
        t = slt[j]
        if bias is None:
            S.add("act", lambda e: e.activation(out=t[:, 0:N], in_=PS[b][:, 0:N], func=func), [kb], ["slt%d" % j])
        else:
            S.add("act", lambda e: e.activation(out=t[:, 0:N], in_=PS[b][:, 0:N], func=func, bias=bias), [kb, "b_glu"], ["slt%d" % j])
        return t, "slt%d" % j

    xrhs = lambda kc, t0, N: xnT[:, kc, t0:t0 + N]
    for blk in range(2):
        sl_ = blk % 2
        for c in range(4):
            jc = blk * 4 + c
            for b, kb, t0, N in feat_mm(sl_, c * 128, ND, xrhs, XK):
                t, tk = act_tmp(b, kb, N, AF.Silu)
                S.add("dve", lambda e, t=t, jc=jc, t0=t0, N=N: e.tensor_tensor(out=AT[:, jc, t0:t0 + N], in0=AT[:, jc, t0:t0 + N], in1=t[:, 0:N], op=ALU.mult),
                      [tk] + ATK, ["A_%d_%d" % (jc, t0)])
    AK = ["A_%d_%d" % (jc, t0) for jc in range(8) for (t0, _) in TB]
    strhs = lambda kc, t0, N: ST[:, kc, t0:t0 + N]
    for blk in range(2):
        sl_ = blk % 2
        load_w(sl_, din["w_glu"], blk * 512, 512, 8)
        for c in range(4):
            jc = blk * 4 + c
            for b, kb, t0, N in feat_mm(sl_, c * 128, 8, strhs, STK):
                t, tk = act_tmp(b, kb, N, AF.Sigmoid, bias=b_glu[:, jc:jc + 1])
                S.add("dve", lambda e, t=t, jc=jc, t0=t0, N=N: e.tensor_tensor(out=ST2[:, jc, t0:t0 + N], in0=ST[:, jc, t0:t0 + N], in1=t[:, 0:N], op=ALU.mult),
                      [tk] + STK, ["S2_%d_%d" % (jc, t0)])
    for blk in range(2):
        sl_ = blk % 2
        load_w(sl_, din["w_in"], 3584 + blk * 512, 512, ND)
        for c in range(4):
            jc = blk * 4 + c
            for b, kb, t0, N in feat_mm(sl_, c * 128, ND, xrhs, XK):
                t, tk = act_tmp(b, kb, N, AF.Silu)
                S.add("dve", lambda e, t=t, jc=jc, t0=t0, N=N: e.tensor_tensor(out=ST2[:, jc, t0:t0 + N], in0=ST2[:, jc, t0:t0 + N], in1=t[:, 0:N], op=ALU.mult),
                      [tk, "S2_%d_%d" % (jc, t0)], ["S2_%d_%d" % (jc, t0)])
    S2K = ["S2_%d_%d" % (jc, t0) for jc in range(8) for (t0, _) in TB]
    arhs = lambda kc, t0, N: AT[:, kc, t0:t0 + N]
    s2rhs = lambda kc, t0, N: ST2[:, kc, t0:t0 + N]
    wq = [carve((68 + 8 * i) * KB, ND * 256, BF16).rearrange("p (c n) -> p c n", c=ND) for i in range(4)]

    def load_wq(slot, src, col0, nk):
        key = "wq%d" % slot
        v = src.rearrange("(c p) n -> p c n", p=128)
        for c0 in range(0, nk, 8):
            dma("pool", wq[slot][:, c0:c0 + 8, :], v[:, c0:c0 + 8, col0:col0 + 256], [], [key], key)

    def feat_mm_q(slot, wc0, nk, rhs_fn, rk):
        for (t0, N) in TB:
            b = c4["ps"] % 4
            c4["ps"] += 1
            kb = "PS%d" % b
            for kc in range(nk):
                S.add("pe", lambda e, kc=kc, b=b, t0=t0, N=N: e.matmul(PS[b][:, 0:N], lhsT=wq[slot][:, kc, wc0:wc0 + 128], rhs=rhs_fn(kc, t0, N),
                                                                       start=(kc == 0), stop=(kc == nk - 1)), rk + ["wq%d" % slot], [kb])
            yield b, kb, t0, N

    stages = ((0, din["w_ao"], 0, 8), (1, din["w_in"], 4608, ND), (2, din["w_so"], 0, 8), (3, din["w_in"], 6656, ND))
    S.add("pool", lambda e: e.memset(dummy[0:1, 6:7], 0.0), [], ["wbuf0", "wbuf1", "wq0", "wq1", "wq2", "wq3"])
    for (sl_, src, c0_, nk_) in stages:
        load_wq(sl_, src, c0_, nk_)
    for gq in range(8):
        for half_ in range(2):
            rhs_fn, rk, last = ((arhs, AK, False), (s2rhs, S2K, True))[half_]
            sa, sg_ = 2 * half_, 2 * half_ + 1
            for c in range(2):
                for b, kb, t0, N in feat_mm_q(sa, c * 128, 8, rhs_fn, rk):
                    S.add("act", lambda e, b=b, c=c, t0=t0, N=N: e.copy(out=bra[:, c, t0:t0 + N], in_=PS[b][:, 0:N]), [kb], ["bra_%d_%d" % (c, t0)])
            if gq < 7:
                load_wq(sa, stages[sa][1], stages[sa][2] + (gq + 1) * 256, stages[sa][3])
            for c in range(2):
                fc = gq * 2 + c
                for b, kb, t0, N in feat_mm_q(sg_, c * 128, ND, xrhs, XK):
                    t, tk = act_tmp(b, kb, N, AF.Sigmoid)
                    if not last:
                        S.add("dve", lambda e, t=t, c=c, t0=t0, N=N: e.tensor_tensor(out=m1[:, c, t0:t0 + N], in0=t[:, 0:N], in1=bra[:, c, t0:t0 + N], op=ALU.mult),
                              [tk, "bra_%d_%d" % (c, t0)], ["m1_%d_%d" % (c, t0)])
                    else:
                        S.add("dve", lambda e, t=t, c=c, t0=t0, N=N: e.tensor_tensor(out=t[:, 0:N], in0=t[:, 0:N], in1=bra[:, c, t0:t0 + N], op=ALU.mult),
                              [tk, "bra_%d_%d" % (c, t0)], [tk])
                        S.add("dve", lambda e, t=t, c=c, fc=fc, t0=t0, N=N: e.tensor_tensor(out=mT[:, fc, t0:t0 + N], in0=t[:, 0:N], in1=m1[:, c, t0:t0 + N], op=ALU.add),
                              [tk, "m1_%d_%d" % (c, t0)], ["mT_%d_%d" % (fc, t0)])
            if gq < 7:
                load_wq(sg_, stages[sg_][1], stages[sg_][2] + (gq + 1) * 256, stages[sg_][3])
    MK = ["mT_%d_%d" % (fc, t0) for fc in range(16) for (t0, _) in TB]
    S.mark('p4')
    S.add("pool", lambda e: e.memset(dummy[0:1, 5:6], 0.0), [], ["wbuf0", "wbuf1", "wq0", "wq1", "wq2", "wq3"])
    tiles5 = []
    for cb in range(4):
        for i in range(9):
            tiles5.append((cb, i))

    def p5_load(k):
        cb, i = tiles5[k]
        n = 128 if i < 8 else NS
        xsrc = din["x_own"][i * 128:(i + 1) * 128, cb * 512:(cb + 1) * 512] if i < 8 else din["x_smp"][:, cb * 512:(cb + 1) * 512]
        j = k % 3
        dma("sp", xin[j][0:n, :], xsrc, [], ["xin%d" % j], "xin%d" % j)

    p5_load(0)
    p5_load(1)
    for k, (cb, i) in enumerate(tiles5):
        sl_ = cb % 2
        if i == 0:
            load_w(sl_, din["w_out"], cb * 512, 512, ND)
        n = 128 if i < 8 else NS
        tc0 = i * 128 if i < 8 else NT
        ydst = dout["y_own"][i * 128:(i + 1) * 128, cb * 512:(cb + 1) * 512] if i < 8 else dout["y_smp"][:, cb * 512:(cb + 1) * 512]
        j = k % 3
        b = c4["ps"] % 4
        c4["ps"] += 1
        kb = "PS%d" % b
        if k + 2 < len(tiles5):
            p5_load(k + 2)
        for fc in range(16):
            S.add("pe", lambda e, fc=fc, b=b, n=n, tc0=tc0, sl_=sl_: e.matmul(PS[b][0:n, 0:512], lhsT=mT[:, fc, tc0:tc0 + n], rhs=wbuf[sl_][:, fc, 0:512],
                                                                              start=(fc == 0), stop=(fc == 15)), MK + ["wbuf%d" % sl_], [kb])
        S.add("dve", lambda e, j=j, b=b, n=n: e.tensor_tensor(out=oti[j][0:n, :], in0=PS[b][0:n, 0:512], in1=xin[j][0:n, :], op=ALU.add),
              [kb, "xin%d" % j], ["oti%d" % j])
        dma("sp", ydst, oti[j][0:n, :], ["oti%d" % j], [], "oti%d" % j)
    S.mark('p5')

    S.emit()
    for cm in reversed(ctxs):
        cm.__exit__(None, None, None)
    return nc


def _rope_tables(pos):
    half = 32
    inv_freq = (10000.0 ** (-np.arange(half, dtype=np.float32) / half)).astype(np.float32)
    ang = pos.astype(np.float32)[:, None] * inv_freq[None, :]
    c = np.cos(ang.astype(np.float64)).astype(np.float32)
    s = np.sin(ang.astype(np.float64)).astype(np.float32)
    return np.concatenate([c, c], 1), np.concatenate([-s, s], 1)


def kernel(**inp):
    f32 = np.float32
    x_prompt = np.asarray(inp["x_prompt"], f32)
    x_sample = np.asarray(inp["x_sample"], f32)
    meta = np.asarray(inp["meta_tokens"], f32)
    shared = {
        "w_in": np.ascontiguousarray(inp["w_in"][0], f32),
        "w_glu": np.ascontiguousarray(inp["w_glu"][0], f32),
        "w_ao": np.ascontiguousarray(inp["w_attn_out"][0], f32),
        "w_so": np.ascontiguousarray(inp["w_ssm_out"][0], f32),
        "w_out": np.ascontiguousarray(inp["w_out"][0], f32),
        "norm_gain": np.ascontiguousarray(inp["norm_gain"], f32).reshape(1, D),
        "qk_gain": np.concatenate([np.asarray(inp["q_norm_gain"], f32).reshape(1, 64),
                                   np.asarray(inp["k_norm_gain"], f32).reshape(1, 64)], 1),
        "sinks": np.asarray(inp["sinks"], f32).reshape(1, 16),
        "b_glu": np.ascontiguousarray(np.asarray(inp["b_glu"], f32).reshape(8, 128).T),
        "dsk": np.ascontiguousarray(np.tile(np.asarray(inp["d_skip"], f32).reshape(64, 16).T, (8, 1))),
        "ident": np.eye(128, dtype=f32),
        "a_re2": np.ascontiguousarray(np.tile(np.asarray(inp["a_re"][0], f32).T, (2, 1))),
        "a_im2": np.ascontiguousarray(np.tile(np.asarray(inp["a_im"][0], f32).T, (2, 1))),
        "logdt": np.asarray(inp["log_dt"], f32).reshape(1, 64),
        "kvals": np.asarray([7, 6, 5, 4, 3, 2, 1, 0, -7, -6, -5, -4, -3, -2, -1, 0, 1, 2, 3, 4, 5, 6, 7, 8], f32).reshape(1, 24),
        "krow": np.arange(260, dtype=f32).reshape(1, 260),
        "blockmask": np.kron(np.triu(np.ones((8, 8), f32)), np.ones((16, 16), f32)).astype(f32),
        "swapm": np.roll(np.eye(128, dtype=f32), 64, axis=0),
    }
    kk_, qq_ = np.meshgrid(np.arange(128), np.arange(128), indexing="ij")
    NEG = np.float32(-30000.0)
    shared["maskc"] = np.where(kk_ <= qq_, np.float32(0), NEG).astype(f32)
    shared["maskp"] = np.where(kk_ > qq_, np.float32(0), NEG).astype(f32)
    b_re = np.asarray(inp["b_re"][0], f32).transpose(1, 0, 2)
    b_im = np.asarray(inp["b_im"][0], f32).transpose(1, 0, 2)
    c_re = np.asarray(inp["c_re"][0], f32).transpose(2, 0, 1)
    c_im = np.asarray(inp["c_im"][0], f32).transpose(2, 0, 1)
    shared["b_self"] = np.ascontiguousarray(np.concatenate([b_re, b_im], 0))
    shared["b_part"] = np.ascontiguousarray(np.concatenate([b_im, b_re], 0))
    shared["c_self"] = np.ascontiguousarray(np.concatenate([c_re, c_im], 0))
    shared["c_part"] = np.ascontiguousarray(np.concatenate([c_im, c_re], 0))
    in_maps = []
    for core in range(N_CORES):
        b, half = core // 2, core % 2
        m = dict(shared)
        m["x_own"] = np.ascontiguousarray(x_prompt[b, half * NT:(half + 1) * NT])
        if half == 1:
            m["x_pre2"] = np.ascontiguousarray(x_prompt[b, 0:NT])
            m["x_pre1"] = meta.copy()
            halo = x_prompt[b, NT - 128:NT]
        else:
            m["x_pre2"] = np.concatenate([np.zeros((NT - 16, D), f32), meta], 0)
            m["x_pre1"] = np.zeros((16, D), f32)
            halo = np.zeros((128, D), f32)
        m["x_kvx"] = np.concatenate([halo, meta], 0)
        m["maskp0"] = shared["maskp"] if half == 1 else np.full((128, 128), NEG, f32)
        m["x_smp"] = np.ascontiguousarray(x_sample[core * NS:(core + 1) * NS, 0])
        base = 16 + half * NT
        cc = np.zeros((11, 128, 64), f32)
        ss = np.zeros((11, 128, 64), f32)
        for t in range(8):
            cc[t], ss[t] = _rope_tables(base + t * 128 + np.arange(128))
        cc[8], ss[8] = _rope_tables(np.maximum(base - 128 + np.arange(128), 0))
        cc[9, :16], ss[9, :16] = _rope_tables(np.arange(16))
        cc[10, :16], ss[10, :16] = _rope_tables(np.full(16, 8192))
        m["rope_cc"] = np.ascontiguousarray(cc.transpose(1, 0, 2))
        m["rope_ss"] = np.ascontiguousarray(ss.transpose(1, 0, 2))
        sl = slice(core * NS, (core + 1) * NS)
        m["cwk"] = np.ascontiguousarray(inp["cache_win_k"][0, sl], f32).reshape(NS, 128, 256)
        m["cwv"] = np.ascontiguousarray(inp["cache_win_v"][0, sl], f32).reshape(NS, 128, 256)
        m["cmk"] = np.ascontiguousarray(inp["cache_meta_k"][0, sl], f32).reshape(NS, 16, 256)
        m["cmv"] = np.ascontiguousarray(inp["cache_meta_v"][0, sl], f32).reshape(NS, 16, 256)
        sre = np.asarray(inp["state_ssm_re"][0, sl], f32)
        sim = np.asarray(inp["state_ssm_im"][0, sl], f32)
        m["st_self"] = np.ascontiguousarray(np.concatenate([sre, sim], 2))
        m["st_part"] = np.ascontiguousarray(np.concatenate([sim, sre], 2))
        in_maps.append(m)

    nc = build_program()
    res = run_bass_kernel_spmd(nc, in_maps, core_ids=list(range(N_CORES)))
    R = res.results

    y_prompt = np.zeros((4, 2048, D), f32)
    y_sample = np.zeros((128, 1, D), f32)
    p_win_k = np.zeros((1, 4, 128, 4, 64), f32)
    p_win_v = np.zeros((1, 4, 128, 4, 64), f32)
    p_meta_k = np.zeros((1, 4, 16, 4, 64), f32)
    p_meta_v = np.zeros((1, 4, 16, 4, 64), f32)
    p_re = np.zeros((1, 4, 64, 64), f32)
    p_im = np.zeros((1, 4, 64, 64), f32)
    s_win_k = np.zeros((1, 128, 128, 4, 64), f32)
    s_win_v = np.zeros((1, 128, 128, 4, 64), f32)
    s_re = np.zeros((1, 128, 64, 64), f32)
    s_im = np.zeros((1, 128, 64, 64), f32)
    for core in range(N_CORES):
        b, half = core // 2, core % 2
        r = R[core]
        y_prompt[b, half * NT:(half + 1) * NT] = r["y_own"]
        sl = slice(core * NS, (core + 1) * NS)
        y_sample[sl, 0] = r["y_smp"]
        if half == 1:
            p_win_k[0, b] = r["pwk"].reshape(128, 4, 64)
            p_win_v[0, b] = r["pwv"].reshape(128, 4, 64)
            p_re[0, b] = r["pssm"][:, 0:64]
            p_im[0, b] = r["pssm"][:, 64:128]
        else:
            p_meta_k[0, b] = r["pmk"].reshape(16, 4, 64)
            p_meta_v[0, b] = r["pmv"].reshape(16, 4, 64)
        s_win_k[0, sl] = r["swk"].reshape(NS, 128, 4, 64)
        s_win_v[0, sl] = r["swv"].reshape(NS, 128, 4, 64)
        s_re[0, sl] = r["sssm"][:, :, 0:64]
        s_im[0, sl] = r["sssm"][:, :, 64:128]
    return (y_prompt, y_sample, p_win_k, p_win_v, p_meta_k, p_meta_v, p_re, p_im, s_win_k, s_win_v, s_re, s_im)
```

```python
import os
import numpy as np
import concourse.bass as bass
import concourse.mybir as mybir
from concourse.bass_utils import run_bass_kernel_spmd

F32 = mybir.dt.float32
BF16 = mybir.dt.bfloat16
ALU = mybir.AluOpType
AF = mybir.ActivationFunctionType
AX = mybir.AxisListType

D = 2048
ND = 16
NT = 1024
NS = 16
NTS = NT + NS
EPS = 1e-6
N_CORES = 8
KTRUNC = ''
KUM = 'ab'


class Sched:
    ENGS = ("pe", "act", "dve", "pool", "sp")

    def __init__(self, nc):
        self.nc = nc
        self.ops = []

    def add(self, eng, fn, reads=(), writes=(), dma=None, fence=False):
        self.ops.append(dict(eng=eng, fn=fn, reads=tuple(reads), writes=tuple(writes), dma=dma, fence=fence))
        return len(self.ops) - 1

    def barrier(self, mk):
        keys = set()
        for o in self.ops:
            keys.update(o["reads"])
            keys.update(o["writes"])
        keys = sorted(keys, key=str)
        self.nbar = getattr(self, "nbar", 0) + 1
        bk = "bar%d" % self.nbar
        self.add("act", mk("act"), reads=keys, writes=keys + [bk])
        for e in ("dve", "pool", "sp"):
            self.add(e, mk(e), reads=[bk], writes=["%s_%s" % (bk, e)], dma=("bar_sp" if e == "sp" else None))

    def mark(self, name):
        if not hasattr(self, "marks"):
            self.marks = {}
        self.marks[name] = len(self.ops)

    def emit(self):
        nc = self.nc
        if KTRUNC:
            self.ops = self.ops[:self.marks[KTRUNC]]
        ops = self.ops
        last_w = {}
        readers = {}
        deps = [set() for _ in ops]
        for i, o in enumerate(ops):
            for b in o["reads"]:
                if b in last_w:
                    deps[i].add(last_w[b])
            for b in o["writes"]:
                if b in last_w:
                    deps[i].add(last_w[b])
                for r in readers.get(b, ()):
                    if r != i:
                        deps[i].add(r)
            for b in o["reads"]:
                readers.setdefault(b, []).append(i)
            for b in o["writes"]:
                last_w[b] = i
                readers[b] = []
        needed = set()
        prev_on = {}
        for i, o in enumerate(ops):
            keep = set()
            for d in deps[i]:
                if ops[d]["dma"] is None and ops[d]["eng"] == "pe" and o["eng"] == "pe" and o["dma"] is None:
                    continue
                keep.add(d)
            if o.get("fence") and o["eng"] in prev_on:
                keep.add(prev_on[o["eng"]])
            if o["dma"] is None:
                prev_on[o["eng"]] = i
            deps[i] = keep
            needed |= keep
        dma_keys = sorted({o["dma"] for o in ops if o["dma"] is not None}, key=str)
        sem_ctx = []
        sems = {}
        for e in self.ENGS:
            cm = nc.semaphore("s_" + e)
            sems[("eng", e)] = cm.__enter__()
            sem_ctx.append(cm)
        for n, k in enumerate(dma_keys):
            cm = nc.semaphore("d%d" % n)
            sems[("dma", k)] = cm.__enter__()
            sem_ctx.append(cm)
        cnt = {k: 0 for k in sems}
        ticket = [None] * len(ops)
        for i, o in enumerate(ops):
            if o["dma"] is not None:
                k = ("dma", o["dma"])
                cnt[k] += 16
                ticket[i] = (k, cnt[k])
            elif i in needed:
                k = ("eng", o["eng"])
                cnt[k] += 1
                ticket[i] = (k, cnt[k])
        streams = {e: [] for e in self.ENGS}
        waited = {e: {} for e in self.ENGS}
        for i, o in enumerate(ops):
            e = o["eng"]
            w = {}
            for d in deps[i]:
                k, v = ticket[d]
                if waited[e].get(k, 0) >= v:
                    continue
                w[k] = max(w.get(k, 0), v)
            for k, v in w.items():
                waited[e][k] = v
            streams[e].append((i, w))
        final = {k: v for k, v in cnt.items() if k[0] == "dma" and v > 0}

        def run_stream(e, engobj):
            for i, w in streams[e]:
                for k, v in w.items():
                    engobj.wait_ge(sems[k], v)
                ins = ops[i]["fn"](engobj)
                if ticket[i] is not None:
                    k, v = ticket[i]
                    ins.then_inc(sems[k], 16 if k[0] == "dma" else 1)
            if e == "sp":
                for k, v in final.items():
                    engobj.wait_ge(sems[k], v)

        with nc.Block() as block:
            @block.sync
            def _(eng):
                run_stream("sp", eng)

            @block.tensor
            def _(eng):
                run_stream("pe", eng)

            @block.scalar
            def _(eng):
                run_stream("act", eng)

            @block.vector
            def _(eng):
                run_stream("dve", eng)

            @block.gpsimd
            def _(eng):
                run_stream("pool", eng)
        for cm in reversed(sem_ctx):
            cm.__exit__(None, None, None)


IN_SPECS = [
    ("x_own", [NT, D]), ("x_pre2", [NT, D]), ("x_pre1", [16, D]), ("x_kvx", [144, D]), ("x_smp", [NS, D]),
    ("w_in", [D, 8704]), ("w_glu", [1024, 1024]), ("w_ao", [1024, D]), ("w_so", [1024, D]), ("w_out", [D, D]),
    ("norm_gain", [1, D]), ("qk_gain", [1, 128]), ("sinks", [1, 16]), ("b_glu", [128, 8]), ("dsk", [128, 64]),
    ("rope_cc", [128, 11, 64]), ("rope_ss", [128, 11, 64]),
    ("ident", [128, 128]),
    ("cwk", [NS, 128, 256]), ("cwv", [NS, 128, 256]), ("cmk", [NS, 16, 256]), ("cmv", [NS, 16, 256]),
    ("a_re2", [128, 64]), ("a_im2", [128, 64]), ("logdt", [1, 64]),
    ("b_self", [128, 64, 16]), ("b_part", [128, 64, 16]), ("c_self", [128, 64, 16]), ("c_part", [128, 64, 16]),
    ("kvals", [1, 24]), ("krow", [1, 260]), ("blockmask", [128, 128]), ("swapm", [128, 128]),
    ("st_self", [NS, 64, 128]), ("st_part", [NS, 64, 128]),
    ("maskc", [128, 128]), ("maskp", [128, 128]), ("maskp0", [128, 128]),
]
OUT_SPECS = [
    ("y_own", [NT, D]), ("y_smp", [NS, D]),
    ("pwk", [128, 256]), ("pwv", [128, 256]), ("pmk", [16, 256]), ("pmv", [16, 256]),
    ("pssm", [64, 128]),
    ("swk", [NS, 128, 256]), ("swv", [NS, 128, 256]), ("sssm", [NS, 64, 128]),
]


def build_program():
    nc = bass.Bass("TRN2", target_bir_lowering=False)
    S = Sched(nc)
    din = {n: nc.dram_tensor(n, s, F32, kind="ExternalInput").ap() for n, s in IN_SPECS}
    dout = {n: nc.dram_tensor(n, s, F32, kind="ExternalOutput").ap() for n, s in OUT_SPECS}
    ctxs = []

    def sb(name, shape, dt):
        cm = nc.sbuf_tensor("sb_" + name, shape, dt)
        t = cm.__enter__()
        ctxs.append(cm)
        return t

    def psum(name, shape, dt):
        cm = nc.psum_tensor(name, shape, dt)
        t = cm.__enter__()
        ctxs.append(cm)
        return t

    KB = 1024
    ARENA = 150 * KB
    AR = sb("arena", [128, ARENA // 2], BF16)

    def carve(off, n, dt):
        assert off % 4 == 0
        if dt == BF16:
            assert off + 2 * n <= ARENA, (off, n)
            return AR[:, off // 2: off // 2 + n]
        assert off + 4 * n <= ARENA, (off, n)
        return AR[:, off // 2: off // 2 + 2 * n].bitcast(F32)

    PS = [psum("ps%d" % i, [128, 512], F32) for i in range(8)]
    PSB = [p.bitcast(BF16) for p in PS]
    identf = sb("identf", [128, 128], F32)
    identb = sb("identb", [128, 128], BF16)
    epsb = sb("epsb", [128, 1], F32)
    qkg_bc = sb("qkg_bc", [128, 128], F32)
    ropecc = sb("ropecc", [128, 11, 64], F32)
    ropess = sb("ropess", [128, 11, 64], F32)
    stat = sb("stat", [128, 64], F32)
    dummy = sb("dummyk", [128, 8], F32)
    xnT = sb("xnT", [128, ND, NTS], BF16)
    Um = sb("Um", [128, 64, 18], BF16)
    kvals = sb("kvals", [128, 24], F32)
    krow = sb("krow", [128, 260], F32)
    sgn = sb("sgn", [128, 2], F32)
    halfpi = sb("halfpi", [128, 1], F32)
    onec = sb("onec", [128, 1], F32)
    pp = sb("pp", [128, 16, 64], F32)
    mumag = sb("mumag", [128, 64], F32)
    G258 = sb("G258", [128, 64], F32)
    C258 = sb("C258", [128, 64], F32)
    S258 = sb("S258", [128, 64], F32)
    dsk = sb("dsk", [128, 64], F32)
    blockmask = sb("blockmask", [128, 128], F32)
    swapm = sb("swapm", [128, 128], F32)
    b_glu = sb("b_glu", [128, 8], F32)
    expsink = sb("expsink", [128, 16], F32)
    masks = sb("masks", [128, 3, 128], BF16)
    qs16 = sb("qs16", [NS, 1024], BF16)
    onesb = sb("onesb", [128, 64], BF16)

    u8own = carve(0, 8192, BF16).rearrange("p (g s h) -> p g s h", g=64, s=8)
    u8p2 = carve(16 * KB, 8192, BF16).rearrange("p (g s h) -> p g s h", g=64, s=8)
    wbuf = [carve((68 + 16 * i) * KB, ND * 512, BF16).rearrange("p (c n) -> p c n", c=ND) for i in range(2)]
    u8m = carve(34 * KB, 8192, BF16).rearrange("p (g s h) -> p g s h", g=64, s=8)
    gain_bc = carve(51 * KB, D, F32)
    xbuf = [carve((100 + 8 * i) * KB, D, F32) for i in range(2)]
    xnb = [carve((116 + 4 * i) * KB, D, BF16) for i in range(2)]
    tmpq = [carve((124 + 4 * i) * KB, 1024, F32) for i in range(3)]
    sqj = carve(128 * KB, D, F32)
    xnTt = [carve((136 + 4 * i) * KB, ND * 128, BF16).rearrange("p (c t) -> p c t", c=ND) for i in range(2)]
    kf = [carve((144 + i) * KB, 256, F32) for i in range(2)]
    vf = [carve((146 + i) * KB, 256, F32) for i in range(2)]
    utok1 = carve(148 * KB, 1024, BF16)

    cnt = {"x": 0, "ps": 0, "kv": 0}

    def dma(eng, out, in_, reads, writes, key):
        S.add(eng, lambda e: e.dma_start(out=out, in_=in_), reads=reads, writes=writes, dma=key)

    def mk_bar(e):
        col = {"act": 0, "dve": 1, "pool": 2, "sp": 3}[e]
        if e == "act":
            return lambda en: en.copy(out=dummy[0:1, col:col + 1], in_=dummy[0:1, 4:5])
        if e == "sp":
            return lambda en: en.dma_start(out=dummy[0:1, col:col + 1], in_=din["kvals"][0:1, 0:1])
        return lambda en: en.memset(dummy[0:1, col:col + 1], 0.0)

    S.add("pool", lambda e: e.memset(dummy[:], 0.0), [], ["dummy"])
    dma("sp", identf[:], din["ident"], [], ["identf"], "identf")
    S.add("dve", lambda e: e.tensor_copy(out=identb[:], in_=identf[:]), ["identf"], ["identb"])
    S.add("pool", lambda e: e.memset(epsb[:], EPS), [], ["epsb"])
    dma("sp", gain_bc, din["norm_gain"].partition_broadcast(128), [], ["gain_bc"], "gain_bc")
    dma("sp", qkg_bc[:], din["qk_gain"].partition_broadcast(128), [], ["qkg_bc"], "qkg_bc")
    dma("sp", ropecc[:], din["rope_cc"], [], ["ropecc"], "ropecc")
    dma("sp", ropess[:], din["rope_ss"], [], ["ropess"], "ropess")

    def load_w(slot, src, col0, ncols, nk):
        key = "wbuf%d" % slot
        v = src.rearrange("(c p) n -> p c n", p=128)
        step = 4
        for c0 in range(0, nk, step):
            c1 = min(nk, c0 + step)
            dma("pool", wbuf[slot][:, c0:c1, 0:ncols], v[:, c0:c1, col0:col0 + ncols], [], [key], key)

    def norm_A(src_rows, n):
        j = (cnt["x"] % 2) if (xbuf[0] is not xbuf[1]) else 0
        cnt["x"] += 1
        xb_, xn_ = xbuf[j], xnb[j]
        sq_, g_ = sqj, gain_bc
        kx, kn = "xbuf%d" % j, "xnb%d" % j
        dma("sp", xb_[0:n, :], src_rows, [], [kx], kx)
        S.add("act", lambda e: e.activation(out=sq_[0:n, :], in_=xb_[0:n, :], func=AF.Square), [kx], ["tq1", "tq2a", "tq2b"])
        S.add("dve", lambda e: e.tensor_reduce(out=stat[0:n, 0:1], in_=sq_[0:n, :], axis=AX.X, op=ALU.add), ["tq1", "tq2a", "tq2b"], ["stat0"])
        S.add("act", lambda e: e.activation(out=stat[0:n, 1:2], in_=stat[0:n, 0:1], func=AF.Sqrt, bias=epsb[0:n, 0:1], scale=1.0 / D),
              ["stat0", "epsb"], ["stat1"])
        S.add("dve", lambda e: e.reciprocal(out=stat[0:n, 2:3], in_=stat[0:n, 1:2]), ["stat1"], ["stat2"])
        S.add("dve", lambda e: e.scalar_tensor_tensor(out=xn_[0:n, :], in0=xb_[0:n, :], scalar=stat[0:n, 2:3], in1=g_[0:n, :],
                                                      op0=ALU.mult, op1=ALU.mult), [kx, "stat2", "gain_bc"], [kn])
        return xn_, kn, n

    def norm_B(h, dst_fn, dst_keys):
        xn_, kn, n = h
        for half in range(2):
            b = 4 + (cnt["ps"] % 2)
            cnt["ps"] += 1
            kb = "PS%d" % b
            pv = PSB[b][:, 0:1024].rearrange("p (c t) -> p c t", c=8)
            for c in range(8):
                dc = half * 8 + c
                S.add("pe", lambda e, c=c, dc=dc, pv=pv: e.transpose(out=pv[:, c, 0:n], in_=xn_[0:n, dc * 128:(dc + 1) * 128],
                                                                      identity=identb[0:n, 0:n]), [kn, "identb"], [kb])
            dst = dst_fn(half * 8)
            if half == 0:
                S.add("act", lambda e, pv=pv, dst=dst: e.copy(out=dst, in_=pv[:, :, 0:n]), [kb], dst_keys)
            else:
                S.add("dve", lambda e, pv=pv, dst=dst: e.tensor_copy(out=dst, in_=pv[:, :, 0:n]), [kb], dst_keys)

    def norm_tile(src_rows, n, dst_fn, dst_keys):
        norm_B(norm_A(src_rows, n), dst_fn, dst_keys)

    def headnorm_rope(src_ps, src_keys, n, NH, ti, goff, extra_scale, dst, dst_keys):
        W = NH * 64
        t0, t1, t2 = tmpq
        c0 = 0
        for ap_, w in src_ps:
            S.add("act", lambda e, ap_=ap_, c0=c0, w=w: e.copy(out=t0[0:n, c0:c0 + w], in_=ap_), src_keys, ["tq0"])
            S.add("act", lambda e, ap_=ap_, c0=c0, w=w: e.activation(out=t1[0:n, c0:c0 + w], in_=ap_, func=AF.Square), src_keys, ["tq1"])
            c0 += w
        v3 = lambda t: t[0:n, 0:W].rearrange("p (h d) -> p h d", d=64)
        S.add("dve", lambda e: e.tensor_reduce(out=stat[0:n, 8:8 + NH], in_=v3(t1), axis=AX.X, op=ALU.add), ["tq1"], ["stq"])
        S.add("act", lambda e: e.activation(out=stat[0:n, 24:24 + NH], in_=stat[0:n, 8:8 + NH], func=AF.Sqrt, bias=epsb[0:n, 0:1], scale=1.0 / 64),
              ["stq", "epsb"], ["stq2"])
        S.add("dve", lambda e: e.reciprocal(out=stat[0:n, 40:40 + NH], in_=stat[0:n, 24:24 + NH]), ["stq2"], ["stq3"])
        if extra_scale != 1.0:
            S.add("dve", lambda e: e.tensor_scalar(out=stat[0:n, 40:40 + NH], in0=stat[0:n, 40:40 + NH], scalar1=float(extra_scale), scalar2=0.0,
                                                   op0=ALU.mult, op1=ALU.add), ["stq3"], ["stq3"])
        gb = qkg_bc[0:n, goff:goff + 64].unsqueeze(1).to_broadcast([n, NH, 64])
        S.add("dve", lambda e: e.tensor_tensor(out=v3(t0), in0=v3(t0), in1=gb, op=ALU.mult), ["tq0", "qkg_bc"], ["tq0"])
        cc = ropecc[0:n, ti, :].unsqueeze(1).to_broadcast([n, NH, 64])
        S.add("dve", lambda e: e.tensor_tensor(out=v3(t1), in0=v3(t0), in1=cc, op=ALU.mult), ["tq0", "ropecc", "stq"], ["tq1"])
        s_lo = ropess[0:n, ti, 0:32].unsqueeze(1).to_broadcast([n, NH, 32])
        s_hi = ropess[0:n, ti, 32:64].unsqueeze(1).to_broadcast([n, NH, 32])
        S.add("dve", lambda e: e.tensor_tensor(out=v3(t2)[:, :, 0:32], in0=v3(t0)[:, :, 32:64], in1=s_lo, op=ALU.mult), ["tq0", "ropess"], ["tq2a"])
        S.add("dve", lambda e: e.tensor_tensor(out=v3(t2)[:, :, 32:64], in0=v3(t0)[:, :, 0:32], in1=s_hi, op=ALU.mult), ["tq0", "ropess"], ["tq2b"])
        S.add("dve", lambda e: e.tensor_tensor(out=v3(t1), in0=v3(t1), in1=v3(t2), op=ALU.add), ["tq1", "tq2a", "tq2b"], ["tq1"])
        rb = stat[0:n, 40:40 + NH].unsqueeze(2).to_broadcast([n, NH, 64])
        dv = dst.rearrange("p (h d) -> p h d", d=64)
        S.add("dve", lambda e: e.tensor_tensor(out=dv, in0=v3(t1), in1=rb, op=ALU.mult), ["tq1", "stq3"], dst_keys)

    def proj_tok(lhs_fn, lhs_keys, n, slot, ncols, bank):
        kb = "PS%d" % bank
        for dc in range(ND):
            S.add("pe", lambda e, dc=dc: e.matmul(PS[bank][0:n, 0:ncols], lhsT=lhs_fn(dc), rhs=wbuf[slot][:, dc, 0:ncols],
                                                  start=(dc == 0), stop=(dc == ND - 1)), lhs_keys + ["wbuf%d" % slot], [kb])
        return kb

    pass
    S.mark('p1a')
    S.add("pool", lambda e: e.memset(u8m, 0.0), [], ["u8m"])
    load_w(0, din["w_in"], 2560, 512, ND)
    load_w(1, din["w_in"], 3072, 512, ND)

    def u_tile(lhs_fn, lhs_keys, n, evac_fn):
        for blk in range(2):
            b = cnt["kv"] % 2
            cnt["kv"] += 1
            kb = proj_tok(lhs_fn, lhs_keys, n, blk, 512, b)
            evac_fn(blk, PS[b], kb)

    S.mark("u_pre1")
    xp2 = din["x_pre2"].rearrange("(c s) d -> c s d", s=8)
    tl = [("smp", 0)]
    for i in range(8):
        tl += [("own", i), ("pre", i)]

    def p1_src(kind, i):
        if kind == "smp":
            return din["x_smp"], NS
        if kind == "own":
            return din["x_own"][i * 128:(i + 1) * 128, :], 128
        return xp2[:, i, :], 128

    def p1_finish(kind, i, h):
        if kind == "smp":
            norm_B(h, lambda dc0: xnT[:, dc0:dc0 + 8, NT:NTS], ["xnT_s"])
        elif kind == "own":
            norm_B(h, lambda dc0, i=i: xnT[:, dc0:dc0 + 8, i * 128:(i + 1) * 128], ["xnT_%d" % i])
        else:
            jt = i % 2
            tt = xnTt[jt]
            norm_B(h, lambda dc0, tt=tt: tt[:, dc0:dc0 + 8, 0:128], ["xnTt%d" % jt])

            def ev_p2(blk, ps_, kb, i=i):
                S.add("act", lambda e: e.copy(out=u8p2[:, blk * 32:(blk + 1) * 32, i, :], in_=ps_[:, 0:512].rearrange("p (g h) -> p g h", h=16)),
                      [kb], ["u8p2_%d_%d" % (i, blk)])
            u_tile(lambda dc, tt=tt: tt[:, dc, 0:128], ["xnTt%d" % jt], 128, ev_p2)

    pend1 = None
    for (kind, i) in tl:
        src_, n_ = p1_src(kind, i)
        h_ = norm_A(src_, n_)
        if pend1 is not None:
            p1_finish(*pend1)
        pend1 = (kind, i, h_)
    p1_finish(*pend1)
    S.mark("u_pre2")
    OWNK = ["xnT_%d" % i for i in range(8)]
    for i in range(8):
        def ev_own(blk, ps_, kb, i=i):
            S.add("act", lambda e: e.copy(out=u8own[:, blk * 32:(blk + 1) * 32, i, :], in_=ps_[:, 0:512].rearrange("p (g h) -> p g h", h=16)),
                  [kb], ["u8own_%d_%d" % (i, blk)])
        u_tile(lambda dc, i=i: xnT[:, dc, i:NT:8], OWNK, 128, ev_own)

    S.mark("u_own")

    def ev_smp(blk, ps_, kb):
        S.add("act", lambda e: e.copy(out=u8m[0:NS, blk * 32:(blk + 1) * 32, 7, :], in_=ps_[0:NS, 0:512].rearrange("p (g h) -> p g h", h=16)),
              [kb, "u8m"], ["u8m_s%d" % blk])
    u_tile(lambda dc: xnT[:, dc, NT:NTS], ["xnT_s"], NS, ev_smp)
    j = cnt["x"] % 2
    tt = xnTt[j]
    norm_tile(din["x_pre1"], 16, lambda dc0, tt=tt: tt[:, dc0:dc0 + 8, 0:16], ["xnTt%d" % j])

    def ev_pre1(blk, ps_, kb):
        S.add("act", lambda e: e.copy(out=utok1[0:16, blk * 512:(blk + 1) * 512], in_=ps_[0:16, 0:512]), [kb], ["utok1_%d" % blk])
    u_tile(lambda dc, tt=tt: tt[:, dc, 0:16], ["xnTt%d" % j], 16, ev_pre1)
    for s_ in range(8):
        dma("sp", u8m[32:34, :, s_, :], utok1[s_:16:8, :].rearrange("p (g h) -> p g h", h=16), ["utok1_0", "utok1_1", "u8m"], ["u8m_p1"], "u8m_p1")

    S.mark("u_smp")
    U8K_P2 = ["u8p2_%d_%d" % (i, b) for i in range(8) for b in range(2)]
    U8K_OWN = ["u8own_%d_%d" % (i, b) for i in range(8) for b in range(2)]
    U8K_M = ["u8m", "u8m_p1", "u8m_s0", "u8m_s1"]
    for part in range(2):
        for q4 in range(4):
            bk = 6 + q4 % 2
            kb = "PS%d" % bk
            pv = PSB[bk][:, 0:1024].rearrange("p (g t) -> p g t", t=64)
            for gl in range(16):
                gg = q4 * 16 + gl
                fence = (part == 1 and q4 == 0 and gl == 0)
                if part == 0:
                    S.add("pe", lambda e, gl=gl, gg=gg, pv=pv: e.transpose(out=pv[:, gl, 0:2], in_=u8m[32:34, gg, :, :].rearrange("p s h -> p (s h)"),
                                                                            identity=identb[32:34, 32:34]), U8K_M + ["identb"], [kb])
                else:
                    S.add("pe", lambda e, gl=gl, gg=gg, pv=pv: e.transpose(out=pv[:, gl, 0:16], in_=u8m[0:NS, gg, :, :].rearrange("p s h -> p (s h)"),
                                                                            identity=identb[0:NS, 0:NS]), U8K_M + ["identb"], [kb], fence=fence)
            if part == 0:
                S.add("dve", lambda e, q4=q4, pv=pv: e.tensor_copy(out=Um[:, q4 * 16:(q4 + 1) * 16, 0:2], in_=pv[:, :, 0:2]), [kb], ["Um_%d" % q4])
            else:
                S.add("dve", lambda e, q4=q4, pv=pv: e.tensor_copy(out=Um[:, q4 * 16:(q4 + 1) * 16, 2:18], in_=pv[:, :, 0:16]), [kb], ["Umb_%d" % q4])
    UMK = ["Um_%d" % q for q in range(4)] + ["Umb_%d" % q for q in range(4)]

    S.mark('p1b')
    S.barrier(mk_bar)
    S.mark('bar1')
    MAGIC = 12582912.0
    TWO_PI = float(2.0 * np.pi)
    SC = 1036
    GB = 4
    Ere = carve(51 * KB, 1536, F32).rearrange("p (g k) -> p g k", k=24)
    Eim = carve(57 * KB, 1536, F32).rearrange("p (g k) -> p g k", k=24)
    bbs = carve(63 * KB, 1024, F32).rearrange("p (g h) -> p g h", h=16)
    bbp = carve(67 * KB, 1024, F32).rearrange("p (g h) -> p g h", h=16)
    cself = carve(71 * KB, 1024, F32).rearrange("p (g h) -> p g h", h=16)
    cpart = carve(75 * KB, 1024, F32).rearrange("p (g h) -> p g h", h=16)
    sc = [carve(79 * KB + i * 4352, SC, F32) for i in range(4)]
    cosT = carve(96 * KB, GB * 259, F32).rearrange("p (g k) -> p g k", k=259)
    sinT = carve(96 * KB + 4352, GB * 259, F32).rearrange("p (g k) -> p g k", k=259)
    o_ = 96 * KB + 2 * 4352
    A32 = carve(o_, GB * 128, F32).rearrange("p (g q) -> p g q", q=128); o_ += 2 * KB
    R32 = carve(o_, GB * 128, F32).rearrange("p (g q) -> p g q", q=128); o_ += 2 * KB
    Ab = carve(o_, 2 * GB * 128, BF16).rearrange("p (w g q) -> p w g q", w=2, q=128); o_ += 2 * KB
    Mab = carve(o_, 2 * GB * 128, BF16).rearrange("p (w g q) -> p w g q", w=2, q=128); o_ += 2 * KB
    Msb = carve(o_, 2 * GB * 128, BF16).rearrange("p (w g q) -> p w g q", w=2, q=128); o_ += 2 * KB
    Mib = carve(o_, GB * 128, BF16).rearrange("p (g q) -> p g q", q=128); o_ += KB
    mt = []
    for i in range(2):
        mt.append(carve(o_, GB * 128, F32).rearrange("p (g q) -> p g q", q=128)); o_ += 2 * KB
    NCH = 258
    NU = 274
    Ub = carve(o_, GB * NU, BF16).rearrange("p (g t) -> p g t", t=NU); o_ += 2304
    Xt = carve(o_, GB * NCH, F32).rearrange("p (g t) -> p g t", t=NCH); o_ += 4352
    Xt2 = [carve(o_ + 1056 * i, NCH, F32) for i in range(2)]; o_ += 2112
    Gs = carve(o_, GB * 259, F32).rearrange("p (g k) -> p g k", k=259); o_ += 4352
    Xsm = carve(o_, GB * NS, F32).rearrange("p (g b) -> p g b", b=NS); o_ += 256
    st_s = carve(o_, GB * 128, F32).rearrange("p (g q) -> p g q", q=128); o_ += 2 * KB
    st_p = carve(o_, GB * 128, F32).rearrange("p (g q) -> p g q", q=128); o_ += 2 * KB
    hs1 = carve(o_, GB * NS, F32).rearrange("p (g b) -> p g b", b=NS); o_ += 256
    hs2 = carve(o_, GB * NS, F32).rearrange("p (g b) -> p g b", b=NS); o_ += 256
    hso = carve(o_, GB * 128, F32).rearrange("p (g q) -> p g q", q=128); o_ += 2 * KB
    sE = carve(o_, 2 * GB * 24, F32).rearrange("p (w g k) -> p w g k", w=2, k=24); o_ += KB
    hfo = carve(o_, 128, F32); o_ += 512
    assert o_ <= ARENA, o_
    o2 = o_
    Pcs = carve(o2, 2 * GB * 128, BF16).rearrange("p (w g q) -> p w g q", w=2, q=128); o2 += 2 * KB
    y8 = carve(o2, 8 * 128, BF16).rearrange("p (t f) -> p t f", f=128); o2 += 2 * KB
    y8s = carve(o2, 128, BF16); o2 += 256
    assert o2 <= ARENA, o2
    yA = sb("yA", [128, GB * 128], F32)[:].rearrange("p (g q) -> p g q", q=128)
    yB = sb("yB", [128, GB * 128], F32)[:].rearrange("p (g q) -> p g q", q=128)
    b3 = o2
    ygb = carve(b3, GB * 128, BF16).rearrange("p (g q) -> p g q", q=128)
    ysA = carve(b3 + 1024, GB * NS, F32).rearrange("p (g b) -> p g b", b=NS)
    ysB = carve(b3 + 1280, GB * NS, F32).rearrange("p (g b) -> p g b", b=NS)
    ygs = carve(b3 + 1536, GB * NS, BF16).rearrange("p (g b) -> p g b", b=NS)
    hn1 = carve(b3 + 1664, GB * NS, F32).rearrange("p (g b) -> p g b", b=NS)
    hn2 = carve(b3 + 1920, GB * NS, F32).rearrange("p (g b) -> p g b", b=NS)
    Hins = carve(b3 + 2176, GB * NS, BF16).rearrange("p (g b) -> p g b", b=NS)
    assert b3 + 2304 <= ARENA, b3
    ST = carve(34 * KB, 8 * NTS, BF16).rearrange("p (c t) -> p c t", t=NTS)
    bself = carve(96 * KB, 1024, F32).rearrange("p (g h) -> p g h", h=16)
    bpart = carve(100 * KB, 1024, F32).rearrange("p (g h) -> p g h", h=16)
    a_re2 = carve(104 * KB, 64, F32)
    a_im2 = carve(104 * KB + 256, 64, F32)
    ldt = carve(104 * KB + 512, 64, F32)

    dma("sp", a_re2, din["a_re2"], [], ["a_re2"], "a_re2")
    dma("sp", a_im2, din["a_im2"], [], ["a_im2"], "a_im2")
    dma("sp", ldt, din["logdt"].partition_broadcast(128), [], ["ldt"], "ldt")
    dma("sp", bself, din["b_self"], [], ["bself"], "bself")
    dma("sp", bpart, din["b_part"], [], ["bpart"], "bpart")
    dma("sp", cself, din["c_self"], [], ["cself"], "cself")
    dma("sp", cpart, din["c_part"], [], ["cpart"], "cpart")
    dma("sp", kvals[:], din["kvals"].partition_broadcast(128), [], ["kvals"], "kvals")
    dma("sp", krow[:], din["krow"].partition_broadcast(128), [], ["krow"], "krow")
    dma("sp", blockmask[:], din["blockmask"], [], ["blockmask"], "blockmask")
    dma("sp", swapm[:], din["swapm"], [], ["swapm"], "swapm")
    dma("sp", dsk[:], din["dsk"], [], ["dsk"], "dsk")
    S.add("pool", lambda e: e.memset(sgn[0:64, 0:1], -1.0), [], ["sgn_a"])
    S.add("pool", lambda e: e.memset(sgn[64:128, 0:1], 1.0), [], ["sgn_b"])
    S.add("pool", lambda e: e.memset(sgn[0:64, 1:2], 1.0), [], ["sgn_c"])
    S.add("pool", lambda e: e.memset(sgn[64:128, 1:2], -1.0), [], ["sgn_d"])
    S.add("pool", lambda e: e.memset(halfpi[:], float(np.pi / 2)), [], ["halfpi"])
    S.add("pool", lambda e: e.memset(onec[:], 1.0), [], ["onec"])
    SGN = ["sgn_a", "sgn_b", "sgn_c", "sgn_d"]

    def sincos(ang, cos_out, sin_out, F, rk, wk_cos, wk_sin):
        tA, tB, tC = sc[1][:, 0:F], sc[2][:, 0:F], sc[3][:, 0:F]
        S.add("dve", lambda e: e.tensor_scalar(out=tA, in0=ang, scalar1=float(1.0 / TWO_PI), scalar2=MAGIC, op0=ALU.mult, op1=ALU.add), rk, ["sc1"])
        S.add("dve", lambda e: e.tensor_scalar(out=tA, in0=tA, scalar1=-MAGIC, scalar2=-TWO_PI, op0=ALU.add, op1=ALU.mult), ["sc1"], ["sc1"])
        S.add("dve", lambda e: e.tensor_tensor(out=tA, in0=tA, in1=ang, op=ALU.add), ["sc1"] + rk, ["sc1"])
        S.add("act", lambda e: e.activation(out=tB, in_=tA, func=AF.Sin, scale=0.5), ["sc1"], ["sc2"])
        S.add("act", lambda e: e.activation(out=tC, in_=tA, func=AF.Sin, scale=0.5, bias=halfpi[:, 0:1]), ["sc1", "halfpi"], ["sc3"])
        S.add("dve", lambda e: e.scalar_tensor_tensor(out=sin_out, in0=tB, scalar=2.0, in1=tC, op0=ALU.mult, op1=ALU.mult), ["sc2", "sc3"], wk_sin)
        S.add("act", lambda e: e.activation(out=tA, in_=tB, func=AF.Square, scale=float(np.sqrt(2.0))), ["sc2", "sc1"], ["sc1"])
        S.add("act", lambda e: e.activation(out=cos_out, in_=tA, func=AF.Identity, scale=-1.0, bias=onec[:, 0:1]), ["sc1", "onec"], wk_cos)

    PPK = lambda i: "pp%d" % i
    S.add("act", lambda e: e.activation(out=pp[:, 0, :], in_=ldt, func=AF.Exp), ["ldt"], [PPK(0)])
    S.add("dve", lambda e: e.tensor_tensor(out=pp[:, 1, :], in0=a_re2, in1=pp[:, 0, :], op=ALU.mult), ["a_re2", PPK(0)], [PPK(1)])
    S.add("dve", lambda e: e.tensor_tensor(out=pp[:, 2, :], in0=a_im2, in1=pp[:, 0, :], op=ALU.mult), ["a_im2", PPK(0)], [PPK(2)])
    kv_b = kvals[:].unsqueeze(1).to_broadcast([128, 32, 24])
    v24 = lambda t: t[:, 0:768].rearrange("p (g k) -> p g k", k=24)
    for hf in range(2):
        gh = slice(hf * 32, hf * 32 + 32)
        S.add("dve", lambda e, gh=gh: e.tensor_tensor(out=v24(sc[0]), in0=pp[:, 2, gh].unsqueeze(2).to_broadcast([128, 32, 24]), in1=kv_b, op=ALU.mult),
              [PPK(2), "kvals", "sc0"], ["sc0"])
        ere_h = Ere[:, gh, :].rearrange("p g k -> p (g k)")
        eim_h = Eim[:, gh, :].rearrange("p g k -> p (g k)")
        sincos(sc[0][:, 0:768], ere_h, eim_h, 768, ["sc0"], ["Ere"], ["Eim"])
        S.add("dve", lambda e, gh=gh: e.tensor_tensor(out=v24(sc[0]), in0=pp[:, 1, gh].unsqueeze(2).to_broadcast([128, 32, 24]), in1=kv_b, op=ALU.mult),
              [PPK(1), "kvals", "sc0"], ["sc0"])
        S.add("act", lambda e: e.activation(out=sc[0][:, 0:768], in_=sc[0][:, 0:768], func=AF.Exp), ["sc0"], ["sc0"])
        S.add("act", lambda e, gh=gh: e.copy(out=mumag[:, gh], in_=v24(sc[0])[:, :, 23]), ["sc0"], ["mumag"])
        S.add("dve", lambda e, ere_h=ere_h: e.tensor_tensor(out=ere_h, in0=ere_h, in1=sc[0][:, 0:768], op=ALU.mult), ["Ere", "sc0"], ["Ere"])
        S.add("dve", lambda e, eim_h=eim_h: e.tensor_tensor(out=eim_h, in0=eim_h, in1=sc[0][:, 0:768], op=ALU.mult), ["Eim", "sc0"], ["Eim"])
    S.add("dve", lambda e: e.tensor_scalar(out=pp[:, 6, :], in0=pp[:, 2, :], scalar1=float(8.0 / TWO_PI), scalar2=MAGIC, op0=ALU.mult, op1=ALU.add), [PPK(2)], [PPK(6)])
    S.add("dve", lambda e: e.tensor_scalar(out=pp[:, 6, :], in0=pp[:, 6, :], scalar1=-MAGIC, scalar2=-TWO_PI, op0=ALU.add, op1=ALU.mult), [PPK(6)], [PPK(6)])
    S.add("dve", lambda e: e.scalar_tensor_tensor(out=pp[:, 3, :], in0=pp[:, 2, :], scalar=8.0, in1=pp[:, 6, :], op0=ALU.mult, op1=ALU.add), [PPK(2), PPK(6)], [PPK(3)])
    e1r, e1i = Ere[:, :, 16], Eim[:, :, 16]
    S.add("dve", lambda e: e.tensor_scalar(out=pp[:, 7, :], in0=e1r, scalar1=-1.0, scalar2=0.0, op0=ALU.add, op1=ALU.add), ["Ere"], [PPK(7)])
    S.add("dve", lambda e: e.tensor_tensor(out=pp[:, 8, :], in0=a_re2, in1=a_re2, op=ALU.mult), ["a_re2"], [PPK(8)])
    S.add("dve", lambda e: e.tensor_tensor(out=pp[:, 9, :], in0=a_im2, in1=a_im2, op=ALU.mult), ["a_im2"], [PPK(9)])
    S.add("dve", lambda e: e.tensor_tensor(out=pp[:, 8, :], in0=pp[:, 8, :], in1=pp[:, 9, :], op=ALU.add), [PPK(8), PPK(9)], [PPK(8)])
    S.add("dve", lambda e: e.reciprocal(out=pp[:, 8, :], in_=pp[:, 8, :]), [PPK(8)], [PPK(8)])
    S.add("dve", lambda e: e.tensor_tensor(out=pp[:, 9, :], in0=pp[:, 7, :], in1=a_re2, op=ALU.mult), [PPK(7), "a_re2", PPK(8)], [PPK(9)])
    S.add("dve", lambda e: e.tensor_tensor(out=pp[:, 10, :], in0=e1i, in1=a_im2, op=ALU.mult), ["Eim", "a_im2"], [PPK(10)])
    S.add("dve", lambda e: e.tensor_tensor(out=pp[:, 9, :], in0=pp[:, 9, :], in1=pp[:, 10, :], op=ALU.add), [PPK(9), PPK(10)], [PPK(9)])
    S.add("dve", lambda e: e.tensor_tensor(out=pp[:, 4, :], in0=pp[:, 9, :], in1=pp[:, 8, :], op=ALU.mult), [PPK(9), PPK(8)], [PPK(4)])
    S.add("dve", lambda e: e.tensor_tensor(out=pp[:, 9, :], in0=e1i, in1=a_re2, op=ALU.mult), ["Eim", "a_re2", PPK(4)], [PPK(9)])
    S.add("dve", lambda e: e.tensor_tensor(out=pp[:, 10, :], in0=pp[:, 7, :], in1=a_im2, op=ALU.mult), [PPK(7), "a_im2", PPK(9)], [PPK(10)])
    S.add("dve", lambda e: e.tensor_tensor(out=pp[:, 9, :], in0=pp[:, 9, :], in1=pp[:, 10, :], op=ALU.subtract), [PPK(9), PPK(10)], [PPK(9)])
    S.add("dve", lambda e: e.tensor_tensor(out=pp[:, 9, :], in0=pp[:, 9, :], in1=pp[:, 8, :], op=ALU.mult), [PPK(9), PPK(8)], [PPK(9)])
    S.add("dve", lambda e: e.tensor_scalar(out=pp[:, 5, :], in0=pp[:, 9, :], scalar1=sgn[:, 0:1], scalar2=1.0, op0=ALU.mult, op1=ALU.mult), [PPK(9)] + SGN, [PPK(5)])
    wr_b = pp[:, 4, :].unsqueeze(2).to_broadcast([128, 64, 16])
    swi_b = pp[:, 5, :].unsqueeze(2).to_broadcast([128, 64, 16])
    v16 = lambda t: t[:, 0:1024].rearrange("p (g h) -> p g h", h=16)
    S.add("dve", lambda e: e.tensor_tensor(out=v16(sc[0]), in0=bself, in1=wr_b, op=ALU.mult), ["bself", PPK(4), "sc0"], ["sc0"])
    S.add("pool", lambda e: e.tensor_tensor(out=v16(sc[1]), in0=bpart, in1=swi_b, op=ALU.mult), ["bpart", PPK(5), "sc1"], ["sc1"])
    S.add("dve", lambda e: e.tensor_tensor(out=bbs, in0=v16(sc[0]), in1=v16(sc[1]), op=ALU.add), ["sc0", "sc1"], ["bbs"])
    S.add("dve", lambda e: e.tensor_tensor(out=v16(sc[0]), in0=bpart, in1=wr_b, op=ALU.mult), ["bpart", PPK(4), "sc0"], ["sc0"])
    S.add("pool", lambda e: e.tensor_tensor(out=v16(sc[1]), in0=bself, in1=swi_b, op=ALU.mult), ["bself", PPK(5), "sc1"], ["sc1"])
    S.add("dve", lambda e: e.tensor_tensor(out=bbp, in0=v16(sc[0]), in1=v16(sc[1]), op=ALU.subtract), ["sc0", "sc1"], ["bbp"])
    S.mark('params')
    S.barrier(mk_bar)
    S.mark('bar2')

    S.add("pool", lambda e: e.memset(Gs, 0.0), [], ["Gs"])

    def bc4(T, V):
        return (T.unsqueeze(3).to_broadcast([128, GB, 8, 16]), V.unsqueeze(2).to_broadcast([128, GB, 8, 16]))

    def v4(t):
        return t.rearrange("p g (s h) -> p g s h", h=16)

    def gen_mat(out_ap, T1, V1, T2, V2, op2, rk, wk, add_eng="dve"):
        a0, a1 = bc4(T1, V1)
        b0, b1 = bc4(T2, V2)
        S.add("dve", lambda e: e.tensor_tensor(out=v4(mt[0]), in0=a0, in1=a1, op=ALU.mult), rk, ["mt0"])
        S.add("pool", lambda e: e.tensor_tensor(out=v4(mt[1]), in0=b0, in1=b1, op=ALU.mult), rk, ["mt1"])
        S.add(add_eng, lambda e: e.tensor_tensor(out=out_ap, in0=mt[0], in1=mt[1], op=op2), ["mt0", "mt1"], wk)

    def ssm_frontA(gb):
        g8 = slice(gb * GB, gb * GB + GB)
        sEim, nsEre = sE[:, 0, :, :], sE[:, 1, :, :]
        EreB, EimB = Ere[:, g8, :], Eim[:, g8, :]
        UBK = ["Ub_%d" % g for g in range(GB)] + ["Ub_a%d" % g for g in range(GB)] + ["Ub_b%d" % g for g in range(GB)]
        GSK = ["Gs_%d" % g for g in range(GB)]
        S.add("dve", lambda e, g8=g8: e.tensor_scalar(out=sE[:, 0, :, :], in0=Eim[:, g8, :], scalar1=sgn[:, 0:1], scalar2=1.0, op0=ALU.mult, op1=ALU.mult), ["Eim"] + SGN, ["sE0"])
        S.add("dve", lambda e, g8=g8: e.tensor_scalar(out=sE[:, 1, :, :], in0=Ere[:, g8, :], scalar1=sgn[:, 1:2], scalar2=1.0, op0=ALU.mult, op1=ALU.mult), ["Ere"] + SGN, ["sE1"])
        sEim, nsEre = sE[:, 0, :, :], sE[:, 1, :, :]
        EreB, EimB = Ere[:, g8, :], Eim[:, g8, :]
        gen_mat(A32, EreB[:, :, 0:8], bbs[:, g8, :], sEim[:, :, 0:8], bbp[:, g8, :], ALU.add, ["Ere", "sE0", "bbs", "bbp"], ["A32"])
        S.add("act", lambda e: e.copy(out=Ab[:, 0, :, :], in_=A32), ["A32"], ["Ab0"])
        gen_mat(Ab[:, 1, :, :], nsEre[:, :, 0:8], bbp[:, g8, :], EimB[:, :, 0:8], bbs[:, g8, :], ALU.add, ["Eim", "sE1", "bbs", "bbp"], ["Ab1"])
        pv = PSB[4][:, 0:1024].rearrange("p (w g t) -> p w g t", w=2, g=GB)
        for w_ in range(2):
            for g in range(GB):
                S.add("pe", lambda e, w_=w_, g=g, pv=pv: e.transpose(out=pv[:, w_, g, :], in_=Ab[:, w_, g, :], identity=identb[:]), ["Ab%d" % w_, "identb"], ["PS4"])
        S.add("act", lambda e, pv=pv: e.copy(out=Msb, in_=pv), ["PS4"], ["Msb"])

    def ssm_sincos(gb):
        g8 = slice(gb * GB, gb * GB + GB)
        sEim, nsEre = sE[:, 0, :, :], sE[:, 1, :, :]
        EreB, EimB = Ere[:, g8, :], Eim[:, g8, :]
        UBK = ["Ub_%d" % g for g in range(GB)] + ["Ub_a%d" % g for g in range(GB)] + ["Ub_b%d" % g for g in range(GB)]
        GSK = ["Gs_%d" % g for g in range(GB)]
        S.add("pool", lambda e, g8=g8: e.tensor_tensor(out=sc[0][:, 0:GB * 259].rearrange("p (g k) -> p g k", k=259),
                                                       in0=pp[:, 3, g8].unsqueeze(2).to_broadcast([128, GB, 259]),
                                                       in1=krow[:, 0:259].unsqueeze(1).to_broadcast([128, GB, 259]), op=ALU.mult),
              [PPK(3), "krow", "sc0"], ["sc0"])
        sincos(sc[0][:, 0:GB * 259], cosT.rearrange("p g k -> p (g k)"), sinT.rearrange("p g k -> p (g k)"), GB * 259, ["sc0"], ["cosT"], ["sinT"])
        S.add("act", lambda e, gb=gb: e.copy(out=C258[:, gb * GB:gb * GB + GB], in_=cosT[:, :, 258]), ["cosT"], ["C258"])
        S.add("act", lambda e, gb=gb: e.copy(out=S258[:, gb * GB:gb * GB + GB], in_=sinT[:, :, 258]), ["sinT"], ["S258"])

    def ssm_frontB(gb):
        g8 = slice(gb * GB, gb * GB + GB)
        sEim, nsEre = sE[:, 0, :, :], sE[:, 1, :, :]
        EreB, EimB = Ere[:, g8, :], Eim[:, g8, :]
        UBK = ["Ub_%d" % g for g in range(GB)] + ["Ub_a%d" % g for g in range(GB)] + ["Ub_b%d" % g for g in range(GB)]
        GSK = ["Gs_%d" % g for g in range(GB)]
        gen_mat(R32, nsEre[:, :, 8:16], cself[:, g8, :], EimB[:, :, 8:16], cpart[:, g8, :], ALU.subtract, ["Eim", "sE1", "cself", "cpart"], ["R32"])
        gen_mat(Mab[:, 0, :, :], nsEre[:, :, 16:24], cself[:, g8, :], EimB[:, :, 16:24], cpart[:, g8, :], ALU.subtract, ["Eim", "sE1", "cself", "cpart"], ["Mab0"])
        gen_mat(Mab[:, 1, :, :], sEim[:, :, 16:24], cself[:, g8, :], EreB[:, :, 16:24], cpart[:, g8, :], ALU.subtract, ["Ere", "sE0", "cself", "cpart"], ["Mab1"])
        for g0 in range(0, GB, 2):
            bk = 6 + (g0 // 2) % 2
            kb = "PS%d" % bk
            pv2 = PSB[bk][:, 30:30 + 640].rearrange("p (g t) -> p g t", t=320)
            for jj in range(2):
                gg = gb * GB + g0 + jj
                S.add("pe", lambda e, jj=jj, gg=gg, pv2=pv2: e.transpose(out=pv2[:, jj, 2:130], in_=u8p2[:, gg, :, :].rearrange("p s h -> p (s h)"), identity=identb[:]),
                      U8K_P2 + ["identb"], [kb])
                S.add("pe", lambda e, jj=jj, gg=gg, pv2=pv2: e.transpose(out=pv2[:, jj, 130:258], in_=u8own[:, gg, :, :].rearrange("p s h -> p (s h)"), identity=identb[:]),
                      U8K_OWN + ["identb"], [kb])
                S.add("act", lambda e, jj=jj, gg=gg, g0=g0, pv2=pv2: e.copy(out=Ub[:, g0 + jj, 2:258], in_=pv2[:, jj, 2:258]), [kb], ["Ub_%d" % (g0 + jj)])
                S.add("act", lambda e, jj=jj, gg=gg, g0=g0: e.copy(out=Ub[:, g0 + jj, 0:2], in_=Um[:, gg, 0:2]), UMK, ["Ub_a%d" % (g0 + jj)])
                S.add("act", lambda e, jj=jj, gg=gg, g0=g0: e.copy(out=Ub[:, g0 + jj, 258:274], in_=Um[:, gg, 2:18]), UMK, ["Ub_b%d" % (g0 + jj)])
        UBK = ["Ub_%d" % g for g in range(GB)] + ["Ub_a%d" % g for g in range(GB)] + ["Ub_b%d" % g for g in range(GB)]
        dma("sp", st_s[0:NS], din["st_self"][:, g8, :], [], ["st_s"], "st_s")
        dma("sp", st_p[0:NS], din["st_part"][:, g8, :], [], ["st_p"], "st_p")
        for g in range(GB):
            S.add("pe", lambda e, g=g: e.transpose(out=PS[2][:, g * NS:(g + 1) * NS], in_=st_s[0:NS, g, :], identity=identf[0:NS, 0:NS]), ["st_s", "identf"], ["PS2"])
            S.add("pe", lambda e, g=g: e.transpose(out=PS[2][:, 128 + g * NS:128 + (g + 1) * NS], in_=st_p[0:NS, g, :], identity=identf[0:NS, 0:NS]), ["st_p", "identf"], ["PS2"])
        h0v = PS[2][:, 0:GB * NS].rearrange("p (g b) -> p g b", b=NS)
        h0s = PS[2][:, 128:128 + GB * NS].rearrange("p (g b) -> p g b", b=NS)
        S.add("dve", lambda e, g8=g8: e.tensor_tensor(out=hs1, in0=h0v, in1=Ere[:, g8, 16:17].to_broadcast([128, GB, NS]), op=ALU.mult), ["PS2", "Ere"], ["hs1"])
        S.add("dve", lambda e: e.tensor_tensor(out=hs2, in0=h0s, in1=sE[:, 0, :, 16:17].to_broadcast([128, GB, NS]), op=ALU.mult), ["PS2", "sE0"], ["hs2"])
        S.add("pool", lambda e: e.tensor_tensor(out=hs1, in0=hs1, in1=hs2, op=ALU.add), ["hs1", "hs2"], ["hs1"])
        S.add("dve", lambda e, g8=g8: e.tensor_tensor(out=hn1, in0=h0v, in1=Ere[:, g8, 8:9].to_broadcast([128, GB, NS]), op=ALU.mult), ["PS2", "Ere"], ["hn1k"])
        S.add("dve", lambda e: e.tensor_tensor(out=hn2, in0=h0s, in1=sE[:, 0, :, 8:9].to_broadcast([128, GB, NS]), op=ALU.mult), ["PS2", "sE0"], ["hn2k"])
        S.add("pool", lambda e: e.tensor_tensor(out=Hins, in0=hn1, in1=hn2, op=ALU.add), ["hn1k", "hn2k"], ["Hinsk"])

    def ssm_loop(gb):
        g8 = slice(gb * GB, gb * GB + GB)
        sEim, nsEre = sE[:, 0, :, :], sE[:, 1, :, :]
        EreB, EimB = Ere[:, g8, :], Eim[:, g8, :]
        UBK = ["Ub_%d" % g for g in range(GB)] + ["Ub_a%d" % g for g in range(GB)] + ["Ub_b%d" % g for g in range(GB)]
        GSK = ["Gs_%d" % g for g in range(GB)]
        for g in range(GB):
            gg = gb * GB + g
            pa, pb_ = (0, 1) if g % 2 == 0 else (6, 7)
            ka, kb_ = "PS%d" % pa, "PS%d" % pb_
            S.add("pe", lambda e, g=g, pa=pa: e.matmul(PS[pa][:, 0:NU], lhsT=Msb[:, 0, g, :], rhs=Ub[:, g, :], start=True, stop=True), ["Msb"] + UBK, [ka])
            S.add("pe", lambda e, g=g, pb_=pb_: e.matmul(PS[pb_][:, 0:NCH], lhsT=Msb[:, 1, g, :], rhs=Ub[:, g, 0:NCH], start=True, stop=True), ["Msb"] + UBK, [kb_])
            S.add("dve", lambda e, g=g, pa=pa: e.tensor_tensor(out=Xt[:, g, :], in0=PS[pa][:, 0:NCH], in1=cosT[:, g, 1:259], op=ALU.mult), [ka, "cosT"], ["Xt_%d" % g])
            S.add("act", lambda e, g=g, pa=pa: e.copy(out=Xsm[:, g, :], in_=PS[pa][:, NCH:NU]), [ka], ["Xsm_%d" % g])
            x2 = Xt2[g % 2]
            S.add("dve", lambda e, g=g, pb_=pb_, x2=x2: e.tensor_tensor(out=x2, in0=PS[pb_][:, 0:NCH], in1=sinT[:, g, 1:259], op=ALU.mult), [kb_, "sinT"], ["Xt2_%d" % (g % 2)])
            S.add("dve", lambda e, g=g, x2=x2: e.tensor_tensor(out=Xt[:, g, :], in0=Xt[:, g, :], in1=x2, op=ALU.add), ["Xt_%d" % g, "Xt2_%d" % (g % 2)], ["Xt_%d" % g])
            S.add("dve", lambda e, g=g, gg=gg: e.tensor_tensor_scan(out=Gs[:, g, 1:259], data0=mumag[:, gg:gg + 1].to_broadcast([128, NCH]), data1=Xt[:, g, :],
                                                                    initial=0.0, op0=ALU.mult, op1=ALU.add), ["Xt_%d" % g, "mumag", "Gs"], ["Gs_%d" % g])
            S.add("act", lambda e, g=g, gg=gg: e.copy(out=G258[:, gg:gg + 1], in_=Gs[:, g, 258:259]), ["Gs_%d" % g], ["G258"])
        S.add("dve", lambda e: e.tensor_tensor(out=hs1, in0=hs1, in1=Xsm, op=ALU.add), ["hs1"] + ["Xsm_%d" % g for g in range(GB)], ["hs1"])

    def ssm_y1(gb):
        g8 = slice(gb * GB, gb * GB + GB)
        sEim, nsEre = sE[:, 0, :, :], sE[:, 1, :, :]
        EreB, EimB = Ere[:, g8, :], Eim[:, g8, :]
        UBK = ["Ub_%d" % g for g in range(GB)] + ["Ub_a%d" % g for g in range(GB)] + ["Ub_b%d" % g for g in range(GB)]
        GSK = ["Gs_%d" % g for g in range(GB)]
        GSK = ["Gs_%d" % g for g in range(GB)]
        for g in range(GB):
            S.add("pe", lambda e, g=g: e.matmul(PS[5][:, g * 128:(g + 1) * 128], lhsT=A32[:, g, :], rhs=R32[:, g, :], start=True, stop=True), ["A32", "R32"], ["PS5"])
        S.add("dve", lambda e: e.tensor_tensor(out=Mib, in0=PS[5][:, 0:GB * 128].rearrange("p (g q) -> p g q", q=128),
                                               in1=blockmask[:].unsqueeze(1).to_broadcast([128, GB, 128]), op=ALU.mult), ["PS5", "blockmask"], ["Mib"])

    def ssm_y2a(gb):
        g8 = slice(gb * GB, gb * GB + GB)
        sEim, nsEre = sE[:, 0, :, :], sE[:, 1, :, :]
        EreB, EimB = Ere[:, g8, :], Eim[:, g8, :]
        UBK = ["Ub_%d" % g for g in range(GB)] + ["Ub_a%d" % g for g in range(GB)] + ["Ub_b%d" % g for g in range(GB)]
        GSK = ["Gs_%d" % g for g in range(GB)]
        S.add("dve", lambda e: e.tensor_tensor(out=Pcs[:, 0, :, :], in0=cosT[:, :, 130:258], in1=Gs[:, :, 130:258], op=ALU.mult), ["cosT"] + GSK, ["Pc0"])
        S.add("dve", lambda e: e.tensor_tensor(out=Pcs[:, 1, :, :], in0=sinT[:, :, 130:258], in1=Gs[:, :, 130:258], op=ALU.mult), ["sinT"] + GSK, ["Pc1"])
        for g in range(GB):
            osl = PS[5][:, g * 128:(g + 1) * 128]
            S.add("pe", lambda e, g=g, osl=osl: e.matmul(osl, lhsT=Mib[:, g, :], rhs=Ub[:, g, 130:258], start=True, stop=False), ["Mib"] + UBK, ["PS5"])
            S.add("pe", lambda e, g=g, osl=osl: e.matmul(osl, lhsT=Mab[:, 0, g, :], rhs=Pcs[:, 0, g, :], start=False, stop=False), ["Mab0", "Pc0"], ["PS5"])
            S.add("pe", lambda e, g=g, osl=osl: e.matmul(osl, lhsT=Mab[:, 1, g, :], rhs=Pcs[:, 1, g, :], start=False, stop=True), ["Mab1", "Pc1"], ["PS5"])
            oss = PS[4][:, g * NS:(g + 1) * NS]
            S.add("pe", lambda e, g=g, oss=oss: e.matmul(oss, lhsT=Mib[:, g, :], rhs=Ub[:, g, 258:274], start=True, stop=False), ["Mib"] + UBK, ["PS4"])
            S.add("pe", lambda e, g=g, oss=oss: e.matmul(oss, lhsT=Mab[:, 0, g, :], rhs=Hins[:, g, :], start=False, stop=True), ["Mab0", "Hinsk"], ["PS4"])


    def ssm_y2b(gb):
        g8 = slice(gb * GB, gb * GB + GB)
        sEim, nsEre = sE[:, 0, :, :], sE[:, 1, :, :]
        EreB, EimB = Ere[:, g8, :], Eim[:, g8, :]
        UBK = ["Ub_%d" % g for g in range(GB)] + ["Ub_a%d" % g for g in range(GB)] + ["Ub_b%d" % g for g in range(GB)]
        GSK = ["Gs_%d" % g for g in range(GB)]
        def gelu_chain(ps_view, u_view, ya, yb, yout, W, kin, kout, tag):
            kk = "ypo" if tag == "o" else "yps"
            dsk_b = dsk[:, g8].unsqueeze(2).to_broadcast([128, GB, W])
            S.add("dve", lambda e: e.tensor_tensor(out=ya, in0=u_view, in1=dsk_b, op=ALU.mult), UBK + ["dsk", kk], [kk])
            S.add("dve", lambda e: e.tensor_tensor(out=ya, in0=ya, in1=ps_view, op=ALU.add), [kk] + kin, [kk])
            S.add("act", lambda e: e.activation(out=yb, in_=ya, func=AF.Square), [kk], [kk])
            S.add("dve", lambda e: e.scalar_tensor_tensor(out=yb, in0=yb, scalar=0.044715, in1=ya, op0=ALU.mult, op1=ALU.mult), [kk], [kk])
            S.add("dve", lambda e: e.tensor_tensor(out=yb, in0=yb, in1=ya, op=ALU.add), [kk], [kk])
            S.add("act", lambda e: e.activation(out=yb, in_=yb, func=AF.Tanh, scale=0.7978845608028654), [kk], [kk])
            S.add("dve", lambda e: e.scalar_tensor_tensor(out=yout, in0=yb, scalar=1.0, in1=ya, op0=ALU.add, op1=ALU.mult), [kk] + kout, kout)
        gelu_chain(PS[5][:, 0:GB * 128].rearrange("p (g q) -> p g q", q=128), Ub[:, :, 130:258], yA, yB, ygb, 128, ["PS5"], ["ygbk"], "o")
        gelu_chain(PS[4][:, 0:GB * NS].rearrange("p (g b) -> p g b", b=NS), Ub[:, :, 258:274], ysA, ysB, ygs, NS, ["PS4"], ["ygsk"], "s")
        goff = (gb % 2) * GB
        pvy = PSB[6][:, 0:GB * 128].rearrange("p (g q) -> p g q", q=128)
        for g in range(GB):
            S.add("pe", lambda e, g=g: e.transpose(out=pvy[:, g, :], in_=ygb[:, g, :], identity=identb[:]), ["ygbk", "identb"], ["PS6"])
        S.add("act", lambda e, goff=goff: e.activation(out=y8.rearrange("p t (g h) -> p g t h", h=16)[:, goff:goff + GB, :, :],
                                                       in_=pvy.rearrange("p g (t h) -> p g t h", h=16), func=AF.Identity, scale=0.5), ["PS6"], ["y8_%d" % (gb % 2)])
        pvs = PSB[7][:, 0:GB * 128].rearrange("p (g q) -> p g q", q=128)
        for g in range(GB):
            S.add("pe", lambda e, g=g: e.transpose(out=pvs[0:NS, g, :], in_=ygs[:, g, :], identity=identb[:]), ["ygsk", "identb"], ["PS7"])
        S.add("act", lambda e, goff=goff: e.activation(out=y8s[0:NS, goff * 16:(goff + GB) * 16].rearrange("p (g h) -> p g h", h=16), in_=pvs[0:NS, :, 112:128],
                                                       func=AF.Identity, scale=0.5), ["PS7"], ["y8s_%d" % (gb % 2)])
        if gb % 2 == 1:
            fc = gb // 2
            pvt = PSB[0][:, 0:1024].rearrange("p (t c) -> p t c", c=128)
            for t in range(8):
                S.add("pe", lambda e, t=t: e.transpose(out=pvt[:, t, :], in_=y8[:, t, :], identity=identb[:]), ["y8_0", "y8_1", "identb"], ["PS0"])
            S.add("act", lambda e, fc=fc: e.copy(out=ST[:, fc, 0:NT].rearrange("p (c t) -> p t c", t=8), in_=pvt), ["PS0"], ["ST_%d" % fc])
            S.add("pe", lambda e: e.transpose(out=PSB[1][:, 0:NS], in_=y8s[0:NS, :], identity=identb[0:NS, 0:NS]), ["y8s_0", "y8s_1", "identb"], ["PS1"])
            S.add("dve", lambda e, fc=fc: e.tensor_copy(out=ST[:, fc, NT:NTS], in_=PSB[1][:, 0:NS]), ["PS1"], ["STs_%d" % fc])
        for g in range(GB):
            S.add("pe", lambda e, g=g: e.transpose(out=PS[3][0:NS, g * 128:(g + 1) * 128], in_=hs1[:, g, :], identity=identf[:]), ["hs1", "identf"], ["PS3"])
        S.add("act", lambda e: e.copy(out=hso[0:NS], in_=PS[3][0:NS, 0:GB * 128].rearrange("p (g q) -> p g q", q=128)), ["PS3"], ["hso"])
        dma("sp", dout["sssm"][:, g8, :], hso[0:NS], ["hso"], [], "hso")

    NBATCH = 64 // GB
    ssm_sincos(0)
    for gb in range(NBATCH):
        ssm_frontA(gb)
        ssm_frontB(gb)
        ssm_loop(gb)
        ssm_y1(gb)
        ssm_y2a(gb)
        if gb + 1 < NBATCH:
            ssm_sincos(gb + 1)
        ssm_y2b(gb)
        S.mark('batch%d' % gb)

    S.add("pe", lambda e: e.matmul(PS[2][:, 0:64], lhsT=swapm[:], rhs=G258[:], start=True, stop=True), ["swapm", "G258"], ["PS2"])
    S.add("dve", lambda e: e.tensor_tensor(out=pp[:, 11, :], in0=C258[:], in1=G258[:], op=ALU.mult), ["C258", "G258"], [PPK(11)])
    S.add("dve", lambda e: e.tensor_scalar(out=pp[:, 12, :], in0=S258[:], scalar1=sgn[:, 0:1], scalar2=1.0, op0=ALU.mult, op1=ALU.mult), ["S258"] + SGN, [PPK(12)])
    S.add("dve", lambda e: e.tensor_tensor(out=pp[:, 12, :], in0=pp[:, 12, :], in1=PS[2][:, 0:64], op=ALU.mult), [PPK(12), "PS2"], [PPK(12)])
    S.add("dve", lambda e: e.tensor_tensor(out=pp[:, 11, :], in0=pp[:, 11, :], in1=pp[:, 12, :], op=ALU.add), [PPK(11), PPK(12)], [PPK(11)])
    S.add("pe", lambda e: e.transpose(out=PS[3][0:64, 0:128], in_=pp[:, 11, :], identity=identf[:]), [PPK(11), "identf"], ["PS3"])
    S.add("act", lambda e: e.copy(out=hfo[0:64, :], in_=PS[3][0:64, 0:128]), ["PS3"], ["hfo"])
    dma("sp", dout["pssm"], hfo[0:64, :], ["hfo"], [], "hfo")

    S.mark('p2')
    S.barrier(mk_bar)
    NKC = 1168
    qT = carve(0, 8 * NTS, BF16).rearrange("p (h t) -> p h t", t=NTS)
    kT = carve(17 * KB, 4 * NKC, BF16).rearrange("p (g t) -> p g t", t=NKC)
    PT = [[carve(27 * KB + (3 * j + i) * KB, 512, BF16) for i in range(3)] for j in range(2)]
    AT = carve(51 * KB, 8 * NTS, BF16).rearrange("p (c t) -> p c t", t=NTS)
    Vaug = carve(100 * KB, 10 * 4 * 128, BF16).rearrange("p (b g q) -> p b g q", b=10, q=128)
    o3 = 100 * KB + 10 * KB
    tmpq = [carve(o3 + 4 * KB * i, 1024, F32) for i in range(3)]; o3 += 12 * KB
    sqj = carve(o3 - 8 * KB, D, F32)
    xbuf_off3 = o3
    xbuf = [carve(o3, D, F32)] * 2; o3 += 8 * KB
    xnb = [carve(o3, D, BF16)] * 2; o3 += 4 * KB
    xnTt = [carve(o3, ND * 128, BF16).rearrange("p (c t) -> p c t", c=ND)] * 2; o3 += 4 * KB
    kf = [carve(o3 + i * KB, 256, F32) for i in range(2)]; o3 += 2 * KB
    vf = [carve(o3 + i * KB, 256, F32) for i in range(2)]; o3 += 2 * KB
    kb16 = carve(o3, 256, BF16); o3 += 512
    qf = carve(o3, 512, F32); o3 += 2 * KB
    qb16 = carve(o3, 512, BF16); o3 += KB
    otmp = [carve(o3 + i * KB, 512, BF16) for i in range(2)]; o3 += 2 * KB
    dtmp2 = [carve(xbuf_off3 + 2 * KB * i, 512, F32) for i in range(2)]
    assert o3 <= ARENA, o3
    gain_bc = carve(0, D, F32)
    cnt["x"] = 0
    dma("sp", gain_bc, din["norm_gain"].partition_broadcast(128), [], ["gain_bc"], "gain_bc")
    dma("pool", masks[:, 0, :], din["maskc"], [], ["masks0"], "masks0")
    dma("pool", masks[:, 1, :], din["maskp"], [], ["masks1"], "masks1")
    dma("pool", masks[:, 2, :], din["maskp0"], [], ["masks2"], "masks2")
    dma("sp", expsink[:], din["sinks"].partition_broadcast(128), [], ["expsink"], "expsink")
    S.add("act", lambda e: e.activation(out=expsink[:], in_=expsink[:], func=AF.Exp), ["expsink"], ["expsink"])
    S.add("pool", lambda e: e.memset(onesb[:], 1.0), [], ["onesb"])
    S.add("pool", lambda e: e.memset(Vaug[:, :, :, 64:128], 1.0), [], ["Vones"])

    load_w(0, din["w_in"], 1024, 512, ND)

    def kv_proj(lhs_fn, lhs_keys, n):
        b = cnt["kv"] % 2
        cnt["kv"] += 1
        kb = proj_tok(lhs_fn, lhs_keys, n, 0, 512, b)
        return b, kb

    def kv_post(b, kb, n, ti, kcol0, vblk, out_k=None, out_v=None):
        kfj, vfj = kf[b], vf[b]
        S.add("act", lambda e: e.copy(out=vfj[0:n, :], in_=PS[b][0:n, 256:512]), [kb], ["vf%d" % b])
        headnorm_rope([(PS[b][0:n, 0:256], 256)], [kb], n, 4, ti, 64, 1.0, kfj[0:n, :], ["kf%d" % b])
        if out_k is not None:
            dma("sp", out_k, kfj[0:n, :], ["kf%d" % b], ["swk_new"], "kf%d" % b)
        if out_v is not None:
            dma("sp", out_v, vfj[0:n, :], ["vf%d" % b], ["swv_new"], "vf%d" % b)
        if vblk is not None:
            S.add("pool", lambda e: e.tensor_copy(out=Vaug[0:n, vblk, :, 0:64], in_=vfj[0:n, :].rearrange("p (g d) -> p g d", d=64)),
                  ["vf%d" % b, "Vones"], ["Vaug_%d" % vblk])
            S.add("act", lambda e: e.copy(out=kb16[0:n, :], in_=kfj[0:n, :]), ["kf%d" % b], ["kb16"])
            pvk = PSB[6][:, 0:512].rearrange("p (g t) -> p g t", t=128)
            for g in range(4):
                S.add("pe", lambda e, g=g: e.transpose(out=pvk[0:64, g, 0:n], in_=kb16[0:n, g * 64:(g + 1) * 64], identity=identb[0:n, 0:n]),
                      ["kb16", "identb"], ["PS6"])
            S.add("act", lambda e: e.copy(out=kT[0:64, :, kcol0:kcol0 + n], in_=pvk[0:64, :, 0:n]), ["PS6"], ["kT_%d" % vblk])

    def kv_tile(lhs_fn, lhs_keys, n, ti, kcol0, vblk, out_k=None, out_v=None):
        b, kb = kv_proj(lhs_fn, lhs_keys, n)
        kv_post(b, kb, n, ti, kcol0, vblk, out_k, out_v)

    for (r0, n, ti, kc0, vb, ok, ov) in ((0, 128, 8, 1024, 8, None, None), (128, 16, 9, 1152, 9, dout["pmk"], dout["pmv"])):
        tt = xnTt[0]
        norm_tile(din["x_kvx"][r0:r0 + n, :], n, lambda dc0, tt=tt, n=n: tt[:, dc0:dc0 + 8, 0:n], ["xnTt0"])
        kv_tile(lambda dc, tt=tt, n=n: tt[:, dc, 0:n], ["xnTt0"], n, ti, kc0, vb, ok, ov)
    S.add("pool", lambda e: e.memset(qT, 0.0), ["gain_bc"], ["qT", "gain_bc"])
    dma("sp", dout["swk"][:, 0:127, :], din["cwk"][:, 1:128, :], [], [], "swk")
    dma("sp", dout["swv"][:, 0:127, :], din["cwv"][:, 1:128, :], [], [], "swv")
    kspecs = [(lambda dc, i=i: xnT[:, dc, i * 128:(i + 1) * 128], ["xnT_%d" % i], 128, i, i * 128, i,
               dout["pwk"] if i == 7 else None, dout["pwv"] if i == 7 else None) for i in range(8)]
    kspecs.append((lambda dc: xnT[:, dc, NT:NTS], ["xnT_s"], NS, 10, 0, None, dout["swk"][:, 127, :], dout["swv"][:, 127, :]))
    pendk = kv_proj(kspecs[0][0], kspecs[0][1], kspecs[0][2])
    for ki, (lf, lk, n_, ti_, kc_, vb_, ok_, ov_) in enumerate(kspecs):
        curk = pendk
        if ki + 1 < len(kspecs):
            pendk = kv_proj(kspecs[ki + 1][0], kspecs[ki + 1][1], kspecs[ki + 1][2])
        kv_post(curk[0], curk[1], n_, ti_, kc_, vb_, ok_, ov_)
    KTK = ["kT_%d" % i for i in range(10)]
    VK = ["Vaug_%d" % i for i in range(10)] + ["Vones"]

    def q_proj(lhs_fn, lhs_keys, n, slot):
        b = cnt["kv"] % 2
        cnt["kv"] += 1
        kb = proj_tok(lhs_fn, lhs_keys, n, slot, 512, b)
        return b, kb

    def q_post(b, kb, n, ti, tokc0, smp_hoff):
        headnorm_rope([(PS[b][0:n, 0:512], 512)], [kb], n, 8, ti, 0, 0.125, qf[0:n, :], ["qf"])
        if smp_hoff is not None:
            S.add("act", lambda e: e.copy(out=qs16[0:n, smp_hoff * 64:(smp_hoff + 8) * 64], in_=qf[0:n, :]), ["qf"], ["qs16_%d" % smp_hoff])
            return
        S.add("act", lambda e: e.copy(out=qb16[0:n, :], in_=qf[0:n, :]), ["qf"], ["qb16"])
        pvq = PSB[7][:, 0:1024].rearrange("p (h t) -> p h t", t=128)
        for h in range(8):
            S.add("pe", lambda e, h=h: e.transpose(out=pvq[0:64, h, 0:n], in_=qb16[0:n, h * 64:(h + 1) * 64], identity=identb[0:n, 0:n]),
                  ["qb16", "identb"], ["PS7"])
        S.add("act", lambda e: e.copy(out=qT[0:64, :, tokc0:tokc0 + n], in_=pvq[0:64, :, 0:n]), ["PS7", "qT"], ["qT_%d" % (tokc0 // 128)])

    def att_setup(nb, g, hl0, pb):
        d_ = dict(nb=nb, g=g, hl0=hl0, pb=pb)
        d_["banks"] = (2, 3, 4, 5) if pb == 0 else (6, 7, 0, 1)
        d_["prev_cols"] = slice(1024, 1152) if nb == 0 else slice((nb - 1) * 128, nb * 128)
        d_["prev_blk"] = 8 if nb == 0 else nb - 1
        d_["mprev"] = 2 if nb == 0 else 1
        return d_

    def att_A(d_):
        nb, g, hl0, pb = d_["nb"], d_["g"], d_["hl0"], d_["pb"]
        PSm, PSp, PSc, PSo = d_["banks"]
        PTm, PTp, PTc = PT[pb]
        qk = ["qT_%d" % nb]
        cur_cols = slice(nb * 128, (nb + 1) * 128)
        prev_cols, mprev = d_["prev_cols"], d_["mprev"]
        for r in range(4):
            rs = slice(r * 128, (r + 1) * 128)
            qa = qT[0:64, hl0 + r, nb * 128:(nb + 1) * 128]
            S.add("pe", lambda e, rs=rs, qa=qa: e.matmul(PS[PSm][0:16, rs], lhsT=kT[0:64, g, 1152:1168], rhs=qa, start=True, stop=True), KTK + qk, ["PS%d" % PSm])
            S.add("pe", lambda e, rs=rs: e.matmul(PS[PSp][:, rs], lhsT=identb[:], rhs=masks[:, mprev, :], start=True, stop=False), ["identb", "masks1", "masks2"], ["PS%d" % PSp])
            S.add("pe", lambda e, rs=rs, qa=qa: e.matmul(PS[PSp][:, rs], lhsT=kT[0:64, g, prev_cols], rhs=qa, start=False, stop=True), KTK + qk, ["PS%d" % PSp])
            S.add("pe", lambda e, rs=rs: e.matmul(PS[PSc][:, rs], lhsT=identb[:], rhs=masks[:, 0, :], start=True, stop=False), ["identb", "masks0"], ["PS%d" % PSc])
            S.add("pe", lambda e, rs=rs, qa=qa: e.matmul(PS[PSc][:, rs], lhsT=kT[0:64, g, cur_cols], rhs=qa, start=False, stop=True), KTK + qk, ["PS%d" % PSc])
        km, kp, kc = "PTm%d" % pb, "PTp%d" % pb, "PTc%d" % pb
        S.add("act", lambda e: e.activation(out=PTm[0:16, :], in_=PS[PSm][0:16, :], func=AF.Exp), ["PS%d" % PSm], [km])
        S.add("act", lambda e: e.activation(out=PTp, in_=PS[PSp][:, :], func=AF.Exp), ["PS%d" % PSp], [kp])
        S.add("act", lambda e: e.activation(out=PTc, in_=PS[PSc][:, :], func=AF.Exp), ["PS%d" % PSc], [kc])

    def att_B(d_):
        nb, g, pb = d_["nb"], d_["g"], d_["pb"]
        PSm, PSp, PSc, PSo = d_["banks"]
        PTm, PTp, PTc = PT[pb]
        km, kp, kc = "PTm%d" % pb, "PTp%d" % pb, "PTc%d" % pb
        prev_blk = d_["prev_blk"]
        for r in range(4):
            rs = slice(r * 128, (r + 1) * 128)
            S.add("pe", lambda e, rs=rs: e.matmul(PS[PSo][:, rs], lhsT=Vaug[0:16, 9, g, :], rhs=PTm[0:16, rs], start=True, stop=False), VK + [km], ["PS%d" % PSo])
            S.add("pe", lambda e, rs=rs: e.matmul(PS[PSo][:, rs], lhsT=Vaug[:, prev_blk, g, :], rhs=PTp[:, rs], start=False, stop=False), VK + [kp], ["PS%d" % PSo])
            S.add("pe", lambda e, rs=rs: e.matmul(PS[PSo][:, rs], lhsT=Vaug[:, nb, g, :], rhs=PTc[:, rs], start=False, stop=True), VK + [kc], ["PS%d" % PSo])

    def att_C(d_):
        nb, g, pb = d_["nb"], d_["g"], d_["pb"]
        PSo = d_["banks"][3]
        dtm = dtmp2[pb]
        dv = dtm[64:128, :].rearrange("p (r q) -> p r q", q=128)
        kd = "dtmp%d" % pb
        S.add("dve", lambda e: e.tensor_tensor(out=dv, in0=PS[PSo][64:128, :].rearrange("p (r q) -> p r q", q=128),
                                               in1=expsink[64:128, 4 * g:4 * g + 4].unsqueeze(2).to_broadcast([64, 4, 128]), op=ALU.add), ["PS%d" % PSo, "expsink"], [kd, "xbuf0"])
        S.add("act", lambda e: e.activation(out=dtm[64:128, :], in_=dtm[64:128, :], func=AF.Ln), [kd], [kd])
        S.add("act", lambda e: e.activation(out=dtm[64:128, :], in_=dtm[64:128, :], func=AF.Exp, scale=-1.0), [kd], [kd])
        ot = otmp[pb]
        S.add("dve", lambda e: e.tensor_tensor(out=ot[0:64, :], in0=PS[PSo][0:64, :], in1=dtm[64:128, :], op=ALU.mult), ["PS%d" % PSo, kd], ["otmp%d" % pb])
        o3v = ot[0:64, :].rearrange("p (r q) -> p r q", q=128)
        dma("sp", AT[0:64, 2 * g:2 * g + 2, nb * 128:(nb + 1) * 128], o3v[:, 0:4:2, :], ["otmp%d" % pb], ["AT_%d_%d" % (g, nb)], "otmp%d" % pb)
        dma("sp", AT[64:128, 2 * g:2 * g + 2, nb * 128:(nb + 1) * 128], o3v[:, 1:4:2, :], ["otmp%d" % pb], ["ATb_%d_%d" % (g, nb)], "otmp%d" % pb)

    acnt = 0
    for qblk in range(2):
        load_w(1, din["w_in"], qblk * 512, 512, ND)
        qspecs = [(lambda dc, i=i: xnT[:, dc, i * 128:(i + 1) * 128], ["xnT_%d" % i], 128, i, i * 128, None) for i in range(8)]
        qspecs.append((lambda dc: xnT[:, dc, NT:NTS], ["xnT_s"], NS, 10, 0, qblk * 8))
        pend = q_proj(qspecs[0][0], qspecs[0][1], qspecs[0][2], 1)
        for qi, (lf, lk, n_, ti_, tc_, sh_) in enumerate(qspecs):
            cur = pend
            if qi + 1 < len(qspecs):
                pend = q_proj(qspecs[qi + 1][0], qspecs[qi + 1][1], qspecs[qi + 1][2], 1)
            q_post(cur[0], cur[1], n_, ti_, tc_, sh_)
        blocks = []
        for nb in range(8):
            for gl in range(2):
                blocks.append(att_setup(nb, qblk * 2 + gl, gl * 4, acnt % 2))
                acnt += 1
        att_A(blocks[0])
        for bi in range(len(blocks)):
            if bi + 1 < len(blocks):
                att_A(blocks[bi + 1])
            att_B(blocks[bi])
            att_C(blocks[bi])
    ATK = ["AT_%d_%d" % (g, nb) for g in range(4) for nb in range(8)] + ["ATb_%d_%d" % (g, nb) for g in range(4) for nb in range(8)]
    S.mark('p3a')

    S.barrier(mk_bar)
    Kc = carve(0, NS * 256, BF16).rearrange("p (b e) -> p b e", e=256)
    Vc = carve(8 * KB, NS * 256, BF16).rearrange("p (b e) -> p b e", e=256)
    Kmc = carve(16 * KB, NS * 256, BF16).rearrange("p (b e) -> p b e", e=256)
    Vmc = carve(24 * KB, NS * 256, BF16).rearrange("p (b e) -> p b e", e=256)
    KsT = carve(100 * KB, NS * 4 * 128, BF16).rearrange("p (b g t) -> p b g t", b=NS, t=128)
    KsT2 = carve(116 * KB, NS * 4 * 32, BF16).rearrange("p (b g t) -> p b g t", b=NS, t=32)
    o4 = 120 * KB
    qsT = carve(o4, 16 * NS, BF16).rearrange("p (h b) -> p h b", b=NS); o4 += 512
    PTs = carve(o4, 256, BF16); o4 += 512
    PTs2 = carve(o4, 256, BF16); o4 += 512
    dts = carve(o4, 256, F32); o4 += KB
    osb = carve(o4, 256, BF16).rearrange("p (h b) -> p h b", b=NS); o4 += 512
    load_w(0, din["w_in"], 1536, 512, ND)
    load_w(1, din["w_in"], 1536 + 512, 512, ND)
    dma("pool", Kc, din["cwk"].rearrange("b j e -> j b e"), [], ["Kc"], "Kc")
    dma("pool", Vc, din["cwv"].rearrange("b j e -> j b e"), [], ["Vc"], "Vc")
    dma("pool", Kmc[0:16], din["cmk"].rearrange("b j e -> j b e"), [], ["Kmc_a"], "Kmc")
    dma("pool", Vmc[0:16], din["cmv"].rearrange("b j e -> j b e"), [], ["Vmc_a"], "Vmc")
    dma("pool", Kmc[16:17], dout["swk"][:, 127:128, :].rearrange("b o e -> o b e"), ["swk_new"], ["Kmc_b"], "Kmc")
    dma("pool", Vmc[16:17], dout["swv"][:, 127:128, :].rearrange("b o e -> o b e"), ["swv_new"], ["Vmc_b"], "Vmc")
    pvq2 = PSB[7][:, 0:256].rearrange("p (h b) -> p h b", b=NS)
    for h in range(16):
        S.add("pe", lambda e, h=h: e.transpose(out=pvq2[0:64, h, :], in_=qs16[0:NS, h * 64:(h + 1) * 64], identity=identb[0:NS, 0:NS]),
              ["qs16_0", "qs16_8", "identb"], ["PS7"])
    S.add("act", lambda e: e.copy(out=qsT[0:64], in_=pvq2[0:64]), ["PS7"], ["qsT"])
    for b0 in range(0, NS, 2):
        bk = 4 + (b0 // 2) % 2
        pvk2 = PSB[bk][:, 0:1024].rearrange("p (b g t) -> p b g t", b=2, t=128)
        for bb in range(2):
            for g in range(4):
                S.add("pe", lambda e, bb=bb, g=g, b0=b0, pvk2=pvk2: e.transpose(out=pvk2[0:64, bb, g, :], in_=Kc[:, b0 + bb, g * 64:(g + 1) * 64], identity=identb[:]),
                      ["Kc", "identb"], ["PS%d" % bk])
        S.add("act" if (b0 // 2) % 2 == 0 else "dve",
              (lambda e, b0=b0, pvk2=pvk2: e.copy(out=KsT[0:64, b0:b0 + 2], in_=pvk2[0:64])) if (b0 // 2) % 2 == 0 else
              (lambda e, b0=b0, pvk2=pvk2: e.tensor_copy(out=KsT[0:64, b0:b0 + 2], in_=pvk2[0:64])), ["PS%d" % bk], ["KsT_%d" % b0])
    KSK = ["KsT_%d" % b0 for b0 in range(0, NS, 2)]
    for b0 in range(0, NS, 8):
        bk = 6 + (b0 // 8) % 2
        pvk3 = PSB[bk][:, 0:1024].rearrange("p (b g t) -> p b g t", b=8, t=32)
        for bb in range(8):
            for g in range(4):
                S.add("pe", lambda e, bb=bb, g=g, b0=b0, pvk3=pvk3: e.transpose(out=pvk3[0:64, bb, g, 0:17], in_=Kmc[0:17, b0 + bb, g * 64:(g + 1) * 64],
                                                                              identity=identb[0:17, 0:17]), ["Kmc_a", "Kmc_b", "identb"], ["PS%d" % bk])
        S.add("act", lambda e, b0=b0, pvk3=pvk3: e.copy(out=KsT2[0:64, b0:b0 + 8, :, 0:17], in_=pvk3[0:64, :, :, 0:17]), ["PS%d" % bk], ["KsT2_%d" % b0])
    KS2K = ["KsT2_0", "KsT2_8"]
    for b in range(NS):
        for g in range(4):
            cs = slice(b * 16 + g * 4, b * 16 + g * 4 + 4)
            S.add("pe", lambda e, b=b, g=g, cs=cs: e.matmul(PS[0][:, cs], lhsT=KsT[0:64, b, g, :], rhs=qsT[0:64, 4 * g:4 * g + 4, b], start=True, stop=True),
                  KSK + ["qsT"], ["PS0"])
            S.add("pe", lambda e, b=b, g=g, cs=cs: e.matmul(PS[1][0:17, cs], lhsT=KsT2[0:64, b, g, 0:17], rhs=qsT[0:64, 4 * g:4 * g + 4, b], start=True, stop=True),
                  KS2K + ["qsT"], ["PS1"])
    S.add("act", lambda e: e.activation(out=PTs, in_=PS[0][:, 0:256], func=AF.Exp), ["PS0"], ["PTs"])
    S.add("act", lambda e: e.activation(out=PTs2[0:17], in_=PS[1][0:17, 0:256], func=AF.Exp), ["PS1"], ["PTs2"])
    S.add("pool", lambda e: e.memset(PTs[0:1, :], 0.0), ["PTs"], ["PTs"])
    S.add("pe", lambda e: e.matmul(PS[2][0:64, 0:256], lhsT=onesb[:, 0:64], rhs=PTs, start=True, stop=False), ["onesb", "PTs"], ["PS2"])
    S.add("pe", lambda e: e.matmul(PS[2][0:64, 0:256], lhsT=onesb[0:17, 0:64], rhs=PTs2[0:17], start=False, stop=True), ["onesb", "PTs2"], ["PS2"])
    for b in range(NS):
        for g in range(4):
            cs = slice(b * 16 + g * 4, b * 16 + g * 4 + 4)
            S.add("pe", lambda e, b=b, g=g, cs=cs: e.matmul(PS[3][0:64, cs], lhsT=Vc[:, b, g * 64:(g + 1) * 64], rhs=PTs[:, cs], start=True, stop=False), ["Vc", "PTs"], ["PS3"])
            S.add("pe", lambda e, b=b, g=g, cs=cs: e.matmul(PS[3][0:64, cs], lhsT=Vmc[0:17, b, g * 64:(g + 1) * 64], rhs=PTs2[0:17, cs], start=False, stop=True),
                  ["Vmc_a", "Vmc_b", "PTs2"], ["PS3"])
    dv3 = dts[0:64].rearrange("p (b h) -> p b h", h=16)
    S.add("dve", lambda e: e.tensor_tensor(out=dv3, in0=PS[2][0:64, 0:256].rearrange("p (b h) -> p b h", h=16),
                                           in1=expsink[0:64, :].unsqueeze(1).to_broadcast([64, NS, 16]), op=ALU.add), ["PS2", "expsink"], ["dts"])
    S.add("dve", lambda e: e.reciprocal(out=dts[0:64], in_=dts[0:64]), ["dts"], ["dts"])
    S.add("dve", lambda e: e.tensor_tensor(out=osb[0:64].rearrange("p h b -> p b h"), in0=PS[3][0:64, 0:256].rearrange("p (b h) -> p b h", h=16), in1=dv3, op=ALU.mult),
          ["PS3", "dts"], ["osb"])
    dma("sp", AT[0:64, :, NT:NTS], osb[0:64, 0:16:2, :], ["osb"], ["AT_s0"], "osb")
    dma("sp", AT[64:128, :, NT:NTS], osb[0:64, 1:16:2, :], ["osb"], ["AT_s1"], "osb")
    ATK = ATK + ["AT_s0", "AT_s1"]
    S.mark('p3b')

    S.barrier(mk_bar)
    mT = carve(0, 16 * NTS, BF16).rearrange("p (c t) -> p c t", t=NTS)
    ST2 = carve(100 * KB, 8 * NTS, BF16).rearrange("p (c t) -> p c t", t=NTS)
    o5 = 117 * KB
    slt = [carve(o5 + 2 * KB * i, 512, F32) for i in range(2)]; o5 += 4 * KB
    bra = carve(o5, 2 * NTS, F32).rearrange("p (c t) -> p c t", t=NTS); o5 += 2 * NTS * 4
    m1 = carve(o5, 2 * NTS, F32).rearrange("p (c t) -> p c t", t=NTS); o5 += 2 * NTS * 4
    o5 = (o5 + 3) // 4 * 4
    xin = [carve(o5 + 2 * KB * i, 512, F32) for i in range(3)]; o5 += 6 * KB
    oti = [carve(o5 + 2 * KB * i, 512, F32) for i in range(3)]; o5 += 6 * KB
    assert o5 <= ARENA, o5
    dma("sp", b_glu[:], din["b_glu"], [], ["b_glu"], "b_glu")
    TB = ((0, 512), (512, 512), (NT, NS))
    XK = ["xnT_%d" % i for i in range(8)] + ["xnT_s"]
    STK = ["ST_%d" % i for i in range(8)] + ["STs_%d" % i for i in range(8)]
    c4 = {"ps": 0, "sl": 0}

    def feat_mm(slot, wc0, nk, rhs_fn, rk):
        for (t0, N) in TB:
            b = c4["ps"] % 4
            c4["ps"] += 1
            kb = "PS%d" % b
            for kc in range(nk):
                S.add("pe", lambda e, kc=kc, b=b, t0=t0, N=N: e.matmul(PS[b][:, 0:N], lhsT=wbuf[slot][:, kc, wc0:wc0 + 128], rhs=rhs_fn(kc, t0, N),
                                                                       start=(kc == 0), stop=(kc == nk - 1)), rk + ["wbuf%d" % slot], [kb])
            yield b, kb, t0, N

    def act_tmp(b, kb, N, func, bias=None):
        j = c4["sl"] % 2
        c4["sl"] += 1
        t = slt[j]
        if bias is None:
            S.add("act", lambda e: e.activation(out=t[:, 0:N], in_=PS[b][:, 0:N], func=func), [kb], ["slt%d" % j])
        else:
            S.add("act", lambda e: e.activation(out=t[:, 0:N], in_=PS[b][:, 0:N], func=func, bias=bias), [kb, "b_glu"], ["slt%d" % j])
        return t, "slt%d" % j

    xrhs = lambda kc, t0, N: xnT[:, kc, t0:t0 + N]
    for blk in range(2):
        sl_ = blk % 2
        for c in range(4):
            jc = blk * 4 + c
            for b, kb, t0, N in feat_mm(sl_, c * 128, ND, xrhs, XK):
                t, tk = act_tmp(b, kb, N, AF.Silu)
                S.add("dve", lambda e, t=t, jc=jc, t0=t0, N=N: e.tensor_tensor(out=AT[:, jc, t0:t0 + N], in0=AT[:, jc, t0:t0 + N], in1=t[:, 0:N], op=ALU.mult),
                      [tk] + ATK, ["A_%d_%d" % (jc, t0)])
    AK = ["A_%d_%d" % (jc, t0) for jc in range(8) for (t0, _) in TB]
    strhs = lambda kc, t0, N: ST[:, kc, t0:t0 + N]
    for blk in range(2):
        sl_ = blk % 2
        load_w(sl_, din["w_glu"], blk * 512, 512, 8)
        for c in range(4):
            jc = blk * 4 + c
            for b, kb, t0, N in feat_mm(sl_, c * 128, 8, strhs, STK):
                t, tk = act_tmp(b, kb, N, AF.Sigmoid, bias=b_glu[:, jc:jc + 1])
                S.add("dve", lambda e, t=t, jc=jc, t0=t0, N=N: e.tensor_tensor(out=ST2[:, jc, t0:t0 + N], in0=ST[:, jc, t0:t0 + N], in1=t[:, 0:N], op=ALU.mult),
                      [tk] + STK, ["S2_%d_%d" % (jc, t0)])
    for blk in range(2):
        sl_ = blk % 2
        load_w(sl_, din["w_in"], 3584 + blk * 512, 512, ND)
        for c in range(4):
            jc = blk * 4 + c
            for b, kb, t0, N in feat_mm(sl_, c * 128, ND, xrhs, XK):
                t, tk = act_tmp(b, kb, N, AF.Silu)
                S.add("dve", lambda e, t=t, jc=jc, t0=t0, N=N: e.tensor_tensor(out=ST2[:, jc, t0:t0 + N], in0=ST2[:, jc, t0:t0 + N], in1=t[:, 0:N], op=ALU.mult),
                      [tk, "S2_%d_%d" % (jc, t0)], ["S2_%d_%d" % (jc, t0)])
    S2K = ["S2_%d_%d" % (jc, t0) for jc in range(8) for (t0, _) in TB]
    arhs = lambda kc, t0, N: AT[:, kc, t0:t0 + N]
    s2rhs = lambda kc, t0, N: ST2[:, kc, t0:t0 + N]
    wq = [carve((68 + 8 * i) * KB, ND * 256, BF16).rearrange("p (c n) -> p c n", c=ND) for i in range(4)]

    def load_wq(slot, src, col0, nk):
        key = "wq%d" % slot
        v = src.rearrange("(c p) n -> p c n", p=128)
        for c0 in range(0, nk, 8):
            dma("pool", wq[slot][:, c0:c0 + 8, :], v[:, c0:c0 + 8, col0:col0 + 256], [], [key], key)

    def feat_mm_q(slot, wc0, nk, rhs_fn, rk):
        for (t0, N) in TB:
            b = c4["ps"] % 4
            c4["ps"] += 1
            kb = "PS%d" % b
            for kc in range(nk):
                S.add("pe", lambda e, kc=kc, b=b, t0=t0, N=N: e.matmul(PS[b][:, 0:N], lhsT=wq[slot][:, kc, wc0:wc0 + 128], rhs=rhs_fn(kc, t0, N),
                                                                       start=(kc == 0), stop=(kc == nk - 1)), rk + ["wq%d" % slot], [kb])
            yield b, kb, t0, N

    stages = ((0, din["w_ao"], 0, 8), (1, din["w_in"], 4608, ND), (2, din["w_so"], 0, 8), (3, din["w_in"], 6656, ND))
    S.add("pool", lambda e: e.memset(dummy[0:1, 6:7], 0.0), [], ["wbuf0", "wbuf1", "wq0", "wq1", "wq2", "wq3"])
    for (sl_, src, c0_, nk_) in stages:
        load_wq(sl_, src, c0_, nk_)
    for gq in range(8):
        for half_ in range(2):
            rhs_fn, rk, last = ((arhs, AK, False), (s2rhs, S2K, True))[half_]
            sa, sg_ = 2 * half_, 2 * half_ + 1
            for c in range(2):
                for b, kb, t0, N in feat_mm_q(sa, c * 128, 8, rhs_fn, rk):
                    S.add("act", lambda e, b=b, c=c, t0=t0, N=N: e.copy(out=bra[:, c, t0:t0 + N], in_=PS[b][:, 0:N]), [kb], ["bra_%d_%d" % (c, t0)])
            if gq < 7:
                load_wq(sa, stages[sa][1], stages[sa][2] + (gq + 1) * 256, stages[sa][3])
            for c in range(2):
                fc = gq * 2 + c
                for b, kb, t0, N in feat_mm_q(sg_, c * 128, ND, xrhs, XK):
                    t, tk = act_tmp(b, kb, N, AF.Sigmoid)
                    if not last:
                        S.add("dve", lambda e, t=t, c=c, t0=t0, N=N: e.tensor_tensor(out=m1[:, c, t0:t0 + N], in0=t[:, 0:N], in1=bra[:, c, t0:t0 + N], op=ALU.mult),
                              [tk, "bra_%d_%d" % (c, t0)], ["m1_%d_%d" % (c, t0)])
                    else:
                        S.add("dve", lambda e, t=t, c=c, t0=t0, N=N: e.tensor_tensor(out=t[:, 0:N], in0=t[:, 0:N], in1=bra[:, c, t0:t0 + N], op=ALU.mult),
                              [tk, "bra_%d_%d" % (c, t0)], [tk])
                        S.add("dve", lambda e, t=t, c=c, fc=fc, t0=t0, N=N: e.tensor_tensor(out=mT[:, fc, t0:t0 + N], in0=t[:, 0:N], in1=m1[:, c, t0:t0 + N], op=ALU.add),
                              [tk, "m1_%d_%d" % (c, t0)], ["mT_%d_%d" % (fc, t0)])
            if gq < 7:
                load_wq(sg_, stages[sg_][1], stages[sg_][2] + (gq + 1) * 256, stages[sg_][3])
    MK = ["mT_%d_%d" % (fc, t0) for fc in range(16) for (t0, _) in TB]
    S.mark('p4')
    S.add("pool", lambda e: e.memset(dummy[0:1, 5:6], 0.0), [], ["wbuf0", "wbuf1", "wq0", "wq1", "wq2", "wq3"])
    tiles5 = []
    for cb in range(4):
        for i in range(9):
            tiles5.append((cb, i))

    def p5_load(k):
        cb, i = tiles5[k]
        n = 128 if i < 8 else NS
        xsrc = din["x_own"][i * 128:(i + 1) * 128, cb * 512:(cb + 1) * 512] if i < 8 else din["x_smp"][:, cb * 512:(cb + 1) * 512]
        j = k % 3
        dma("sp", xin[j][0:n, :], xsrc, [], ["xin%d" % j], "xin%d" % j)

    p5_load(0)
    p5_load(1)
    for k, (cb, i) in enumerate(tiles5):
        sl_ = cb % 2
        if i == 0:
            load_w(sl_, din["w_out"], cb * 512, 512, ND)
        n = 128 if i < 8 else NS
        tc0 = i * 128 if i < 8 else NT
        ydst = dout["y_own"][i * 128:(i + 1) * 128, cb * 512:(cb + 1) * 512] if i < 8 else dout["y_smp"][:, cb * 512:(cb + 1) * 512]
        j = k % 3
        b = c4["ps"] % 4
        c4["ps"] += 1
        kb = "PS%d" % b
        if k + 2 < len(tiles5):
            p5_load(k + 2)
        for fc in range(16):
            S.add("pe", lambda e, fc=fc, b=b, n=n, tc0=tc0, sl_=sl_: e.matmul(PS[b][0:n, 0:512], lhsT=mT[:, fc, tc0:tc0 + n], rhs=wbuf[sl_][:, fc, 0:512],
                                                                              start=(fc == 0), stop=(fc == 15)), MK + ["wbuf%d" % sl_], [kb])
        S.add("dve", lambda e, j=j, b=b, n=n: e.tensor_tensor(out=oti[j][0:n, :], in0=PS[b][0:n, 0:512], in1=xin[j][0:n, :], op=ALU.add),
              [kb, "xin%d" % j], ["oti%d" % j])
        dma("sp", ydst, oti[j][0:n, :], ["oti%d" % j], [], "oti%d" % j)
    S.mark('p5')

    S.emit()
    for cm in reversed(ctxs):
        cm.__exit__(None, None, None)
    return nc


def _rope_tables(pos):
    half = 32
    inv_freq = (10000.0 ** (-np.arange(half, dtype=np.float32) / half)).astype(np.float32)
    ang = pos.astype(np.float32)[:, None] * inv_freq[None, :]
    c = np.cos(ang.astype(np.float64)).astype(np.float32)
    s = np.sin(ang.astype(np.float64)).astype(np.float32)
    return np.concatenate([c, c], 1), np.concatenate([-s, s], 1)


def kernel(**inp):
    f32 = np.float32
    x_prompt = np.asarray(inp["x_prompt"], f32)
    x_sample = np.asarray(inp["x_sample"], f32)
    meta = np.asarray(inp["meta_tokens"], f32)
    shared = {
        "w_in": np.ascontiguousarray(inp["w_in"][0], f32),
        "w_glu": np.ascontiguousarray(inp["w_glu"][0], f32),
        "w_ao": np.ascontiguousarray(inp["w_attn_out"][0], f32),
        "w_so": np.ascontiguousarray(inp["w_ssm_out"][0], f32),
        "w_out": np.ascontiguousarray(inp["w_out"][0], f32),
        "norm_gain": np.ascontiguousarray(inp["norm_gain"], f32).reshape(1, D),
        "qk_gain": np.concatenate([np.asarray(inp["q_norm_gain"], f32).reshape(1, 64),
                                   np.asarray(inp["k_norm_gain"], f32).reshape(1, 64)], 1),
        "sinks": np.asarray(inp["sinks"], f32).reshape(1, 16),
        "b_glu": np.ascontiguousarray(np.asarray(inp["b_glu"], f32).reshape(8, 128).T),
        "dsk": np.ascontiguousarray(np.tile(np.asarray(inp["d_skip"], f32).reshape(64, 16).T, (8, 1))),
        "ident": np.eye(128, dtype=f32),
        "a_re2": np.ascontiguousarray(np.tile(np.asarray(inp["a_re"][0], f32).T, (2, 1))),
        "a_im2": np.ascontiguousarray(np.tile(np.asarray(inp["a_im"][0], f32).T, (2, 1))),
        "logdt": np.asarray(inp["log_dt"], f32).reshape(1, 64),
        "kvals": np.asarray([7, 6, 5, 4, 3, 2, 1, 0, -7, -6, -5, -4, -3, -2, -1, 0, 1, 2, 3, 4, 5, 6, 7, 8], f32).reshape(1, 24),
        "krow": np.arange(260, dtype=f32).reshape(1, 260),
        "blockmask": np.kron(np.triu(np.ones((8, 8), f32)), np.ones((16, 16), f32)).astype(f32),
        "swapm": np.roll(np.eye(128, dtype=f32), 64, axis=0),
    }
    kk_, qq_ = np.meshgrid(np.arange(128), np.arange(128), indexing="ij")
    NEG = np.float32(-30000.0)
    shared["maskc"] = np.where(kk_ <= qq_, np.float32(0), NEG).astype(f32)
    shared["maskp"] = np.where(kk_ > qq_, np.float32(0), NEG).astype(f32)
    b_re = np.asarray(inp["b_re"][0], f32).transpose(1, 0, 2)
    b_im = np.asarray(inp["b_im"][0], f32).transpose(1, 0, 2)
    c_re = np.asarray(inp["c_re"][0], f32).transpose(2, 0, 1)
    c_im = np.asarray(inp["c_im"][0], f32).transpose(2, 0, 1)
    shared["b_self"] = np.ascontiguousarray(np.concatenate([b_re, b_im], 0))
    shared["b_part"] = np.ascontiguousarray(np.concatenate([b_im, b_re], 0))
    shared["c_self"] = np.ascontiguousarray(np.concatenate([c_re, c_im], 0))
    shared["c_part"] = np.ascontiguousarray(np.concatenate([c_im, c_re], 0))
    in_maps = []
    for core in range(N_CORES):
        b, half = core // 2, core % 2
        m = dict(shared)
        m["x_own"] = np.ascontiguousarray(x_prompt[b, half * NT:(half + 1) * NT])
        if half == 1:
            m["x_pre2"] = np.ascontiguousarray(x_prompt[b, 0:NT])
            m["x_pre1"] = meta.copy()
            halo = x_prompt[b, NT - 128:NT]
        else:
            m["x_pre2"] = np.concatenate([np.zeros((NT - 16, D), f32), meta], 0)
            m["x_pre1"] = np.zeros((16, D), f32)
            halo = np.zeros((128, D), f32)
        m["x_kvx"] = np.concatenate([halo, meta], 0)
        m["maskp0"] = shared["maskp"] if half == 1 else np.full((128, 128), NEG, f32)
        m["x_smp"] = np.ascontiguousarray(x_sample[core * NS:(core + 1) * NS, 0])
        base = 16 + half * NT
        cc = np.zeros((11, 128, 64), f32)
        ss = np.zeros((11, 128, 64), f32)
        for t in range(8):
            cc[t], ss[t] = _rope_tables(base + t * 128 + np.arange(128))
        cc[8], ss[8] = _rope_tables(np.maximum(base - 128 + np.arange(128), 0))
        cc[9, :16], ss[9, :16] = _rope_tables(np.arange(16))
        cc[10, :16], ss[10, :16] = _rope_tables(np.full(16, 8192))
        m["rope_cc"] = np.ascontiguousarray(cc.transpose(1, 0, 2))
        m["rope_ss"] = np.ascontiguousarray(ss.transpose(1, 0, 2))
        sl = slice(core * NS, (core + 1) * NS)
        m["cwk"] = np.ascontiguousarray(inp["cache_win_k"][0, sl], f32).reshape(NS, 128, 256)
        m["cwv"] = np.ascontiguousarray(inp["cache_win_v"][0, sl], f32).reshape(NS, 128, 256)
        m["cmk"] = np.ascontiguousarray(inp["cache_meta_k"][0, sl], f32).reshape(NS, 16, 256)
        m["cmv"] = np.ascontiguousarray(inp["cache_meta_v"][0, sl], f32).reshape(NS, 16, 256)
        sre = np.asarray(inp["state_ssm_re"][0, sl], f32)
        sim = np.asarray(inp["state_ssm_im"][0, sl], f32)
        m["st_self"] = np.ascontiguousarray(np.concatenate([sre, sim], 2))
        m["st_part"] = np.ascontiguousarray(np.concatenate([sim, sre], 2))
        in_maps.append(m)

    nc = build_program()
    res = run_bass_kernel_spmd(nc, in_maps, core_ids=list(range(N_CORES)))
    R = res.results

    y_prompt = np.zeros((4, 2048, D), f32)
    y_sample = np.zeros((128, 1, D), f32)
    p_win_k = np.zeros((1, 4, 128, 4, 64), f32)
    p_win_v = np.zeros((1, 4, 128, 4, 64), f32)
    p_meta_k = np.zeros((1, 4, 16, 4, 64), f32)
    p_meta_v = np.zeros((1, 4, 16, 4, 64), f32)
    p_re = np.zeros((1, 4, 64, 64), f32)
    p_im = np.zeros((1, 4, 64, 64), f32)
    s_win_k = np.zeros((1, 128, 128, 4, 64), f32)
    s_win_v = np.zeros((1, 128, 128, 4, 64), f32)
    s_re = np.zeros((1, 128, 64, 64), f32)
    s_im = np.zeros((1, 128, 64, 64), f32)
    for core in range(N_CORES):
        b, half = core // 2, core % 2
        r = R[core]
        y_prompt[b, half * NT:(half + 1) * NT] = r["y_own"]
        sl = slice(core * NS, (core + 1) * NS)
        y_sample[sl, 0] = r["y_smp"]
        if half == 1:
            p_win_k[0, b] = r["pwk"].reshape(128, 4, 64)
            p_win_v[0, b] = r["pwv"].reshape(128, 4, 64)
            p_re[0, b] = r["pssm"][:, 0:64]
            p_im[0, b] = r["pssm"][:, 64:128]
        else:
            p_meta_k[0, b] = r["pmk"].reshape(16, 4, 64)
            p_meta_v[0, b] = r["pmv"].reshape(16, 4, 64)
        s_win_k[0, sl] = r["swk"].reshape(NS, 128, 4, 64)
        s_win_v[0, sl] = r["swv"].reshape(NS, 128, 4, 64)
        s_re[0, sl] = r["sssm"][:, :, 0:64]
        s_im[0, sl] = r["sssm"][:, :, 64:128]
    return (y_prompt, y_sample, p_win_k, p_win_v, p_meta_k, p_meta_v, p_re, p_im, s_win_k, s_win_v, s_re, s_im)
```

```python
import os
import numpy as np
import concourse.bass as bass
import concourse.mybir as mybir
from concourse.bass_utils import run_bass_kernel_spmd

F32 = mybir.dt.float32
BF16 = mybir.dt.bfloat16
ALU = mybir.AluOpType
AF = mybir.ActivationFunctionType
AX = mybir.AxisListType

D = 2048
ND = 16
NT = 1024
NS = 16
NTS = NT + NS
EPS = 1e-6
N_CORES = 8
KTRUNC = ''
KUM = 'ab'


class Sched:
    ENGS = ("pe", "act", "dve", "pool", "sp")

    def __init__(self, nc):
        self.nc = nc
        self.ops = []

    def add(self, eng, fn, reads=(), writes=(), dma=None, fence=False):
        self.ops.append(dict(eng=eng, fn=fn, reads=tuple(reads), writes=tuple(writes), dma=dma, fence=fence))
        return len(self.ops) - 1

    def barrier(self, mk):
        keys = set()
        for o in self.ops:
            keys.update(o["reads"])
            keys.update(o["writes"])
        keys = sorted(keys, key=str)
        self.nbar = getattr(self, "nbar", 0) + 1
        bk = "bar%d" % self.nbar
        self.add("act", mk("act"), reads=keys, writes=keys + [bk])
        for e in ("dve", "pool", "sp"):
            self.add(e, mk(e), reads=[bk], writes=["%s_%s" % (bk, e)], dma=("bar_sp" if e == "sp" else None))

    def mark(self, name):
        if not hasattr(self, "marks"):
            self.marks = {}
        self.marks[name] = len(self.ops)

    def emit(self):
        nc = self.nc
        if KTRUNC:
            self.ops = self.ops[:self.marks[KTRUNC]]
        ops = self.ops
        last_w = {}
        readers = {}
        deps = [set() for _ in ops]
        for i, o in enumerate(ops):
            for b in o["reads"]:
                if b in last_w:
                    deps[i].add(last_w[b])
            for b in o["writes"]:
                if b in last_w:
                    deps[i].add(last_w[b])
                for r in readers.get(b, ()):
                    if r != i:
                        deps[i].add(r)
            for b in o["reads"]:
                readers.setdefault(b, []).append(i)
            for b in o["writes"]:
                last_w[b] = i
                readers[b] = []
        needed = set()
        prev_on = {}
        for i, o in enumerate(ops):
            keep = set()
            for d in deps[i]:
                if ops[d]["dma"] is None and ops[d]["eng"] == "pe" and o["eng"] == "pe" and o["dma"] is None:
                    continue
                keep.add(d)
            if o.get("fence") and o["eng"] in prev_on:
                keep.add(prev_on[o["eng"]])
            if o["dma"] is None:
                prev_on[o["eng"]] = i
            deps[i] = keep
            needed |= keep
        dma_keys = sorted({o["dma"] for o in ops if o["dma"] is not None}, key=str)
        sem_ctx = []
        sems = {}
        for e in self.ENGS:
            cm = nc.semaphore("s_" + e)
            sems[("eng", e)] = cm.__enter__()
            sem_ctx.append(cm)
        for n, k in enumerate(dma_keys):
            cm = nc.semaphore("d%d" % n)
            sems[("dma", k)] = cm.__enter__()
            sem_ctx.append(cm)
        cnt = {k: 0 for k in sems}
        ticket = [None] * len(ops)
        for i, o in enumerate(ops):
            if o["dma"] is not None:
                k = ("dma", o["dma"])
                cnt[k] += 16
                ticket[i] = (k, cnt[k])
            elif i in needed:
                k = ("eng", o["eng"])
                cnt[k] += 1
                ticket[i] = (k, cnt[k])
        streams = {e: [] for e in self.ENGS}
        waited = {e: {} for e in self.ENGS}
        for i, o in enumerate(ops):
            e = o["eng"]
            w = {}
            for d in deps[i]:
                k, v = ticket[d]
                if waited[e].get(k, 0) >= v:
                    continue
                w[k] = max(w.get(k, 0), v)
            for k, v in w.items():
                waited[e][k] = v
            streams[e].append((i, w))
        final = {k: v for k, v in cnt.items() if k[0] == "dma" and v > 0}

        def run_stream(e, engobj):
            for i, w in streams[e]:
                for k, v in w.items():
                    engobj.wait_ge(sems[k], v)
                ins = ops[i]["fn"](engobj)
                if ticket[i] is not None:
                    k, v = ticket[i]
                    ins.then_inc(sems[k], 16 if k[0] == "dma" else 1)
            if e == "sp":
                for k, v in final.items():
                    engobj.wait_ge(sems[k], v)

        with nc.Block() as block:
            @block.sync
            def _(eng):
                run_stream("sp", eng)

            @block.tensor
            def _(eng):
                run_stream("pe", eng)

            @block.scalar
            def _(eng):
                run_stream("act", eng)

            @block.vector
            def _(eng):
                run_stream("dve", eng)

            @block.gpsimd
            def _(eng):
                run_stream("pool", eng)
        for cm in reversed(sem_ctx):
            cm.__exit__(None, None, None)


IN_SPECS = [
    ("x_own", [NT, D]), ("x_pre2", [NT, D]), ("x_pre1", [16, D]), ("x_kvx", [144, D]), ("x_smp", [NS, D]),
    ("w_in", [D, 8704]), ("w_glu", [1024, 1024]), ("w_ao", [1024, D]), ("w_so", [1024, D]), ("w_out", [D, D]),
    ("norm_gain", [1, D]), ("qk_gain", [1, 128]), ("sinks", [1, 16]), ("b_glu", [128, 8]), ("dsk", [128, 64]),
    ("rope_cc", [128, 11, 64]), ("rope_ss", [128, 11, 64]),
    ("ident", [128, 128]),
    ("cwk", [NS, 128, 256]), ("cwv", [NS, 128, 256]), ("cmk", [NS, 16, 256]), ("cmv", [NS, 16, 256]),
    ("a_re2", [128, 64]), ("a_im2", [128, 64]), ("logdt", [1, 64]),
    ("b_self", [128, 64, 16]), ("b_part", [128, 64, 16]), ("c_self", [128, 64, 16]), ("c_part", [128, 64, 16]),
    ("kvals", [1, 24]), ("krow", [1, 260]), ("blockmask", [128, 128]), ("swapm", [128, 128]),
    ("st_self", [NS, 64, 128]), ("st_part", [NS, 64, 128]),
    ("maskc", [128, 128]), ("maskp", [128, 128]), ("maskp0", [128, 128]),
]
OUT_SPECS = [
    ("y_own", [NT, D]), ("y_smp", [NS, D]),
    ("pwk", [128, 256]), ("pwv", [128, 256]), ("pmk", [16, 256]), ("pmv", [16, 256]),
    ("pssm", [64, 128]),
    ("swk", [NS, 128, 256]), ("swv", [NS, 128, 256]), ("sssm", [NS, 64, 128]),
]


def build_program():
    nc = bass.Bass("TRN2", target_bir_lowering=False)
    S = Sched(nc)
    din = {n: nc.dram_tensor(n, s, F32, kind="ExternalInput").ap() for n, s in IN_SPECS}
    dout = {n: nc.dram_tensor(n, s, F32, kind="ExternalOutput").ap() for n, s in OUT_SPECS}
    ctxs = []

    def sb(name, shape, dt):
        cm = nc.sbuf_tensor("sb_" + name, shape, dt)
        t = cm.__enter__()
        ctxs.append(cm)
        return t

    def psum(name, shape, dt):
        cm = nc.psum_tensor(name, shape, dt)
        t = cm.__enter__()
        ctxs.append(cm)
        return t

    KB = 1024
    ARENA = 150 * KB
    AR = sb("arena", [128, ARENA // 2], BF16)

    def carve(off, n, dt):
        assert off % 4 == 0
        if dt == BF16:
            assert off + 2 * n <= ARENA, (off, n)
            return AR[:, off // 2: off // 2 + n]
        assert off + 4 * n <= ARENA, (off, n)
        return AR[:, off // 2: off // 2 + 2 * n].bitcast(F32)

    PS = [psum("ps%d" % i, [128, 512], F32) for i in range(8)]
    PSB = [p.bitcast(BF16) for p in PS]
    identf = sb("identf", [128, 128], F32)
    identb = sb("identb", [128, 128], BF16)
    epsb = sb("epsb", [128, 1], F32)
    qkg_bc = sb("qkg_bc", [128, 128], F32)
    ropecc = sb("ropecc", [128, 11, 64], F32)
    ropess = sb("ropess", [128, 11, 64], F32)
    stat = sb("stat", [128, 64], F32)
    dummy = sb("dummyk", [128, 8], F32)
    xnT = sb("xnT", [128, ND, NTS], BF16)
    Um = sb("Um", [128, 64, 18], BF16)
    kvals = sb("kvals", [128, 24], F32)
    krow = sb("krow", [128, 260], F32)
    sgn = sb("sgn", [128, 2], F32)
    halfpi = sb("halfpi", [128, 1], F32)
    onec = sb("onec", [128, 1], F32)
    pp = sb("pp", [128, 16, 64], F32)
    mumag = sb("mumag", [128, 64], F32)
    G258 = sb("G258", [128, 64], F32)
    C258 = sb("C258", [128, 64], F32)
    S258 = sb("S258", [128, 64], F32)
    dsk = sb("dsk", [128, 64], F32)
    blockmask = sb("blockmask", [128, 128], F32)
    swapm = sb("swapm", [128, 128], F32)
    b_glu = sb("b_glu", [128, 8], F32)
    expsink = sb("expsink", [128, 16], F32)
    masks = sb("masks", [128, 3, 128], BF16)
    qs16 = sb("qs16", [NS, 1024], BF16)
    onesb = sb("onesb", [128, 64], BF16)

    u8own = carve(0, 8192, BF16).rearrange("p (g s h) -> p g s h", g=64, s=8)
    u8p2 = carve(16 * KB, 8192, BF16).rearrange("p (g s h) -> p g s h", g=64, s=8)
    wbuf = [carve((68 + 16 * i) * KB, ND * 512, BF16).rearrange("p (c n) -> p c n", c=ND) for i in range(2)]
    u8m = carve(34 * KB, 8192, BF16).rearrange("p (g s h) -> p g s h", g=64, s=8)
    gain_bc = carve(51 * KB, D, F32)
    xbuf = [carve((100 + 8 * i) * KB, D, F32) for i in range(2)]
    xnb = [carve((116 + 4 * i) * KB, D, BF16) for i in range(2)]
    tmpq = [carve((124 + 4 * i) * KB, 1024, F32) for i in range(3)]
    sqj = carve(128 * KB, D, F32)
    xnTt = [carve((136 + 4 * i) * KB, ND * 128, BF16).rearrange("p (c t) -> p c t", c=ND) for i in range(2)]
    kf = [carve((144 + i) * KB, 256, F32) for i in range(2)]
    vf = [carve((146 + i) * KB, 256, F32) for i in range(2)]
    utok1 = carve(148 * KB, 1024, BF16)

    cnt = {"x": 0, "ps": 0, "kv": 0}

    def dma(eng, out, in_, reads, writes, key):
        S.add(eng, lambda e: e.dma_start(out=out, in_=in_), reads=reads, writes=writes, dma=key)

    def mk_bar(e):
        col = {"act": 0, "dve": 1, "pool": 2, "sp": 3}[e]
        if e == "act":
            return lambda en: en.copy(out=dummy[0:1, col:col + 1], in_=dummy[0:1, 4:5])
        if e == "sp":
            return lambda en: en.dma_start(out=dummy[0:1, col:col + 1], in_=din["kvals"][0:1, 0:1])
        return lambda en: en.memset(dummy[0:1, col:col + 1], 0.0)

    S.add("pool", lambda e: e.memset(dummy[:], 0.0), [], ["dummy"])
    dma("sp", identf[:], din["ident"], [], ["identf"], "identf")
    S.add("dve", lambda e: e.tensor_copy(out=identb[:], in_=identf[:]), ["identf"], ["identb"])
    S.add("pool", lambda e: e.memset(epsb[:], EPS), [], ["epsb"])
    dma("sp", gain_bc, din["norm_gain"].partition_broadcast(128), [], ["gain_bc"], "gain_bc")
    dma("sp", qkg_bc[:], din["qk_gain"].partition_broadcast(128), [], ["qkg_bc"], "qkg_bc")
    dma("sp", ropecc[:], din["rope_cc"], [], ["ropecc"], "ropecc")
    dma("sp", ropess[:], din["rope_ss"], [], ["ropess"], "ropess")

    def load_w(slot, src, col0, ncols, nk):
        key = "wbuf%d" % slot
        v = src.rearrange("(c p) n -> p c n", p=128)
        step = 4
        for c0 in range(0, nk, step):
            c1 = min(nk, c0 + step)
            dma("pool", wbuf[slot][:, c0:c1, 0:ncols], v[:, c0:c1, col0:col0 + ncols], [], [key], key)

    def norm_A(src_rows, n):
        j = (cnt["x"] % 2) if (xbuf[0] is not xbuf[1]) else 0
        cnt["x"] += 1
        xb_, xn_ = xbuf[j], xnb[j]
        sq_, g_ = sqj, gain_bc
        kx, kn = "xbuf%d" % j, "xnb%d" % j
        dma("sp", xb_[0:n, :], src_rows, [], [kx], kx)
        S.add("act", lambda e: e.activation(out=sq_[0:n, :], in_=xb_[0:n, :], func=AF.Square), [kx], ["tq1", "tq2a", "tq2b"])
        S.add("dve", lambda e: e.tensor_reduce(out=stat[0:n, 0:1], in_=sq_[0:n, :], axis=AX.X, op=ALU.add), ["tq1", "tq2a", "tq2b"], ["stat0"])
        S.add("act", lambda e: e.activation(out=stat[0:n, 1:2], in_=stat[0:n, 0:1], func=AF.Sqrt, bias=epsb[0:n, 0:1], scale=1.0 / D),
              ["stat0", "epsb"], ["stat1"])
        S.add("dve", lambda e: e.reciprocal(out=stat[0:n, 2:3], in_=stat[0:n, 1:2]), ["stat1"], ["stat2"])
        S.add("dve", lambda e: e.scalar_tensor_tensor(out=xn_[0:n, :], in0=xb_[0:n, :], scalar=stat[0:n, 2:3], in1=g_[0:n, :],
                                                      op0=ALU.mult, op1=ALU.mult), [kx, "stat2", "gain_bc"], [kn])
        return xn_, kn, n

    def norm_B(h, dst_fn, dst_keys):
        xn_, kn, n = h
        for half in range(2):
            b = 4 + (cnt["ps"] % 2)
            cnt["ps"] += 1
            kb = "PS%d" % b
            pv = PSB[b][:, 0:1024].rearrange("p (c t) -> p c t", c=8)
            for c in range(8):
                dc = half * 8 + c
                S.add("pe", lambda e, c=c, dc=dc, pv=pv: e.transpose(out=pv[:, c, 0:n], in_=xn_[0:n, dc * 128:(dc + 1) * 128],
                                                                      identity=identb[0:n, 0:n]), [kn, "identb"], [kb])
            dst = dst_fn(half * 8)
            if half == 0:
                S.add("act", lambda e, pv=pv, dst=dst: e.copy(out=dst, in_=pv[:, :, 0:n]), [kb], dst_keys)
            else:
                S.add("dve", lambda e, pv=pv, dst=dst: e.tensor_copy(out=dst, in_=pv[:, :, 0:n]), [kb], dst_keys)

    def norm_tile(src_rows, n, dst_fn, dst_keys):
        norm_B(norm_A(src_rows, n), dst_fn, dst_keys)

    def headnorm_rope(src_ps, src_keys, n, NH, ti, goff, extra_scale, dst, dst_keys):
        W = NH * 64
        t0, t1, t2 = tmpq
        c0 = 0
        for ap_, w in src_ps:
            S.add("act", lambda e, ap_=ap_, c0=c0, w=w: e.copy(out=t0[0:n, c0:c0 + w], in_=ap_), src_keys, ["tq0"])
            S.add("act", lambda e, ap_=ap_, c0=c0, w=w: e.activation(out=t1[0:n, c0:c0 + w], in_=ap_, func=AF.Square), src_keys, ["tq1"])
            c0 += w
        v3 = lambda t: t[0:n, 0:W].rearrange("p (h d) -> p h d", d=64)
        S.add("dve", lambda e: e.tensor_reduce(out=stat[0:n, 8:8 + NH], in_=v3(t1), axis=AX.X, op=ALU.add), ["tq1"], ["stq"])
        S.add("act", lambda e: e.activation(out=stat[0:n, 24:24 + NH], in_=stat[0:n, 8:8 + NH], func=AF.Sqrt, bias=epsb[0:n, 0:1], scale=1.0 / 64),
              ["stq", "epsb"], ["stq2"])
        S.add("dve", lambda e: e.reciprocal(out=stat[0:n, 40:40 + NH], in_=stat[0:n, 24:24 + NH]), ["stq2"], ["stq3"])
        if extra_scale != 1.0:
            S.add("dve", lambda e: e.tensor_scalar(out=stat[0:n, 40:40 + NH], in0=stat[0:n, 40:40 + NH], scalar1=float(extra_scale), scalar2=0.0,
                                                   op0=ALU.mult, op1=ALU.add), ["stq3"], ["stq3"])
        gb = qkg_bc[0:n, goff:goff + 64].unsqueeze(1).to_broadcast([n, NH, 64])
        S.add("dve", lambda e: e.tensor_tensor(out=v3(t0), in0=v3(t0), in1=gb, op=ALU.mult), ["tq0", "qkg_bc"], ["tq0"])
        cc = ropecc[0:n, ti, :].unsqueeze(1).to_broadcast([n, NH, 64])
        S.add("dve", lambda e: e.tensor_tensor(out=v3(t1), in0=v3(t0), in1=cc, op=ALU.mult), ["tq0", "ropecc", "stq"], ["tq1"])
        s_lo = ropess[0:n, ti, 0:32].unsqueeze(1).to_broadcast([n, NH, 32])
        s_hi = ropess[0:n, ti, 32:64].unsqueeze(1).to_broadcast([n, NH, 32])
        S.add("dve", lambda e: e.tensor_tensor(out=v3(t2)[:, :, 0:32], in0=v3(t0)[:, :, 32:64], in1=s_lo, op=ALU.mult), ["tq0", "ropess"], ["tq2a"])
        S.add("dve", lambda e: e.tensor_tensor(out=v3(t2)[:, :, 32:64], in0=v3(t0)[:, :, 0:32], in1=s_hi, op=ALU.mult), ["tq0", "ropess"], ["tq2b"])
        S.add("dve", lambda e: e.tensor_tensor(out=v3(t1), in0=v3(t1), in1=v3(t2), op=ALU.add), ["tq1", "tq2a", "tq2b"], ["tq1"])
        rb = stat[0:n, 40:40 + NH].unsqueeze(2).to_broadcast([n, NH, 64])
        dv = dst.rearrange("p (h d) -> p h d", d=64)
        S.add("dve", lambda e: e.tensor_tensor(out=dv, in0=v3(t1), in1=rb, op=ALU.mult), ["tq1", "stq3"], dst_keys)

    def proj_tok(lhs_fn, lhs_keys, n, slot, ncols, bank):
        kb = "PS%d" % bank
        for dc in range(ND):
            S.add("pe", lambda e, dc=dc: e.matmul(PS[bank][0:n, 0:ncols], lhsT=lhs_fn(dc), rhs=wbuf[slot][:, dc, 0:ncols],
                                                  start=(dc == 0), stop=(dc == ND - 1)), lhs_keys + ["wbuf%d" % slot], [kb])
        return kb

    pass
    S.mark('p1a')
    S.add("pool", lambda e: e.memset(u8m, 0.0), [], ["u8m"])
    load_w(0, din["w_in"], 2560, 512, ND)
    load_w(1, din["w_in"], 3072, 512, ND)

    def u_tile(lhs_fn, lhs_keys, n, evac_fn):
        for blk in range(2):
            b = cnt["kv"] % 2
            cnt["kv"] += 1
            kb = proj_tok(lhs_fn, lhs_keys, n, blk, 512, b)
            evac_fn(blk, PS[b], kb)

    S.mark("u_pre1")
    xp2 = din["x_pre2"].rearrange("(c s) d -> c s d", s=8)
    tl = [("smp", 0)]
    for i in range(8):
        tl += [("own", i), ("pre", i)]

    def p1_src(kind, i):
        if kind == "smp":
            return din["x_smp"], NS
        if kind == "own":
            return din["x_own"][i * 128:(i + 1) * 128, :], 128
        return xp2[:, i, :], 128

    def p1_finish(kind, i, h):
        if kind == "smp":
            norm_B(h, lambda dc0: xnT[:, dc0:dc0 + 8, NT:NTS], ["xnT_s"])
        elif kind == "own":
            norm_B(h, lambda dc0, i=i: xnT[:, dc0:dc0 + 8, i * 128:(i + 1) * 128], ["xnT_%d" % i])
        else:
            jt = i % 2
            tt = xnTt[jt]
            norm_B(h, lambda dc0, tt=tt: tt[:, dc0:dc0 + 8, 0:128], ["xnTt%d" % jt])

            def ev_p2(blk, ps_, kb, i=i):
                S.add("act", lambda e: e.copy(out=u8p2[:, blk * 32:(blk + 1) * 32, i, :], in_=ps_[:, 0:512].rearrange("p (g h) -> p g h", h=16)),
                      [kb], ["u8p2_%d_%d" % (i, blk)])
            u_tile(lambda dc, tt=tt: tt[:, dc, 0:128], ["xnTt%d" % jt], 128, ev_p2)

    pend1 = None
    for (kind, i) in tl:
        src_, n_ = p1_src(kind, i)
        h_ = norm_A(src_, n_)
        if pend1 is not None:
            p1_finish(*pend1)
        pend1 = (kind, i, h_)
    p1_finish(*pend1)
    S.mark("u_pre2")
    OWNK = ["xnT_%d" % i for i in range(8)]
    for i in range(8):
        def ev_own(blk, ps_, kb, i=i):
            S.add("act", lambda e: e.copy(out=u8own[:, blk * 32:(blk + 1) * 32, i, :], in_=ps_[:, 0:512].rearrange("p (g h) -> p g h", h=16)),
                  [kb], ["u8own_%d_%d" % (i, blk)])
        u_tile(lambda dc, i=i: xnT[:, dc, i:NT:8], OWNK, 128, ev_own)

    S.mark("u_own")

    def ev_smp(blk, ps_, kb):
        S.add("act", lambda e: e.copy(out=u8m[0:NS, blk * 32:(blk + 1) * 32, 7, :], in_=ps_[0:NS, 0:512].rearrange("p (g h) -> p g h", h=16)),
              [kb, "u8m"], ["u8m_s%d" % blk])
    u_tile(lambda dc: xnT[:, dc, NT:NTS], ["xnT_s"], NS, ev_smp)
    j = cnt["x"] % 2
    tt = xnTt[j]
    norm_tile(din["x_pre1"], 16, lambda dc0, tt=tt: tt[:, dc0:dc0 + 8, 0:16], ["xnTt%d" % j])

    def ev_pre1(blk, ps_, kb):
        S.add("act", lambda e: e.copy(out=utok1[0:16, blk * 512:(blk + 1) * 512], in_=ps_[0:16, 0:512]), [kb], ["utok1_%d" % blk])
    u_tile(lambda dc, tt=tt: tt[:, dc, 0:16], ["xnTt%d" % j], 16, ev_pre1)
    for s_ in range(8):
        dma("sp", u8m[32:34, :, s_, :], utok1[s_:16:8, :].rearrange("p (g h) -> p g h", h=16), ["utok1_0", "utok1_1", "u8m"], ["u8m_p1"], "u8m_p1")

    S.mark("u_smp")
    U8K_P2 = ["u8p2_%d_%d" % (i, b) for i in range(8) for b in range(2)]
    U8K_OWN = ["u8own_%d_%d" % (i, b) for i in range(8) for b in range(2)]
    U8K_M = ["u8m", "u8m_p1", "u8m_s0", "u8m_s1"]
    for part in range(2):
        for q4 in range(4):
            bk = 6 + q4 % 2
            kb = "PS%d" % bk
            pv = PSB[bk][:, 0:1024].rearrange("p (g t) -> p g t", t=64)
            for gl in range(16):
                gg = q4 * 16 + gl
                fence = (part == 1 and q4 == 0 and gl == 0)
                if part == 0:
                    S.add("pe", lambda e, gl=gl, gg=gg, pv=pv: e.transpose(out=pv[:, gl, 0:2], in_=u8m[32:34, gg, :, :].rearrange("p s h -> p (s h)"),
                                                                            identity=identb[32:34, 32:34]), U8K_M + ["identb"], [kb])
                else:
                    S.add("pe", lambda e, gl=gl, gg=gg, pv=pv: e.transpose(out=pv[:, gl, 0:16], in_=u8m[0:NS, gg, :, :].rearrange("p s h -> p (s h)"),
                                                                            identity=identb[0:NS, 0:NS]), U8K_M + ["identb"], [kb], fence=fence)
            if part == 0:
                S.add("dve", lambda e, q4=q4, pv=pv: e.tensor_copy(out=Um[:, q4 * 16:(q4 + 1) * 16, 0:2], in_=pv[:, :, 0:2]), [kb], ["Um_%d" % q4])
            else:
                S.add("dve", lambda e, q4=q4, pv=pv: e.tensor_copy(out=Um[:, q4 * 16:(q4 + 1) * 16, 2:18], in_=pv[:, :, 0:16]), [kb], ["Umb_%d" % q4])
    UMK = ["Um_%d" % q for q in range(4)] + ["Umb_%d" % q for q in range(4)]

    S.mark('p1b')
    S.barrier(mk_bar)
    S.mark('bar1')
    MAGIC = 12582912.0
    TWO_PI = float(2.0 * np.pi)
    SC = 1036
    GB = 4
    Ere = carve(51 * KB, 1536, F32).rearrange("p (g k) -> p g k", k=24)
    Eim = carve(57 * KB, 1536, F32).rearrange("p (g k) -> p g k", k=24)
    bbs = carve(63 * KB, 1024, F32).rearrange("p (g h) -> p g h", h=16)
    bbp = carve(67 * KB, 1024, F32).rearrange("p (g h) -> p g h", h=16)
    cself = carve(71 * KB, 1024, F32).rearrange("p (g h) -> p g h", h=16)
    cpart = carve(75 * KB, 1024, F32).rearrange("p (g h) -> p g h", h=16)
    sc = [carve(79 * KB + i * 4352, SC, F32) for i in range(4)]
    cosT = carve(96 * KB, GB * 259, F32).rearrange("p (g k) -> p g k", k=259)
    sinT = carve(96 * KB + 4352, GB * 259, F32).rearrange("p (g k) -> p g k", k=259)
    o_ = 96 * KB + 2 * 4352
    A32 = carve(o_, GB * 128, F32).rearrange("p (g q) -> p g q", q=128); o_ += 2 * KB
    R32 = carve(o_, GB * 128, F32).rearrange("p (g q) -> p g q", q=128); o_ += 2 * KB
    Ab = carve(o_, 2 * GB * 128, BF16).rearrange("p (w g q) -> p w g q", w=2, q=128); o_ += 2 * KB
    Mab = carve(o_, 2 * GB * 128, BF16).rearrange("p (w g q) -> p w g q", w=2, q=128); o_ += 2 * KB
    Msb = carve(o_, 2 * GB * 128, BF16).rearrange("p (w g q) -> p w g q", w=2, q=128); o_ += 2 * KB
    Mib = carve(o_, GB * 128, BF16).rearrange("p (g q) -> p g q", q=128); o_ += KB
    mt = []
    for i in range(2):
        mt.append(carve(o_, GB * 128, F32).rearrange("p (g q) -> p g q", q=128)); o_ += 2 * KB
    NCH = 258
    NU = 274
    Ub = carve(o_, GB * NU, BF16).rearrange("p (g t) -> p g t", t=NU); o_ += 2304
    Xt = carve(o_, GB * NCH, F32).rearrange("p (g t) -> p g t", t=NCH); o_ += 4352
    Xt2 = [carve(o_ + 1056 * i, NCH, F32) for i in range(2)]; o_ += 2112
    Gs = carve(o_, GB * 259, F32).rearrange("p (g k) -> p g k", k=259); o_ += 4352
    Xsm = carve(o_, GB * NS, F32).rearrange("p (g b) -> p g b", b=NS); o_ += 256
    st_s = carve(o_, GB * 128, F32).rearrange("p (g q) -> p g q", q=128); o_ += 2 * KB
    st_p = carve(o_, GB * 128, F32).rearrange("p (g q) -> p g q", q=128); o_ += 2 * KB
    hs1 = carve(o_, GB * NS, F32).rearrange("p (g b) -> p g b", b=NS); o_ += 256
    hs2 = carve(o_, GB * NS, F32).rearrange("p (g b) -> p g b", b=NS); o_ += 256
    hso = carve(o_, GB * 128, F32).rearrange("p (g q) -> p g q", q=128); o_ += 2 * KB
    sE = carve(o_, 2 * GB * 24, F32).rearrange("p (w g k) -> p w g k", w=2, k=24); o_ += KB
    hfo = carve(o_, 128, F32); o_ += 512
    assert o_ <= ARENA, o_
    o2 = o_
    Pcs = carve(o2, 2 * GB * 128, BF16).rearrange("p (w g q) -> p w g q", w=2, q=128); o2 += 2 * KB
    y8 = carve(o2, 8 * 128, BF16).rearrange("p (t f) -> p t f", f=128); o2 += 2 * KB
    y8s = carve(o2, 128, BF16); o2 += 256
    assert o2 <= ARENA, o2
    yA = sb("yA", [128, GB * 128], F32)[:].rearrange("p (g q) -> p g q", q=128)
    yB = sb("yB", [128, GB * 128], F32)[:].rearrange("p (g q) -> p g q", q=128)
    b3 = o2
    ygb = carve(b3, GB * 128, BF16).rearrange("p (g q) -> p g q", q=128)
    ysA = carve(b3 + 1024, GB * NS, F32).rearrange("p (g b) -> p g b", b=NS)
    ysB = carve(b3 + 1280, GB * NS, F32).rearrange("p (g b) -> p g b", b=NS)
    ygs = carve(b3 + 1536, GB * NS, BF16).rearrange("p (g b) -> p g b", b=NS)
    hn1 = carve(b3 + 1664, GB * NS, F32).rearrange("p (g b) -> p g b", b=NS)
    hn2 = carve(b3 + 1920, GB * NS, F32).rearrange("p (g b) -> p g b", b=NS)
    Hins = carve(b3 + 2176, GB * NS, BF16).rearrange("p (g b) -> p g b", b=NS)
    assert b3 + 2304 <= ARENA, b3
    ST = carve(34 * KB, 8 * NTS, BF16).rearrange("p (c t) -> p c t", t=NTS)
    bself = carve(96 * KB, 1024, F32).rearrange("p (g h) -> p g h", h=16)
    bpart = carve(100 * KB, 1024, F32).rearrange("p (g h) -> p g h", h=16)
    a_re2 = carve(104 * KB, 64, F32)
    a_im2 = carve(104 * KB + 256, 64, F32)
    ldt = carve(104 * KB + 512, 64, F32)

    dma("sp", a_re2, din["a_re2"], [], ["a_re2"], "a_re2")
    dma("sp", a_im2, din["a_im2"], [], ["a_im2"], "a_im2")
    dma("sp", ldt, din["logdt"].partition_broadcast(128), [], ["ldt"], "ldt")
    dma("sp", bself, din["b_self"], [], ["bself"], "bself")
    dma("sp", bpart, din["b_part"], [], ["bpart"], "bpart")
    dma("sp", cself, din["c_self"], [], ["cself"], "cself")
    dma("sp", cpart, din["c_part"], [], ["cpart"], "cpart")
    dma("sp", kvals[:], din["kvals"].partition_broadcast(128), [], ["kvals"], "kvals")
    dma("sp", krow[:], din["krow"].partition_broadcast(128), [], ["krow"], "krow")
    dma("sp", blockmask[:], din["blockmask"], [], ["blockmask"], "blockmask")
    dma("sp", swapm[:], din["swapm"], [], ["swapm"], "swapm")
    dma("sp", dsk[:], din["dsk"], [], ["dsk"], "dsk")
    S.add("pool", lambda e: e.memset(sgn[0:64, 0:1], -1.0), [], ["sgn_a"])
    S.add("pool", lambda e: e.memset(sgn[64:128, 0:1], 1.0), [], ["sgn_b"])
    S.add("pool", lambda e: e.memset(sgn[0:64, 1:2], 1.0), [], ["sgn_c"])
    S.add("pool", lambda e: e.memset(sgn[64:128, 1:2], -1.0), [], ["sgn_d"])
    S.add("pool", lambda e: e.memset(halfpi[:], float(np.pi / 2)), [], ["halfpi"])
    S.add("pool", lambda e: e.memset(onec[:], 1.0), [], ["onec"])
    SGN = ["sgn_a", "sgn_b", "sgn_c", "sgn_d"]

    def sincos(ang, cos_out, sin_out, F, rk, wk_cos, wk_sin):
        tA, tB, tC = sc[1][:, 0:F], sc[2][:, 0:F], sc[3][:, 0:F]
        S.add("dve", lambda e: e.tensor_scalar(out=tA, in0=ang, scalar1=float(1.0 / TWO_PI), scalar2=MAGIC, op0=ALU.mult, op1=ALU.add), rk, ["sc1"])
        S.add("dve", lambda e: e.tensor_scalar(out=tA, in0=tA, scalar1=-MAGIC, scalar2=-TWO_PI, op0=ALU.add, op1=ALU.mult), ["sc1"], ["sc1"])
        S.add("dve", lambda e: e.tensor_tensor(out=tA, in0=tA, in1=ang, op=ALU.add), ["sc1"] + rk, ["sc1"])
        S.add("act", lambda e: e.activation(out=tB, in_=tA, func=AF.Sin, scale=0.5), ["sc1"], ["sc2"])
        S.add("act", lambda e: e.activation(out=tC, in_=tA, func=AF.Sin, scale=0.5, bias=halfpi[:, 0:1]), ["sc1", "halfpi"], ["sc3"])
        S.add("dve", lambda e: e.scalar_tensor_tensor(out=sin_out, in0=tB, scalar=2.0, in1=tC, op0=ALU.mult, op1=ALU.mult), ["sc2", "sc3"], wk_sin)
        S.add("act", lambda e: e.activation(out=tA, in_=tB, func=AF.Square, scale=float(np.sqrt(2.0))), ["sc2", "sc1"], ["sc1"])
        S.add("act", lambda e: e.activation(out=cos_out, in_=tA, func=AF.Identity, scale=-1.0, bias=onec[:, 0:1]), ["sc1", "onec"], wk_cos)

    PPK = lambda i: "pp%d" % i
    S.add("act", lambda e: e.activation(out=pp[:, 0, :], in_=ldt, func=AF.Exp), ["ldt"], [PPK(0)])
    S.add("dve", lambda e: e.tensor_tensor(out=pp[:, 1, :], in0=a_re2, in1=pp[:, 0, :], op=ALU.mult), ["a_re2", PPK(0)], [PPK(1)])
    S.add("dve", lambda e: e.tensor_tensor(out=pp[:, 2, :], in0=a_im2, in1=pp[:, 0, :], op=ALU.mult), ["a_im2", PPK(0)], [PPK(2)])
    kv_b = kvals[:].unsqueeze(1).to_broadcast([128, 32, 24])
    v24 = lambda t: t[:, 0:768].rearrange("p (g k) -> p g k", k=24)
    for hf in range(2):
        gh = slice(hf * 32, hf * 32 + 32)
        S.add("dve", lambda e, gh=gh: e.tensor_tensor(out=v24(sc[0]), in0=pp[:, 2, gh].unsqueeze(2).to_broadcast([128, 32, 24]), in1=kv_b, op=ALU.mult),
              [PPK(2), "kvals", "sc0"], ["sc0"])
        ere_h = Ere[:, gh, :].rearrange("p g k -> p (g k)")
        eim_h = Eim[:, gh, :].rearrange("p g k -> p (g k)")
        sincos(sc[0][:, 0:768], ere_h, eim_h, 768, ["sc0"], ["Ere"], ["Eim"])
        S.add("dve", lambda e, gh=gh: e.tensor_tensor(out=v24(sc[0]), in0=pp[:, 1, gh].unsqueeze(2).to_broadcast([128, 32, 24]), in1=kv_b, op=ALU.mult),
              [PPK(1), "kvals", "sc0"], ["sc0"])
        S.add("act", lambda e: e.activation(out=sc[0][:, 0:768], in_=sc[0][:, 0:768], func=AF.Exp), ["sc0"], ["sc0"])
        S.add("act", lambda e, gh=gh: e.copy(out=mumag[:, gh], in_=v24(sc[0])[:, :, 23]), ["sc0"], ["mumag"])
        S.add("dve", lambda e, ere_h=ere_h: e.tensor_tensor(out=ere_h, in0=ere_h, in1=sc[0][:, 0:768], op=ALU.mult), ["Ere", "sc0"], ["Ere"])
        S.add("dve", lambda e, eim_h=eim_h: e.tensor_tensor(out=eim_h, in0=eim_h, in1=sc[0][:, 0:768], op=ALU.mult), ["Eim", "sc0"], ["Eim"])
    S.add("dve", lambda e: e.tensor_scalar(out=pp[:, 6, :], in0=pp[:, 2, :], scalar1=float(8.0 / TWO_PI), scalar2=MAGIC, op0=ALU.mult, op1=ALU.add), [PPK(2)], [PPK(6)])
    S.add("dve", lambda e: e.tensor_scalar(out=pp[:, 6, :], in0=pp[:, 6, :], scalar1=-MAGIC, scalar2=-TWO_PI, op0=ALU.add, op1=ALU.mult), [PPK(6)], [PPK(6)])
    S.add("dve", lambda e: e.scalar_tensor_tensor(out=pp[:, 3, :], in0=pp[:, 2, :], scalar=8.0, in1=pp[:, 6, :], op0=ALU.mult, op1=ALU.add), [PPK(2), PPK(6)], [PPK(3)])
    e1r, e1i = Ere[:, :, 16], Eim[:, :, 16]
    S.add("dve", lambda e: e.tensor_scalar(out=pp[:, 7, :], in0=e1r, scalar1=-1.0, scalar2=0.0, op0=ALU.add, op1=ALU.add), ["Ere"], [PPK(7)])
    S.add("dve", lambda e: e.tensor_tensor(out=pp[:, 8, :], in0=a_re2, in1=a_re2, op=ALU.mult), ["a_re2"], [PPK(8)])
    S.add("dve", lambda e: e.tensor_tensor(out=pp[:, 9, :], in0=a_im2, in1=a_im2, op=ALU.mult), ["a_im2"], [PPK(9)])
    S.add("dve", lambda e: e.tensor_tensor(out=pp[:, 8, :], in0=pp[:, 8, :], in1=pp[:, 9, :], op=ALU.add), [PPK(8), PPK(9)], [PPK(8)])
    S.add("dve", lambda e: e.reciprocal(out=pp[:, 8, :], in_=pp[:, 8, :]), [PPK(8)], [PPK(8)])
    S.add("dve", lambda e: e.tensor_tensor(out=pp[:, 9, :], in0=pp[:, 7, :], in1=a_re2, op=ALU.mult), [PPK(7), "a_re2", PPK(8)], [PPK(9)])
    S.add("dve", lambda e: e.tensor_tensor(out=pp[:, 10, :], in0=e1i, in1=a_im2, op=ALU.mult), ["Eim", "a_im2"], [PPK(10)])
    S.add("dve", lambda e: e.tensor_tensor(out=pp[:, 9, :], in0=pp[:, 9, :], in1=pp[:, 10, :], op=ALU.add), [PPK(9), PPK(10)], [PPK(9)])
    S.add("dve", lambda e: e.tensor_tensor(out=pp[:, 4, :], in0=pp[:, 9, :], in1=pp[:, 8, :], op=ALU.mult), [PPK(9), PPK(8)], [PPK(4)])
    S.add("dve", lambda e: e.tensor_tensor(out=pp[:, 9, :], in0=e1i, in1=a_re2, op=ALU.mult), ["Eim", "a_re2", PPK(4)], [PPK(9)])
    S.add("dve", lambda e: e.tensor_tensor(out=pp[:, 10, :], in0=pp[:, 7, :], in1=a_im2, op=ALU.mult), [PPK(7), "a_im2", PPK(9)], [PPK(10)])
    S.add("dve", lambda e: e.tensor_tensor(out=pp[:, 9, :], in0=pp[:, 9, :], in1=pp[:, 10, :], op=ALU.subtract), [PPK(9), PPK(10)], [PPK(9)])
    S.add("dve", lambda e: e.tensor_tensor(out=pp[:, 9, :], in0=pp[:, 9, :], in1=pp[:, 8, :], op=ALU.mult), [PPK(9), PPK(8)], [PPK(9)])
    S.add("dve", lambda e: e.tensor_scalar(out=pp[:, 5, :], in0=pp[:, 9, :], scalar1=sgn[:, 0:1], scalar2=1.0, op0=ALU.mult, op1=ALU.mult), [PPK(9)] + SGN, [PPK(5)])
    wr_b = pp[:, 4, :].unsqueeze(2).to_broadcast([128, 64, 16])
    swi_b = pp[:, 5, :].unsqueeze(2).to_broadcast([128, 64, 16])
    v16 = lambda t: t[:, 0:1024].rearrange("p (g h) -> p g h", h=16)
    S.add("dve", lambda e: e.tensor_tensor(out=v16(sc[0]), in0=bself, in1=wr_b, op=ALU.mult), ["bself", PPK(4), "sc0"], ["sc0"])
    S.add("pool", lambda e: e.tensor_tensor(out=v16(sc[1]), in0=bpart, in1=swi_b, op=ALU.mult), ["bpart", PPK(5), "sc1"], ["sc1"])
    S.add("dve", lambda e: e.tensor_tensor(out=bbs, in0=v16(sc[0]), in1=v16(sc[1]), op=ALU.add), ["sc0", "sc1"], ["bbs"])
    S.add("dve", lambda e: e.tensor_tensor(out=v16(sc[0]), in0=bpart, in1=wr_b, op=ALU.mult), ["bpart", PPK(4), "sc0"], ["sc0"])
    S.add("pool", lambda e: e.tensor_tensor(out=v16(sc[1]), in0=bself, in1=swi_b, op=ALU.mult), ["bself", PPK(5), "sc1"], ["sc1"])
    S.add("dve", lambda e: e.tensor_tensor(out=bbp, in0=v16(sc[0]), in1=v16(sc[1]), op=ALU.subtract), ["sc0", "sc1"], ["bbp"])
    S.mark('params')
    S.barrier(mk_bar)
    S.mark('bar2')

    S.add("pool", lambda e: e.memset(Gs, 0.0), [], ["Gs"])

    def bc4(T, V):
        return (T.unsqueeze(3).to_broadcast([128, GB, 8, 16]), V.unsqueeze(2).to_broadcast([128, GB, 8, 16]))

    def v4(t):
        return t.rearrange("p g (s h) -> p g s h", h=16)

    def gen_mat(out_ap, T1, V1, T2, V2, op2, rk, wk, add_eng="dve", mul_eng="pool"):
        a0, a1 = bc4(T1, V1)
        b0, b1 = bc4(T2, V2)
        S.add("dve", lambda e: e.tensor_tensor(out=v4(mt[0]), in0=a0, in1=a1, op=ALU.mult), rk, ["mt0"])
        S.add(mul_eng, lambda e: e.tensor_tensor(out=v4(mt[1]), in0=b0, in1=b1, op=ALU.mult), rk, ["mt1"])
        S.add(add_eng, lambda e: e.tensor_tensor(out=out_ap, in0=mt[0], in1=mt[1], op=op2), ["mt0", "mt1"], wk)

    def ssm_frontA(gb):
        g8 = slice(gb * GB, gb * GB + GB)
        sEim, nsEre = sE[:, 0, :, :], sE[:, 1, :, :]
        EreB, EimB = Ere[:, g8, :], Eim[:, g8, :]
        UBK = ["Ub_%d" % g for g in range(GB)] + ["Ub_a%d" % g for g in range(GB)] + ["Ub_b%d" % g for g in range(GB)]
        GSK = ["Gs_%d" % g for g in range(GB)]
        S.add("dve", lambda e, g8=g8: e.tensor_scalar(out=sE[:, 0, :, :], in0=Eim[:, g8, :], scalar1=sgn[:, 0:1], scalar2=1.0, op0=ALU.mult, op1=ALU.mult), ["Eim"] + SGN, ["sE0"])
        S.add("dve", lambda e, g8=g8: e.tensor_scalar(out=sE[:, 1, :, :], in0=Ere[:, g8, :], scalar1=sgn[:, 1:2], scalar2=1.0, op0=ALU.mult, op1=ALU.mult), ["Ere"] + SGN, ["sE1"])
        sEim, nsEre = sE[:, 0, :, :], sE[:, 1, :, :]
        EreB, EimB = Ere[:, g8, :], Eim[:, g8, :]
        gen_mat(A32, EreB[:, :, 0:8], bbs[:, g8, :], sEim[:, :, 0:8], bbp[:, g8, :], ALU.add, ["Ere", "sE0", "bbs", "bbp"], ["A32"], mul_eng="dve")
        S.add("act", lambda e: e.copy(out=Ab[:, 0, :, :], in_=A32), ["A32"], ["Ab0"])
        gen_mat(Ab[:, 1, :, :], nsEre[:, :, 0:8], bbp[:, g8, :], EimB[:, :, 0:8], bbs[:, g8, :], ALU.add, ["Eim", "sE1", "bbs", "bbp"], ["Ab1"], mul_eng="dve")
        pv = PSB[4][:, 0:1024].rearrange("p (w g t) -> p w g t", w=2, g=GB)
        for w_ in range(2):
            for g in range(GB):
                S.add("pe", lambda e, w_=w_, g=g, pv=pv: e.transpose(out=pv[:, w_, g, :], in_=Ab[:, w_, g, :], identity=identb[:]), ["Ab%d" % w_, "identb"], ["PS4"])
        S.add("act", lambda e, pv=pv: e.copy(out=Msb, in_=pv), ["PS4"], ["Msb"])

    def ssm_sincos(gb):
        g8 = slice(gb * GB, gb * GB + GB)
        sEim, nsEre = sE[:, 0, :, :], sE[:, 1, :, :]
        EreB, EimB = Ere[:, g8, :], Eim[:, g8, :]
        UBK = ["Ub_%d" % g for g in range(GB)] + ["Ub_a%d" % g for g in range(GB)] + ["Ub_b%d" % g for g in range(GB)]
        GSK = ["Gs_%d" % g for g in range(GB)]
        S.add("pool", lambda e, g8=g8: e.tensor_tensor(out=sc[0][:, 0:GB * 259].rearrange("p (g k) -> p g k", k=259),
                                                       in0=pp[:, 3, g8].unsqueeze(2).to_broadcast([128, GB, 259]),
                                                       in1=krow[:, 0:259].unsqueeze(1).to_broadcast([128, GB, 259]), op=ALU.mult),
              [PPK(3), "krow", "sc0"], ["sc0"])
        sincos(sc[0][:, 0:GB * 259], cosT.rearrange("p g k -> p (g k)"), sinT.rearrange("p g k -> p (g k)"), GB * 259, ["sc0"], ["cosT"], ["sinT"])
        S.add("act", lambda e, gb=gb: e.copy(out=C258[:, gb * GB:gb * GB + GB], in_=cosT[:, :, 258]), ["cosT"], ["C258"])
        S.add("act", lambda e, gb=gb: e.copy(out=S258[:, gb * GB:gb * GB + GB], in_=sinT[:, :, 258]), ["sinT"], ["S258"])

    def ssm_frontB(gb):
        g8 = slice(gb * GB, gb * GB + GB)
        sEim, nsEre = sE[:, 0, :, :], sE[:, 1, :, :]
        EreB, EimB = Ere[:, g8, :], Eim[:, g8, :]
        UBK = ["Ub_%d" % g for g in range(GB)] + ["Ub_a%d" % g for g in range(GB)] + ["Ub_b%d" % g for g in range(GB)]
        GSK = ["Gs_%d" % g for g in range(GB)]
        gen_mat(R32, nsEre[:, :, 8:16], cself[:, g8, :], EimB[:, :, 8:16], cpart[:, g8, :], ALU.subtract, ["Eim", "sE1", "cself", "cpart"], ["R32"])
        gen_mat(Mab[:, 0, :, :], nsEre[:, :, 16:24], cself[:, g8, :], EimB[:, :, 16:24], cpart[:, g8, :], ALU.subtract, ["Eim", "sE1", "cself", "cpart"], ["Mab0"])
        gen_mat(Mab[:, 1, :, :], sEim[:, :, 16:24], cself[:, g8, :], EreB[:, :, 16:24], cpart[:, g8, :], ALU.subtract, ["Ere", "sE0", "cself", "cpart"], ["Mab1"])
        for g0 in range(0, GB, 2):
            bk = 6 + (g0 // 2) % 2
            kb = "PS%d" % bk
            pv2 = PSB[bk][:, 30:30 + 640].rearrange("p (g t) -> p g t", t=320)
            for jj in range(2):
                gg = gb * GB + g0 + jj
                S.add("pe", lambda e, jj=jj, gg=gg, pv2=pv2: e.transpose(out=pv2[:, jj, 2:130], in_=u8p2[:, gg, :, :].rearrange("p s h -> p (s h)"), identity=identb[:]),
                      U8K_P2 + ["identb"], [kb])
                S.add("pe", lambda e, jj=jj, gg=gg, pv2=pv2: e.transpose(out=pv2[:, jj, 130:258], in_=u8own[:, gg, :, :].rearrange("p s h -> p (s h)"), identity=identb[:]),
                      U8K_OWN + ["identb"], [kb])
                S.add("act", lambda e, jj=jj, gg=gg, g0=g0, pv2=pv2: e.copy(out=Ub[:, g0 + jj, 2:258], in_=pv2[:, jj, 2:258]), [kb], ["Ub_%d" % (g0 + jj)])
                S.add("act", lambda e, jj=jj, gg=gg, g0=g0: e.copy(out=Ub[:, g0 + jj, 0:2], in_=Um[:, gg, 0:2]), UMK, ["Ub_a%d" % (g0 + jj)])
                S.add("act", lambda e, jj=jj, gg=gg, g0=g0: e.copy(out=Ub[:, g0 + jj, 258:274], in_=Um[:, gg, 2:18]), UMK, ["Ub_b%d" % (g0 + jj)])
        UBK = ["Ub_%d" % g for g in range(GB)] + ["Ub_a%d" % g for g in range(GB)] + ["Ub_b%d" % g for g in range(GB)]
        dma("sp", st_s[0:NS], din["st_self"][:, g8, :], [], ["st_s"], "st_s")
        dma("sp", st_p[0:NS], din["st_part"][:, g8, :], [], ["st_p"], "st_p")
        for g in range(GB):
            S.add("pe", lambda e, g=g: e.transpose(out=PS[2][:, g * NS:(g + 1) * NS], in_=st_s[0:NS, g, :], identity=identf[0:NS, 0:NS]), ["st_s", "identf"], ["PS2"])
            S.add("pe", lambda e, g=g: e.transpose(out=PS[2][:, 128 + g * NS:128 + (g + 1) * NS], in_=st_p[0:NS, g, :], identity=identf[0:NS, 0:NS]), ["st_p", "identf"], ["PS2"])
        h0v = PS[2][:, 0:GB * NS].rearrange("p (g b) -> p g b", b=NS)
        h0s = PS[2][:, 128:128 + GB * NS].rearrange("p (g b) -> p g b", b=NS)
        S.add("dve", lambda e, g8=g8: e.tensor_tensor(out=hs1, in0=h0v, in1=Ere[:, g8, 16:17].to_broadcast([128, GB, NS]), op=ALU.mult), ["PS2", "Ere"], ["hs1"])
        S.add("dve", lambda e: e.tensor_tensor(out=hs2, in0=h0s, in1=sE[:, 0, :, 16:17].to_broadcast([128, GB, NS]), op=ALU.mult), ["PS2", "sE0"], ["hs2"])
        S.add("pool", lambda e: e.tensor_tensor(out=hs1, in0=hs1, in1=hs2, op=ALU.add), ["hs1", "hs2"], ["hs1"])
        S.add("dve", lambda e, g8=g8: e.tensor_tensor(out=hn1, in0=h0v, in1=Ere[:, g8, 8:9].to_broadcast([128, GB, NS]), op=ALU.mult), ["PS2", "Ere"], ["hn1k"])
        S.add("dve", lambda e: e.tensor_tensor(out=hn2, in0=h0s, in1=sE[:, 0, :, 8:9].to_broadcast([128, GB, NS]), op=ALU.mult), ["PS2", "sE0"], ["hn2k"])
        S.add("pool", lambda e: e.tensor_tensor(out=Hins, in0=hn1, in1=hn2, op=ALU.add), ["hn1k", "hn2k"], ["Hinsk"])

    def ssm_loop(gb):
        g8 = slice(gb * GB, gb * GB + GB)
        sEim, nsEre = sE[:, 0, :, :], sE[:, 1, :, :]
        EreB, EimB = Ere[:, g8, :], Eim[:, g8, :]
        UBK = ["Ub_%d" % g for g in range(GB)] + ["Ub_a%d" % g for g in range(GB)] + ["Ub_b%d" % g for g in range(GB)]
        GSK = ["Gs_%d" % g for g in range(GB)]
        for g in range(GB):
            gg = gb * GB + g
            pa, pb_ = (0, 1) if g % 2 == 0 else (6, 7)
            ka, kb_ = "PS%d" % pa, "PS%d" % pb_
            S.add("pe", lambda e, g=g, pa=pa: e.matmul(PS[pa][:, 0:NU], lhsT=Msb[:, 0, g, :], rhs=Ub[:, g, :], start=True, stop=True), ["Msb"] + UBK, [ka])
            S.add("pe", lambda e, g=g, pb_=pb_: e.matmul(PS[pb_][:, 0:NCH], lhsT=Msb[:, 1, g, :], rhs=Ub[:, g, 0:NCH], start=True, stop=True), ["Msb"] + UBK, [kb_])
            S.add("dve", lambda e, g=g, pa=pa: e.tensor_tensor(out=Xt[:, g, :], in0=PS[pa][:, 0:NCH], in1=cosT[:, g, 1:259], op=ALU.mult), [ka, "cosT"], ["Xt_%d" % g])
            S.add("act", lambda e, g=g, pa=pa: e.copy(out=Xsm[:, g, :], in_=PS[pa][:, NCH:NU]), [ka], ["Xsm_%d" % g])
            x2 = Xt2[g % 2]
            S.add("dve", lambda e, g=g, pb_=pb_, x2=x2: e.tensor_tensor(out=x2, in0=PS[pb_][:, 0:NCH], in1=sinT[:, g, 1:259], op=ALU.mult), [kb_, "sinT"], ["Xt2_%d" % (g % 2)])
            S.add("dve", lambda e, g=g, x2=x2: e.tensor_tensor(out=Xt[:, g, :], in0=Xt[:, g, :], in1=x2, op=ALU.add), ["Xt_%d" % g, "Xt2_%d" % (g % 2)], ["Xt_%d" % g])
            S.add("dve", lambda e, g=g, gg=gg: e.tensor_tensor_scan(out=Gs[:, g, 1:259], data0=mumag[:, gg:gg + 1].to_broadcast([128, NCH]), data1=Xt[:, g, :],
                                                                    initial=0.0, op0=ALU.mult, op1=ALU.add), ["Xt_%d" % g, "mumag", "Gs"], ["Gs_%d" % g])
            S.add("act", lambda e, g=g, gg=gg: e.copy(out=G258[:, gg:gg + 1], in_=Gs[:, g, 258:259]), ["Gs_%d" % g], ["G258"])
        S.add("dve", lambda e: e.tensor_tensor(out=hs1, in0=hs1, in1=Xsm, op=ALU.add), ["hs1"] + ["Xsm_%d" % g for g in range(GB)], ["hs1"])

    def ssm_y1(gb):
        g8 = slice(gb * GB, gb * GB + GB)
        sEim, nsEre = sE[:, 0, :, :], sE[:, 1, :, :]
        EreB, EimB = Ere[:, g8, :], Eim[:, g8, :]
        UBK = ["Ub_%d" % g for g in range(GB)] + ["Ub_a%d" % g for g in range(GB)] + ["Ub_b%d" % g for g in range(GB)]
        GSK = ["Gs_%d" % g for g in range(GB)]
        GSK = ["Gs_%d" % g for g in range(GB)]
        for g in range(GB):
            S.add("pe", lambda e, g=g: e.matmul(PS[5][:, g * 128:(g + 1) * 128], lhsT=A32[:, g, :], rhs=R32[:, g, :], start=True, stop=True), ["A32", "R32"], ["PS5"])
        S.add("dve", lambda e: e.tensor_tensor(out=Mib, in0=PS[5][:, 0:GB * 128].rearrange("p (g q) -> p g q", q=128),
                                               in1=blockmask[:].unsqueeze(1).to_broadcast([128, GB, 128]), op=ALU.mult), ["PS5", "blockmask"], ["Mib"])

    def ssm_y2a(gb):
        g8 = slice(gb * GB, gb * GB + GB)
        sEim, nsEre = sE[:, 0, :, :], sE[:, 1, :, :]
        EreB, EimB = Ere[:, g8, :], Eim[:, g8, :]
        UBK = ["Ub_%d" % g for g in range(GB)] + ["Ub_a%d" % g for g in range(GB)] + ["Ub_b%d" % g for g in range(GB)]
        GSK = ["Gs_%d" % g for g in range(GB)]
        S.add("dve", lambda e: e.tensor_tensor(out=Pcs[:, 0, :, :], in0=cosT[:, :, 130:258], in1=Gs[:, :, 130:258], op=ALU.mult), ["cosT"] + GSK, ["Pc0"])
        S.add("dve", lambda e: e.tensor_tensor(out=Pcs[:, 1, :, :], in0=sinT[:, :, 130:258], in1=Gs[:, :, 130:258], op=ALU.mult), ["sinT"] + GSK, ["Pc1"])
        for g in range(GB):
            osl = PS[5][:, g * 128:(g + 1) * 128]
            S.add("pe", lambda e, g=g, osl=osl: e.matmul(osl, lhsT=Mib[:, g, :], rhs=Ub[:, g, 130:258], start=True, stop=False), ["Mib"] + UBK, ["PS5"])
            S.add("pe", lambda e, g=g, osl=osl: e.matmul(osl, lhsT=Mab[:, 0, g, :], rhs=Pcs[:, 0, g, :], start=False, stop=False), ["Mab0", "Pc0"], ["PS5"])
            S.add("pe", lambda e, g=g, osl=osl: e.matmul(osl, lhsT=Mab[:, 1, g, :], rhs=Pcs[:, 1, g, :], start=False, stop=True), ["Mab1", "Pc1"], ["PS5"])
            oss = PS[4][:, g * NS:(g + 1) * NS]
            S.add("pe", lambda e, g=g, oss=oss: e.matmul(oss, lhsT=Mib[:, g, :], rhs=Ub[:, g, 258:274], start=True, stop=False), ["Mib"] + UBK, ["PS4"])
            S.add("pe", lambda e, g=g, oss=oss: e.matmul(oss, lhsT=Mab[:, 0, g, :], rhs=Hins[:, g, :], start=False, stop=True), ["Mab0", "Hinsk"], ["PS4"])


    def ssm_y2b(gb):
        g8 = slice(gb * GB, gb * GB + GB)
        sEim, nsEre = sE[:, 0, :, :], sE[:, 1, :, :]
        EreB, EimB = Ere[:, g8, :], Eim[:, g8, :]
        UBK = ["Ub_%d" % g for g in range(GB)] + ["Ub_a%d" % g for g in range(GB)] + ["Ub_b%d" % g for g in range(GB)]
        GSK = ["Gs_%d" % g for g in range(GB)]
        def gelu_chain(ps_view, u_view, ya, yb, yout, W, kin, kout, tag):
            kk = "ypo" if tag == "o" else "yps"
            dsk_b = dsk[:, g8].unsqueeze(2).to_broadcast([128, GB, W])
            S.add("dve", lambda e: e.tensor_tensor(out=ya, in0=u_view, in1=dsk_b, op=ALU.mult), UBK + ["dsk", kk], [kk])
            S.add("dve", lambda e: e.tensor_tensor(out=ya, in0=ya, in1=ps_view, op=ALU.add), [kk] + kin, [kk])
            S.add("act", lambda e: e.activation(out=yb, in_=ya, func=AF.Square), [kk], [kk])
            S.add("dve", lambda e: e.scalar_tensor_tensor(out=yb, in0=yb, scalar=0.044715, in1=ya, op0=ALU.mult, op1=ALU.mult), [kk], [kk])
            S.add("dve", lambda e: e.tensor_tensor(out=yb, in0=yb, in1=ya, op=ALU.add), [kk], [kk])
            S.add("act", lambda e: e.activation(out=yb, in_=yb, func=AF.Tanh, scale=0.7978845608028654), [kk], [kk])
            S.add("dve", lambda e: e.scalar_tensor_tensor(out=yout, in0=yb, scalar=1.0, in1=ya, op0=ALU.add, op1=ALU.mult), [kk] + kout, kout)
        gelu_chain(PS[5][:, 0:GB * 128].rearrange("p (g q) -> p g q", q=128), Ub[:, :, 130:258], yA, yB, ygb, 128, ["PS5"], ["ygbk"], "o")
        gelu_chain(PS[4][:, 0:GB * NS].rearrange("p (g b) -> p g b", b=NS), Ub[:, :, 258:274], ysA, ysB, ygs, NS, ["PS4"], ["ygsk"], "s")
        goff = (gb % 2) * GB
        pvy = PSB[6][:, 0:GB * 128].rearrange("p (g q) -> p g q", q=128)
        for g in range(GB):
            S.add("pe", lambda e, g=g: e.transpose(out=pvy[:, g, :], in_=ygb[:, g, :], identity=identb[:]), ["ygbk", "identb"], ["PS6"])
        S.add("act", lambda e, goff=goff: e.activation(out=y8.rearrange("p t (g h) -> p g t h", h=16)[:, goff:goff + GB, :, :],
                                                       in_=pvy.rearrange("p g (t h) -> p g t h", h=16), func=AF.Identity, scale=0.5), ["PS6"], ["y8_%d" % (gb % 2)])
        pvs = PSB[7][:, 0:GB * 128].rearrange("p (g q) -> p g q", q=128)
        for g in range(GB):
            S.add("pe", lambda e, g=g: e.transpose(out=pvs[0:NS, g, :], in_=ygs[:, g, :], identity=identb[:]), ["ygsk", "identb"], ["PS7"])
        S.add("act", lambda e, goff=goff: e.activation(out=y8s[0:NS, goff * 16:(goff + GB) * 16].rearrange("p (g h) -> p g h", h=16), in_=pvs[0:NS, :, 112:128],
                                                       func=AF.Identity, scale=0.5), ["PS7"], ["y8s_%d" % (gb % 2)])
        if gb % 2 == 1:
            fc = gb // 2
            pvt = PSB[0][:, 0:1024].rearrange("p (t c) -> p t c", c=128)
            for t in range(8):
                S.add("pe", lambda e, t=t: e.transpose(out=pvt[:, t, :], in_=y8[:, t, :], identity=identb[:]), ["y8_0", "y8_1", "identb"], ["PS0"])
            S.add("act", lambda e, fc=fc: e.copy(out=ST[:, fc, 0:NT].rearrange("p (c t) -> p t c", t=8), in_=pvt), ["PS0"], ["ST_%d" % fc])
            S.add("pe", lambda e: e.transpose(out=PSB[1][:, 0:NS], in_=y8s[0:NS, :], identity=identb[0:NS, 0:NS]), ["y8s_0", "y8s_1", "identb"], ["PS1"])
            S.add("dve", lambda e, fc=fc: e.tensor_copy(out=ST[:, fc, NT:NTS], in_=PSB[1][:, 0:NS]), ["PS1"], ["STs_%d" % fc])
        for g in range(GB):
            S.add("pe", lambda e, g=g: e.transpose(out=PS[3][0:NS, g * 128:(g + 1) * 128], in_=hs1[:, g, :], identity=identf[:]), ["hs1", "identf"], ["PS3"])
        S.add("act", lambda e: e.copy(out=hso[0:NS], in_=PS[3][0:NS, 0:GB * 128].rearrange("p (g q) -> p g q", q=128)), ["PS3"], ["hso"])
        dma("sp", dout["sssm"][:, g8, :], hso[0:NS], ["hso"], [], "hso")

    NBATCH = 64 // GB
    ssm_sincos(0)
    for gb in range(NBATCH):
        ssm_frontA(gb)
        ssm_frontB(gb)
        ssm_loop(gb)
        ssm_y1(gb)
        ssm_y2a(gb)
        if gb + 1 < NBATCH:
            ssm_sincos(gb + 1)
        ssm_y2b(gb)
        S.mark('batch%d' % gb)

    S.add("pe", lambda e: e.matmul(PS[2][:, 0:64], lhsT=swapm[:], rhs=G258[:], start=True, stop=True), ["swapm", "G258"], ["PS2"])
    S.add("dve", lambda e: e.tensor_tensor(out=pp[:, 11, :], in0=C258[:], in1=G258[:], op=ALU.mult), ["C258", "G258"], [PPK(11)])
    S.add("dve", lambda e: e.tensor_scalar(out=pp[:, 12, :], in0=S258[:], scalar1=sgn[:, 0:1], scalar2=1.0, op0=ALU.mult, op1=ALU.mult), ["S258"] + SGN, [PPK(12)])
    S.add("dve", lambda e: e.tensor_tensor(out=pp[:, 12, :], in0=pp[:, 12, :], in1=PS[2][:, 0:64], op=ALU.mult), [PPK(12), "PS2"], [PPK(12)])
    S.add("dve", lambda e: e.tensor_tensor(out=pp[:, 11, :], in0=pp[:, 11, :], in1=pp[:, 12, :], op=ALU.add), [PPK(11), PPK(12)], [PPK(11)])
    S.add("pe", lambda e: e.transpose(out=PS[3][0:64, 0:128], in_=pp[:, 11, :], identity=identf[:]), [PPK(11), "identf"], ["PS3"])
    S.add("act", lambda e: e.copy(out=hfo[0:64, :], in_=PS[3][0:64, 0:128]), ["PS3"], ["hfo"])
    dma("sp", dout["pssm"], hfo[0:64, :], ["hfo"], [], "hfo")

    S.mark('p2')
    S.barrier(mk_bar)
    NKC = 1168
    qT = carve(0, 8 * NTS, BF16).rearrange("p (h t) -> p h t", t=NTS)
    kT = carve(17 * KB, 4 * NKC, BF16).rearrange("p (g t) -> p g t", t=NKC)
    PT = [[carve(27 * KB + (3 * j + i) * KB, 512, BF16) for i in range(3)] for j in range(2)]
    AT = carve(51 * KB, 8 * NTS, BF16).rearrange("p (c t) -> p c t", t=NTS)
    Vaug = carve(100 * KB, 10 * 4 * 128, BF16).rearrange("p (b g q) -> p b g q", b=10, q=128)
    o3 = 100 * KB + 10 * KB
    tmpq = [carve(o3 + 4 * KB * i, 1024, F32) for i in range(3)]; o3 += 12 * KB
    sqj = carve(o3 - 8 * KB, D, F32)
    xbuf_off3 = o3
    xbuf = [carve(o3, D, F32)] * 2; o3 += 8 * KB
    xnb = [carve(o3, D, BF16)] * 2; o3 += 4 * KB
    xnTt = [carve(o3, ND * 128, BF16).rearrange("p (c t) -> p c t", c=ND)] * 2; o3 += 4 * KB
    kf = [carve(o3 + i * KB, 256, F32) for i in range(2)]; o3 += 2 * KB
    vf = [carve(o3 + i * KB, 256, F32) for i in range(2)]; o3 += 2 * KB
    kb16 = carve(o3, 256, BF16); o3 += 512
    qf = carve(o3, 512, F32); o3 += 2 * KB
    qb16 = carve(o3, 512, BF16); o3 += KB
    otmp = [carve(o3 + i * KB, 512, BF16) for i in range(2)]; o3 += 2 * KB
    dtmp2 = [carve(xbuf_off3 + 2 * KB * i, 512, F32) for i in range(2)]
    assert o3 <= ARENA, o3
    gain_bc = carve(0, D, F32)
    cnt["x"] = 0
    dma("sp", gain_bc, din["norm_gain"].partition_broadcast(128), [], ["gain_bc"], "gain_bc")
    dma("pool", masks[:, 0, :], din["maskc"], [], ["masks0"], "masks0")
    dma("pool", masks[:, 1, :], din["maskp"], [], ["masks1"], "masks1")
    dma("pool", masks[:, 2, :], din["maskp0"], [], ["masks2"], "masks2")
    dma("sp", expsink[:], din["sinks"].partition_broadcast(128), [], ["expsink"], "expsink")
    S.add("act", lambda e: e.activation(out=expsink[:], in_=expsink[:], func=AF.Exp), ["expsink"], ["expsink"])
    S.add("pool", lambda e: e.memset(onesb[:], 1.0), [], ["onesb"])
    S.add("pool", lambda e: e.memset(Vaug[:, :, :, 64:128], 1.0), [], ["Vones"])

    load_w(0, din["w_in"], 1024, 512, ND)

    def kv_proj(lhs_fn, lhs_keys, n):
        b = cnt["kv"] % 2
        cnt["kv"] += 1
        kb = proj_tok(lhs_fn, lhs_keys, n, 0, 512, b)
        return b, kb

    def kv_post(b, kb, n, ti, kcol0, vblk, out_k=None, out_v=None):
        kfj, vfj = kf[b], vf[b]
        S.add("act", lambda e: e.copy(out=vfj[0:n, :], in_=PS[b][0:n, 256:512]), [kb], ["vf%d" % b])
        headnorm_rope([(PS[b][0:n, 0:256], 256)], [kb], n, 4, ti, 64, 1.0, kfj[0:n, :], ["kf%d" % b])
        if out_k is not None:
            dma("sp", out_k, kfj[0:n, :], ["kf%d" % b], ["swk_new"], "kf%d" % b)
        if out_v is not None:
            dma("sp", out_v, vfj[0:n, :], ["vf%d" % b], ["swv_new"], "vf%d" % b)
        if vblk is not None:
            S.add("pool", lambda e: e.tensor_copy(out=Vaug[0:n, vblk, :, 0:64], in_=vfj[0:n, :].rearrange("p (g d) -> p g d", d=64)),
                  ["vf%d" % b, "Vones"], ["Vaug_%d" % vblk])
            S.add("act", lambda e: e.copy(out=kb16[0:n, :], in_=kfj[0:n, :]), ["kf%d" % b], ["kb16"])
            pvk = PSB[6][:, 0:512].rearrange("p (g t) -> p g t", t=128)
            for g in range(4):
                S.add("pe", lambda e, g=g: e.transpose(out=pvk[0:64, g, 0:n], in_=kb16[0:n, g * 64:(g + 1) * 64], identity=identb[0:n, 0:n]),
                      ["kb16", "identb"], ["PS6"])
            S.add("act", lambda e: e.copy(out=kT[0:64, :, kcol0:kcol0 + n], in_=pvk[0:64, :, 0:n]), ["PS6"], ["kT_%d" % vblk])

    def kv_tile(lhs_fn, lhs_keys, n, ti, kcol0, vblk, out_k=None, out_v=None):
        b, kb = kv_proj(lhs_fn, lhs_keys, n)
        kv_post(b, kb, n, ti, kcol0, vblk, out_k, out_v)

    for (r0, n, ti, kc0, vb, ok, ov) in ((0, 128, 8, 1024, 8, None, None), (128, 16, 9, 1152, 9, dout["pmk"], dout["pmv"])):
        tt = xnTt[0]
        norm_tile(din["x_kvx"][r0:r0 + n, :], n, lambda dc0, tt=tt, n=n: tt[:, dc0:dc0 + 8, 0:n], ["xnTt0"])
        kv_tile(lambda dc, tt=tt, n=n: tt[:, dc, 0:n], ["xnTt0"], n, ti, kc0, vb, ok, ov)
    S.add("pool", lambda e: e.memset(qT, 0.0), ["gain_bc"], ["qT", "gain_bc"])
    dma("sp", dout["swk"][:, 0:127, :], din["cwk"][:, 1:128, :], [], [], "swk")
    dma("sp", dout["swv"][:, 0:127, :], din["cwv"][:, 1:128, :], [], [], "swv")
    kspecs = [(lambda dc, i=i: xnT[:, dc, i * 128:(i + 1) * 128], ["xnT_%d" % i], 128, i, i * 128, i,
               dout["pwk"] if i == 7 else None, dout["pwv"] if i == 7 else None) for i in range(8)]
    kspecs.append((lambda dc: xnT[:, dc, NT:NTS], ["xnT_s"], NS, 10, 0, None, dout["swk"][:, 127, :], dout["swv"][:, 127, :]))
    pendk = kv_proj(kspecs[0][0], kspecs[0][1], kspecs[0][2])
    for ki, (lf, lk, n_, ti_, kc_, vb_, ok_, ov_) in enumerate(kspecs):
        curk = pendk
        if ki + 1 < len(kspecs):
            pendk = kv_proj(kspecs[ki + 1][0], kspecs[ki + 1][1], kspecs[ki + 1][2])
        kv_post(curk[0], curk[1], n_, ti_, kc_, vb_, ok_, ov_)
    KTK = ["kT_%d" % i for i in range(10)]
    VK = ["Vaug_%d" % i for i in range(10)] + ["Vones"]

    def q_proj(lhs_fn, lhs_keys, n, slot):
        b = cnt["kv"] % 2
        cnt["kv"] += 1
        kb = proj_tok(lhs_fn, lhs_keys, n, slot, 512, b)
        return b, kb

    def q_post(b, kb, n, ti, tokc0, smp_hoff):
        headnorm_rope([(PS[b][0:n, 0:512], 512)], [kb], n, 8, ti, 0, 0.125, qf[0:n, :], ["qf"])
        if smp_hoff is not None:
            S.add("act", lambda e: e.copy(out=qs16[0:n, smp_hoff * 64:(smp_hoff + 8) * 64], in_=qf[0:n, :]), ["qf"], ["qs16_%d" % smp_hoff])
            return
        S.add("act", lambda e: e.copy(out=qb16[0:n, :], in_=qf[0:n, :]), ["qf"], ["qb16"])
        pvq = PSB[7][:, 0:1024].rearrange("p (h t) -> p h t", t=128)
        for h in range(8):
            S.add("pe", lambda e, h=h: e.transpose(out=pvq[0:64, h, 0:n], in_=qb16[0:n, h * 64:(h + 1) * 64], identity=identb[0:n, 0:n]),
                  ["qb16", "identb"], ["PS7"])
        S.add("act", lambda e: e.copy(out=qT[0:64, :, tokc0:tokc0 + n], in_=pvq[0:64, :, 0:n]), ["PS7", "qT"], ["qT_%d" % (tokc0 // 128)])

    def att_setup(nb, g, hl0, pb):
        d_ = dict(nb=nb, g=g, hl0=hl0, pb=pb)
        d_["banks"] = (2, 3, 4, 5) if pb == 0 else (6, 7, 0, 1)
        d_["prev_cols"] = slice(1024, 1152) if nb == 0 else slice((nb - 1) * 128, nb * 128)
        d_["prev_blk"] = 8 if nb == 0 else nb - 1
        d_["mprev"] = 2 if nb == 0 else 1
        return d_

    def att_A(d_):
        nb, g, hl0, pb = d_["nb"], d_["g"], d_["hl0"], d_["pb"]
        PSm, PSp, PSc, PSo = d_["banks"]
        PTm, PTp, PTc = PT[pb]
        qk = ["qT_%d" % nb]
        cur_cols = slice(nb * 128, (nb + 1) * 128)
        prev_cols, mprev = d_["prev_cols"], d_["mprev"]
        for r in range(4):
            rs = slice(r * 128, (r + 1) * 128)
            qa = qT[0:64, hl0 + r, nb * 128:(nb + 1) * 128]
            S.add("pe", lambda e, rs=rs, qa=qa: e.matmul(PS[PSm][0:16, rs], lhsT=kT[0:64, g, 1152:1168], rhs=qa, start=True, stop=True), KTK + qk, ["PS%d" % PSm])
            S.add("pe", lambda e, rs=rs: e.matmul(PS[PSp][:, rs], lhsT=identb[:], rhs=masks[:, mprev, :], start=True, stop=False), ["identb", "masks1", "masks2"], ["PS%d" % PSp])
            S.add("pe", lambda e, rs=rs, qa=qa: e.matmul(PS[PSp][:, rs], lhsT=kT[0:64, g, prev_cols], rhs=qa, start=False, stop=True), KTK + qk, ["PS%d" % PSp])
            S.add("pe", lambda e, rs=rs: e.matmul(PS[PSc][:, rs], lhsT=identb[:], rhs=masks[:, 0, :], start=True, stop=False), ["identb", "masks0"], ["PS%d" % PSc])
            S.add("pe", lambda e, rs=rs, qa=qa: e.matmul(PS[PSc][:, rs], lhsT=kT[0:64, g, cur_cols], rhs=qa, start=False, stop=True), KTK + qk, ["PS%d" % PSc])
        km, kp, kc = "PTm%d" % pb, "PTp%d" % pb, "PTc%d" % pb
        S.add("act", lambda e: e.activation(out=PTm[0:16, :], in_=PS[PSm][0:16, :], func=AF.Exp), ["PS%d" % PSm], [km])
        S.add("act", lambda e: e.activation(out=PTp, in_=PS[PSp][:, :], func=AF.Exp), ["PS%d" % PSp], [kp])
        S.add("act", lambda e: e.activation(out=PTc, in_=PS[PSc][:, :], func=AF.Exp), ["PS%d" % PSc], [kc])

    def att_B(d_):
        nb, g, pb = d_["nb"], d_["g"], d_["pb"]
        PSm, PSp, PSc, PSo = d_["banks"]
        PTm, PTp, PTc = PT[pb]
        km, kp, kc = "PTm%d" % pb, "PTp%d" % pb, "PTc%d" % pb
        prev_blk = d_["prev_blk"]
        for r in range(4):
            rs = slice(r * 128, (r + 1) * 128)
            S.add("pe", lambda e, rs=rs: e.matmul(PS[PSo][:, rs], lhsT=Vaug[0:16, 9, g, :], rhs=PTm[0:16, rs], start=True, stop=False), VK + [km], ["PS%d" % PSo])
            S.add("pe", lambda e, rs=rs: e.matmul(PS[PSo][:, rs], lhsT=Vaug[:, prev_blk, g, :], rhs=PTp[:, rs], start=False, stop=False), VK + [kp], ["PS%d" % PSo])
            S.add("pe", lambda e, rs=rs: e.matmul(PS[PSo][:, rs], lhsT=Vaug[:, nb, g, :], rhs=PTc[:, rs], start=False, stop=True), VK + [kc], ["PS%d" % PSo])

    def att_C(d_):
        nb, g, pb = d_["nb"], d_["g"], d_["pb"]
        PSo = d_["banks"][3]
        dtm = dtmp2[pb]
        dv = dtm[64:128, :].rearrange("p (r q) -> p r q", q=128)
        kd = "dtmp%d" % pb
        S.add("dve", lambda e: e.tensor_tensor(out=dv, in0=PS[PSo][64:128, :].rearrange("p (r q) -> p r q", q=128),
                                               in1=expsink[64:128, 4 * g:4 * g + 4].unsqueeze(2).to_broadcast([64, 4, 128]), op=ALU.add), ["PS%d" % PSo, "expsink"], [kd, "xbuf0"])
        S.add("act", lambda e: e.activation(out=dtm[64:128, :], in_=dtm[64:128, :], func=AF.Ln), [kd], [kd])
        S.add("act", lambda e: e.activation(out=dtm[64:128, :], in_=dtm[64:128, :], func=AF.Exp, scale=-1.0), [kd], [kd])
        ot = otmp[pb]
        S.add("dve", lambda e: e.tensor_tensor(out=ot[0:64, :], in0=PS[PSo][0:64, :], in1=dtm[64:128, :], op=ALU.mult), ["PS%d" % PSo, kd], ["otmp%d" % pb])
        o3v = ot[0:64, :].rearrange("p (r q) -> p r q", q=128)
        dma("sp", AT[0:64, 2 * g:2 * g + 2, nb * 128:(nb + 1) * 128], o3v[:, 0:4:2, :], ["otmp%d" % pb], ["AT_%d_%d" % (g, nb)], "otmp%d" % pb)
        dma("sp", AT[64:128, 2 * g:2 * g + 2, nb * 128:(nb + 1) * 128], o3v[:, 1:4:2, :], ["otmp%d" % pb], ["ATb_%d_%d" % (g, nb)], "otmp%d" % pb)

    acnt = 0
    for qblk in range(2):
        load_w(1, din["w_in"], qblk * 512, 512, ND)
        qspecs = [(lambda dc, i=i: xnT[:, dc, i * 128:(i + 1) * 128], ["xnT_%d" % i], 128, i, i * 128, None) for i in range(8)]
        qspecs.append((lambda dc: xnT[:, dc, NT:NTS], ["xnT_s"], NS, 10, 0, qblk * 8))
        pend = q_proj(qspecs[0][0], qspecs[0][1], qspecs[0][2], 1)
        for qi, (lf, lk, n_, ti_, tc_, sh_) in enumerate(qspecs):
            cur = pend
            if qi + 1 < len(qspecs):
                pend = q_proj(qspecs[qi + 1][0], qspecs[qi + 1][1], qspecs[qi + 1][2], 1)
            q_post(cur[0], cur[1], n_, ti_, tc_, sh_)
        blocks = []
        for nb in range(8):
            for gl in range(2):
                blocks.append(att_setup(nb, qblk * 2 + gl, gl * 4, acnt % 2))
                acnt += 1
        att_A(blocks[0])
        for bi in range(len(blocks)):
            if bi + 1 < len(blocks):
                att_A(blocks[bi + 1])
            att_B(blocks[bi])
            att_C(blocks[bi])
    ATK = ["AT_%d_%d" % (g, nb) for g in range(4) for nb in range(8)] + ["ATb_%d_%d" % (g, nb) for g in range(4) for nb in range(8)]
    S.mark('p3a')

    S.barrier(mk_bar)
    Kc = carve(0, NS * 256, BF16).rearrange("p (b e) -> p b e", e=256)
    Vc = carve(8 * KB, NS * 256, BF16).rearrange("p (b e) -> p b e", e=256)
    Kmc = carve(16 * KB, NS * 256, BF16).rearrange("p (b e) -> p b e", e=256)
    Vmc = carve(24 * KB, NS * 256, BF16).rearrange("p (b e) -> p b e", e=256)
    KsT = carve(100 * KB, NS * 4 * 128, BF16).rearrange("p (b g t) -> p b g t", b=NS, t=128)
    KsT2 = carve(116 * KB, NS * 4 * 32, BF16).rearrange("p (b g t) -> p b g t", b=NS, t=32)
    o4 = 120 * KB
    qsT = carve(o4, 16 * NS, BF16).rearrange("p (h b) -> p h b", b=NS); o4 += 512
    PTs = carve(o4, 256, BF16); o4 += 512
    PTs2 = carve(o4, 256, BF16); o4 += 512
    dts = carve(o4, 256, F32); o4 += KB
    osb = carve(o4, 256, BF16).rearrange("p (h b) -> p h b", b=NS); o4 += 512
    load_w(0, din["w_in"], 1536, 512, ND)
    load_w(1, din["w_in"], 1536 + 512, 512, ND)
    dma("pool", Kc, din["cwk"].rearrange("b j e -> j b e"), [], ["Kc"], "Kc")
    dma("pool", Vc, din["cwv"].rearrange("b j e -> j b e"), [], ["Vc"], "Vc")
    dma("pool", Kmc[0:16], din["cmk"].rearrange("b j e -> j b e"), [], ["Kmc_a"], "Kmc")
    dma("pool", Vmc[0:16], din["cmv"].rearrange("b j e -> j b e"), [], ["Vmc_a"], "Vmc")
    dma("pool", Kmc[16:17], dout["swk"][:, 127:128, :].rearrange("b o e -> o b e"), ["swk_new"], ["Kmc_b"], "Kmc")
    dma("pool", Vmc[16:17], dout["swv"][:, 127:128, :].rearrange("b o e -> o b e"), ["swv_new"], ["Vmc_b"], "Vmc")
    pvq2 = PSB[7][:, 0:256].rearrange("p (h b) -> p h b", b=NS)
    for h in range(16):
        S.add("pe", lambda e, h=h: e.transpose(out=pvq2[0:64, h, :], in_=qs16[0:NS, h * 64:(h + 1) * 64], identity=identb[0:NS, 0:NS]),
              ["qs16_0", "qs16_8", "identb"], ["PS7"])
    S.add("act", lambda e: e.copy(out=qsT[0:64], in_=pvq2[0:64]), ["PS7"], ["qsT"])
    for b0 in range(0, NS, 2):
        bk = 4 + (b0 // 2) % 2
        pvk2 = PSB[bk][:, 0:1024].rearrange("p (b g t) -> p b g t", b=2, t=128)
        for bb in range(2):
            for g in range(4):
                S.add("pe", lambda e, bb=bb, g=g, b0=b0, pvk2=pvk2: e.transpose(out=pvk2[0:64, bb, g, :], in_=Kc[:, b0 + bb, g * 64:(g + 1) * 64], identity=identb[:]),
                      ["Kc", "identb"], ["PS%d" % bk])
        S.add("act" if (b0 // 2) % 2 == 0 else "dve",
              (lambda e, b0=b0, pvk2=pvk2: e.copy(out=KsT[0:64, b0:b0 + 2], in_=pvk2[0:64])) if (b0 // 2) % 2 == 0 else
              (lambda e, b0=b0, pvk2=pvk2: e.tensor_copy(out=KsT[0:64, b0:b0 + 2], in_=pvk2[0:64])), ["PS%d" % bk], ["KsT_%d" % b0])
    KSK = ["KsT_%d" % b0 for b0 in range(0, NS, 2)]
    for b0 in range(0, NS, 8):
        bk = 6 + (b0 // 8) % 2
        pvk3 = PSB[bk][:, 0:1024].rearrange("p (b g t) -> p b g t", b=8, t=32)
        for bb in range(8):
            for g in range(4):
                S.add("pe", lambda e, bb=bb, g=g, b0=b0, pvk3=pvk3: e.transpose(out=pvk3[0:64, bb, g, 0:17], in_=Kmc[0:17, b0 + bb, g * 64:(g + 1) * 64],
                                                                              identity=identb[0:17, 0:17]), ["Kmc_a", "Kmc_b", "identb"], ["PS%d" % bk])
        S.add("act", lambda e, b0=b0, pvk3=pvk3: e.copy(out=KsT2[0:64, b0:b0 + 8, :, 0:17], in_=pvk3[0:64, :, :, 0:17]), ["PS%d" % bk], ["KsT2_%d" % b0])
    KS2K = ["KsT2_0", "KsT2_8"]
    for b in range(NS):
        for g in range(4):
            cs = slice(b * 16 + g * 4, b * 16 + g * 4 + 4)
            S.add("pe", lambda e, b=b, g=g, cs=cs: e.matmul(PS[0][:, cs], lhsT=KsT[0:64, b, g, :], rhs=qsT[0:64, 4 * g:4 * g + 4, b], start=True, stop=True),
                  KSK + ["qsT"], ["PS0"])
            S.add("pe", lambda e, b=b, g=g, cs=cs: e.matmul(PS[1][0:17, cs], lhsT=KsT2[0:64, b, g, 0:17], rhs=qsT[0:64, 4 * g:4 * g + 4, b], start=True, stop=True),
                  KS2K + ["qsT"], ["PS1"])
    S.add("act", lambda e: e.activation(out=PTs, in_=PS[0][:, 0:256], func=AF.Exp), ["PS0"], ["PTs"])
    S.add("act", lambda e: e.activation(out=PTs2[0:17], in_=PS[1][0:17, 0:256], func=AF.Exp), ["PS1"], ["PTs2"])
    S.add("pool", lambda e: e.memset(PTs[0:1, :], 0.0), ["PTs"], ["PTs"])
    S.add("pe", lambda e: e.matmul(PS[2][0:64, 0:256], lhsT=onesb[:, 0:64], rhs=PTs, start=True, stop=False), ["onesb", "PTs"], ["PS2"])
    S.add("pe", lambda e: e.matmul(PS[2][0:64, 0:256], lhsT=onesb[0:17, 0:64], rhs=PTs2[0:17], start=False, stop=True), ["onesb", "PTs2"], ["PS2"])
    for b in range(NS):
        for g in range(4):
            cs = slice(b * 16 + g * 4, b * 16 + g * 4 + 4)
            S.add("pe", lambda e, b=b, g=g, cs=cs: e.matmul(PS[3][0:64, cs], lhsT=Vc[:, b, g * 64:(g + 1) * 64], rhs=PTs[:, cs], start=True, stop=False), ["Vc", "PTs"], ["PS3"])
            S.add("pe", lambda e, b=b, g=g, cs=cs: e.matmul(PS[3][0:64, cs], lhsT=Vmc[0:17, b, g * 64:(g + 1) * 64], rhs=PTs2[0:17, cs], start=False, stop=True),
                  ["Vmc_a", "Vmc_b", "PTs2"], ["PS3"])
    dv3 = dts[0:64].rearrange("p (b h) -> p b h", h=16)
    S.add("dve", lambda e: e.tensor_tensor(out=dv3, in0=PS[2][0:64, 0:256].rearrange("p (b h) -> p b h", h=16),
                                           in1=expsink[0:64, :].unsqueeze(1).to_broadcast([64, NS, 16]), op=ALU.add), ["PS2", "expsink"], ["dts"])
    S.add("dve", lambda e: e.reciprocal(out=dts[0:64], in_=dts[0:64]), ["dts"], ["dts"])
    S.add("dve", lambda e: e.tensor_tensor(out=osb[0:64].rearrange("p h b -> p b h"), in0=PS[3][0:64, 0:256].rearrange("p (b h) -> p b h", h=16), in1=dv3, op=ALU.mult),
          ["PS3", "dts"], ["osb"])
    dma("sp", AT[0:64, :, NT:NTS], osb[0:64, 0:16:2, :], ["osb"], ["AT_s0"], "osb")
    dma("sp", AT[64:128, :, NT:NTS], osb[0:64, 1:16:2, :], ["osb"], ["AT_s1"], "osb")
    ATK = ATK + ["AT_s0", "AT_s1"]
    S.mark('p3b')

    S.barrier(mk_bar)
    mT = carve(0, 16 * NTS, BF16).rearrange("p (c t) -> p c t", t=NTS)
    ST2 = carve(100 * KB, 8 * NTS, BF16).rearrange("p (c t) -> p c t", t=NTS)
    o5 = 117 * KB
    slt = [carve(o5 + 2 * KB * i, 512, F32) for i in range(2)]; o5 += 4 * KB
    bra = carve(o5, 2 * NTS, F32).rearrange("p (c t) -> p c t", t=NTS); o5 += 2 * NTS * 4
    m1 = carve(o5, 2 * NTS, F32).rearrange("p (c t) -> p c t", t=NTS); o5 += 2 * NTS * 4
    o5 = (o5 + 3) // 4 * 4
    xin = [carve(o5 + 2 * KB * i, 512, F32) for i in range(3)]; o5 += 6 * KB
    oti = [carve(o5 + 2 * KB * i, 512, F32) for i in range(3)]; o5 += 6 * KB
    assert o5 <= ARENA, o5
    dma("sp", b_glu[:], din["b_glu"], [], ["b_glu"], "b_glu")
    TB = ((0, 512), (512, 512), (NT, NS))
    XK = ["xnT_%d" % i for i in range(8)] + ["xnT_s"]
    STK = ["ST_%d" % i for i in range(8)] + ["STs_%d" % i for i in range(8)]
    c4 = {"ps": 0, "sl": 0}

    def feat_mm(slot, wc0, nk, rhs_fn, rk):
        for (t0, N) in TB:
            b = c4["ps"] % 4
            c4["ps"] += 1
            kb = "PS%d" % b
            for kc in range(nk):
                S.add("pe", lambda e, kc=kc, b=b, t0=t0, N=N: e.matmul(PS[b][:, 0:N], lhsT=wbuf[slot][:, kc, wc0:wc0 + 128], rhs=rhs_fn(kc, t0, N),
                                                                       start=(kc == 0), stop=(kc == nk - 1)), rk + ["wbuf%d" % slot], [kb])
            yield b, kb, t0, N

    def act_tmp(b, kb, N, func, bias=None):
        j = c4["sl"] % 2
        c4["sl"] += 1
        t = slt[j]
        if bias is None:
            S.add("act", lambda e: e.activation(out=t[:, 0:N], in_=PS[b][:, 0:N], func=func), [kb], ["slt%d" % j])
        else:
            S.add("act", lambda e: e.activation(out=t[:, 0:N], in_=PS[b][:, 0:N], func=func, bias=bias), [kb, "b_glu"], ["slt%d" % j])
        return t, "slt%d" % j

    xrhs = lambda kc, t0, N: xnT[:, kc, t0:t0 + N]
    for blk in range(2):
        sl_ = blk % 2
        for c in range(4):
            jc = blk * 4 + c
            for b, kb, t0, N in feat_mm(sl_, c * 128, ND, xrhs, XK):
                t, tk = act_tmp(b, kb, N, AF.Silu)
                S.add("dve", lambda e, t=t, jc=jc, t0=t0, N=N: e.tensor_tensor(out=AT[:, jc, t0:t0 + N], in0=AT[:, jc, t0:t0 + N], in1=t[:, 0:N], op=ALU.mult),
                      [tk] + ATK, ["A_%d_%d" % (jc, t0)])
    AK = ["A_%d_%d" % (jc, t0) for jc in range(8) for (t0, _) in TB]
    strhs = lambda kc, t0, N: ST[:, kc, t0:t0 + N]
    for blk in range(2):
        sl_ = blk % 2
        load_w(sl_, din["w_glu"], blk * 512, 512, 8)
        for c in range(4):
            jc = blk * 4 + c
            for b, kb, t0, N in feat_mm(sl_, c * 128, 8, strhs, STK):
                t, tk = act_tmp(b, kb, N, AF.Sigmoid, bias=b_glu[:, jc:jc + 1])
                S.add("dve", lambda e, t=t, jc=jc, t0=t0, N=N: e.tensor_tensor(out=ST2[:, jc, t0:t0 + N], in0=ST[:, jc, t0:t0 + N], in1=t[:, 0:N], op=ALU.mult),
                      [tk] + STK, ["S2_%d_%d" % (jc, t0)])
    for blk in range(2):
        sl_ = blk % 2
        load_w(sl_, din["w_in"], 3584 + blk * 512, 512, ND)
        for c in range(4):
            jc = blk * 4 + c
            for b, kb, t0, N in feat_mm(sl_, c * 128, ND, xrhs, XK):
                t, tk = act_tmp(b, kb, N, AF.Silu)
                S.add("dve", lambda e, t=t, jc=jc, t0=t0, N=N: e.tensor_tensor(out=ST2[:, jc, t0:t0 + N], in0=ST2[:, jc, t0:t0 + N], in1=t[:, 0:N], op=ALU.mult),
                      [tk, "S2_%d_%d" % (jc, t0)], ["S2_%d_%d" % (jc, t0)])
    S2K = ["S2_%d_%d" % (jc, t0) for jc in range(8) for (t0, _) in TB]
    arhs = lambda kc, t0, N: AT[:, kc, t0:t0 + N]
    s2rhs = lambda kc, t0, N: ST2[:, kc, t0:t0 + N]
    wq = [carve((68 + 8 * i) * KB, ND * 256, BF16).rearrange("p (c n) -> p c n", c=ND) for i in range(4)]

    def load_wq(slot, src, col0, nk):
        key = "wq%d" % slot
        v = src.rearrange("(c p) n -> p c n", p=128)
        for c0 in range(0, nk, 8):
            dma("pool", wq[slot][:, c0:c0 + 8, :], v[:, c0:c0 + 8, col0:col0 + 256], [], [key], key)

    def feat_mm_q(slot, wc0, nk, rhs_fn, rk):
        for (t0, N) in TB:
            b = c4["ps"] % 4
            c4["ps"] += 1
            kb = "PS%d" % b
            for kc in range(nk):
                S.add("pe", lambda e, kc=kc, b=b, t0=t0, N=N: e.matmul(PS[b][:, 0:N], lhsT=wq[slot][:, kc, wc0:wc0 + 128], rhs=rhs_fn(kc, t0, N),
                                                                       start=(kc == 0), stop=(kc == nk - 1)), rk + ["wq%d" % slot], [kb])
            yield b, kb, t0, N

    stages = ((0, din["w_ao"], 0, 8), (1, din["w_in"], 4608, ND), (2, din["w_so"], 0, 8), (3, din["w_in"], 6656, ND))
    S.add("pool", lambda e: e.memset(dummy[0:1, 6:7], 0.0), [], ["wbuf0", "wbuf1", "wq0", "wq1", "wq2", "wq3"])
    for (sl_, src, c0_, nk_) in stages:
        load_wq(sl_, src, c0_, nk_)
    for gq in range(8):
        for half_ in range(2):
            rhs_fn, rk, last = ((arhs, AK, False), (s2rhs, S2K, True))[half_]
            sa, sg_ = 2 * half_, 2 * half_ + 1
            for c in range(2):
                for b, kb, t0, N in feat_mm_q(sa, c * 128, 8, rhs_fn, rk):
                    S.add("act", lambda e, b=b, c=c, t0=t0, N=N: e.copy(out=bra[:, c, t0:t0 + N], in_=PS[b][:, 0:N]), [kb], ["bra_%d_%d" % (c, t0)])
            if gq < 7:
                load_wq(sa, stages[sa][1], stages[sa][2] + (gq + 1) * 256, stages[sa][3])
            for c in range(2):
                fc = gq * 2 + c
                for b, kb, t0, N in feat_mm_q(sg_, c * 128, ND, xrhs, XK):
                    t, tk = act_tmp(b, kb, N, AF.Sigmoid)
                    if not last:
                        S.add("dve", lambda e, t=t, c=c, t0=t0, N=N: e.tensor_tensor(out=m1[:, c, t0:t0 + N], in0=t[:, 0:N], in1=bra[:, c, t0:t0 + N], op=ALU.mult),
                              [tk, "bra_%d_%d" % (c, t0)], ["m1_%d_%d" % (c, t0)])
                    else:
                        S.add("dve", lambda e, t=t, c=c, t0=t0, N=N: e.tensor_tensor(out=t[:, 0:N], in0=t[:, 0:N], in1=bra[:, c, t0:t0 + N], op=ALU.mult),
                              [tk, "bra_%d_%d" % (c, t0)], [tk])
                        S.add("dve", lambda e, t=t, c=c, fc=fc, t0=t0, N=N: e.tensor_tensor(out=mT[:, fc, t0:t0 + N], in0=t[:, 0:N], in1=m1[:, c, t0:t0 + N], op=ALU.add),
                              [tk, "m1_%d_%d" % (c, t0)], ["mT_%d_%d" % (fc, t0)])
            if gq < 7:
                load_wq(sg_, stages[sg_][1], stages[sg_][2] + (gq + 1) * 256, stages[sg_][3])
    MK = ["mT_%d_%d" % (fc, t0) for fc in range(16) for (t0, _) in TB]
    S.mark('p4')
    S.add("pool", lambda e: e.memset(dummy[0:1, 5:6], 0.0), [], ["wbuf0", "wbuf1", "wq0", "wq1", "wq2", "wq3"])
    tiles5 = []
    for cb in range(4):
        for i in range(9):
            tiles5.append((cb, i))

    def p5_load(k):
        cb, i = tiles5[k]
        n = 128 if i < 8 else NS
        xsrc = din["x_own"][i * 128:(i + 1) * 128, cb * 512:(cb + 1) * 512] if i < 8 else din["x_smp"][:, cb * 512:(cb + 1) * 512]
        j = k % 3
        dma("sp", xin[j][0:n, :], xsrc, [], ["xin%d" % j], "xin%d" % j)

    p5_load(0)
    p5_load(1)
    for k, (cb, i) in enumerate(tiles5):
        sl_ = cb % 2
        if i == 0:
            load_w(sl_, din["w_out"], cb * 512, 512, ND)
        n = 128 if i < 8 else NS
        tc0 = i * 128 if i < 8 else NT
        ydst = dout["y_own"][i * 128:(i + 1) * 128, cb * 512:(cb + 1) * 512] if i < 8 else dout["y_smp"][:, cb * 512:(cb + 1) * 512]
        j = k % 3
        b = c4["ps"] % 4
        c4["ps"] += 1
        kb = "PS%d" % b
        if k + 2 < len(tiles5):
            p5_load(k + 2)
        for fc in range(16):
            S.add("pe", lambda e, fc=fc, b=b, n=n, tc0=tc0, sl_=sl_: e.matmul(PS[b][0:n, 0:512], lhsT=mT[:, fc, tc0:tc0 + n], rhs=wbuf[sl_][:, fc, 0:512],
                                                                              start=(fc == 0), stop=(fc == 15)), MK + ["wbuf%d" % sl_], [kb])
        S.add("dve", lambda e, j=j, b=b, n=n: e.tensor_tensor(out=oti[j][0:n, :], in0=PS[b][0:n, 0:512], in1=xin[j][0:n, :], op=ALU.add),
              [kb, "xin%d" % j], ["oti%d" % j])
        dma("sp", ydst, oti[j][0:n, :], ["oti%d" % j], [], "oti%d" % j)
    S.mark('p5')

    S.emit()
    for cm in reversed(ctxs):
        cm.__exit__(None, None, None)
    return nc


def _rope_tables(pos):
    half = 32
    inv_freq = (10000.0 ** (-np.arange(half, dtype=np.float32) / half)).astype(np.float32)
    ang = pos.astype(np.float32)[:, None] * inv_freq[None, :]
    c = np.cos(ang.astype(np.float64)).astype(np.float32)
    s = np.sin(ang.astype(np.float64)).astype(np.float32)
    return np.concatenate([c, c], 1), np.concatenate([-s, s], 1)


def kernel(**inp):
    f32 = np.float32
    x_prompt = np.asarray(inp["x_prompt"], f32)
    x_sample = np.asarray(inp["x_sample"], f32)
    meta = np.asarray(inp["meta_tokens"], f32)
    shared = {
        "w_in": np.ascontiguousarray(inp["w_in"][0], f32),
        "w_glu": np.ascontiguousarray(inp["w_glu"][0], f32),
        "w_ao": np.ascontiguousarray(inp["w_attn_out"][0], f32),
        "w_so": np.ascontiguousarray(inp["w_ssm_out"][0], f32),
        "w_out": np.ascontiguousarray(inp["w_out"][0], f32),
        "norm_gain": np.ascontiguousarray(inp["norm_gain"], f32).reshape(1, D),
        "qk_gain": np.concatenate([np.asarray(inp["q_norm_gain"], f32).reshape(1, 64),
                                   np.asarray(inp["k_norm_gain"], f32).reshape(1, 64)], 1),
        "sinks": np.asarray(inp["sinks"], f32).reshape(1, 16),
        "b_glu": np.ascontiguousarray(np.asarray(inp["b_glu"], f32).reshape(8, 128).T),
        "dsk": np.ascontiguousarray(np.tile(np.asarray(inp["d_skip"], f32).reshape(64, 16).T, (8, 1))),
        "ident": np.eye(128, dtype=f32),
        "a_re2": np.ascontiguousarray(np.tile(np.asarray(inp["a_re"][0], f32).T, (2, 1))),
        "a_im2": np.ascontiguousarray(np.tile(np.asarray(inp["a_im"][0], f32).T, (2, 1))),
        "logdt": np.asarray(inp["log_dt"], f32).reshape(1, 64),
        "kvals": np.asarray([7, 6, 5, 4, 3, 2, 1, 0, -7, -6, -5, -4, -3, -2, -1, 0, 1, 2, 3, 4, 5, 6, 7, 8], f32).reshape(1, 24),
        "krow": np.arange(260, dtype=f32).reshape(1, 260),
        "blockmask": np.kron(np.triu(np.ones((8, 8), f32)), np.ones((16, 16), f32)).astype(f32),
        "swapm": np.roll(np.eye(128, dtype=f32), 64, axis=0),
    }
    kk_, qq_ = np.meshgrid(np.arange(128), np.arange(128), indexing="ij")
    NEG = np.float32(-30000.0)
    shared["maskc"] = np.where(kk_ <= qq_, np.float32(0), NEG).astype(f32)
    shared["maskp"] = np.where(kk_ > qq_, np.float32(0), NEG).astype(f32)
    b_re = np.asarray(inp["b_re"][0], f32).transpose(1, 0, 2)
    b_im = np.asarray(inp["b_im"][0], f32).transpose(1, 0, 2)
    c_re = np.asarray(inp["c_re"][0], f32).transpose(2, 0, 1)
    c_im = np.asarray(inp["c_im"][0], f32).transpose(2, 0, 1)
    shared["b_self"] = np.ascontiguousarray(np.concatenate([b_re, b_im], 0))
    shared["b_part"] = np.ascontiguousarray(np.concatenate([b_im, b_re], 0))
    shared["c_self"] = np.ascontiguousarray(np.concatenate([c_re, c_im], 0))
    shared["c_part"] = np.ascontiguousarray(np.concatenate([c_im, c_re], 0))
    in_maps = []
    for core in range(N_CORES):
        b, half = core // 2, core % 2
        m = dict(shared)
        m["x_own"] = np.ascontiguousarray(x_prompt[b, half * NT:(half + 1) * NT])
        if half == 1:
            m["x_pre2"] = np.ascontiguousarray(x_prompt[b, 0:NT])
            m["x_pre1"] = meta.copy()
            halo = x_prompt[b, NT - 128:NT]
        else:
            m["x_pre2"] = np.concatenate([np.zeros((NT - 16, D), f32), meta], 0)
            m["x_pre1"] = np.zeros((16, D), f32)
            halo = np.zeros((128, D), f32)
        m["x_kvx"] = np.concatenate([halo, meta], 0)
        m["maskp0"] = shared["maskp"] if half == 1 else np.full((128, 128), NEG, f32)
        m["x_smp"] = np.ascontiguousarray(x_sample[core * NS:(core + 1) * NS, 0])
        base = 16 + half * NT
        cc = np.zeros((11, 128, 64), f32)
        ss = np.zeros((11, 128, 64), f32)
        for t in range(8):
            cc[t], ss[t] = _rope_tables(base + t * 128 + np.arange(128))
        cc[8], ss[8] = _rope_tables(np.maximum(base - 128 + np.arange(128), 0))
        cc[9, :16], ss[9, :16] = _rope_tables(np.arange(16))
        cc[10, :16], ss[10, :16] = _rope_tables(np.full(16, 8192))
        m["rope_cc"] = np.ascontiguousarray(cc.transpose(1, 0, 2))
        m["rope_ss"] = np.ascontiguousarray(ss.transpose(1, 0, 2))
        sl = slice(core * NS, (core + 1) * NS)
        m["cwk"] = np.ascontiguousarray(inp["cache_win_k"][0, sl], f32).reshape(NS, 128, 256)
        m["cwv"] = np.ascontiguousarray(inp["cache_win_v"][0, sl], f32).reshape(NS, 128, 256)
        m["cmk"] = np.ascontiguousarray(inp["cache_meta_k"][0, sl], f32).reshape(NS, 16, 256)
        m["cmv"] = np.ascontiguousarray(inp["cache_meta_v"][0, sl], f32).reshape(NS, 16, 256)
        sre = np.asarray(inp["state_ssm_re"][0, sl], f32)
        sim = np.asarray(inp["state_ssm_im"][0, sl], f32)
        m["st_self"] = np.ascontiguousarray(np.concatenate([sre, sim], 2))
        m["st_part"] = np.ascontiguousarray(np.concatenate([sim, sre], 2))
        in_maps.append(m)

    nc = build_program()
    res = run_bass_kernel_spmd(nc, in_maps, core_ids=list(range(N_CORES)))
    R = res.results

    y_prompt = np.zeros((4, 2048, D), f32)
    y_sample = np.zeros((128, 1, D), f32)
    p_win_k = np.zeros((1, 4, 128, 4, 64), f32)
    p_win_v = np.zeros((1, 4, 128, 4, 64), f32)
    p_meta_k = np.zeros((1, 4, 16, 4, 64), f32)
    p_meta_v = np.zeros((1, 4, 16, 4, 64), f32)
    p_re = np.zeros((1, 4, 64, 64), f32)
    p_im = np.zeros((1, 4, 64, 64), f32)
    s_win_k = np.zeros((1, 128, 128, 4, 64), f32)
    s_win_v = np.zeros((1, 128, 128, 4, 64), f32)
    s_re = np.zeros((1, 128, 64, 64), f32)
    s_im = np.zeros((1, 128, 64, 64), f32)
    for core in range(N_CORES):
        b, half = core // 2, core % 2
        r = R[core]
        y_prompt[b, half * NT:(half + 1) * NT] = r["y_own"]
        sl = slice(core * NS, (core + 1) * NS)
        y_sample[sl, 0] = r["y_smp"]
        if half == 1:
            p_win_k[0, b] = r["pwk"].reshape(128, 4, 64)
            p_win_v[0, b] = r["pwv"].reshape(128, 4, 64)
            p_re[0, b] = r["pssm"][:, 0:64]
            p_im[0, b] = r["pssm"][:, 64:128]
        else:
            p_meta_k[0, b] = r["pmk"].reshape(16, 4, 64)
            p_meta_v[0, b] = r["pmv"].reshape(16, 4, 64)
        s_win_k[0, sl] = r["swk"].reshape(NS, 128, 4, 64)
        s_win_v[0, sl] = r["swv"].reshape(NS, 128, 4, 64)
        s_re[0, sl] = r["sssm"][:, :, 0:64]
        s_im[0, sl] = r["sssm"][:, :, 64:128]
    return (y_prompt, y_sample, p_win_k, p_win_v, p_meta_k, p_meta_v, p_re, p_im, s_win_k, s_win_v, s_re, s_im)
```

```python
import os
import numpy as np
import concourse.bass as bass
import concourse.mybir as mybir
from concourse.bass_utils import run_bass_kernel_spmd

F32 = mybir.dt.float32
BF16 = mybir.dt.bfloat16
ALU = mybir.AluOpType
AF = mybir.ActivationFunctionType
AX = mybir.AxisListType

D = 2048
ND = 16
NT = 1024
NS = 16
NTS = NT + NS
EPS = 1e-6
N_CORES = 8
KTRUNC = ''
KUM = 'ab'


class Sched:
    ENGS = ("pe", "act", "dve", "pool", "sp")

    def __init__(self, nc):
        self.nc = nc
        self.ops = []

    def add(self, eng, fn, reads=(), writes=(), dma=None, fence=False):
        self.ops.append(dict(eng=eng, fn=fn, reads=tuple(reads), writes=tuple(writes), dma=dma, fence=fence))
        return len(self.ops) - 1

    def barrier(self, mk):
        keys = set()
        for o in self.ops:
            keys.update(o["reads"])
            keys.update(o["writes"])
        keys = sorted(keys, key=str)
        self.nbar = getattr(self, "nbar", 0) + 1
        bk = "bar%d" % self.nbar
        self.add("act", mk("act"), reads=keys, writes=keys + [bk])
        for e in ("dve", "pool", "sp"):
            self.add(e, mk(e), reads=[bk], writes=["%s_%s" % (bk, e)], dma=("bar_sp" if e == "sp" else None))

    def mark(self, name):
        if not hasattr(self, "marks"):
            self.marks = {}
        self.marks[name] = len(self.ops)

    def emit(self):
        nc = self.nc
        if KTRUNC:
            self.ops = self.ops[:self.marks[KTRUNC]]
        ops = self.ops
        last_w = {}
        readers = {}
        deps = [set() for _ in ops]
        for i, o in enumerate(ops):
            for b in o["reads"]:
                if b in last_w:
                    deps[i].add(last_w[b])
            for b in o["writes"]:
                if b in last_w:
                    deps[i].add(last_w[b])
                for r in readers.get(b, ()):
                    if r != i:
                        deps[i].add(r)
            for b in o["reads"]:
                readers.setdefault(b, []).append(i)
            for b in o["writes"]:
                last_w[b] = i
                readers[b] = []
        needed = set()
        prev_on = {}
        for i, o in enumerate(ops):
            keep = set()
            for d in deps[i]:
                if ops[d]["dma"] is None and ops[d]["eng"] == "pe" and o["eng"] == "pe" and o["dma"] is None:
                    continue
                keep.add(d)
            if o.get("fence") and o["eng"] in prev_on:
                keep.add(prev_on[o["eng"]])
            if o["dma"] is None:
                prev_on[o["eng"]] = i
            deps[i] = keep
            needed |= keep
        dma_keys = sorted({o["dma"] for o in ops if o["dma"] is not None}, key=str)
        sem_ctx = []
        sems = {}
        for e in self.ENGS:
            cm = nc.semaphore("s_" + e)
            sems[("eng", e)] = cm.__enter__()
            sem_ctx.append(cm)
        for n, k in enumerate(dma_keys):
            cm = nc.semaphore("d%d" % n)
            sems[("dma", k)] = cm.__enter__()
            sem_ctx.append(cm)
        cnt = {k: 0 for k in sems}
        ticket = [None] * len(ops)
        for i, o in enumerate(ops):
            if o["dma"] is not None:
                k = ("dma", o["dma"])
                cnt[k] += 16
                ticket[i] = (k, cnt[k])
            elif i in needed:
                k = ("eng", o["eng"])
                cnt[k] += 1
                ticket[i] = (k, cnt[k])
        streams = {e: [] for e in self.ENGS}
        waited = {e: {} for e in self.ENGS}
        for i, o in enumerate(ops):
            e = o["eng"]
            w = {}
            for d in deps[i]:
                k, v = ticket[d]
                if waited[e].get(k, 0) >= v:
                    continue
                w[k] = max(w.get(k, 0), v)
            for k, v in w.items():
                waited[e][k] = v
            streams[e].append((i, w))
        final = {k: v for k, v in cnt.items() if k[0] == "dma" and v > 0}

        def run_stream(e, engobj):
            for i, w in streams[e]:
                for k, v in w.items():
                    engobj.wait_ge(sems[k], v)
                ins = ops[i]["fn"](engobj)
                if ticket[i] is not None:
                    k, v = ticket[i]
                    ins.then_inc(sems[k], 16 if k[0] == "dma" else 1)
            if e == "sp":
                for k, v in final.items():
                    engobj.wait_ge(sems[k], v)

        with nc.Block() as block:
            @block.sync
            def _(eng):
                run_stream("sp", eng)

            @block.tensor
            def _(eng):
                run_stream("pe", eng)

            @block.scalar
            def _(eng):
                run_stream("act", eng)

            @block.vector
            def _(eng):
                run_stream("dve", eng)

            @block.gpsimd
            def _(eng):
                run_stream("pool", eng)
        for cm in reversed(sem_ctx):
            cm.__exit__(None, None, None)


IN_SPECS = [
    ("x_own", [NT, D]), ("x_pre2", [NT, D]), ("x_pre1", [16, D]), ("x_kvx", [144, D]), ("x_smp", [NS, D]),
    ("w_in", [D, 8704]), ("w_glu", [1024, 1024]), ("w_ao", [1024, D]), ("w_so", [1024, D]), ("w_out", [D, D]),
    ("norm_gain", [1, D]), ("qk_gain", [1, 128]), ("sinks", [1, 16]), ("b_glu", [128, 8]), ("dsk", [128, 64]),
    ("rope_cc", [128, 11, 64]), ("rope_ss", [128, 11, 64]),
    ("ident", [128, 128]),
    ("cwk", [NS, 128, 256]), ("cwv", [NS, 128, 256]), ("cmk", [NS, 16, 256]), ("cmv", [NS, 16, 256]),
    ("a_re2", [128, 64]), ("a_im2", [128, 64]), ("logdt", [1, 64]),
    ("b_self", [128, 64, 16]), ("b_part", [128, 64, 16]), ("c_self", [128, 64, 16]), ("c_part", [128, 64, 16]),
    ("kvals", [1, 24]), ("krow", [1, 260]), ("blockmask", [128, 128]), ("swapm", [128, 128]),
    ("st_self", [NS, 64, 128]), ("st_part", [NS, 64, 128]),
    ("maskc", [128, 128]), ("maskp", [128, 128]), ("maskp0", [128, 128]),
]
OUT_SPECS = [
    ("y_own", [NT, D]), ("y_smp", [NS, D]),
    ("pwk", [128, 256]), ("pwv", [128, 256]), ("pmk", [16, 256]), ("pmv", [16, 256]),
    ("pssm", [64, 128]),
    ("swk", [NS, 128, 256]), ("swv", [NS, 128, 256]), ("sssm", [NS, 64, 128]),
]


def build_program():
    nc = bass.Bass("TRN2", target_bir_lowering=False)
    S = Sched(nc)
    din = {n: nc.dram_tensor(n, s, F32, kind="ExternalInput").ap() for n, s in IN_SPECS}
    dout = {n: nc.dram_tensor(n, s, F32, kind="ExternalOutput").ap() for n, s in OUT_SPECS}
    ctxs = []

    def sb(name, shape, dt):
        cm = nc.sbuf_tensor("sb_" + name, shape, dt)
        t = cm.__enter__()
        ctxs.append(cm)
        return t

    def psum(name, shape, dt):
        cm = nc.psum_tensor(name, shape, dt)
        t = cm.__enter__()
        ctxs.append(cm)
        return t

    KB = 1024
    ARENA = 150 * KB
    AR = sb("arena", [128, ARENA // 2], BF16)

    def carve(off, n, dt):
        assert off % 4 == 0
        if dt == BF16:
            assert off + 2 * n <= ARENA, (off, n)
            return AR[:, off // 2: off // 2 + n]
        assert off + 4 * n <= ARENA, (off, n)
        return AR[:, off // 2: off // 2 + 2 * n].bitcast(F32)

    PS = [psum("ps%d" % i, [128, 512], F32) for i in range(8)]
    PSB = [p.bitcast(BF16) for p in PS]
    identf = sb("identf", [128, 128], F32)
    identb = sb("identb", [128, 128], BF16)
    epsb = sb("epsb", [128, 1], F32)
    qkg_bc = sb("qkg_bc", [128, 128], F32)
    ropecc = sb("ropecc", [128, 11, 64], F32)
    ropess = sb("ropess", [128, 11, 64], F32)
    stat = sb("stat", [128, 64], F32)
    dummy = sb("dummyk", [128, 8], F32)
    xnT = sb("xnT", [128, ND, NTS], BF16)
    Um = sb("Um", [128, 64, 18], BF16)
    kvals = sb("kvals", [128, 24], F32)
    krow = sb("krow", [128, 260], F32)
    sgn = sb("sgn", [128, 2], F32)
    halfpi = sb("halfpi", [128, 1], F32)
    onec = sb("onec", [128, 1], F32)
    pp = sb("pp", [128, 16, 64], F32)
    mumag = sb("mumag", [128, 64], F32)
    G258 = sb("G258", [128, 64], F32)
    C258 = sb("C258", [128, 64], F32)
    S258 = sb("S258", [128, 64], F32)
    dsk = sb("dsk", [128, 64], F32)
    blockmask = sb("blockmask", [128, 128], F32)
    swapm = sb("swapm", [128, 128], F32)
    b_glu = sb("b_glu", [128, 8], F32)
    expsink = sb("expsink", [128, 16], F32)
    masks = sb("masks", [128, 3, 128], BF16)
    qs16 = sb("qs16", [NS, 1024], BF16)
    onesb = sb("onesb", [128, 64], BF16)

    u8own = carve(0, 8192, BF16).rearrange("p (g s h) -> p g s h", g=64, s=8)
    u8p2 = carve(16 * KB, 8192, BF16).rearrange("p (g s h) -> p g s h", g=64, s=8)
    wbuf = [carve((68 + 16 * i) * KB, ND * 512, BF16).rearrange("p (c n) -> p c n", c=ND) for i in range(2)]
    u8m = carve(34 * KB, 8192, BF16).rearrange("p (g s h) -> p g s h", g=64, s=8)
    gain_bc = carve(51 * KB, D, F32)
    xbuf = [carve((100 + 8 * i) * KB, D, F32) for i in range(2)]
    xnb = [carve((116 + 4 * i) * KB, D, BF16) for i in range(2)]
    tmpq = [carve((124 + 4 * i) * KB, 1024, F32) for i in range(3)]
    sqj = carve(128 * KB, D, F32)
    xnTt = [carve((136 + 4 * i) * KB, ND * 128, BF16).rearrange("p (c t) -> p c t", c=ND) for i in range(2)]
    kf = [carve((144 + i) * KB, 256, F32) for i in range(2)]
    vf = [carve((146 + i) * KB, 256, F32) for i in range(2)]
    utok1 = carve(148 * KB, 1024, BF16)

    cnt = {"x": 0, "ps": 0, "kv": 0}

    def dma(eng, out, in_, reads, writes, key):
        S.add(eng, lambda e: e.dma_start(out=out, in_=in_), reads=reads, writes=writes, dma=key)

    def mk_bar(e):
        col = {"act": 0, "dve": 1, "pool": 2, "sp": 3}[e]
        if e == "act":
            return lambda en: en.copy(out=dummy[0:1, col:col + 1], in_=dummy[0:1, 4:5])
        if e == "sp":
            return lambda en: en.dma_start(out=dummy[0:1, col:col + 1], in_=din["kvals"][0:1, 0:1])
        return lambda en: en.memset(dummy[0:1, col:col + 1], 0.0)

    S.add("pool", lambda e: e.memset(dummy[:], 0.0), [], ["dummy"])
    dma("sp", identf[:], din["ident"], [], ["identf"], "identf")
    S.add("dve", lambda e: e.tensor_copy(out=identb[:], in_=identf[:]), ["identf"], ["identb"])
    S.add("pool", lambda e: e.memset(epsb[:], EPS), [], ["epsb"])
    dma("sp", gain_bc, din["norm_gain"].partition_broadcast(128), [], ["gain_bc"], "gain_bc")
    dma("sp", qkg_bc[:], din["qk_gain"].partition_broadcast(128), [], ["qkg_bc"], "qkg_bc")
    dma("sp", ropecc[:], din["rope_cc"], [], ["ropecc"], "ropecc")
    dma("sp", ropess[:], din["rope_ss"], [], ["ropess"], "ropess")

    def load_w(slot, src, col0, ncols, nk):
        key = "wbuf%d" % slot
        v = src.rearrange("(c p) n -> p c n", p=128)
        step = 4
        for c0 in range(0, nk, step):
            c1 = min(nk, c0 + step)
            dma("pool", wbuf[slot][:, c0:c1, 0:ncols], v[:, c0:c1, col0:col0 + ncols], [], [key], key)

    def norm_A(src_rows, n):
        j = (cnt["x"] % 2) if (xbuf[0] is not xbuf[1]) else 0
        cnt["x"] += 1
        xb_, xn_ = xbuf[j], xnb[j]
        sq_, g_ = sqj, gain_bc
        kx, kn = "xbuf%d" % j, "xnb%d" % j
        dma("sp", xb_[0:n, :], src_rows, [], [kx], kx)
        S.add("act", lambda e: e.activation(out=sq_[0:n, :], in_=xb_[0:n, :], func=AF.Square), [kx], ["tq1", "tq2a", "tq2b"])
        S.add("dve", lambda e: e.tensor_reduce(out=stat[0:n, 0:1], in_=sq_[0:n, :], axis=AX.X, op=ALU.add), ["tq1", "tq2a", "tq2b"], ["stat0"])
        S.add("act", lambda e: e.activation(out=stat[0:n, 1:2], in_=stat[0:n, 0:1], func=AF.Sqrt, bias=epsb[0:n, 0:1], scale=1.0 / D),
              ["stat0", "epsb"], ["stat1"])
        S.add("dve", lambda e: e.reciprocal(out=stat[0:n, 2:3], in_=stat[0:n, 1:2]), ["stat1"], ["stat2"])
        S.add("dve", lambda e: e.scalar_tensor_tensor(out=xn_[0:n, :], in0=xb_[0:n, :], scalar=stat[0:n, 2:3], in1=g_[0:n, :],
                                                      op0=ALU.mult, op1=ALU.mult), [kx, "stat2", "gain_bc"], [kn])
        return xn_, kn, n

    def norm_B(h, dst_fn, dst_keys):
        xn_, kn, n = h
        for half in range(2):
            b = 4 + (cnt["ps"] % 2)
            cnt["ps"] += 1
            kb = "PS%d" % b
            pv = PSB[b][:, 0:1024].rearrange("p (c t) -> p c t", c=8)
            for c in range(8):
                dc = half * 8 + c
                S.add("pe", lambda e, c=c, dc=dc, pv=pv: e.transpose(out=pv[:, c, 0:n], in_=xn_[0:n, dc * 128:(dc + 1) * 128],
                                                                      identity=identb[0:n, 0:n]), [kn, "identb"], [kb])
            dst = dst_fn(half * 8)
            if half == 0:
                S.add("act", lambda e, pv=pv, dst=dst: e.copy(out=dst, in_=pv[:, :, 0:n]), [kb], dst_keys)
            else:
                S.add("dve", lambda e, pv=pv, dst=dst: e.tensor_copy(out=dst, in_=pv[:, :, 0:n]), [kb], dst_keys)

    def norm_tile(src_rows, n, dst_fn, dst_keys):
        norm_B(norm_A(src_rows, n), dst_fn, dst_keys)

    def headnorm_rope(src_ps, src_keys, n, NH, ti, goff, extra_scale, dst, dst_keys):
        W = NH * 64
        t0, t1, t2 = tmpq
        c0 = 0
        for ap_, w in src_ps:
            S.add("act", lambda e, ap_=ap_, c0=c0, w=w: e.copy(out=t0[0:n, c0:c0 + w], in_=ap_), src_keys, ["tq0"])
            S.add("act", lambda e, ap_=ap_, c0=c0, w=w: e.activation(out=t1[0:n, c0:c0 + w], in_=ap_, func=AF.Square), src_keys, ["tq1"])
            c0 += w
        v3 = lambda t: t[0:n, 0:W].rearrange("p (h d) -> p h d", d=64)
        S.add("dve", lambda e: e.tensor_reduce(out=stat[0:n, 8:8 + NH], in_=v3(t1), axis=AX.X, op=ALU.add), ["tq1"], ["stq"])
        S.add("act", lambda e: e.activation(out=stat[0:n, 24:24 + NH], in_=stat[0:n, 8:8 + NH], func=AF.Sqrt, bias=epsb[0:n, 0:1], scale=1.0 / 64),
              ["stq", "epsb"], ["stq2"])
        S.add("dve", lambda e: e.reciprocal(out=stat[0:n, 40:40 + NH], in_=stat[0:n, 24:24 + NH]), ["stq2"], ["stq3"])
        if extra_scale != 1.0:
            S.add("dve", lambda e: e.tensor_scalar(out=stat[0:n, 40:40 + NH], in0=stat[0:n, 40:40 + NH], scalar1=float(extra_scale), scalar2=0.0,
                                                   op0=ALU.mult, op1=ALU.add), ["stq3"], ["stq3"])
        gb = qkg_bc[0:n, goff:goff + 64].unsqueeze(1).to_broadcast([n, NH, 64])
        S.add("dve", lambda e: e.tensor_tensor(out=v3(t0), in0=v3(t0), in1=gb, op=ALU.mult), ["tq0", "qkg_bc"], ["tq0"])
        cc = ropecc[0:n, ti, :].unsqueeze(1).to_broadcast([n, NH, 64])
        S.add("dve", lambda e: e.tensor_tensor(out=v3(t1), in0=v3(t0), in1=cc, op=ALU.mult), ["tq0", "ropecc", "stq"], ["tq1"])
        s_lo = ropess[0:n, ti, 0:32].unsqueeze(1).to_broadcast([n, NH, 32])
        s_hi = ropess[0:n, ti, 32:64].unsqueeze(1).to_broadcast([n, NH, 32])
        S.add("dve", lambda e: e.tensor_tensor(out=v3(t2)[:, :, 0:32], in0=v3(t0)[:, :, 32:64], in1=s_lo, op=ALU.mult), ["tq0", "ropess"], ["tq2a"])
        S.add("dve", lambda e: e.tensor_tensor(out=v3(t2)[:, :, 32:64], in0=v3(t0)[:, :, 0:32], in1=s_hi, op=ALU.mult), ["tq0", "ropess"], ["tq2b"])
        S.add("dve", lambda e: e.tensor_tensor(out=v3(t1), in0=v3(t1), in1=v3(t2), op=ALU.add), ["tq1", "tq2a", "tq2b"], ["tq1"])
        rb = stat[0:n, 40:40 + NH].unsqueeze(2).to_broadcast([n, NH, 64])
        dv = dst.rearrange("p (h d) -> p h d", d=64)
        S.add("dve", lambda e: e.tensor_tensor(out=dv, in0=v3(t1), in1=rb, op=ALU.mult), ["tq1", "stq3"], dst_keys)

    def proj_tok(lhs_fn, lhs_keys, n, slot, ncols, bank):
        kb = "PS%d" % bank
        for dc in range(ND):
            S.add("pe", lambda e, dc=dc: e.matmul(PS[bank][0:n, 0:ncols], lhsT=lhs_fn(dc), rhs=wbuf[slot][:, dc, 0:ncols],
                                                  start=(dc == 0), stop=(dc == ND - 1)), lhs_keys + ["wbuf%d" % slot], [kb])
        return kb

    pass
    S.mark('p1a')
    S.add("pool", lambda e: e.memset(u8m, 0.0), [], ["u8m"])
    load_w(0, din["w_in"], 2560, 512, ND)
    load_w(1, din["w_in"], 3072, 512, ND)

    def u_tile(lhs_fn, lhs_keys, n, evac_fn):
        for blk in range(2):
            b = cnt["kv"] % 2
            cnt["kv"] += 1
            kb = proj_tok(lhs_fn, lhs_keys, n, blk, 512, b)
            evac_fn(blk, PS[b], kb)

    S.mark("u_pre1")
    xp2 = din["x_pre2"].rearrange("(c s) d -> c s d", s=8)
    tl = [("smp", 0)]
    for i in range(8):
        tl += [("own", i), ("pre", i)]

    def p1_src(kind, i):
        if kind == "smp":
            return din["x_smp"], NS
        if kind == "own":
            return din["x_own"][i * 128:(i + 1) * 128, :], 128
        return xp2[:, i, :], 128

    def p1_finish(kind, i, h):
        if kind == "smp":
            norm_B(h, lambda dc0: xnT[:, dc0:dc0 + 8, NT:NTS], ["xnT_s"])
        elif kind == "own":
            norm_B(h, lambda dc0, i=i: xnT[:, dc0:dc0 + 8, i * 128:(i + 1) * 128], ["xnT_%d" % i])
        else:
            jt = i % 2
            tt = xnTt[jt]
            norm_B(h, lambda dc0, tt=tt: tt[:, dc0:dc0 + 8, 0:128], ["xnTt%d" % jt])

            def ev_p2(blk, ps_, kb, i=i):
                S.add("act", lambda e: e.copy(out=u8p2[:, blk * 32:(blk + 1) * 32, i, :], in_=ps_[:, 0:512].rearrange("p (g h) -> p g h", h=16)),
                      [kb], ["u8p2_%d_%d" % (i, blk)])
            u_tile(lambda dc, tt=tt: tt[:, dc, 0:128], ["xnTt%d" % jt], 128, ev_p2)

    pend1 = None
    for (kind, i) in tl:
        src_, n_ = p1_src(kind, i)
        h_ = norm_A(src_, n_)
        if pend1 is not None:
            p1_finish(*pend1)
        pend1 = (kind, i, h_)
    p1_finish(*pend1)
    S.mark("u_pre2")
    OWNK = ["xnT_%d" % i for i in range(8)]
    for i in range(8):
        def ev_own(blk, ps_, kb, i=i):
            S.add("act", lambda e: e.copy(out=u8own[:, blk * 32:(blk + 1) * 32, i, :], in_=ps_[:, 0:512].rearrange("p (g h) -> p g h", h=16)),
                  [kb], ["u8own_%d_%d" % (i, blk)])
        u_tile(lambda dc, i=i: xnT[:, dc, i:NT:8], OWNK, 128, ev_own)

    S.mark("u_own")

    def ev_smp(blk, ps_, kb):
        S.add("act", lambda e: e.copy(out=u8m[0:NS, blk * 32:(blk + 1) * 32, 7, :], in_=ps_[0:NS, 0:512].rearrange("p (g h) -> p g h", h=16)),
              [kb, "u8m"], ["u8m_s%d" % blk])
    u_tile(lambda dc: xnT[:, dc, NT:NTS], ["xnT_s"], NS, ev_smp)
    j = cnt["x"] % 2
    tt = xnTt[j]
    norm_tile(din["x_pre1"], 16, lambda dc0, tt=tt: tt[:, dc0:dc0 + 8, 0:16], ["xnTt%d" % j])

    def ev_pre1(blk, ps_, kb):
        S.add("act", lambda e: e.copy(out=utok1[0:16, blk * 512:(blk + 1) * 512], in_=ps_[0:16, 0:512]), [kb], ["utok1_%d" % blk])
    u_tile(lambda dc, tt=tt: tt[:, dc, 0:16], ["xnTt%d" % j], 16, ev_pre1)
    for s_ in range(8):
        dma("sp", u8m[32:34, :, s_, :], utok1[s_:16:8, :].rearrange("p (g h) -> p g h", h=16), ["utok1_0", "utok1_1", "u8m"], ["u8m_p1"], "u8m_p1")

    S.mark("u_smp")
    U8K_P2 = ["u8p2_%d_%d" % (i, b) for i in range(8) for b in range(2)]
    U8K_OWN = ["u8own_%d_%d" % (i, b) for i in range(8) for b in range(2)]
    U8K_M = ["u8m", "u8m_p1", "u8m_s0", "u8m_s1"]
    for part in range(2):
        for q4 in range(4):
            bk = 6 + q4 % 2
            kb = "PS%d" % bk
            pv = PSB[bk][:, 0:1024].rearrange("p (g t) -> p g t", t=64)
            for gl in range(16):
                gg = q4 * 16 + gl
                fence = (part == 1 and q4 == 0 and gl == 0)
                if part == 0:
                    S.add("pe", lambda e, gl=gl, gg=gg, pv=pv: e.transpose(out=pv[:, gl, 0:2], in_=u8m[32:34, gg, :, :].rearrange("p s h -> p (s h)"),
                                                                            identity=identb[32:34, 32:34]), U8K_M + ["identb"], [kb])
                else:
                    S.add("pe", lambda e, gl=gl, gg=gg, pv=pv: e.transpose(out=pv[:, gl, 0:16], in_=u8m[0:NS, gg, :, :].rearrange("p s h -> p (s h)"),
                                                                            identity=identb[0:NS, 0:NS]), U8K_M + ["identb"], [kb], fence=fence)
            if part == 0:
                S.add("dve", lambda e, q4=q4, pv=pv: e.tensor_copy(out=Um[:, q4 * 16:(q4 + 1) * 16, 0:2], in_=pv[:, :, 0:2]), [kb], ["Um_%d" % q4])
            else:
                S.add("dve", lambda e, q4=q4, pv=pv: e.tensor_copy(out=Um[:, q4 * 16:(q4 + 1) * 16, 2:18], in_=pv[:, :, 0:16]), [kb], ["Umb_%d" % q4])
    UMK = ["Um_%d" % q for q in range(4)] + ["Umb_%d" % q for q in range(4)]

    S.mark('p1b')
    S.barrier(mk_bar)
    S.mark('bar1')
    MAGIC = 12582912.0
    TWO_PI = float(2.0 * np.pi)
    SC = 1036
    GB = 4
    Ere = carve(51 * KB, 1536, F32).rearrange("p (g k) -> p g k", k=24)
    Eim = carve(57 * KB, 1536, F32).rearrange("p (g k) -> p g k", k=24)
    bbs = carve(63 * KB, 1024, F32).rearrange("p (g h) -> p g h", h=16)
    bbp = carve(67 * KB, 1024, F32).rearrange("p (g h) -> p g h", h=16)
    cself = carve(71 * KB, 1024, F32).rearrange("p (g h) -> p g h", h=16)
    cpart = carve(75 * KB, 1024, F32).rearrange("p (g h) -> p g h", h=16)
    sc = [carve(79 * KB + i * 4352, SC, F32) for i in range(4)]
    cosT = carve(96 * KB, GB * 259, F32).rearrange("p (g k) -> p g k", k=259)
    sinT = carve(96 * KB + 4352, GB * 259, F32).rearrange("p (g k) -> p g k", k=259)
    o_ = 96 * KB + 2 * 4352
    A32 = carve(o_, GB * 128, F32).rearrange("p (g q) -> p g q", q=128); o_ += 2 * KB
    R32 = carve(o_, GB * 128, F32).rearrange("p (g q) -> p g q", q=128); o_ += 2 * KB
    Ab = carve(o_, 2 * GB * 128, BF16).rearrange("p (w g q) -> p w g q", w=2, q=128); o_ += 2 * KB
    Mab = carve(o_, 2 * GB * 128, BF16).rearrange("p (w g q) -> p w g q", w=2, q=128); o_ += 2 * KB
    Msb = carve(o_, 2 * GB * 128, BF16).rearrange("p (w g q) -> p w g q", w=2, q=128); o_ += 2 * KB
    Mib = carve(o_, GB * 128, BF16).rearrange("p (g q) -> p g q", q=128); o_ += KB
    mt = []
    for i in range(2):
        mt.append(carve(o_, GB * 128, F32).rearrange("p (g q) -> p g q", q=128)); o_ += 2 * KB
    NCH = 258
    NU = 274
    Ub = carve(o_, GB * NU, BF16).rearrange("p (g t) -> p g t", t=NU); o_ += 2304
    Xt = carve(o_, GB * NCH, F32).rearrange("p (g t) -> p g t", t=NCH); o_ += 4352
    Xt2 = [carve(o_ + 1056 * i, NCH, F32) for i in range(2)]; o_ += 2112
    Gs = carve(o_, GB * 259, F32).rearrange("p (g k) -> p g k", k=259); o_ += 4352
    Xsm = carve(o_, GB * NS, F32).rearrange("p (g b) -> p g b", b=NS); o_ += 256
    st_s = carve(o_, GB * 128, F32).rearrange("p (g q) -> p g q", q=128); o_ += 2 * KB
    st_p = carve(o_, GB * 128, F32).rearrange("p (g q) -> p g q", q=128); o_ += 2 * KB
    hs1 = carve(o_, GB * NS, F32).rearrange("p (g b) -> p g b", b=NS); o_ += 256
    hs2 = carve(o_, GB * NS, F32).rearrange("p (g b) -> p g b", b=NS); o_ += 256
    hso = carve(o_, GB * 128, F32).rearrange("p (g q) -> p g q", q=128); o_ += 2 * KB
    sE = carve(o_, 2 * GB * 24, F32).rearrange("p (w g k) -> p w g k", w=2, k=24); o_ += KB
    hfo = carve(o_, 128, F32); o_ += 512
    assert o_ <= ARENA, o_
    o2 = o_
    Pcs = carve(o2, 2 * GB * 128, BF16).rearrange("p (w g q) -> p w g q", w=2, q=128); o2 += 2 * KB
    y8 = carve(o2, 8 * 128, BF16).rearrange("p (t f) -> p t f", f=128); o2 += 2 * KB
    y8s = carve(o2, 128, BF16); o2 += 256
    assert o2 <= ARENA, o2
    yA = sb("yA", [128, GB * 128], F32)[:].rearrange("p (g q) -> p g q", q=128)
    yB = sb("yB", [128, GB * 128], F32)[:].rearrange("p (g q) -> p g q", q=128)
    b3 = o2
    ygb = carve(b3, GB * 128, BF16).rearrange("p (g q) -> p g q", q=128)
    ysA = carve(b3 + 1024, GB * NS, F32).rearrange("p (g b) -> p g b", b=NS)
    ysB = carve(b3 + 1280, GB * NS, F32).rearrange("p (g b) -> p g b", b=NS)
    ygs = carve(b3 + 1536, GB * NS, BF16).rearrange("p (g b) -> p g b", b=NS)
    hn1 = carve(b3 + 1664, GB * NS, F32).rearrange("p (g b) -> p g b", b=NS)
    hn2 = carve(b3 + 1920, GB * NS, F32).rearrange("p (g b) -> p g b", b=NS)
    Hins = carve(b3 + 2176, GB * NS, BF16).rearrange("p (g b) -> p g b", b=NS)
    assert b3 + 2304 <= ARENA, b3
    ST = carve(34 * KB, 8 * NTS, BF16).rearrange("p (c t) -> p c t", t=NTS)
    bself = carve(96 * KB, 1024, F32).rearrange("p (g h) -> p g h", h=16)
    bpart = carve(100 * KB, 1024, F32).rearrange("p (g h) -> p g h", h=16)
    a_re2 = carve(104 * KB, 64, F32)
    a_im2 = carve(104 * KB + 256, 64, F32)
    ldt = carve(104 * KB + 512, 64, F32)

    dma("sp", a_re2, din["a_re2"], [], ["a_re2"], "a_re2")
    dma("sp", a_im2, din["a_im2"], [], ["a_im2"], "a_im2")
    dma("sp", ldt, din["logdt"].partition_broadcast(128), [], ["ldt"], "ldt")
    dma("sp", bself, din["b_self"], [], ["bself"], "bself")
    dma("sp", bpart, din["b_part"], [], ["bpart"], "bpart")
    dma("sp", cself, din["c_self"], [], ["cself"], "cself")
    dma("sp", cpart, din["c_part"], [], ["cpart"], "cpart")
    dma("sp", kvals[:], din["kvals"].partition_broadcast(128), [], ["kvals"], "kvals")
    dma("sp", krow[:], din["krow"].partition_broadcast(128), [], ["krow"], "krow")
    dma("sp", blockmask[:], din["blockmask"], [], ["blockmask"], "blockmask")
    dma("sp", swapm[:], din["swapm"], [], ["swapm"], "swapm")
    dma("sp", dsk[:], din["dsk"], [], ["dsk"], "dsk")
    S.add("pool", lambda e: e.memset(sgn[0:64, 0:1], -1.0), [], ["sgn_a"])
    S.add("pool", lambda e: e.memset(sgn[64:128, 0:1], 1.0), [], ["sgn_b"])
    S.add("pool", lambda e: e.memset(sgn[0:64, 1:2], 1.0), [], ["sgn_c"])
    S.add("pool", lambda e: e.memset(sgn[64:128, 1:2], -1.0), [], ["sgn_d"])
    S.add("pool", lambda e: e.memset(halfpi[:], float(np.pi / 2)), [], ["halfpi"])
    S.add("pool", lambda e: e.memset(onec[:], 1.0), [], ["onec"])
    SGN = ["sgn_a", "sgn_b", "sgn_c", "sgn_d"]

    def sincos(ang, cos_out, sin_out, F, rk, wk_cos, wk_sin):
        tA, tB, tC = sc[1][:, 0:F], sc[2][:, 0:F], sc[3][:, 0:F]
        S.add("dve", lambda e: e.tensor_scalar(out=tA, in0=ang, scalar1=float(1.0 / TWO_PI), scalar2=MAGIC, op0=ALU.mult, op1=ALU.add), rk, ["sc1"])
        S.add("dve", lambda e: e.tensor_scalar(out=tA, in0=tA, scalar1=-MAGIC, scalar2=-TWO_PI, op0=ALU.add, op1=ALU.mult), ["sc1"], ["sc1"])
        S.add("dve", lambda e: e.tensor_tensor(out=tA, in0=tA, in1=ang, op=ALU.add), ["sc1"] + rk, ["sc1"])
        S.add("act", lambda e: e.activation(out=tB, in_=tA, func=AF.Sin, scale=0.5), ["sc1"], ["sc2"])
        S.add("act", lambda e: e.activation(out=tC, in_=tA, func=AF.Sin, scale=0.5, bias=halfpi[:, 0:1]), ["sc1", "halfpi"], ["sc3"])
        S.add("dve", lambda e: e.scalar_tensor_tensor(out=sin_out, in0=tB, scalar=2.0, in1=tC, op0=ALU.mult, op1=ALU.mult), ["sc2", "sc3"], wk_sin)
        S.add("act", lambda e: e.activation(out=tA, in_=tB, func=AF.Square, scale=float(np.sqrt(2.0))), ["sc2", "sc1"], ["sc1"])
        S.add("act", lambda e: e.activation(out=cos_out, in_=tA, func=AF.Identity, scale=-1.0, bias=onec[:, 0:1]), ["sc1", "onec"], wk_cos)

    PPK = lambda i: "pp%d" % i
    S.add("act", lambda e: e.activation(out=pp[:, 0, :], in_=ldt, func=AF.Exp), ["ldt"], [PPK(0)])
    S.add("dve", lambda e: e.tensor_tensor(out=pp[:, 1, :], in0=a_re2, in1=pp[:, 0, :], op=ALU.mult), ["a_re2", PPK(0)], [PPK(1)])
    S.add("dve", lambda e: e.tensor_tensor(out=pp[:, 2, :], in0=a_im2, in1=pp[:, 0, :], op=ALU.mult), ["a_im2", PPK(0)], [PPK(2)])
    kv_b = kvals[:].unsqueeze(1).to_broadcast([128, 32, 24])
    v24 = lambda t: t[:, 0:768].rearrange("p (g k) -> p g k", k=24)
    for hf in range(2):
        gh = slice(hf * 32, hf * 32 + 32)
        S.add("dve", lambda e, gh=gh: e.tensor_tensor(out=v24(sc[0]), in0=pp[:, 2, gh].unsqueeze(2).to_broadcast([128, 32, 24]), in1=kv_b, op=ALU.mult),
              [PPK(2), "kvals", "sc0"], ["sc0"])
        ere_h = Ere[:, gh, :].rearrange("p g k -> p (g k)")
        eim_h = Eim[:, gh, :].rearrange("p g k -> p (g k)")
        sincos(sc[0][:, 0:768], ere_h, eim_h, 768, ["sc0"], ["Ere"], ["Eim"])
        S.add("dve", lambda e, gh=gh: e.tensor_tensor(out=v24(sc[0]), in0=pp[:, 1, gh].unsqueeze(2).to_broadcast([128, 32, 24]), in1=kv_b, op=ALU.mult),
              [PPK(1), "kvals", "sc0"], ["sc0"])
        S.add("act", lambda e: e.activation(out=sc[0][:, 0:768], in_=sc[0][:, 0:768], func=AF.Exp), ["sc0"], ["sc0"])
        S.add("act", lambda e, gh=gh: e.copy(out=mumag[:, gh], in_=v24(sc[0])[:, :, 23]), ["sc0"], ["mumag"])
        S.add("dve", lambda e, ere_h=ere_h: e.tensor_tensor(out=ere_h, in0=ere_h, in1=sc[0][:, 0:768], op=ALU.mult), ["Ere", "sc0"], ["Ere"])
        S.add("dve", lambda e, eim_h=eim_h: e.tensor_tensor(out=eim_h, in0=eim_h, in1=sc[0][:, 0:768], op=ALU.mult), ["Eim", "sc0"], ["Eim"])
    S.add("dve", lambda e: e.tensor_scalar(out=pp[:, 6, :], in0=pp[:, 2, :], scalar1=float(8.0 / TWO_PI), scalar2=MAGIC, op0=ALU.mult, op1=ALU.add), [PPK(2)], [PPK(6)])
    S.add("dve", lambda e: e.tensor_scalar(out=pp[:, 6, :], in0=pp[:, 6, :], scalar1=-MAGIC, scalar2=-TWO_PI, op0=ALU.add, op1=ALU.mult), [PPK(6)], [PPK(6)])
    S.add("dve", lambda e: e.scalar_tensor_tensor(out=pp[:, 3, :], in0=pp[:, 2, :], scalar=8.0, in1=pp[:, 6, :], op0=ALU.mult, op1=ALU.add), [PPK(2), PPK(6)], [PPK(3)])
    e1r, e1i = Ere[:, :, 16], Eim[:, :, 16]
    S.add("dve", lambda e: e.tensor_scalar(out=pp[:, 7, :], in0=e1r, scalar1=-1.0, scalar2=0.0, op0=ALU.add, op1=ALU.add), ["Ere"], [PPK(7)])
    S.add("dve", lambda e: e.tensor_tensor(out=pp[:, 8, :], in0=a_re2, in1=a_re2, op=ALU.mult), ["a_re2"], [PPK(8)])
    S.add("dve", lambda e: e.tensor_tensor(out=pp[:, 9, :], in0=a_im2, in1=a_im2, op=ALU.mult), ["a_im2"], [PPK(9)])
    S.add("dve", lambda e: e.tensor_tensor(out=pp[:, 8, :], in0=pp[:, 8, :], in1=pp[:, 9, :], op=ALU.add), [PPK(8), PPK(9)], [PPK(8)])
    S.add("dve", lambda e: e.reciprocal(out=pp[:, 8, :], in_=pp[:, 8, :]), [PPK(8)], [PPK(8)])
    S.add("dve", lambda e: e.tensor_tensor(out=pp[:, 9, :], in0=pp[:, 7, :], in1=a_re2, op=ALU.mult), [PPK(7), "a_re2", PPK(8)], [PPK(9)])
    S.add("dve", lambda e: e.tensor_tensor(out=pp[:, 10, :], in0=e1i, in1=a_im2, op=ALU.mult), ["Eim", "a_im2"], [PPK(10)])
    S.add("dve", lambda e: e.tensor_tensor(out=pp[:, 9, :], in0=pp[:, 9, :], in1=pp[:, 10, :], op=ALU.add), [PPK(9), PPK(10)], [PPK(9)])
    S.add("dve", lambda e: e.tensor_tensor(out=pp[:, 4, :], in0=pp[:, 9, :], in1=pp[:, 8, :], op=ALU.mult), [PPK(9), PPK(8)], [PPK(4)])
    S.add("dve", lambda e: e.tensor_tensor(out=pp[:, 9, :], in0=e1i, in1=a_re2, op=ALU.mult), ["Eim", "a_re2", PPK(4)], [PPK(9)])
    S.add("dve", lambda e: e.tensor_tensor(out=pp[:, 10, :], in0=pp[:, 7, :], in1=a_im2, op=ALU.mult), [PPK(7), "a_im2", PPK(9)], [PPK(10)])
    S.add("dve", lambda e: e.tensor_tensor(out=pp[:, 9, :], in0=pp[:, 9, :], in1=pp[:, 10, :], op=ALU.subtract), [PPK(9), PPK(10)], [PPK(9)])
    S.add("dve", lambda e: e.tensor_tensor(out=pp[:, 9, :], in0=pp[:, 9, :], in1=pp[:, 8, :], op=ALU.mult), [PPK(9), PPK(8)], [PPK(9)])
    S.add("dve", lambda e: e.tensor_scalar(out=pp[:, 5, :], in0=pp[:, 9, :], scalar1=sgn[:, 0:1], scalar2=1.0, op0=ALU.mult, op1=ALU.mult), [PPK(9)] + SGN, [PPK(5)])
    wr_b = pp[:, 4, :].unsqueeze(2).to_broadcast([128, 64, 16])
    swi_b = pp[:, 5, :].unsqueeze(2).to_broadcast([128, 64, 16])
    v16 = lambda t: t[:, 0:1024].rearrange("p (g h) -> p g h", h=16)
    S.add("dve", lambda e: e.tensor_tensor(out=v16(sc[0]), in0=bself, in1=wr_b, op=ALU.mult), ["bself", PPK(4), "sc0"], ["sc0"])
    S.add("pool", lambda e: e.tensor_tensor(out=v16(sc[1]), in0=bpart, in1=swi_b, op=ALU.mult), ["bpart", PPK(5), "sc1"], ["sc1"])
    S.add("dve", lambda e: e.tensor_tensor(out=bbs, in0=v16(sc[0]), in1=v16(sc[1]), op=ALU.add), ["sc0", "sc1"], ["bbs"])
    S.add("dve", lambda e: e.tensor_tensor(out=v16(sc[0]), in0=bpart, in1=wr_b, op=ALU.mult), ["bpart", PPK(4), "sc0"], ["sc0"])
    S.add("pool", lambda e: e.tensor_tensor(out=v16(sc[1]), in0=bself, in1=swi_b, op=ALU.mult), ["bself", PPK(5), "sc1"], ["sc1"])
    S.add("dve", lambda e: e.tensor_tensor(out=bbp, in0=v16(sc[0]), in1=v16(sc[1]), op=ALU.subtract), ["sc0", "sc1"], ["bbp"])
    S.mark('params')
    S.barrier(mk_bar)
    S.mark('bar2')

    S.add("pool", lambda e: e.memset(Gs, 0.0), [], ["Gs"])

    def bc4(T, V):
        return (T.unsqueeze(3).to_broadcast([128, GB, 8, 16]), V.unsqueeze(2).to_broadcast([128, GB, 8, 16]))

    def v4(t):
        return t.rearrange("p g (s h) -> p g s h", h=16)

    def gen_mat(out_ap, T1, V1, T2, V2, op2, rk, wk, add_eng="dve", mul_eng="pool"):
        a0, a1 = bc4(T1, V1)
        b0, b1 = bc4(T2, V2)
        S.add("dve", lambda e: e.tensor_tensor(out=v4(mt[0]), in0=a0, in1=a1, op=ALU.mult), rk, ["mt0"])
        S.add(mul_eng, lambda e: e.tensor_tensor(out=v4(mt[1]), in0=b0, in1=b1, op=ALU.mult), rk, ["mt1"])
        S.add(add_eng, lambda e: e.tensor_tensor(out=out_ap, in0=mt[0], in1=mt[1], op=op2), ["mt0", "mt1"], wk)

    def ssm_frontA(gb):
        g8 = slice(gb * GB, gb * GB + GB)
        sEim, nsEre = sE[:, 0, :, :], sE[:, 1, :, :]
        EreB, EimB = Ere[:, g8, :], Eim[:, g8, :]
        UBK = ["Ub_%d" % g for g in range(GB)] + ["Ub_a%d" % g for g in range(GB)] + ["Ub_b%d" % g for g in range(GB)]
        GSK = ["Gs_%d" % g for g in range(GB)]
        S.add("dve", lambda e, g8=g8: e.tensor_scalar(out=sE[:, 0, :, :], in0=Eim[:, g8, :], scalar1=sgn[:, 0:1], scalar2=1.0, op0=ALU.mult, op1=ALU.mult), ["Eim"] + SGN, ["sE0"])
        S.add("dve", lambda e, g8=g8: e.tensor_scalar(out=sE[:, 1, :, :], in0=Ere[:, g8, :], scalar1=sgn[:, 1:2], scalar2=1.0, op0=ALU.mult, op1=ALU.mult), ["Ere"] + SGN, ["sE1"])
        sEim, nsEre = sE[:, 0, :, :], sE[:, 1, :, :]
        EreB, EimB = Ere[:, g8, :], Eim[:, g8, :]
        gen_mat(A32, EreB[:, :, 0:8], bbs[:, g8, :], sEim[:, :, 0:8], bbp[:, g8, :], ALU.add, ["Ere", "sE0", "bbs", "bbp"], ["A32"], mul_eng="dve")
        S.add("act", lambda e: e.copy(out=Ab[:, 0, :, :], in_=A32), ["A32"], ["Ab0"])
        gen_mat(Ab[:, 1, :, :], nsEre[:, :, 0:8], bbp[:, g8, :], EimB[:, :, 0:8], bbs[:, g8, :], ALU.add, ["Eim", "sE1", "bbs", "bbp"], ["Ab1"], mul_eng="dve")
        pv = PSB[4][:, 0:1024].rearrange("p (w g t) -> p w g t", w=2, g=GB)
        for w_ in range(2):
            for g in range(GB):
                S.add("pe", lambda e, w_=w_, g=g, pv=pv: e.transpose(out=pv[:, w_, g, :], in_=Ab[:, w_, g, :], identity=identb[:]), ["Ab%d" % w_, "identb"], ["PS4"])
        S.add("act", lambda e, pv=pv: e.copy(out=Msb, in_=pv), ["PS4"], ["Msb"])

    def ssm_sincos(gb):
        g8 = slice(gb * GB, gb * GB + GB)
        sEim, nsEre = sE[:, 0, :, :], sE[:, 1, :, :]
        EreB, EimB = Ere[:, g8, :], Eim[:, g8, :]
        UBK = ["Ub_%d" % g for g in range(GB)] + ["Ub_a%d" % g for g in range(GB)] + ["Ub_b%d" % g for g in range(GB)]
        GSK = ["Gs_%d" % g for g in range(GB)]
        S.add("pool", lambda e, g8=g8: e.tensor_tensor(out=sc[0][:, 0:GB * 259].rearrange("p (g k) -> p g k", k=259),
                                                       in0=pp[:, 3, g8].unsqueeze(2).to_broadcast([128, GB, 259]),
                                                       in1=krow[:, 0:259].unsqueeze(1).to_broadcast([128, GB, 259]), op=ALU.mult),
              [PPK(3), "krow", "sc0"], ["sc0"])
        sincos(sc[0][:, 0:GB * 259], cosT.rearrange("p g k -> p (g k)"), sinT.rearrange("p g k -> p (g k)"), GB * 259, ["sc0"], ["cosT"], ["sinT"])
        S.add("act", lambda e, gb=gb: e.copy(out=C258[:, gb * GB:gb * GB + GB], in_=cosT[:, :, 258]), ["cosT"], ["C258"])
        S.add("act", lambda e, gb=gb: e.copy(out=S258[:, gb * GB:gb * GB + GB], in_=sinT[:, :, 258]), ["sinT"], ["S258"])

    def ssm_frontB(gb):
        g8 = slice(gb * GB, gb * GB + GB)
        sEim, nsEre = sE[:, 0, :, :], sE[:, 1, :, :]
        EreB, EimB = Ere[:, g8, :], Eim[:, g8, :]
        UBK = ["Ub_%d" % g for g in range(GB)] + ["Ub_a%d" % g for g in range(GB)] + ["Ub_b%d" % g for g in range(GB)]
        GSK = ["Gs_%d" % g for g in range(GB)]
        gen_mat(R32, nsEre[:, :, 8:16], cself[:, g8, :], EimB[:, :, 8:16], cpart[:, g8, :], ALU.subtract, ["Eim", "sE1", "cself", "cpart"], ["R32"], mul_eng="dve")
        gen_mat(Mab[:, 0, :, :], nsEre[:, :, 16:24], cself[:, g8, :], EimB[:, :, 16:24], cpart[:, g8, :], ALU.subtract, ["Eim", "sE1", "cself", "cpart"], ["Mab0"], mul_eng="dve")
        gen_mat(Mab[:, 1, :, :], sEim[:, :, 16:24], cself[:, g8, :], EreB[:, :, 16:24], cpart[:, g8, :], ALU.subtract, ["Ere", "sE0", "cself", "cpart"], ["Mab1"], mul_eng="dve")
        for g0 in range(0, GB, 2):
            bk = 6 + (g0 // 2) % 2
            kb = "PS%d" % bk
            pv2 = PSB[bk][:, 30:30 + 640].rearrange("p (g t) -> p g t", t=320)
            for jj in range(2):
                gg = gb * GB + g0 + jj
                S.add("pe", lambda e, jj=jj, gg=gg, pv2=pv2: e.transpose(out=pv2[:, jj, 2:130], in_=u8p2[:, gg, :, :].rearrange("p s h -> p (s h)"), identity=identb[:]),
                      U8K_P2 + ["identb"], [kb])
                S.add("pe", lambda e, jj=jj, gg=gg, pv2=pv2: e.transpose(out=pv2[:, jj, 130:258], in_=u8own[:, gg, :, :].rearrange("p s h -> p (s h)"), identity=identb[:]),
                      U8K_OWN + ["identb"], [kb])
                S.add("act", lambda e, jj=jj, gg=gg, g0=g0, pv2=pv2: e.copy(out=Ub[:, g0 + jj, 2:258], in_=pv2[:, jj, 2:258]), [kb], ["Ub_%d" % (g0 + jj)])
                S.add("act", lambda e, jj=jj, gg=gg, g0=g0: e.copy(out=Ub[:, g0 + jj, 0:2], in_=Um[:, gg, 0:2]), UMK, ["Ub_a%d" % (g0 + jj)])
                S.add("act", lambda e, jj=jj, gg=gg, g0=g0: e.copy(out=Ub[:, g0 + jj, 258:274], in_=Um[:, gg, 2:18]), UMK, ["Ub_b%d" % (g0 + jj)])
        UBK = ["Ub_%d" % g for g in range(GB)] + ["Ub_a%d" % g for g in range(GB)] + ["Ub_b%d" % g for g in range(GB)]
        dma("sp", st_s[0:NS], din["st_self"][:, g8, :], [], ["st_s"], "st_s")
        dma("sp", st_p[0:NS], din["st_part"][:, g8, :], [], ["st_p"], "st_p")
        for g in range(GB):
            S.add("pe", lambda e, g=g: e.transpose(out=PS[2][:, g * NS:(g + 1) * NS], in_=st_s[0:NS, g, :], identity=identf[0:NS, 0:NS]), ["st_s", "identf"], ["PS2"])
            S.add("pe", lambda e, g=g: e.transpose(out=PS[2][:, 128 + g * NS:128 + (g + 1) * NS], in_=st_p[0:NS, g, :], identity=identf[0:NS, 0:NS]), ["st_p", "identf"], ["PS2"])
        h0v = PS[2][:, 0:GB * NS].rearrange("p (g b) -> p g b", b=NS)
        h0s = PS[2][:, 128:128 + GB * NS].rearrange("p (g b) -> p g b", b=NS)
        S.add("dve", lambda e, g8=g8: e.tensor_tensor(out=hs1, in0=h0v, in1=Ere[:, g8, 16:17].to_broadcast([128, GB, NS]), op=ALU.mult), ["PS2", "Ere"], ["hs1"])
        S.add("dve", lambda e: e.tensor_tensor(out=hs2, in0=h0s, in1=sE[:, 0, :, 16:17].to_broadcast([128, GB, NS]), op=ALU.mult), ["PS2", "sE0"], ["hs2"])
        S.add("pool", lambda e: e.tensor_tensor(out=hs1, in0=hs1, in1=hs2, op=ALU.add), ["hs1", "hs2"], ["hs1"])
        S.add("dve", lambda e, g8=g8: e.tensor_tensor(out=hn1, in0=h0v, in1=Ere[:, g8, 8:9].to_broadcast([128, GB, NS]), op=ALU.mult), ["PS2", "Ere"], ["hn1k"])
        S.add("dve", lambda e: e.tensor_tensor(out=hn2, in0=h0s, in1=sE[:, 0, :, 8:9].to_broadcast([128, GB, NS]), op=ALU.mult), ["PS2", "sE0"], ["hn2k"])
        S.add("pool", lambda e: e.tensor_tensor(out=Hins, in0=hn1, in1=hn2, op=ALU.add), ["hn1k", "hn2k"], ["Hinsk"])

    def ssm_loop(gb):
        g8 = slice(gb * GB, gb * GB + GB)
        sEim, nsEre = sE[:, 0, :, :], sE[:, 1, :, :]
        EreB, EimB = Ere[:, g8, :], Eim[:, g8, :]
        UBK = ["Ub_%d" % g for g in range(GB)] + ["Ub_a%d" % g for g in range(GB)] + ["Ub_b%d" % g for g in range(GB)]
        GSK = ["Gs_%d" % g for g in range(GB)]
        for g in range(GB):
            gg = gb * GB + g
            pa, pb_ = (0, 1) if g % 2 == 0 else (6, 7)
            ka, kb_ = "PS%d" % pa, "PS%d" % pb_
            S.add("pe", lambda e, g=g, pa=pa: e.matmul(PS[pa][:, 0:NU], lhsT=Msb[:, 0, g, :], rhs=Ub[:, g, :], start=True, stop=True), ["Msb"] + UBK, [ka])
            S.add("pe", lambda e, g=g, pb_=pb_: e.matmul(PS[pb_][:, 0:NCH], lhsT=Msb[:, 1, g, :], rhs=Ub[:, g, 0:NCH], start=True, stop=True), ["Msb"] + UBK, [kb_])
            S.add("dve", lambda e, g=g, pa=pa: e.tensor_tensor(out=Xt[:, g, :], in0=PS[pa][:, 0:NCH], in1=cosT[:, g, 1:259], op=ALU.mult), [ka, "cosT"], ["Xt_%d" % g])
            S.add("act", lambda e, g=g, pa=pa: e.copy(out=Xsm[:, g, :], in_=PS[pa][:, NCH:NU]), [ka], ["Xsm_%d" % g])
            x2 = Xt2[g % 2]
            S.add("dve", lambda e, g=g, pb_=pb_, x2=x2: e.tensor_tensor(out=x2, in0=PS[pb_][:, 0:NCH], in1=sinT[:, g, 1:259], op=ALU.mult), [kb_, "sinT"], ["Xt2_%d" % (g % 2)])
            S.add("dve", lambda e, g=g, x2=x2: e.tensor_tensor(out=Xt[:, g, :], in0=Xt[:, g, :], in1=x2, op=ALU.add), ["Xt_%d" % g, "Xt2_%d" % (g % 2)], ["Xt_%d" % g])
            S.add("dve", lambda e, g=g, gg=gg: e.tensor_tensor_scan(out=Gs[:, g, 1:259], data0=mumag[:, gg:gg + 1].to_broadcast([128, NCH]), data1=Xt[:, g, :],
                                                                    initial=0.0, op0=ALU.mult, op1=ALU.add), ["Xt_%d" % g, "mumag", "Gs"], ["Gs_%d" % g])
            S.add("act", lambda e, g=g, gg=gg: e.copy(out=G258[:, gg:gg + 1], in_=Gs[:, g, 258:259]), ["Gs_%d" % g], ["G258"])
        S.add("dve", lambda e: e.tensor_tensor(out=hs1, in0=hs1, in1=Xsm, op=ALU.add), ["hs1"] + ["Xsm_%d" % g for g in range(GB)], ["hs1"])

    def ssm_y1(gb):
        g8 = slice(gb * GB, gb * GB + GB)
        sEim, nsEre = sE[:, 0, :, :], sE[:, 1, :, :]
        EreB, EimB = Ere[:, g8, :], Eim[:, g8, :]
        UBK = ["Ub_%d" % g for g in range(GB)] + ["Ub_a%d" % g for g in range(GB)] + ["Ub_b%d" % g for g in range(GB)]
        GSK = ["Gs_%d" % g for g in range(GB)]
        GSK = ["Gs_%d" % g for g in range(GB)]
        for g in range(GB):
            S.add("pe", lambda e, g=g: e.matmul(PS[5][:, g * 128:(g + 1) * 128], lhsT=A32[:, g, :], rhs=R32[:, g, :], start=True, stop=True), ["A32", "R32"], ["PS5"])
        S.add("dve", lambda e: e.tensor_tensor(out=Mib, in0=PS[5][:, 0:GB * 128].rearrange("p (g q) -> p g q", q=128),
                                               in1=blockmask[:].unsqueeze(1).to_broadcast([128, GB, 128]), op=ALU.mult), ["PS5", "blockmask"], ["Mib"])

    def ssm_y2a(gb):
        g8 = slice(gb * GB, gb * GB + GB)
        sEim, nsEre = sE[:, 0, :, :], sE[:, 1, :, :]
        EreB, EimB = Ere[:, g8, :], Eim[:, g8, :]
        UBK = ["Ub_%d" % g for g in range(GB)] + ["Ub_a%d" % g for g in range(GB)] + ["Ub_b%d" % g for g in range(GB)]
        GSK = ["Gs_%d" % g for g in range(GB)]
        S.add("dve", lambda e: e.tensor_tensor(out=Pcs[:, 0, :, :], in0=cosT[:, :, 130:258], in1=Gs[:, :, 130:258], op=ALU.mult), ["cosT"] + GSK, ["Pc0"])
        S.add("dve", lambda e: e.tensor_tensor(out=Pcs[:, 1, :, :], in0=sinT[:, :, 130:258], in1=Gs[:, :, 130:258], op=ALU.mult), ["sinT"] + GSK, ["Pc1"])
        for g in range(GB):
            osl = PS[5][:, g * 128:(g + 1) * 128]
            S.add("pe", lambda e, g=g, osl=osl: e.matmul(osl, lhsT=Mib[:, g, :], rhs=Ub[:, g, 130:258], start=True, stop=False), ["Mib"] + UBK, ["PS5"])
            S.add("pe", lambda e, g=g, osl=osl: e.matmul(osl, lhsT=Mab[:, 0, g, :], rhs=Pcs[:, 0, g, :], start=False, stop=False), ["Mab0", "Pc0"], ["PS5"])
            S.add("pe", lambda e, g=g, osl=osl: e.matmul(osl, lhsT=Mab[:, 1, g, :], rhs=Pcs[:, 1, g, :], start=False, stop=True), ["Mab1", "Pc1"], ["PS5"])
            oss = PS[4][:, g * NS:(g + 1) * NS]
            S.add("pe", lambda e, g=g, oss=oss: e.matmul(oss, lhsT=Mib[:, g, :], rhs=Ub[:, g, 258:274], start=True, stop=False), ["Mib"] + UBK, ["PS4"])
            S.add("pe", lambda e, g=g, oss=oss: e.matmul(oss, lhsT=Mab[:, 0, g, :], rhs=Hins[:, g, :], start=False, stop=True), ["Mab0", "Hinsk"], ["PS4"])


    def ssm_y2b(gb):
        g8 = slice(gb * GB, gb * GB + GB)
        sEim, nsEre = sE[:, 0, :, :], sE[:, 1, :, :]
        EreB, EimB = Ere[:, g8, :], Eim[:, g8, :]
        UBK = ["Ub_%d" % g for g in range(GB)] + ["Ub_a%d" % g for g in range(GB)] + ["Ub_b%d" % g for g in range(GB)]
        GSK = ["Gs_%d" % g for g in range(GB)]
        def gelu_chain(ps_view, u_view, ya, yb, yout, W, kin, kout, tag):
            kk = "ypo" if tag == "o" else "yps"
            dsk_b = dsk[:, g8].unsqueeze(2).to_broadcast([128, GB, W])
            S.add("dve", lambda e: e.tensor_tensor(out=ya, in0=u_view, in1=dsk_b, op=ALU.mult), UBK + ["dsk", kk], [kk])
            S.add("dve", lambda e: e.tensor_tensor(out=ya, in0=ya, in1=ps_view, op=ALU.add), [kk] + kin, [kk])
            S.add("act", lambda e: e.activation(out=yb, in_=ya, func=AF.Square), [kk], [kk])
            S.add("dve", lambda e: e.scalar_tensor_tensor(out=yb, in0=yb, scalar=0.044715, in1=ya, op0=ALU.mult, op1=ALU.mult), [kk], [kk])
            S.add("dve", lambda e: e.tensor_tensor(out=yb, in0=yb, in1=ya, op=ALU.add), [kk], [kk])
            S.add("act", lambda e: e.activation(out=yb, in_=yb, func=AF.Tanh, scale=0.7978845608028654), [kk], [kk])
            S.add("dve", lambda e: e.scalar_tensor_tensor(out=yout, in0=yb, scalar=1.0, in1=ya, op0=ALU.add, op1=ALU.mult), [kk] + kout, kout)
        gelu_chain(PS[5][:, 0:GB * 128].rearrange("p (g q) -> p g q", q=128), Ub[:, :, 130:258], yA, yB, ygb, 128, ["PS5"], ["ygbk"], "o")
        gelu_chain(PS[4][:, 0:GB * NS].rearrange("p (g b) -> p g b", b=NS), Ub[:, :, 258:274], ysA, ysB, ygs, NS, ["PS4"], ["ygsk"], "s")
        goff = (gb % 2) * GB
        pvy = PSB[6][:, 0:GB * 128].rearrange("p (g q) -> p g q", q=128)
        for g in range(GB):
            S.add("pe", lambda e, g=g: e.transpose(out=pvy[:, g, :], in_=ygb[:, g, :], identity=identb[:]), ["ygbk", "identb"], ["PS6"])
        S.add("act", lambda e, goff=goff: e.activation(out=y8.rearrange("p t (g h) -> p g t h", h=16)[:, goff:goff + GB, :, :],
                                                       in_=pvy.rearrange("p g (t h) -> p g t h", h=16), func=AF.Identity, scale=0.5), ["PS6"], ["y8_%d" % (gb % 2)])
        pvs = PSB[7][:, 0:GB * 128].rearrange("p (g q) -> p g q", q=128)
        for g in range(GB):
            S.add("pe", lambda e, g=g: e.transpose(out=pvs[0:NS, g, :], in_=ygs[:, g, :], identity=identb[:]), ["ygsk", "identb"], ["PS7"])
        S.add("act", lambda e, goff=goff: e.activation(out=y8s[0:NS, goff * 16:(goff + GB) * 16].rearrange("p (g h) -> p g h", h=16), in_=pvs[0:NS, :, 112:128],
                                                       func=AF.Identity, scale=0.5), ["PS7"], ["y8s_%d" % (gb % 2)])
        if gb % 2 == 1:
            fc = gb // 2
            pvt = PSB[0][:, 0:1024].rearrange("p (t c) -> p t c", c=128)
            for t in range(8):
                S.add("pe", lambda e, t=t: e.transpose(out=pvt[:, t, :], in_=y8[:, t, :], identity=identb[:]), ["y8_0", "y8_1", "identb"], ["PS0"])
            S.add("act", lambda e, fc=fc: e.copy(out=ST[:, fc, 0:NT].rearrange("p (c t) -> p t c", t=8), in_=pvt), ["PS0"], ["ST_%d" % fc])
            S.add("pe", lambda e: e.transpose(out=PSB[1][:, 0:NS], in_=y8s[0:NS, :], identity=identb[0:NS, 0:NS]), ["y8s_0", "y8s_1", "identb"], ["PS1"])
            S.add("dve", lambda e, fc=fc: e.tensor_copy(out=ST[:, fc, NT:NTS], in_=PSB[1][:, 0:NS]), ["PS1"], ["STs_%d" % fc])
        for g in range(GB):
            S.add("pe", lambda e, g=g: e.transpose(out=PS[3][0:NS, g * 128:(g + 1) * 128], in_=hs1[:, g, :], identity=identf[:]), ["hs1", "identf"], ["PS3"])
        S.add("act", lambda e: e.copy(out=hso[0:NS], in_=PS[3][0:NS, 0:GB * 128].rearrange("p (g q) -> p g q", q=128)), ["PS3"], ["hso"])
        dma("sp", dout["sssm"][:, g8, :], hso[0:NS], ["hso"], [], "hso")

    NBATCH = 64 // GB
    ssm_sincos(0)
    for gb in range(NBATCH):
        ssm_frontA(gb)
        ssm_frontB(gb)
        ssm_loop(gb)
        ssm_y1(gb)
        ssm_y2a(gb)
        if gb + 1 < NBATCH:
            ssm_sincos(gb + 1)
        ssm_y2b(gb)
        S.mark('batch%d' % gb)

    S.add("pe", lambda e: e.matmul(PS[2][:, 0:64], lhsT=swapm[:], rhs=G258[:], start=True, stop=True), ["swapm", "G258"], ["PS2"])
    S.add("dve", lambda e: e.tensor_tensor(out=pp[:, 11, :], in0=C258[:], in1=G258[:], op=ALU.mult), ["C258", "G258"], [PPK(11)])
    S.add("dve", lambda e: e.tensor_scalar(out=pp[:, 12, :], in0=S258[:], scalar1=sgn[:, 0:1], scalar2=1.0, op0=ALU.mult, op1=ALU.mult), ["S258"] + SGN, [PPK(12)])
    S.add("dve", lambda e: e.tensor_tensor(out=pp[:, 12, :], in0=pp[:, 12, :], in1=PS[2][:, 0:64], op=ALU.mult), [PPK(12), "PS2"], [PPK(12)])
    S.add("dve", lambda e: e.tensor_tensor(out=pp[:, 11, :], in0=pp[:, 11, :], in1=pp[:, 12, :], op=ALU.add), [PPK(11), PPK(12)], [PPK(11)])
    S.add("pe", lambda e: e.transpose(out=PS[3][0:64, 0:128], in_=pp[:, 11, :], identity=identf[:]), [PPK(11), "identf"], ["PS3"])
    S.add("act", lambda e: e.copy(out=hfo[0:64, :], in_=PS[3][0:64, 0:128]), ["PS3"], ["hfo"])
    dma("sp", dout["pssm"], hfo[0:64, :], ["hfo"], [], "hfo")

    S.mark('p2')
    S.barrier(mk_bar)
    NKC = 1168
    qT = carve(0, 8 * NTS, BF16).rearrange("p (h t) -> p h t", t=NTS)
    kT = carve(17 * KB, 4 * NKC, BF16).rearrange("p (g t) -> p g t", t=NKC)
    PT = [[carve(27 * KB + (3 * j + i) * KB, 512, BF16) for i in range(3)] for j in range(2)]
    AT = carve(51 * KB, 8 * NTS, BF16).rearrange("p (c t) -> p c t", t=NTS)
    Vaug = carve(100 * KB, 10 * 4 * 128, BF16).rearrange("p (b g q) -> p b g q", b=10, q=128)
    o3 = 100 * KB + 10 * KB
    tmpq = [carve(o3 + 4 * KB * i, 1024, F32) for i in range(3)]; o3 += 12 * KB
    sqj = carve(o3 - 8 * KB, D, F32)
    xbuf_off3 = o3
    xbuf = [carve(o3, D, F32)] * 2; o3 += 8 * KB
    xnb = [carve(o3, D, BF16)] * 2; o3 += 4 * KB
    xnTt = [carve(o3, ND * 128, BF16).rearrange("p (c t) -> p c t", c=ND)] * 2; o3 += 4 * KB
    kf = [carve(o3 + i * KB, 256, F32) for i in range(2)]; o3 += 2 * KB
    vf = [carve(o3 + i * KB, 256, F32) for i in range(2)]; o3 += 2 * KB
    kb16 = carve(o3, 256, BF16); o3 += 512
    qf = carve(o3, 512, F32); o3 += 2 * KB
    qb16 = carve(o3, 512, BF16); o3 += KB
    otmp = [carve(o3 + i * KB, 512, BF16) for i in range(2)]; o3 += 2 * KB
    dtmp2 = [carve(xbuf_off3 + 2 * KB * i, 512, F32) for i in range(2)]
    assert o3 <= ARENA, o3
    gain_bc = carve(0, D, F32)
    cnt["x"] = 0
    dma("sp", gain_bc, din["norm_gain"].partition_broadcast(128), [], ["gain_bc"], "gain_bc")
    dma("pool", masks[:, 0, :], din["maskc"], [], ["masks0"], "masks0")
    dma("pool", masks[:, 1, :], din["maskp"], [], ["masks1"], "masks1")
    dma("pool", masks[:, 2, :], din["maskp0"], [], ["masks2"], "masks2")
    dma("sp", expsink[:], din["sinks"].partition_broadcast(128), [], ["expsink"], "expsink")
    S.add("act", lambda e: e.activation(out=expsink[:], in_=expsink[:], func=AF.Exp), ["expsink"], ["expsink"])
    S.add("pool", lambda e: e.memset(onesb[:], 1.0), [], ["onesb"])
    S.add("pool", lambda e: e.memset(Vaug[:, :, :, 64:128], 1.0), [], ["Vones"])

    load_w(0, din["w_in"], 1024, 512, ND)

    def kv_proj(lhs_fn, lhs_keys, n):
        b = cnt["kv"] % 2
        cnt["kv"] += 1
        kb = proj_tok(lhs_fn, lhs_keys, n, 0, 512, b)
        return b, kb

    def kv_post(b, kb, n, ti, kcol0, vblk, out_k=None, out_v=None):
        kfj, vfj = kf[b], vf[b]
        S.add("act", lambda e: e.copy(out=vfj[0:n, :], in_=PS[b][0:n, 256:512]), [kb], ["vf%d" % b])
        headnorm_rope([(PS[b][0:n, 0:256], 256)], [kb], n, 4, ti, 64, 1.0, kfj[0:n, :], ["kf%d" % b])
        if out_k is not None:
            dma("sp", out_k, kfj[0:n, :], ["kf%d" % b], ["swk_new"], "kf%d" % b)
        if out_v is not None:
            dma("sp", out_v, vfj[0:n, :], ["vf%d" % b], ["swv_new"], "vf%d" % b)
        if vblk is not None:
            S.add("pool", lambda e: e.tensor_copy(out=Vaug[0:n, vblk, :, 0:64], in_=vfj[0:n, :].rearrange("p (g d) -> p g d", d=64)),
                  ["vf%d" % b, "Vones"], ["Vaug_%d" % vblk])
            S.add("act", lambda e: e.copy(out=kb16[0:n, :], in_=kfj[0:n, :]), ["kf%d" % b], ["kb16"])
            pvk = PSB[6][:, 0:512].rearrange("p (g t) -> p g t", t=128)
            for g in range(4):
                S.add("pe", lambda e, g=g: e.transpose(out=pvk[0:64, g, 0:n], in_=kb16[0:n, g * 64:(g + 1) * 64], identity=identb[0:n, 0:n]),
                      ["kb16", "identb"], ["PS6"])
            S.add("act", lambda e: e.copy(out=kT[0:64, :, kcol0:kcol0 + n], in_=pvk[0:64, :, 0:n]), ["PS6"], ["kT_%d" % vblk])

    def kv_tile(lhs_fn, lhs_keys, n, ti, kcol0, vblk, out_k=None, out_v=None):
        b, kb = kv_proj(lhs_fn, lhs_keys, n)
        kv_post(b, kb, n, ti, kcol0, vblk, out_k, out_v)

    for (r0, n, ti, kc0, vb, ok, ov) in ((0, 128, 8, 1024, 8, None, None), (128, 16, 9, 1152, 9, dout["pmk"], dout["pmv"])):
        tt = xnTt[0]
        norm_tile(din["x_kvx"][r0:r0 + n, :], n, lambda dc0, tt=tt, n=n: tt[:, dc0:dc0 + 8, 0:n], ["xnTt0"])
        kv_tile(lambda dc, tt=tt, n=n: tt[:, dc, 0:n], ["xnTt0"], n, ti, kc0, vb, ok, ov)
    S.add("pool", lambda e: e.memset(qT, 0.0), ["gain_bc"], ["qT", "gain_bc"])
    dma("sp", dout["swk"][:, 0:127, :], din["cwk"][:, 1:128, :], [], [], "swk")
    dma("sp", dout["swv"][:, 0:127, :], din["cwv"][:, 1:128, :], [], [], "swv")
    kspecs = [(lambda dc, i=i: xnT[:, dc, i * 128:(i + 1) * 128], ["xnT_%d" % i], 128, i, i * 128, i,
               dout["pwk"] if i == 7 else None, dout["pwv"] if i == 7 else None) for i in range(8)]
    kspecs.append((lambda dc: xnT[:, dc, NT:NTS], ["xnT_s"], NS, 10, 0, None, dout["swk"][:, 127, :], dout["swv"][:, 127, :]))
    pendk = kv_proj(kspecs[0][0], kspecs[0][1], kspecs[0][2])
    for ki, (lf, lk, n_, ti_, kc_, vb_, ok_, ov_) in enumerate(kspecs):
        curk = pendk
        if ki + 1 < len(kspecs):
            pendk = kv_proj(kspecs[ki + 1][0], kspecs[ki + 1][1], kspecs[ki + 1][2])
        kv_post(curk[0], curk[1], n_, ti_, kc_, vb_, ok_, ov_)
    KTK = ["kT_%d" % i for i in range(10)]
    VK = ["Vaug_%d" % i for i in range(10)] + ["Vones"]

    def q_proj(lhs_fn, lhs_keys, n, slot):
        b = cnt["kv"] % 2
        cnt["kv"] += 1
        kb = proj_tok(lhs_fn, lhs_keys, n, slot, 512, b)
        return b, kb

    def q_post(b, kb, n, ti, tokc0, smp_hoff):
        headnorm_rope([(PS[b][0:n, 0:512], 512)], [kb], n, 8, ti, 0, 0.125, qf[0:n, :], ["qf"])
        if smp_hoff is not None:
            S.add("act", lambda e: e.copy(out=qs16[0:n, smp_hoff * 64:(smp_hoff + 8) * 64], in_=qf[0:n, :]), ["qf"], ["qs16_%d" % smp_hoff])
            return
        S.add("act", lambda e: e.copy(out=qb16[0:n, :], in_=qf[0:n, :]), ["qf"], ["qb16"])
        pvq = PSB[7][:, 0:1024].rearrange("p (h t) -> p h t", t=128)
        for h in range(8):
            S.add("pe", lambda e, h=h: e.transpose(out=pvq[0:64, h, 0:n], in_=qb16[0:n, h * 64:(h + 1) * 64], identity=identb[0:n, 0:n]),
                  ["qb16", "identb"], ["PS7"])
        S.add("act", lambda e: e.copy(out=qT[0:64, :, tokc0:tokc0 + n], in_=pvq[0:64, :, 0:n]), ["PS7", "qT"], ["qT_%d" % (tokc0 // 128)])

    def att_setup(nb, g, hl0, pb):
        d_ = dict(nb=nb, g=g, hl0=hl0, pb=pb)
        d_["banks"] = (2, 3, 4, 5) if pb == 0 else (6, 7, 0, 1)
        d_["prev_cols"] = slice(1024, 1152) if nb == 0 else slice((nb - 1) * 128, nb * 128)
        d_["prev_blk"] = 8 if nb == 0 else nb - 1
        d_["mprev"] = 2 if nb == 0 else 1
        return d_

    def att_A(d_):
        nb, g, hl0, pb = d_["nb"], d_["g"], d_["hl0"], d_["pb"]
        PSm, PSp, PSc, PSo = d_["banks"]
        PTm, PTp, PTc = PT[pb]
        qk = ["qT_%d" % nb]
        cur_cols = slice(nb * 128, (nb + 1) * 128)
        prev_cols, mprev = d_["prev_cols"], d_["mprev"]
        for r in range(4):
            rs = slice(r * 128, (r + 1) * 128)
            qa = qT[0:64, hl0 + r, nb * 128:(nb + 1) * 128]
            S.add("pe", lambda e, rs=rs, qa=qa: e.matmul(PS[PSm][0:16, rs], lhsT=kT[0:64, g, 1152:1168], rhs=qa, start=True, stop=True), KTK + qk, ["PS%d" % PSm])
            S.add("pe", lambda e, rs=rs: e.matmul(PS[PSp][:, rs], lhsT=identb[:], rhs=masks[:, mprev, :], start=True, stop=False), ["identb", "masks1", "masks2"], ["PS%d" % PSp])
            S.add("pe", lambda e, rs=rs, qa=qa: e.matmul(PS[PSp][:, rs], lhsT=kT[0:64, g, prev_cols], rhs=qa, start=False, stop=True), KTK + qk, ["PS%d" % PSp])
            S.add("pe", lambda e, rs=rs: e.matmul(PS[PSc][:, rs], lhsT=identb[:], rhs=masks[:, 0, :], start=True, stop=False), ["identb", "masks0"], ["PS%d" % PSc])
            S.add("pe", lambda e, rs=rs, qa=qa: e.matmul(PS[PSc][:, rs], lhsT=kT[0:64, g, cur_cols], rhs=qa, start=False, stop=True), KTK + qk, ["PS%d" % PSc])
        km, kp, kc = "PTm%d" % pb, "PTp%d" % pb, "PTc%d" % pb
        S.add("act", lambda e: e.activation(out=PTm[0:16, :], in_=PS[PSm][0:16, :], func=AF.Exp), ["PS%d" % PSm], [km])
        S.add("act", lambda e: e.activation(out=PTp, in_=PS[PSp][:, :], func=AF.Exp), ["PS%d" % PSp], [kp])
        S.add("act", lambda e: e.activation(out=PTc, in_=PS[PSc][:, :], func=AF.Exp), ["PS%d" % PSc], [kc])

    def att_B(d_):
        nb, g, pb = d_["nb"], d_["g"], d_["pb"]
        PSm, PSp, PSc, PSo = d_["banks"]
        PTm, PTp, PTc = PT[pb]
        km, kp, kc = "PTm%d" % pb, "PTp%d" % pb, "PTc%d" % pb
        prev_blk = d_["prev_blk"]
        for r in range(4):
            rs = slice(r * 128, (r + 1) * 128)
            S.add("pe", lambda e, rs=rs: e.matmul(PS[PSo][:, rs], lhsT=Vaug[0:16, 9, g, :], rhs=PTm[0:16, rs], start=True, stop=False), VK + [km], ["PS%d" % PSo])
            S.add("pe", lambda e, rs=rs: e.matmul(PS[PSo][:, rs], lhsT=Vaug[:, prev_blk, g, :], rhs=PTp[:, rs], start=False, stop=False), VK + [kp], ["PS%d" % PSo])
            S.add("pe", lambda e, rs=rs: e.matmul(PS[PSo][:, rs], lhsT=Vaug[:, nb, g, :], rhs=PTc[:, rs], start=False, stop=True), VK + [kc], ["PS%d" % PSo])

    def att_C(d_):
        nb, g, pb = d_["nb"], d_["g"], d_["pb"]
        PSo = d_["banks"][3]
        dtm = dtmp2[pb]
        dv = dtm[64:128, :].rearrange("p (r q) -> p r q", q=128)
        kd = "dtmp%d" % pb
        S.add("dve", lambda e: e.tensor_tensor(out=dv, in0=PS[PSo][64:128, :].rearrange("p (r q) -> p r q", q=128),
                                               in1=expsink[64:128, 4 * g:4 * g + 4].unsqueeze(2).to_broadcast([64, 4, 128]), op=ALU.add), ["PS%d" % PSo, "expsink"], [kd, "xbuf0"])
        S.add("act", lambda e: e.activation(out=dtm[64:128, :], in_=dtm[64:128, :], func=AF.Ln), [kd], [kd])
        S.add("act", lambda e: e.activation(out=dtm[64:128, :], in_=dtm[64:128, :], func=AF.Exp, scale=-1.0), [kd], [kd])
        ot = otmp[pb]
        S.add("dve", lambda e: e.tensor_tensor(out=ot[0:64, :], in0=PS[PSo][0:64, :], in1=dtm[64:128, :], op=ALU.mult), ["PS%d" % PSo, kd], ["otmp%d" % pb])
        o3v = ot[0:64, :].rearrange("p (r q) -> p r q", q=128)
        dma("sp", AT[0:64, 2 * g:2 * g + 2, nb * 128:(nb + 1) * 128], o3v[:, 0:4:2, :], ["otmp%d" % pb], ["AT_%d_%d" % (g, nb)], "otmp%d" % pb)
        dma("sp", AT[64:128, 2 * g:2 * g + 2, nb * 128:(nb + 1) * 128], o3v[:, 1:4:2, :], ["otmp%d" % pb], ["ATb_%d_%d" % (g, nb)], "otmp%d" % pb)

    acnt = 0
    for qblk in range(2):
        load_w(1, din["w_in"], qblk * 512, 512, ND)
        qspecs = [(lambda dc, i=i: xnT[:, dc, i * 128:(i + 1) * 128], ["xnT_%d" % i], 128, i, i * 128, None) for i in range(8)]
        qspecs.append((lambda dc: xnT[:, dc, NT:NTS], ["xnT_s"], NS, 10, 0, qblk * 8))
        pend = q_proj(qspecs[0][0], qspecs[0][1], qspecs[0][2], 1)
        for qi, (lf, lk, n_, ti_, tc_, sh_) in enumerate(qspecs):
            cur = pend
            if qi + 1 < len(qspecs):
                pend = q_proj(qspecs[qi + 1][0], qspecs[qi + 1][1], qspecs[qi + 1][2], 1)
            q_post(cur[0], cur[1], n_, ti_, tc_, sh_)
        blocks = []
        for nb in range(8):
            for gl in range(2):
                blocks.append(att_setup(nb, qblk * 2 + gl, gl * 4, acnt % 2))
                acnt += 1
        att_A(blocks[0])
        for bi in range(len(blocks)):
            if bi + 1 < len(blocks):
                att_A(blocks[bi + 1])
            att_B(blocks[bi])
            att_C(blocks[bi])
    ATK = ["AT_%d_%d" % (g, nb) for g in range(4) for nb in range(8)] + ["ATb_%d_%d" % (g, nb) for g in range(4) for nb in range(8)]
    S.mark('p3a')

    S.barrier(mk_bar)
    Kc = carve(0, NS * 256, BF16).rearrange("p (b e) -> p b e", e=256)
    Vc = carve(8 * KB, NS * 256, BF16).rearrange("p (b e) -> p b e", e=256)
    Kmc = carve(16 * KB, NS * 256, BF16).rearrange("p (b e) -> p b e", e=256)
    Vmc = carve(24 * KB, NS * 256, BF16).rearrange("p (b e) -> p b e", e=256)
    KsT = carve(100 * KB, NS * 4 * 128, BF16).rearrange("p (b g t) -> p b g t", b=NS, t=128)
    KsT2 = carve(116 * KB, NS * 4 * 32, BF16).rearrange("p (b g t) -> p b g t", b=NS, t=32)
    o4 = 120 * KB
    qsT = carve(o4, 16 * NS, BF16).rearrange("p (h b) -> p h b", b=NS); o4 += 512
    PTs = carve(o4, 256, BF16); o4 += 512
    PTs2 = carve(o4, 256, BF16); o4 += 512
    dts = carve(o4, 256, F32); o4 += KB
    osb = carve(o4, 256, BF16).rearrange("p (h b) -> p h b", b=NS); o4 += 512
    load_w(0, din["w_in"], 1536, 512, ND)
    load_w(1, din["w_in"], 1536 + 512, 512, ND)
    dma("pool", Kc, din["cwk"].rearrange("b j e -> j b e"), [], ["Kc"], "Kc")
    dma("pool", Vc, din["cwv"].rearrange("b j e -> j b e"), [], ["Vc"], "Vc")
    dma("pool", Kmc[0:16], din["cmk"].rearrange("b j e -> j b e"), [], ["Kmc_a"], "Kmc")
    dma("pool", Vmc[0:16], din["cmv"].rearrange("b j e -> j b e"), [], ["Vmc_a"], "Vmc")
    dma("pool", Kmc[16:17], dout["swk"][:, 127:128, :].rearrange("b o e -> o b e"), ["swk_new"], ["Kmc_b"], "Kmc")
    dma("pool", Vmc[16:17], dout["swv"][:, 127:128, :].rearrange("b o e -> o b e"), ["swv_new"], ["Vmc_b"], "Vmc")
    pvq2 = PSB[7][:, 0:256].rearrange("p (h b) -> p h b", b=NS)
    for h in range(16):
        S.add("pe", lambda e, h=h: e.transpose(out=pvq2[0:64, h, :], in_=qs16[0:NS, h * 64:(h + 1) * 64], identity=identb[0:NS, 0:NS]),
              ["qs16_0", "qs16_8", "identb"], ["PS7"])
    S.add("act", lambda e: e.copy(out=qsT[0:64], in_=pvq2[0:64]), ["PS7"], ["qsT"])
    for b0 in range(0, NS, 2):
        bk = 4 + (b0 // 2) % 2
        pvk2 = PSB[bk][:, 0:1024].rearrange("p (b g t) -> p b g t", b=2, t=128)
        for bb in range(2):
            for g in range(4):
                S.add("pe", lambda e, bb=bb, g=g, b0=b0, pvk2=pvk2: e.transpose(out=pvk2[0:64, bb, g, :], in_=Kc[:, b0 + bb, g * 64:(g + 1) * 64], identity=identb[:]),
                      ["Kc", "identb"], ["PS%d" % bk])
        S.add("act" if (b0 // 2) % 2 == 0 else "dve",
              (lambda e, b0=b0, pvk2=pvk2: e.copy(out=KsT[0:64, b0:b0 + 2], in_=pvk2[0:64])) if (b0 // 2) % 2 == 0 else
              (lambda e, b0=b0, pvk2=pvk2: e.tensor_copy(out=KsT[0:64, b0:b0 + 2], in_=pvk2[0:64])), ["PS%d" % bk], ["KsT_%d" % b0])
    KSK = ["KsT_%d" % b0 for b0 in range(0, NS, 2)]
    for b0 in range(0, NS, 8):
        bk = 6 + (b0 // 8) % 2
        pvk3 = PSB[bk][:, 0:1024].rearrange("p (b g t) -> p b g t", b=8, t=32)
        for bb in range(8):
            for g in range(4):
                S.add("pe", lambda e, bb=bb, g=g, b0=b0, pvk3=pvk3: e.transpose(out=pvk3[0:64, bb, g, 0:17], in_=Kmc[0:17, b0 + bb, g * 64:(g + 1) * 64],
                                                                              identity=identb[0:17, 0:17]), ["Kmc_a", "Kmc_b", "identb"], ["PS%d" % bk])
        S.add("act", lambda e, b0=b0, pvk3=pvk3: e.copy(out=KsT2[0:64, b0:b0 + 8, :, 0:17], in_=pvk3[0:64, :, :, 0:17]), ["PS%d" % bk], ["KsT2_%d" % b0])
    KS2K = ["KsT2_0", "KsT2_8"]
    for b in range(NS):
        for g in range(4):
            cs = slice(b * 16 + g * 4, b * 16 + g * 4 + 4)
            S.add("pe", lambda e, b=b, g=g, cs=cs: e.matmul(PS[0][:, cs], lhsT=KsT[0:64, b, g, :], rhs=qsT[0:64, 4 * g:4 * g + 4, b], start=True, stop=True),
                  KSK + ["qsT"], ["PS0"])
            S.add("pe", lambda e, b=b, g=g, cs=cs: e.matmul(PS[1][0:17, cs], lhsT=KsT2[0:64, b, g, 0:17], rhs=qsT[0:64, 4 * g:4 * g + 4, b], start=True, stop=True),
                  KS2K + ["qsT"], ["PS1"])
    S.add("act", lambda e: e.activation(out=PTs, in_=PS[0][:, 0:256], func=AF.Exp), ["PS0"], ["PTs"])
    S.add("act", lambda e: e.activation(out=PTs2[0:17], in_=PS[1][0:17, 0:256], func=AF.Exp), ["PS1"], ["PTs2"])
    S.add("pool", lambda e: e.memset(PTs[0:1, :], 0.0), ["PTs"], ["PTs"])
    S.add("pe", lambda e: e.matmul(PS[2][0:64, 0:256], lhsT=onesb[:, 0:64], rhs=PTs, start=True, stop=False), ["onesb", "PTs"], ["PS2"])
    S.add("pe", lambda e: e.matmul(PS[2][0:64, 0:256], lhsT=onesb[0:17, 0:64], rhs=PTs2[0:17], start=False, stop=True), ["onesb", "PTs2"], ["PS2"])
    for b in range(NS):
        for g in range(4):
            cs = slice(b * 16 + g * 4, b * 16 + g * 4 + 4)
            S.add("pe", lambda e, b=b, g=g, cs=cs: e.matmul(PS[3][0:64, cs], lhsT=Vc[:, b, g * 64:(g + 1) * 64], rhs=PTs[:, cs], start=True, stop=False), ["Vc", "PTs"], ["PS3"])
            S.add("pe", lambda e, b=b, g=g, cs=cs: e.matmul(PS[3][0:64, cs], lhsT=Vmc[0:17, b, g * 64:(g + 1) * 64], rhs=PTs2[0:17, cs], start=False, stop=True),
                  ["Vmc_a", "Vmc_b", "PTs2"], ["PS3"])
    dv3 = dts[0:64].rearrange("p (b h) -> p b h", h=16)
    S.add("dve", lambda e: e.tensor_tensor(out=dv3, in0=PS[2][0:64, 0:256].rearrange("p (b h) -> p b h", h=16),
                                           in1=expsink[0:64, :].unsqueeze(1).to_broadcast([64, NS, 16]), op=ALU.add), ["PS2", "expsink"], ["dts"])
    S.add("dve", lambda e: e.reciprocal(out=dts[0:64], in_=dts[0:64]), ["dts"], ["dts"])
    S.add("dve", lambda e: e.tensor_tensor(out=osb[0:64].rearrange("p h b -> p b h"), in0=PS[3][0:64, 0:256].rearrange("p (b h) -> p b h", h=16), in1=dv3, op=ALU.mult),
          ["PS3", "dts"], ["osb"])
    dma("sp", AT[0:64, :, NT:NTS], osb[0:64, 0:16:2, :], ["osb"], ["AT_s0"], "osb")
    dma("sp", AT[64:128, :, NT:NTS], osb[0:64, 1:16:2, :], ["osb"], ["AT_s1"], "osb")
    ATK = ATK + ["AT_s0", "AT_s1"]
    S.mark('p3b')

    S.barrier(mk_bar)
    mT = carve(0, 16 * NTS, BF16).rearrange("p (c t) -> p c t", t=NTS)
    ST2 = carve(100 * KB, 8 * NTS, BF16).rearrange("p (c t) -> p c t", t=NTS)
    o5 = 117 * KB
    slt = [carve(o5 + 2 * KB * i, 512, F32) for i in range(2)]; o5 += 4 * KB
    bra = carve(o5, 2 * NTS, F32).rearrange("p (c t) -> p c t", t=NTS); o5 += 2 * NTS * 4
    m1 = carve(o5, 2 * NTS, F32).rearrange("p (c t) -> p c t", t=NTS); o5 += 2 * NTS * 4
    o5 = (o5 + 3) // 4 * 4
    xin = [carve(o5 + 2 * KB * i, 512, F32) for i in range(3)]; o5 += 6 * KB
    oti = [carve(o5 + 2 * KB * i, 512, F32) for i in range(3)]; o5 += 6 * KB
    assert o5 <= ARENA, o5
    dma("sp", b_glu[:], din["b_glu"], [], ["b_glu"], "b_glu")
    TB = ((0, 512), (512, 512), (NT, NS))
    XK = ["xnT_%d" % i for i in range(8)] + ["xnT_s"]
    STK = ["ST_%d" % i for i in range(8)] + ["STs_%d" % i for i in range(8)]
    c4 = {"ps": 0, "sl": 0}

    def feat_mm(slot, wc0, nk, rhs_fn, rk):
        for (t0, N) in TB:
            b = c4["ps"] % 4
            c4["ps"] += 1
            kb = "PS%d" % b
            for kc in range(nk):
                S.add("pe", lambda e, kc=kc, b=b, t0=t0, N=N: e.matmul(PS[b][:, 0:N], lhsT=wbuf[slot][:, kc, wc0:wc0 + 128], rhs=rhs_fn(kc, t0, N),
                                                                       start=(kc == 0), stop=(kc == nk - 1)), rk + ["wbuf%d" % slot], [kb])
            yield b, kb, t0, N

    def act_tmp(b, kb, N, func, bias=None):
        j = c4["sl"] % 2
        c4["sl"] += 1
        t = slt[j]
        if bias is None:
            S.add("act", lambda e: e.activation(out=t[:, 0:N], in_=PS[b][:, 0:N], func=func), [kb], ["slt%d" % j])
        else:
            S.add("act", lambda e: e.activation(out=t[:, 0:N], in_=PS[b][:, 0:N], func=func, bias=bias), [kb, "b_glu"], ["slt%d" % j])
        return t, "slt%d" % j

    xrhs = lambda kc, t0, N: xnT[:, kc, t0:t0 + N]
    for blk in range(2):
        sl_ = blk % 2
        for c in range(4):
            jc = blk * 4 + c
            for b, kb, t0, N in feat_mm(sl_, c * 128, ND, xrhs, XK):
                t, tk = act_tmp(b, kb, N, AF.Silu)
                S.add("dve", lambda e, t=t, jc=jc, t0=t0, N=N: e.tensor_tensor(out=AT[:, jc, t0:t0 + N], in0=AT[:, jc, t0:t0 + N], in1=t[:, 0:N], op=ALU.mult),
                      [tk] + ATK, ["A_%d_%d" % (jc, t0)])
    AK = ["A_%d_%d" % (jc, t0) for jc in range(8) for (t0, _) in TB]
    strhs = lambda kc, t0, N: ST[:, kc, t0:t0 + N]
    for blk in range(2):
        sl_ = blk % 2
        load_w(sl_, din["w_glu"], blk * 512, 512, 8)
        for c in range(4):
            jc = blk * 4 + c
            for b, kb, t0, N in feat_mm(sl_, c * 128, 8, strhs, STK):
                t, tk = act_tmp(b, kb, N, AF.Sigmoid, bias=b_glu[:, jc:jc + 1])
                S.add("dve", lambda e, t=t, jc=jc, t0=t0, N=N: e.tensor_tensor(out=ST2[:, jc, t0:t0 + N], in0=ST[:, jc, t0:t0 + N], in1=t[:, 0:N], op=ALU.mult),
                      [tk] + STK, ["S2_%d_%d" % (jc, t0)])
    for blk in range(2):
        sl_ = blk % 2
        load_w(sl_, din["w_in"], 3584 + blk * 512, 512, ND)
        for c in range(4):
            jc = blk * 4 + c
            for b, kb, t0, N in feat_mm(sl_, c * 128, ND, xrhs, XK):
                t, tk = act_tmp(b, kb, N, AF.Silu)
                S.add("dve", lambda e, t=t, jc=jc, t0=t0, N=N: e.tensor_tensor(out=ST2[:, jc, t0:t0 + N], in0=ST2[:, jc, t0:t0 + N], in1=t[:, 0:N], op=ALU.mult),
                      [tk, "S2_%d_%d" % (jc, t0)], ["S2_%d_%d" % (jc, t0)])
    S2K = ["S2_%d_%d" % (jc, t0) for jc in range(8) for (t0, _) in TB]
    arhs = lambda kc, t0, N: AT[:, kc, t0:t0 + N]
    s2rhs = lambda kc, t0, N: ST2[:, kc, t0:t0 + N]
    wq = [carve((68 + 8 * i) * KB, ND * 256, BF16).rearrange("p (c n) -> p c n", c=ND) for i in range(4)]

    def load_wq(slot, src, col0, nk):
        key = "wq%d" % slot
        v = src.rearrange("(c p) n -> p c n", p=128)
        for c0 in range(0, nk, 8):
            dma("pool", wq[slot][:, c0:c0 + 8, :], v[:, c0:c0 + 8, col0:col0 + 256], [], [key], key)

    def feat_mm_q(slot, wc0, nk, rhs_fn, rk):
        for (t0, N) in TB:
            b = c4["ps"] % 4
            c4["ps"] += 1
            kb = "PS%d" % b
            for kc in range(nk):
                S.add("pe", lambda e, kc=kc, b=b, t0=t0, N=N: e.matmul(PS[b][:, 0:N], lhsT=wq[slot][:, kc, wc0:wc0 + 128], rhs=rhs_fn(kc, t0, N),
                                                                       start=(kc == 0), stop=(kc == nk - 1)), rk + ["wq%d" % slot], [kb])
            yield b, kb, t0, N

    stages = ((0, din["w_ao"], 0, 8), (1, din["w_in"], 4608, ND), (2, din["w_so"], 0, 8), (3, din["w_in"], 6656, ND))
    S.add("pool", lambda e: e.memset(dummy[0:1, 6:7], 0.0), [], ["wbuf0", "wbuf1", "wq0", "wq1", "wq2", "wq3"])
    for (sl_, src, c0_, nk_) in stages:
        load_wq(sl_, src, c0_, nk_)
    for gq in range(8):
        for half_ in range(2):
            rhs_fn, rk, last = ((arhs, AK, False), (s2rhs, S2K, True))[half_]
            sa, sg_ = 2 * half_, 2 * half_ + 1
            for c in range(2):
                for b, kb, t0, N in feat_mm_q(sa, c * 128, 8, rhs_fn, rk):
                    S.add("act", lambda e, b=b, c=c, t0=t0, N=N: e.copy(out=bra[:, c, t0:t0 + N], in_=PS[b][:, 0:N]), [kb], ["bra_%d_%d" % (c, t0)])
            if gq < 7:
                load_wq(sa, stages[sa][1], stages[sa][2] + (gq + 1) * 256, stages[sa][3])
            for c in range(2):
                fc = gq * 2 + c
                for b, kb, t0, N in feat_mm_q(sg_, c * 128, ND, xrhs, XK):
                    t, tk = act_tmp(b, kb, N, AF.Sigmoid)
                    if not last:
                        S.add("dve", lambda e, t=t, c=c, t0=t0, N=N: e.tensor_tensor(out=m1[:, c, t0:t0 + N], in0=t[:, 0:N], in1=bra[:, c, t0:t0 + N], op=ALU.mult),
                              [tk, "bra_%d_%d" % (c, t0)], ["m1_%d_%d" % (c, t0)])
                    else:
                        S.add("dve", lambda e, t=t, c=c, t0=t0, N=N: e.tensor_tensor(out=t[:, 0:N], in0=t[:, 0:N], in1=bra[:, c, t0:t0 + N], op=ALU.mult),
                              [tk, "bra_%d_%d" % (c, t0)], [tk])
                        S.add("dve", lambda e, t=t, c=c, fc=fc, t0=t0, N=N: e.tensor_tensor(out=mT[:, fc, t0:t0 + N], in0=t[:, 0:N], in1=m1[:, c, t0:t0 + N], op=ALU.add),
                              [tk, "m1_%d_%d" % (c, t0)], ["mT_%d_%d" % (fc, t0)])
            if gq < 7:
                load_wq(sg_, stages[sg_][1], stages[sg_][2] + (gq + 1) * 256, stages[sg_][3])
    MK = ["mT_%d_%d" % (fc, t0) for fc in range(16) for (t0, _) in TB]
    S.mark('p4')
    S.add("pool", lambda e: e.memset(dummy[0:1, 5:6], 0.0), [], ["wbuf0", "wbuf1", "wq0", "wq1", "wq2", "wq3"])
    tiles5 = []
    for cb in range(4):
        for i in range(9):
            tiles5.append((cb, i))

    def p5_load(k):
        cb, i = tiles5[k]
        n = 128 if i < 8 else NS
        xsrc = din["x_own"][i * 128:(i + 1) * 128, cb * 512:(cb + 1) * 512] if i < 8 else din["x_smp"][:, cb * 512:(cb + 1) * 512]
        j = k % 3
        dma("sp", xin[j][0:n, :], xsrc, [], ["xin%d" % j], "xin%d" % j)

    p5_load(0)
    p5_load(1)
    for k, (cb, i) in enumerate(tiles5):
        sl_ = cb % 2
        if i == 0:
            load_w(sl_, din["w_out"], cb * 512, 512, ND)
        n = 128 if i < 8 else NS
        tc0 = i * 128 if i < 8 else NT
        ydst = dout["y_own"][i * 128:(i + 1) * 128, cb * 512:(cb + 1) * 512] if i < 8 else dout["y_smp"][:, cb * 512:(cb + 1) * 512]
        j = k % 3
        b = c4["ps"] % 4
        c4["ps"] += 1
        kb = "PS%d" % b
        if k + 2 < len(tiles5):
            p5_load(k + 2)
        for fc in range(16):
            S.add("pe", lambda e, fc=fc, b=b, n=n, tc0=tc0, sl_=sl_: e.matmul(PS[b][0:n, 0:512], lhsT=mT[:, fc, tc0:tc0 + n], rhs=wbuf[sl_][:, fc, 0:512],
                                                                              start=(fc == 0), stop=(fc == 15)), MK + ["wbuf%d" % sl_], [kb])
        S.add("dve", lambda e, j=j, b=b, n=n: e.tensor_tensor(out=oti[j][0:n, :], in0=PS[b][0:n, 0:512], in1=xin[j][0:n, :], op=ALU.add),
              [kb, "xin%d" % j], ["oti%d" % j])
        dma("sp", ydst, oti[j][0:n, :], ["oti%d" % j], [], "oti%d" % j)
    S.mark('p5')

    S.emit()
    for cm in reversed(ctxs):
        cm.__exit__(None, None, None)
    return nc


def _rope_tables(pos):
    half = 32
    inv_freq = (10000.0 ** (-np.arange(half, dtype=np.float32) / half)).astype(np.float32)
    ang = pos.astype(np.float32)[:, None] * inv_freq[None, :]
    c = np.cos(ang.astype(np.float64)).astype(np.float32)
    s = np.sin(ang.astype(np.float64)).astype(np.float32)
    return np.concatenate([c, c], 1), np.concatenate([-s, s], 1)


def kernel(**inp):
    f32 = np.float32
    x_prompt = np.asarray(inp["x_prompt"], f32)
    x_sample = np.asarray(inp["x_sample"], f32)
    meta = np.asarray(inp["meta_tokens"], f32)
    shared = {
        "w_in": np.ascontiguousarray(inp["w_in"][0], f32),
        "w_glu": np.ascontiguousarray(inp["w_glu"][0], f32),
        "w_ao": np.ascontiguousarray(inp["w_attn_out"][0], f32),
        "w_so": np.ascontiguousarray(inp["w_ssm_out"][0], f32),
        "w_out": np.ascontiguousarray(inp["w_out"][0], f32),
        "norm_gain": np.ascontiguousarray(inp["norm_gain"], f32).reshape(1, D),
        "qk_gain": np.concatenate([np.asarray(inp["q_norm_gain"], f32).reshape(1, 64),
                                   np.asarray(inp["k_norm_gain"], f32).reshape(1, 64)], 1),
        "sinks": np.asarray(inp["sinks"], f32).reshape(1, 16),
        "b_glu": np.ascontiguousarray(np.asarray(inp["b_glu"], f32).reshape(8, 128).T),
        "dsk": np.ascontiguousarray(np.tile(np.asarray(inp["d_skip"], f32).reshape(64, 16).T, (8, 1))),
        "ident": np.eye(128, dtype=f32),
        "a_re2": np.ascontiguousarray(np.tile(np.asarray(inp["a_re"][0], f32).T, (2, 1))),
        "a_im2": np.ascontiguousarray(np.tile(np.asarray(inp["a_im"][0], f32).T, (2, 1))),
        "logdt": np.asarray(inp["log_dt"], f32).reshape(1, 64),
        "kvals": np.asarray([7, 6, 5, 4, 3, 2, 1, 0, -7, -6, -5, -4, -3, -2, -1, 0, 1, 2, 3, 4, 5, 6, 7, 8], f32).reshape(1, 24),
        "krow": np.arange(260, dtype=f32).reshape(1, 260),
        "blockmask": np.kron(np.triu(np.ones((8, 8), f32)), np.ones((16, 16), f32)).astype(f32),
        "swapm": np.roll(np.eye(128, dtype=f32), 64, axis=0),
    }
    kk_, qq_ = np.meshgrid(np.arange(128), np.arange(128), indexing="ij")
    NEG = np.float32(-30000.0)
    shared["maskc"] = np.where(kk_ <= qq_, np.float32(0), NEG).astype(f32)
    shared["maskp"] = np.where(kk_ > qq_, np.float32(0), NEG).astype(f32)
    b_re = np.asarray(inp["b_re"][0], f32).transpose(1, 0, 2)
    b_im = np.asarray(inp["b_im"][0], f32).transpose(1, 0, 2)
    c_re = np.asarray(inp["c_re"][0], f32).transpose(2, 0, 1)
    c_im = np.asarray(inp["c_im"][0], f32).transpose(2, 0, 1)
    shared["b_self"] = np.ascontiguousarray(np.concatenate([b_re, b_im], 0))
    shared["b_part"] = np.ascontiguousarray(np.concatenate([b_im, b_re], 0))
    shared["c_self"] = np.ascontiguousarray(np.concatenate([c_re, c_im], 0))
    shared["c_part"] = np.ascontiguousarray(np.concatenate([c_im, c_re], 0))
    in_maps = []
    for core in range(N_CORES):
        b, half = core // 2, core % 2
        m = dict(shared)
        m["x_own"] = np.ascontiguousarray(x_prompt[b, half * NT:(half + 1) * NT])
        if half == 1:
            m["x_pre2"] = np.ascontiguousarray(x_prompt[b, 0:NT])
            m["x_pre1"] = meta.copy()
            halo = x_prompt[b, NT - 128:NT]
        else:
            m["x_pre2"] = np.concatenate([np.zeros((NT - 16, D), f32), meta], 0)
            m["x_pre1"] = np.zeros((16, D), f32)
            halo = np.zeros((128, D), f32)
        m["x_kvx"] = np.concatenate([halo, meta], 0)
        m["maskp0"] = shared["maskp"] if half == 1 else np.full((128, 128), NEG, f32)
        m["x_smp"] = np.ascontiguousarray(x_sample[core * NS:(core + 1) * NS, 0])
        base = 16 + half * NT
        cc = np.zeros((11, 128, 64), f32)
        ss = np.zeros((11, 128, 64), f32)
        for t in range(8):
            cc[t], ss[t] = _rope_tables(base + t * 128 + np.arange(128))
        cc[8], ss[8] = _rope_tables(np.maximum(base - 128 + np.arange(128), 0))
        cc[9, :16], ss[9, :16] = _rope_tables(np.arange(16))
        cc[10, :16], ss[10, :16] = _rope_tables(np.full(16, 8192))
        m["rope_cc"] = np.ascontiguousarray(cc.transpose(1, 0, 2))
        m["rope_ss"] = np.ascontiguousarray(ss.transpose(1, 0, 2))
        sl = slice(core * NS, (core + 1) * NS)
        m["cwk"] = np.ascontiguousarray(inp["cache_win_k"][0, sl], f32).reshape(NS, 128, 256)
        m["cwv"] = np.ascontiguousarray(inp["cache_win_v"][0, sl], f32).reshape(NS, 128, 256)
        m["cmk"] = np.ascontiguousarray(inp["cache_meta_k"][0, sl], f32).reshape(NS, 16, 256)
        m["cmv"] = np.ascontiguousarray(inp["cache_meta_v"][0, sl], f32).reshape(NS, 16, 256)
        sre = np.asarray(inp["state_ssm_re"][0, sl], f32)
        sim = np.asarray(inp["state_ssm_im"][0, sl], f32)
        m["st_self"] = np.ascontiguousarray(np.concatenate([sre, sim], 2))
        m["st_part"] = np.ascontiguousarray(np.concatenate([sim, sre], 2))
        in_maps.append(m)

    nc = build_program()
    res = run_bass_kernel_spmd(nc, in_maps, core_ids=list(range(N_CORES)))
    R = res.results

    y_prompt = np.zeros((4, 2048, D), f32)
    y_sample = np.zeros((128, 1, D), f32)
    p_win_k = np.zeros((1, 4, 128, 4, 64), f32)
    p_win_v = np.zeros((1, 4, 128, 4, 64), f32)
    p_meta_k = np.zeros((1, 4, 16, 4, 64), f32)
    p_meta_v = np.zeros((1, 4, 16, 4, 64), f32)
    p_re = np.zeros((1, 4, 64, 64), f32)
    p_im = np.zeros((1, 4, 64, 64), f32)
    s_win_k = np.zeros((1, 128, 128, 4, 64), f32)
    s_win_v = np.zeros((1, 128, 128, 4, 64), f32)
    s_re = np.zeros((1, 128, 64, 64), f32)
    s_im = np.zeros((1, 128, 64, 64), f32)
    for core in range(N_CORES):
        b, half = core // 2, core % 2
        r = R[core]
        y_prompt[b, half * NT:(half + 1) * NT] = r["y_own"]
        sl = slice(core * NS, (core + 1) * NS)
        y_sample[sl, 0] = r["y_smp"]
        if half == 1:
            p_win_k[0, b] = r["pwk"].reshape(128, 4, 64)
            p_win_v[0, b] = r["pwv"].reshape(128, 4, 64)
            p_re[0, b] = r["pssm"][:, 0:64]
            p_im[0, b] = r["pssm"][:, 64:128]
        else:
            p_meta_k[0, b] = r["pmk"].reshape(16, 4, 64)
            p_meta_v[0, b] = r["pmv"].reshape(16, 4, 64)
        s_win_k[0, sl] = r["swk"].reshape(NS, 128, 4, 64)
        s_win_v[0, sl] = r["swv"].reshape(NS, 128, 4, 64)
        s_re[0, sl] = r["sssm"][:, :, 0:64]
        s_im[0, sl] = r["sssm"][:, :, 64:128]
    return (y_prompt, y_sample, p_win_k, p_win_v, p_meta_k, p_meta_v, p_re, p_im, s_win_k, s_win_v, s_re, s_im)
```

```python
import os
import numpy as np
import concourse.bass as bass
import concourse.mybir as mybir
from concourse.bass_utils import run_bass_kernel_spmd

F32 = mybir.dt.float32
BF16 = mybir.dt.bfloat16
ALU = mybir.AluOpType
AF = mybir.ActivationFunctionType
AX = mybir.AxisListType

D = 2048
ND = 16
NT = 1024
NS = 16
NTS = NT + NS
EPS = 1e-6
N_CORES = 8
KTRUNC = ''
KUM = 'ab'


class Sched:
    ENGS = ("pe", "act", "dve", "pool", "sp")

    def __init__(self, nc):
        self.nc = nc
        self.ops = []

    def add(self, eng, fn, reads=(), writes=(), dma=None, fence=False):
        self.ops.append(dict(eng=eng, fn=fn, reads=tuple(reads), writes=tuple(writes), dma=dma, fence=fence))
        return len(self.ops) - 1

    def barrier(self, mk):
        keys = set()
        for o in self.ops:
            keys.update(o["reads"])
            keys.update(o["writes"])
        keys = sorted(keys, key=str)
        self.nbar = getattr(self, "nbar", 0) + 1
        bk = "bar%d" % self.nbar
        self.add("act", mk("act"), reads=keys, writes=keys + [bk])
        for e in ("dve", "pool", "sp"):
            self.add(e, mk(e), reads=[bk], writes=["%s_%s" % (bk, e)], dma=("bar_sp" if e == "sp" else None))

    def mark(self, name):
        if not hasattr(self, "marks"):
            self.marks = {}
        self.marks[name] = len(self.ops)

    def emit(self):
        nc = self.nc
        if KTRUNC:
            self.ops = self.ops[:self.marks[KTRUNC]]
        ops = self.ops
        last_w = {}
        readers = {}
        deps = [set() for _ in ops]
        for i, o in enumerate(ops):
            for b in o["reads"]:
                if b in last_w:
                    deps[i].add(last_w[b])
            for b in o["writes"]:
                if b in last_w:
                    deps[i].add(last_w[b])
                for r in readers.get(b, ()):
                    if r != i:
                        deps[i].add(r)
            for b in o["reads"]:
                readers.setdefault(b, []).append(i)
            for b in o["writes"]:
                last_w[b] = i
                readers[b] = []
        needed = set()
        prev_on = {}
        for i, o in enumerate(ops):
            keep = set()
            for d in deps[i]:
                if ops[d]["dma"] is None and ops[d]["eng"] == "pe" and o["eng"] == "pe" and o["dma"] is None:
                    continue
                keep.add(d)
            if o.get("fence") and o["eng"] in prev_on:
                keep.add(prev_on[o["eng"]])
            if o["dma"] is None:
                prev_on[o["eng"]] = i
            deps[i] = keep
            needed |= keep
        dma_keys = sorted({o["dma"] for o in ops if o["dma"] is not None}, key=str)
        sem_ctx = []
        sems = {}
        for e in self.ENGS:
            cm = nc.semaphore("s_" + e)
            sems[("eng", e)] = cm.__enter__()
            sem_ctx.append(cm)
        for n, k in enumerate(dma_keys):
            cm = nc.semaphore("d%d" % n)
            sems[("dma", k)] = cm.__enter__()
            sem_ctx.append(cm)
        cnt = {k: 0 for k in sems}
        ticket = [None] * len(ops)
        for i, o in enumerate(ops):
            if o["dma"] is not None:
                k = ("dma", o["dma"])
                cnt[k] += 16
                ticket[i] = (k, cnt[k])
            elif i in needed:
                k = ("eng", o["eng"])
                cnt[k] += 1
                ticket[i] = (k, cnt[k])
        streams = {e: [] for e in self.ENGS}
        waited = {e: {} for e in self.ENGS}
        for i, o in enumerate(ops):
            e = o["eng"]
            w = {}
            for d in deps[i]:
                k, v = ticket[d]
                if waited[e].get(k, 0) >= v:
                    continue
                w[k] = max(w.get(k, 0), v)
            for k, v in w.items():
                waited[e][k] = v
            streams[e].append((i, w))
        final = {k: v for k, v in cnt.items() if k[0] == "dma" and v > 0}

        def run_stream(e, engobj):
            for i, w in streams[e]:
                for k, v in w.items():
                    engobj.wait_ge(sems[k], v)
                ins = ops[i]["fn"](engobj)
                if ticket[i] is not None:
                    k, v = ticket[i]
                    ins.then_inc(sems[k], 16 if k[0] == "dma" else 1)
            if e == "sp":
                for k, v in final.items():
                    engobj.wait_ge(sems[k], v)

        with nc.Block() as block:
            @block.sync
            def _(eng):
                run_stream("sp", eng)

            @block.tensor
            def _(eng):
                run_stream("pe", eng)

            @block.scalar
            def _(eng):
                run_stream("act", eng)

            @block.vector
            def _(eng):
                run_stream("dve", eng)

            @block.gpsimd
            def _(eng):
                run_stream("pool", eng)
        for cm in reversed(sem_ctx):
            cm.__exit__(None, None, None)


IN_SPECS = [
    ("x_own", [NT, D]), ("x_pre2", [NT, D]), ("x_pre1", [16, D]), ("x_kvx", [144, D]), ("x_smp", [NS, D]),
    ("w_in", [D, 8704]), ("w_glu", [1024, 1024]), ("w_ao", [1024, D]), ("w_so", [1024, D]), ("w_out", [D, D]),
    ("norm_gain", [1, D]), ("qk_gain", [1, 128]), ("sinks", [1, 16]), ("b_glu", [128, 8]), ("dsk", [128, 64]),
    ("rope_cc", [128, 11, 64]), ("rope_ss", [128, 11, 64]),
    ("ident", [128, 128]),
    ("cwk", [NS, 128, 256]), ("cwv", [NS, 128, 256]), ("cmk", [NS, 16, 256]), ("cmv", [NS, 16, 256]),
    ("a_re2", [128, 64]), ("a_im2", [128, 64]), ("logdt", [1, 64]),
    ("b_self", [128, 64, 16]), ("b_part", [128, 64, 16]), ("c_self", [128, 64, 16]), ("c_part", [128, 64, 16]),
    ("kvals", [1, 24]), ("krow", [1, 260]), ("blockmask", [128, 128]), ("swapm", [128, 128]),
    ("st_self", [NS, 64, 128]), ("st_part", [NS, 64, 128]),
    ("maskc", [128, 128]), ("maskp", [128, 128]), ("maskp0", [128, 128]),
]
OUT_SPECS = [
    ("y_own", [NT, D]), ("y_smp", [NS, D]),
    ("pwk", [128, 256]), ("pwv", [128, 256]), ("pmk", [16, 256]), ("pmv", [16, 256]),
    ("pssm", [64, 128]),
    ("swk", [NS, 128, 256]), ("swv", [NS, 128, 256]), ("sssm", [NS, 64, 128]),
]


def build_program():
    nc = bass.Bass("TRN2", target_bir_lowering=False)
    S = Sched(nc)
    din = {n: nc.dram_tensor(n, s, F32, kind="ExternalInput").ap() for n, s in IN_SPECS}
    dout = {n: nc.dram_tensor(n, s, F32, kind="ExternalOutput").ap() for n, s in OUT_SPECS}
    ctxs = []

    def sb(name, shape, dt):
        cm = nc.sbuf_tensor("sb_" + name, shape, dt)
        t = cm.__enter__()
        ctxs.append(cm)
        return t

    def psum(name, shape, dt):
        cm = nc.psum_tensor(name, shape, dt)
        t = cm.__enter__()
        ctxs.append(cm)
        return t

    KB = 1024
    ARENA = 150 * KB
    AR = sb("arena", [128, ARENA // 2], BF16)

    def carve(off, n, dt):
        assert off % 4 == 0
        if dt == BF16:
            assert off + 2 * n <= ARENA, (off, n)
            return AR[:, off // 2: off // 2 + n]
        assert off + 4 * n <= ARENA, (off, n)
        return AR[:, off // 2: off // 2 + 2 * n].bitcast(F32)

    PS = [psum("ps%d" % i, [128, 512], F32) for i in range(8)]
    PSB = [p.bitcast(BF16) for p in PS]
    identf = sb("identf", [128, 128], F32)
    identb = sb("identb", [128, 128], BF16)
    epsb = sb("epsb", [128, 1], F32)
    qkg_bc = sb("qkg_bc", [128, 128], F32)
    ropecc = sb("ropecc", [128, 11, 64], F32)
    ropess = sb("ropess", [128, 11, 64], F32)
    stat = sb("stat", [128, 64], F32)
    dummy = sb("dummyk", [128, 8], F32)
    xnT = sb("xnT", [128, ND, NTS], BF16)
    Um = sb("Um", [128, 64, 18], BF16)
    kvals = sb("kvals", [128, 24], F32)
    krow = sb("krow", [128, 260], F32)
    sgn = sb("sgn", [128, 2], F32)
    halfpi = sb("halfpi", [128, 1], F32)
    onec = sb("onec", [128, 1], F32)
    pp = sb("pp", [128, 16, 64], F32)
    mumag = sb("mumag", [128, 64], F32)
    G258 = sb("G258", [128, 64], F32)
    C258 = sb("C258", [128, 64], F32)
    S258 = sb("S258", [128, 64], F32)
    dsk = sb("dsk", [128, 64], F32)
    blockmask = sb("blockmask", [128, 128], F32)
    swapm = sb("swapm", [128, 128], F32)
    b_glu = sb("b_glu", [128, 8], F32)
    expsink = sb("expsink", [128, 16], F32)
    masks = sb("masks", [128, 3, 128], BF16)
    qs16 = sb("qs16", [NS, 1024], BF16)
    onesb = sb("onesb", [128, 64], BF16)

    u8own = carve(0, 8192, BF16).rearrange("p (g s h) -> p g s h", g=64, s=8)
    u8p2 = carve(16 * KB, 8192, BF16).rearrange("p (g s h) -> p g s h", g=64, s=8)
    wbuf = [carve((68 + 16 * i) * KB, ND * 512, BF16).rearrange("p (c n) -> p c n", c=ND) for i in range(2)]
    u8m = carve(34 * KB, 8192, BF16).rearrange("p (g s h) -> p g s h", g=64, s=8)
    gain_bc = carve(51 * KB, D, F32)
    xbuf = [carve((100 + 8 * i) * KB, D, F32) for i in range(2)]
    xnb = [carve((116 + 4 * i) * KB, D, BF16) for i in range(2)]
    tmpq = [carve((124 + 4 * i) * KB, 1024, F32) for i in range(3)]
    sqj = carve(128 * KB, D, F32)
    xnTt = [carve((136 + 4 * i) * KB, ND * 128, BF16).rearrange("p (c t) -> p c t", c=ND) for i in range(2)]
    kf = [carve((144 + i) * KB, 256, F32) for i in range(2)]
    vf = [carve((146 + i) * KB, 256, F32) for i in range(2)]
    utok1 = carve(148 * KB, 1024, BF16)

    cnt = {"x": 0, "ps": 0, "kv": 0}

    def dma(eng, out, in_, reads, writes, key):
        S.add(eng, lambda e: e.dma_start(out=out, in_=in_), reads=reads, writes=writes, dma=key)

    def mk_bar(e):
        col = {"act": 0, "dve": 1, "pool": 2, "sp": 3}[e]
        if e == "act":
            return lambda en: en.copy(out=dummy[0:1, col:col + 1], in_=dummy[0:1, 4:5])
        if e == "sp":
            return lambda en: en.dma_start(out=dummy[0:1, col:col + 1], in_=din["kvals"][0:1, 0:1])
        return lambda en: en.memset(dummy[0:1, col:col + 1], 0.0)

    S.add("pool", lambda e: e.memset(dummy[:], 0.0), [], ["dummy"])
    dma("sp", identf[:], din["ident"], [], ["identf"], "identf")
    S.add("dve", lambda e: e.tensor_copy(out=identb[:], in_=identf[:]), ["identf"], ["identb"])
    S.add("pool", lambda e: e.memset(epsb[:], EPS), [], ["epsb"])
    dma("sp", gain_bc, din["norm_gain"].partition_broadcast(128), [], ["gain_bc"], "gain_bc")
    dma("sp", qkg_bc[:], din["qk_gain"].partition_broadcast(128), [], ["qkg_bc"], "qkg_bc")
    dma("sp", ropecc[:], din["rope_cc"], [], ["ropecc"], "ropecc")
    dma("sp", ropess[:], din["rope_ss"], [], ["ropess"], "ropess")

    def load_w(slot, src, col0, ncols, nk):
        key = "wbuf%d" % slot
        v = src.rearrange("(c p) n -> p c n", p=128)
        step = 4
        for c0 in range(0, nk, step):
            c1 = min(nk, c0 + step)
            dma("pool", wbuf[slot][:, c0:c1, 0:ncols], v[:, c0:c1, col0:col0 + ncols], [], [key], key)

    def norm_A(src_rows, n):
        j = (cnt["x"] % 2) if (xbuf[0] is not xbuf[1]) else 0
        cnt["x"] += 1
        xb_, xn_ = xbuf[j], xnb[j]
        sq_, g_ = sqj, gain_bc
        kx, kn = "xbuf%d" % j, "xnb%d" % j
        dma("sp", xb_[0:n, :], src_rows, [], [kx], kx)
        S.add("act", lambda e: e.activation(out=sq_[0:n, :], in_=xb_[0:n, :], func=AF.Square), [kx], ["tq1", "tq2a", "tq2b"])
        S.add("dve", lambda e: e.tensor_reduce(out=stat[0:n, 0:1], in_=sq_[0:n, :], axis=AX.X, op=ALU.add), ["tq1", "tq2a", "tq2b"], ["stat0"])
        S.add("act", lambda e: e.activation(out=stat[0:n, 1:2], in_=stat[0:n, 0:1], func=AF.Sqrt, bias=epsb[0:n, 0:1], scale=1.0 / D),
              ["stat0", "epsb"], ["stat1"])
        S.add("dve", lambda e: e.reciprocal(out=stat[0:n, 2:3], in_=stat[0:n, 1:2]), ["stat1"], ["stat2"])
        S.add("dve", lambda e: e.scalar_tensor_tensor(out=xn_[0:n, :], in0=xb_[0:n, :], scalar=stat[0:n, 2:3], in1=g_[0:n, :],
                                                      op0=ALU.mult, op1=ALU.mult), [kx, "stat2", "gain_bc"], [kn])
        return xn_, kn, n

    def norm_B(h, dst_fn, dst_keys):
        xn_, kn, n = h
        for half in range(2):
            b = 4 + (cnt["ps"] % 2)
            cnt["ps"] += 1
            kb = "PS%d" % b
            pv = PSB[b][:, 0:1024].rearrange("p (c t) -> p c t", c=8)
            for c in range(8):
                dc = half * 8 + c
                S.add("pe", lambda e, c=c, dc=dc, pv=pv: e.transpose(out=pv[:, c, 0:n], in_=xn_[0:n, dc * 128:(dc + 1) * 128],
                                                                      identity=identb[0:n, 0:n]), [kn, "identb"], [kb])
            dst = dst_fn(half * 8)
            if half == 0:
                S.add("act", lambda e, pv=pv, dst=dst: e.copy(out=dst, in_=pv[:, :, 0:n]), [kb], dst_keys)
            else:
                S.add("dve", lambda e, pv=pv, dst=dst: e.tensor_copy(out=dst, in_=pv[:, :, 0:n]), [kb], dst_keys)

    def norm_tile(src_rows, n, dst_fn, dst_keys):
        norm_B(norm_A(src_rows, n), dst_fn, dst_keys)

    def headnorm_rope(src_ps, src_keys, n, NH, ti, goff, extra_scale, dst, dst_keys):
        W = NH * 64
        t0, t1, t2 = tmpq
        c0 = 0
        for ap_, w in src_ps:
            S.add("act", lambda e, ap_=ap_, c0=c0, w=w: e.copy(out=t0[0:n, c0:c0 + w], in_=ap_), src_keys, ["tq0"])
            S.add("act", lambda e, ap_=ap_, c0=c0, w=w: e.activation(out=t1[0:n, c0:c0 + w], in_=ap_, func=AF.Square), src_keys, ["tq1"])
            c0 += w
        v3 = lambda t: t[0:n, 0:W].rearrange("p (h d) -> p h d", d=64)
        S.add("dve", lambda e: e.tensor_reduce(out=stat[0:n, 8:8 + NH], in_=v3(t1), axis=AX.X, op=ALU.add), ["tq1"], ["stq"])
        S.add("act", lambda e: e.activation(out=stat[0:n, 24:24 + NH], in_=stat[0:n, 8:8 + NH], func=AF.Sqrt, bias=epsb[0:n, 0:1], scale=1.0 / 64),
              ["stq", "epsb"], ["stq2"])
        S.add("dve", lambda e: e.reciprocal(out=stat[0:n, 40:40 + NH], in_=stat[0:n, 24:24 + NH]), ["stq2"], ["stq3"])
        if extra_scale != 1.0:
            S.add("dve", lambda e: e.tensor_scalar(out=stat[0:n, 40:40 + NH], in0=stat[0:n, 40:40 + NH], scalar1=float(extra_scale), scalar2=0.0,
                                                   op0=ALU.mult, op1=ALU.add), ["stq3"], ["stq3"])
        gb = qkg_bc[0:n, goff:goff + 64].unsqueeze(1).to_broadcast([n, NH, 64])
        S.add("dve", lambda e: e.tensor_tensor(out=v3(t0), in0=v3(t0), in1=gb, op=ALU.mult), ["tq0", "qkg_bc"], ["tq0"])
        cc = ropecc[0:n, ti, :].unsqueeze(1).to_broadcast([n, NH, 64])
        S.add("dve", lambda e: e.tensor_tensor(out=v3(t1), in0=v3(t0), in1=cc, op=ALU.mult), ["tq0", "ropecc", "stq"], ["tq1"])
        s_lo = ropess[0:n, ti, 0:32].unsqueeze(1).to_broadcast([n, NH, 32])
        s_hi = ropess[0:n, ti, 32:64].unsqueeze(1).to_broadcast([n, NH, 32])
        S.add("dve", lambda e: e.tensor_tensor(out=v3(t2)[:, :, 0:32], in0=v3(t0)[:, :, 32:64], in1=s_lo, op=ALU.mult), ["tq0", "ropess"], ["tq2a"])
        S.add("dve", lambda e: e.tensor_tensor(out=v3(t2)[:, :, 32:64], in0=v3(t0)[:, :, 0:32], in1=s_hi, op=ALU.mult), ["tq0", "ropess"], ["tq2b"])
        S.add("dve", lambda e: e.tensor_tensor(out=v3(t1), in0=v3(t1), in1=v3(t2), op=ALU.add), ["tq1", "tq2a", "tq2b"], ["tq1"])
        rb = stat[0:n, 40:40 + NH].unsqueeze(2).to_broadcast([n, NH, 64])
        dv = dst.rearrange("p (h d) -> p h d", d=64)
        S.add("dve", lambda e: e.tensor_tensor(out=dv, in0=v3(t1), in1=rb, op=ALU.mult), ["tq1", "stq3"], dst_keys)

    def proj_tok(lhs_fn, lhs_keys, n, slot, ncols, bank):
        kb = "PS%d" % bank
        for dc in range(ND):
            S.add("pe", lambda e, dc=dc: e.matmul(PS[bank][0:n, 0:ncols], lhsT=lhs_fn(dc), rhs=wbuf[slot][:, dc, 0:ncols],
                                                  start=(dc == 0), stop=(dc == ND - 1)), lhs_keys + ["wbuf%d" % slot], [kb])
        return kb

    pass
    S.mark('p1a')
    S.add("pool", lambda e: e.memset(u8m, 0.0), [], ["u8m"])
    load_w(0, din["w_in"], 2560, 512, ND)
    load_w(1, din["w_in"], 3072, 512, ND)

    def u_tile(lhs_fn, lhs_keys, n, evac_fn):
        for blk in range(2):
            b = cnt["kv"] % 2
            cnt["kv"] += 1
            kb = proj_tok(lhs_fn, lhs_keys, n, blk, 512, b)
            evac_fn(blk, PS[b], kb)

    S.mark("u_pre1")
    xp2 = din["x_pre2"].rearrange("(c s) d -> c s d", s=8)
    tl = [("smp", 0)]
    for i in range(8):
        tl += [("own", i), ("pre", i)]

    def p1_src(kind, i):
        if kind == "smp":
            return din["x_smp"], NS
        if kind == "own":
            return din["x_own"][i * 128:(i + 1) * 128, :], 128
        return xp2[:, i, :], 128

    def p1_finish(kind, i, h):
        if kind == "smp":
            norm_B(h, lambda dc0: xnT[:, dc0:dc0 + 8, NT:NTS], ["xnT_s"])
        elif kind == "own":
            norm_B(h, lambda dc0, i=i: xnT[:, dc0:dc0 + 8, i * 128:(i + 1) * 128], ["xnT_%d" % i])
        else:
            jt = i % 2
            tt = xnTt[jt]
            norm_B(h, lambda dc0, tt=tt: tt[:, dc0:dc0 + 8, 0:128], ["xnTt%d" % jt])

            def ev_p2(blk, ps_, kb, i=i):
                S.add("act", lambda e: e.copy(out=u8p2[:, blk * 32:(blk + 1) * 32, i, :], in_=ps_[:, 0:512].rearrange("p (g h) -> p g h", h=16)),
                      [kb], ["u8p2_%d_%d" % (i, blk)])
            u_tile(lambda dc, tt=tt: tt[:, dc, 0:128], ["xnTt%d" % jt], 128, ev_p2)

    pend1 = None
    for (kind, i) in tl:
        src_, n_ = p1_src(kind, i)
        h_ = norm_A(src_, n_)
        if pend1 is not None:
            p1_finish(*pend1)
        pend1 = (kind, i, h_)
    p1_finish(*pend1)
    S.mark("u_pre2")
    OWNK = ["xnT_%d" % i for i in range(8)]
    for i in range(8):
        def ev_own(blk, ps_, kb, i=i):
            S.add("act", lambda e: e.copy(out=u8own[:, blk * 32:(blk + 1) * 32, i, :], in_=ps_[:, 0:512].rearrange("p (g h) -> p g h", h=16)),
                  [kb], ["u8own_%d_%d" % (i, blk)])
        u_tile(lambda dc, i=i: xnT[:, dc, i:NT:8], OWNK, 128, ev_own)

    S.mark("u_own")

    def ev_smp(blk, ps_, kb):
        S.add("act", lambda e: e.copy(out=u8m[0:NS, blk * 32:(blk + 1) * 32, 7, :], in_=ps_[0:NS, 0:512].rearrange("p (g h) -> p g h", h=16)),
              [kb, "u8m"], ["u8m_s%d" % blk])
    u_tile(lambda dc: xnT[:, dc, NT:NTS], ["xnT_s"], NS, ev_smp)
    j = cnt["x"] % 2
    tt = xnTt[j]
    norm_tile(din["x_pre1"], 16, lambda dc0, tt=tt: tt[:, dc0:dc0 + 8, 0:16], ["xnTt%d" % j])

    def ev_pre1(blk, ps_, kb):
        S.add("act", lambda e: e.copy(out=utok1[0:16, blk * 512:(blk + 1) * 512], in_=ps_[0:16, 0:512]), [kb], ["utok1_%d" % blk])
    u_tile(lambda dc, tt=tt: tt[:, dc, 0:16], ["xnTt%d" % j], 16, ev_pre1)
    for s_ in range(8):
        dma("sp", u8m[32:34, :, s_, :], utok1[s_:16:8, :].rearrange("p (g h) -> p g h", h=16), ["utok1_0", "utok1_1", "u8m"], ["u8m_p1"], "u8m_p1")

    S.mark("u_smp")
    U8K_P2 = ["u8p2_%d_%d" % (i, b) for i in range(8) for b in range(2)]
    U8K_OWN = ["u8own_%d_%d" % (i, b) for i in range(8) for b in range(2)]
    U8K_M = ["u8m", "u8m_p1", "u8m_s0", "u8m_s1"]
    for part in range(2):
        for q4 in range(4):
            bk = 6 + q4 % 2
            kb = "PS%d" % bk
            pv = PSB[bk][:, 0:1024].rearrange("p (g t) -> p g t", t=64)
            for gl in range(16):
                gg = q4 * 16 + gl
                fence = (part == 1 and q4 == 0 and gl == 0)
                if part == 0:
                    S.add("pe", lambda e, gl=gl, gg=gg, pv=pv: e.transpose(out=pv[:, gl, 0:2], in_=u8m[32:34, gg, :, :].rearrange("p s h -> p (s h)"),
                                                                            identity=identb[32:34, 32:34]), U8K_M + ["identb"], [kb])
                else:
                    S.add("pe", lambda e, gl=gl, gg=gg, pv=pv: e.transpose(out=pv[:, gl, 0:16], in_=u8m[0:NS, gg, :, :].rearrange("p s h -> p (s h)"),
                                                                            identity=identb[0:NS, 0:NS]), U8K_M + ["identb"], [kb], fence=fence)
            if part == 0:
                S.add("dve", lambda e, q4=q4, pv=pv: e.tensor_copy(out=Um[:, q4 * 16:(q4 + 1) * 16, 0:2], in_=pv[:, :, 0:2]), [kb], ["Um_%d" % q4])
            else:
                S.add("dve", lambda e, q4=q4, pv=pv: e.tensor_copy(out=Um[:, q4 * 16:(q4 + 1) * 16, 2:18], in_=pv[:, :, 0:16]), [kb], ["Umb_%d" % q4])
    UMK = ["Um_%d" % q for q in range(4)] + ["Umb_%d" % q for q in range(4)]

    S.mark('p1b')
    S.barrier(mk_bar)
    S.mark('bar1')
    MAGIC = 12582912.0
    TWO_PI = float(2.0 * np.pi)
    SC = 1036
    GB = 4
    Ere = carve(51 * KB, 1536, F32).rearrange("p (g k) -> p g k", k=24)
    Eim = carve(57 * KB, 1536, F32).rearrange("p (g k) -> p g k", k=24)
    bbs = carve(63 * KB, 1024, F32).rearrange("p (g h) -> p g h", h=16)
    bbp = carve(67 * KB, 1024, F32).rearrange("p (g h) -> p g h", h=16)
    cself = carve(71 * KB, 1024, F32).rearrange("p (g h) -> p g h", h=16)
    cpart = carve(75 * KB, 1024, F32).rearrange("p (g h) -> p g h", h=16)
    sc = [carve(79 * KB + i * 4352, SC, F32) for i in range(4)]
    cosT = carve(96 * KB, GB * 259, F32).rearrange("p (g k) -> p g k", k=259)
    sinT = carve(96 * KB + 4352, GB * 259, F32).rearrange("p (g k) -> p g k", k=259)
    o_ = 96 * KB + 2 * 4352
    A32 = carve(o_, GB * 128, F32).rearrange("p (g q) -> p g q", q=128); o_ += 2 * KB
    R32 = carve(o_, GB * 128, F32).rearrange("p (g q) -> p g q", q=128); o_ += 2 * KB
    Ab = carve(o_, 2 * GB * 128, BF16).rearrange("p (w g q) -> p w g q", w=2, q=128); o_ += 2 * KB
    Mab = carve(o_, 2 * GB * 128, BF16).rearrange("p (w g q) -> p w g q", w=2, q=128); o_ += 2 * KB
    Msb = carve(o_, 2 * GB * 128, BF16).rearrange("p (w g q) -> p w g q", w=2, q=128); o_ += 2 * KB
    Mib = carve(o_, GB * 128, BF16).rearrange("p (g q) -> p g q", q=128); o_ += KB
    mt = []
    for i in range(2):
        mt.append(carve(o_, GB * 128, F32).rearrange("p (g q) -> p g q", q=128)); o_ += 2 * KB
    NCH = 258
    NU = 274
    Ub = carve(o_, GB * NU, BF16).rearrange("p (g t) -> p g t", t=NU); o_ += 2304
    Xt = carve(o_, GB * NCH, F32).rearrange("p (g t) -> p g t", t=NCH); o_ += 4352
    Xt2 = [carve(o_ + 1056 * i, NCH, F32) for i in range(2)]; o_ += 2112
    Gs = carve(o_, GB * 259, F32).rearrange("p (g k) -> p g k", k=259); o_ += 4352
    Xsm = carve(o_, GB * NS, F32).rearrange("p (g b) -> p g b", b=NS); o_ += 256
    st_s = carve(o_, GB * 128, F32).rearrange("p (g q) -> p g q", q=128); o_ += 2 * KB
    st_p = carve(o_, GB * 128, F32).rearrange("p (g q) -> p g q", q=128); o_ += 2 * KB
    hs1 = carve(o_, GB * NS, F32).rearrange("p (g b) -> p g b", b=NS); o_ += 256
    hs2 = carve(o_, GB * NS, F32).rearrange("p (g b) -> p g b", b=NS); o_ += 256
    hso = carve(o_, GB * 128, F32).rearrange("p (g q) -> p g q", q=128); o_ += 2 * KB
    sE = carve(o_, 2 * GB * 24, F32).rearrange("p (w g k) -> p w g k", w=2, k=24); o_ += KB
    hfo = carve(o_, 128, F32); o_ += 512
    assert o_ <= ARENA, o_
    o2 = o_
    Pcs = carve(o2, 2 * GB * 128, BF16).rearrange("p (w g q) -> p w g q", w=2, q=128); o2 += 2 * KB
    y8 = carve(o2, 8 * 128, BF16).rearrange("p (t f) -> p t f", f=128); o2 += 2 * KB
    y8s = carve(o2, 128, BF16); o2 += 256
    assert o2 <= ARENA, o2
    yA = sb("yA", [128, GB * 128], F32)[:].rearrange("p (g q) -> p g q", q=128)
    yB = sb("yB", [128, GB * 128], F32)[:].rearrange("p (g q) -> p g q", q=128)
    b3 = o2
    ygb = carve(b3, GB * 128, BF16).rearrange("p (g q) -> p g q", q=128)
    ysA = carve(b3 + 1024, GB * NS, F32).rearrange("p (g b) -> p g b", b=NS)
    ysB = carve(b3 + 1280, GB * NS, F32).rearrange("p (g b) -> p g b", b=NS)
    ygs = carve(b3 + 1536, GB * NS, BF16).rearrange("p (g b) -> p g b", b=NS)
    hn1 = carve(b3 + 1664, GB * NS, F32).rearrange("p (g b) -> p g b", b=NS)
    hn2 = carve(b3 + 1920, GB * NS, F32).rearrange("p (g b) -> p g b", b=NS)
    Hins = carve(b3 + 2176, GB * NS, BF16).rearrange("p (g b) -> p g b", b=NS)
    assert b3 + 2304 <= ARENA, b3
    ST = carve(34 * KB, 8 * NTS, BF16).rearrange("p (c t) -> p c t", t=NTS)
    bself = carve(96 * KB, 1024, F32).rearrange("p (g h) -> p g h", h=16)
    bpart = carve(100 * KB, 1024, F32).rearrange("p (g h) -> p g h", h=16)
    a_re2 = carve(104 * KB, 64, F32)
    a_im2 = carve(104 * KB + 256, 64, F32)
    ldt = carve(104 * KB + 512, 64, F32)

    dma("sp", a_re2, din["a_re2"], [], ["a_re2"], "a_re2")
    dma("sp", a_im2, din["a_im2"], [], ["a_im2"], "a_im2")
    dma("sp", ldt, din["logdt"].partition_broadcast(128), [], ["ldt"], "ldt")
    dma("sp", bself, din["b_self"], [], ["bself"], "bself")
    dma("sp", bpart, din["b_part"], [], ["bpart"], "bpart")
    dma("sp", cself, din["c_self"], [], ["cself"], "cself")
    dma("sp", cpart, din["c_part"], [], ["cpart"], "cpart")
    dma("sp", kvals[:], din["kvals"].partition_broadcast(128), [], ["kvals"], "kvals")
    dma("sp", krow[:], din["krow"].partition_broadcast(128), [], ["krow"], "krow")
    dma("sp", blockmask[:], din["blockmask"], [], ["blockmask"], "blockmask")
    dma("sp", swapm[:], din["swapm"], [], ["swapm"], "swapm")
    dma("sp", dsk[:], din["dsk"], [], ["dsk"], "dsk")
    S.add("pool", lambda e: e.memset(sgn[0:64, 0:1], -1.0), [], ["sgn_a"])
    S.add("pool", lambda e: e.memset(sgn[64:128, 0:1], 1.0), [], ["sgn_b"])
    S.add("pool", lambda e: e.memset(sgn[0:64, 1:2], 1.0), [], ["sgn_c"])
    S.add("pool", lambda e: e.memset(sgn[64:128, 1:2], -1.0), [], ["sgn_d"])
    S.add("pool", lambda e: e.memset(halfpi[:], float(np.pi / 2)), [], ["halfpi"])
    S.add("pool", lambda e: e.memset(onec[:], 1.0), [], ["onec"])
    SGN = ["sgn_a", "sgn_b", "sgn_c", "sgn_d"]

    def sincos(ang, cos_out, sin_out, F, rk, wk_cos, wk_sin):
        tA, tB, tC = sc[1][:, 0:F], sc[2][:, 0:F], sc[3][:, 0:F]
        S.add("dve", lambda e: e.tensor_scalar(out=tA, in0=ang, scalar1=float(1.0 / TWO_PI), scalar2=MAGIC, op0=ALU.mult, op1=ALU.add), rk, ["sc1"])
        S.add("dve", lambda e: e.tensor_scalar(out=tA, in0=tA, scalar1=-MAGIC, scalar2=-TWO_PI, op0=ALU.add, op1=ALU.mult), ["sc1"], ["sc1"])
        S.add("dve", lambda e: e.tensor_tensor(out=tA, in0=tA, in1=ang, op=ALU.add), ["sc1"] + rk, ["sc1"])
        S.add("act", lambda e: e.activation(out=tB, in_=tA, func=AF.Sin, scale=0.5), ["sc1"], ["sc2"])
        S.add("act", lambda e: e.activation(out=tC, in_=tA, func=AF.Sin, scale=0.5, bias=halfpi[:, 0:1]), ["sc1", "halfpi"], ["sc3"])
        S.add("dve", lambda e: e.scalar_tensor_tensor(out=sin_out, in0=tB, scalar=2.0, in1=tC, op0=ALU.mult, op1=ALU.mult), ["sc2", "sc3"], wk_sin)
        S.add("act", lambda e: e.activation(out=tA, in_=tB, func=AF.Square, scale=float(np.sqrt(2.0))), ["sc2", "sc1"], ["sc1"])
        S.add("act", lambda e: e.activation(out=cos_out, in_=tA, func=AF.Identity, scale=-1.0, bias=onec[:, 0:1]), ["sc1", "onec"], wk_cos)

    PPK = lambda i: "pp%d" % i
    S.add("act", lambda e: e.activation(out=pp[:, 0, :], in_=ldt, func=AF.Exp), ["ldt"], [PPK(0)])
    S.add("dve", lambda e: e.tensor_tensor(out=pp[:, 1, :], in0=a_re2, in1=pp[:, 0, :], op=ALU.mult), ["a_re2", PPK(0)], [PPK(1)])
    S.add("dve", lambda e: e.tensor_tensor(out=pp[:, 2, :], in0=a_im2, in1=pp[:, 0, :], op=ALU.mult), ["a_im2", PPK(0)], [PPK(2)])
    kv_b = kvals[:].unsqueeze(1).to_broadcast([128, 32, 24])
    v24 = lambda t: t[:, 0:768].rearrange("p (g k) -> p g k", k=24)
    for hf in range(2):
        gh = slice(hf * 32, hf * 32 + 32)
        S.add("dve", lambda e, gh=gh: e.tensor_tensor(out=v24(sc[0]), in0=pp[:, 2, gh].unsqueeze(2).to_broadcast([128, 32, 24]), in1=kv_b, op=ALU.mult),
              [PPK(2), "kvals", "sc0"], ["sc0"])
        ere_h = Ere[:, gh, :].rearrange("p g k -> p (g k)")
        eim_h = Eim[:, gh, :].rearrange("p g k -> p (g k)")
        sincos(sc[0][:, 0:768], ere_h, eim_h, 768, ["sc0"], ["Ere"], ["Eim"])
        S.add("dve", lambda e, gh=gh: e.tensor_tensor(out=v24(sc[0]), in0=pp[:, 1, gh].unsqueeze(2).to_broadcast([128, 32, 24]), in1=kv_b, op=ALU.mult),
              [PPK(1), "kvals", "sc0"], ["sc0"])
        S.add("act", lambda e: e.activation(out=sc[0][:, 0:768], in_=sc[0][:, 0:768], func=AF.Exp), ["sc0"], ["sc0"])
        S.add("act", lambda e, gh=gh: e.copy(out=mumag[:, gh], in_=v24(sc[0])[:, :, 23]), ["sc0"], ["mumag"])
        S.add("dve", lambda e, ere_h=ere_h: e.tensor_tensor(out=ere_h, in0=ere_h, in1=sc[0][:, 0:768], op=ALU.mult), ["Ere", "sc0"], ["Ere"])
        S.add("dve", lambda e, eim_h=eim_h: e.tensor_tensor(out=eim_h, in0=eim_h, in1=sc[0][:, 0:768], op=ALU.mult), ["Eim", "sc0"], ["Eim"])
    S.add("dve", lambda e: e.tensor_scalar(out=pp[:, 6, :], in0=pp[:, 2, :], scalar1=float(8.0 / TWO_PI), scalar2=MAGIC, op0=ALU.mult, op1=ALU.add), [PPK(2)], [PPK(6)])
    S.add("dve", lambda e: e.tensor_scalar(out=pp[:, 6, :], in0=pp[:, 6, :], scalar1=-MAGIC, scalar2=-TWO_PI, op0=ALU.add, op1=ALU.mult), [PPK(6)], [PPK(6)])
    S.add("dve", lambda e: e.scalar_tensor_tensor(out=pp[:, 3, :], in0=pp[:, 2, :], scalar=8.0, in1=pp[:, 6, :], op0=ALU.mult, op1=ALU.add), [PPK(2), PPK(6)], [PPK(3)])
    e1r, e1i = Ere[:, :, 16], Eim[:, :, 16]
    S.add("dve", lambda e: e.tensor_scalar(out=pp[:, 7, :], in0=e1r, scalar1=-1.0, scalar2=0.0, op0=ALU.add, op1=ALU.add), ["Ere"], [PPK(7)])
    S.add("dve", lambda e: e.tensor_tensor(out=pp[:, 8, :], in0=a_re2, in1=a_re2, op=ALU.mult), ["a_re2"], [PPK(8)])
    S.add("dve", lambda e: e.tensor_tensor(out=pp[:, 9, :], in0=a_im2, in1=a_im2, op=ALU.mult), ["a_im2"], [PPK(9)])
    S.add("dve", lambda e: e.tensor_tensor(out=pp[:, 8, :], in0=pp[:, 8, :], in1=pp[:, 9, :], op=ALU.add), [PPK(8), PPK(9)], [PPK(8)])
    S.add("dve", lambda e: e.reciprocal(out=pp[:, 8, :], in_=pp[:, 8, :]), [PPK(8)], [PPK(8)])
    S.add("dve", lambda e: e.tensor_tensor(out=pp[:, 9, :], in0=pp[:, 7, :], in1=a_re2, op=ALU.mult), [PPK(7), "a_re2", PPK(8)], [PPK(9)])
    S.add("dve", lambda e: e.tensor_tensor(out=pp[:, 10, :], in0=e1i, in1=a_im2, op=ALU.mult), ["Eim", "a_im2"], [PPK(10)])
    S.add("dve", lambda e: e.tensor_tensor(out=pp[:, 9, :], in0=pp[:, 9, :], in1=pp[:, 10, :], op=ALU.add), [PPK(9), PPK(10)], [PPK(9)])
    S.add("dve", lambda e: e.tensor_tensor(out=pp[:, 4, :], in0=pp[:, 9, :], in1=pp[:, 8, :], op=ALU.mult), [PPK(9), PPK(8)], [PPK(4)])
    S.add("dve", lambda e: e.tensor_tensor(out=pp[:, 9, :], in0=e1i, in1=a_re2, op=ALU.mult), ["Eim", "a_re2", PPK(4)], [PPK(9)])
    S.add("dve", lambda e: e.tensor_tensor(out=pp[:, 10, :], in0=pp[:, 7, :], in1=a_im2, op=ALU.mult), [PPK(7), "a_im2", PPK(9)], [PPK(10)])
    S.add("dve", lambda e: e.tensor_tensor(out=pp[:, 9, :], in0=pp[:, 9, :], in1=pp[:, 10, :], op=ALU.subtract), [PPK(9), PPK(10)], [PPK(9)])
    S.add("dve", lambda e: e.tensor_tensor(out=pp[:, 9, :], in0=pp[:, 9, :], in1=pp[:, 8, :], op=ALU.mult), [PPK(9), PPK(8)], [PPK(9)])
    S.add("dve", lambda e: e.tensor_scalar(out=pp[:, 5, :], in0=pp[:, 9, :], scalar1=sgn[:, 0:1], scalar2=1.0, op0=ALU.mult, op1=ALU.mult), [PPK(9)] + SGN, [PPK(5)])
    wr_b = pp[:, 4, :].unsqueeze(2).to_broadcast([128, 64, 16])
    swi_b = pp[:, 5, :].unsqueeze(2).to_broadcast([128, 64, 16])
    v16 = lambda t: t[:, 0:1024].rearrange("p (g h) -> p g h", h=16)
    S.add("dve", lambda e: e.tensor_tensor(out=v16(sc[0]), in0=bself, in1=wr_b, op=ALU.mult), ["bself", PPK(4), "sc0"], ["sc0"])
    S.add("pool", lambda e: e.tensor_tensor(out=v16(sc[1]), in0=bpart, in1=swi_b, op=ALU.mult), ["bpart", PPK(5), "sc1"], ["sc1"])
    S.add("dve", lambda e: e.tensor_tensor(out=bbs, in0=v16(sc[0]), in1=v16(sc[1]), op=ALU.add), ["sc0", "sc1"], ["bbs"])
    S.add("dve", lambda e: e.tensor_tensor(out=v16(sc[0]), in0=bpart, in1=wr_b, op=ALU.mult), ["bpart", PPK(4), "sc0"], ["sc0"])
    S.add("pool", lambda e: e.tensor_tensor(out=v16(sc[1]), in0=bself, in1=swi_b, op=ALU.mult), ["bself", PPK(5), "sc1"], ["sc1"])
    S.add("dve", lambda e: e.tensor_tensor(out=bbp, in0=v16(sc[0]), in1=v16(sc[1]), op=ALU.subtract), ["sc0", "sc1"], ["bbp"])
    S.mark('params')
    S.barrier(mk_bar)
    S.mark('bar2')

    S.add("pool", lambda e: e.memset(Gs, 0.0), [], ["Gs"])

    def bc4(T, V):
        return (T.unsqueeze(3).to_broadcast([128, GB, 8, 16]), V.unsqueeze(2).to_broadcast([128, GB, 8, 16]))

    def v4(t):
        return t.rearrange("p g (s h) -> p g s h", h=16)

    def gen_mat(out_ap, T1, V1, T2, V2, op2, rk, wk, add_eng="dve", mul_eng="pool"):
        a0, a1 = bc4(T1, V1)
        b0, b1 = bc4(T2, V2)
        S.add("dve", lambda e: e.tensor_tensor(out=v4(mt[0]), in0=a0, in1=a1, op=ALU.mult), rk, ["mt0"])
        S.add(mul_eng, lambda e: e.tensor_tensor(out=v4(mt[1]), in0=b0, in1=b1, op=ALU.mult), rk, ["mt1"])
        S.add(add_eng, lambda e: e.tensor_tensor(out=out_ap, in0=mt[0], in1=mt[1], op=op2), ["mt0", "mt1"], wk)

    def ssm_frontA(gb):
        g8 = slice(gb * GB, gb * GB + GB)
        sEim, nsEre = sE[:, 0, :, :], sE[:, 1, :, :]
        EreB, EimB = Ere[:, g8, :], Eim[:, g8, :]
        UBK = ["Ub_%d" % g for g in range(GB)] + ["Ub_a%d" % g for g in range(GB)] + ["Ub_b%d" % g for g in range(GB)]
        GSK = ["Gs_%d" % g for g in range(GB)]
        S.add("dve", lambda e, g8=g8: e.tensor_scalar(out=sE[:, 0, :, :], in0=Eim[:, g8, :], scalar1=sgn[:, 0:1], scalar2=1.0, op0=ALU.mult, op1=ALU.mult), ["Eim"] + SGN, ["sE0"])
        S.add("dve", lambda e, g8=g8: e.tensor_scalar(out=sE[:, 1, :, :], in0=Ere[:, g8, :], scalar1=sgn[:, 1:2], scalar2=1.0, op0=ALU.mult, op1=ALU.mult), ["Ere"] + SGN, ["sE1"])
        sEim, nsEre = sE[:, 0, :, :], sE[:, 1, :, :]
        EreB, EimB = Ere[:, g8, :], Eim[:, g8, :]
        gen_mat(A32, EreB[:, :, 0:8], bbs[:, g8, :], sEim[:, :, 0:8], bbp[:, g8, :], ALU.add, ["Ere", "sE0", "bbs", "bbp"], ["A32"], mul_eng="dve")
        S.add("act", lambda e: e.copy(out=Ab[:, 0, :, :], in_=A32), ["A32"], ["Ab0"])
        gen_mat(Ab[:, 1, :, :], nsEre[:, :, 0:8], bbp[:, g8, :], EimB[:, :, 0:8], bbs[:, g8, :], ALU.add, ["Eim", "sE1", "bbs", "bbp"], ["Ab1"], mul_eng="dve")
        pv = PSB[4][:, 0:1024].rearrange("p (w g t) -> p w g t", w=2, g=GB)
        for w_ in range(2):
            for g in range(GB):
                S.add("pe", lambda e, w_=w_, g=g, pv=pv: e.transpose(out=pv[:, w_, g, :], in_=Ab[:, w_, g, :], identity=identb[:]), ["Ab%d" % w_, "identb"], ["PS4"])
        S.add("act", lambda e, pv=pv: e.copy(out=Msb, in_=pv), ["PS4"], ["Msb"])

    def ssm_sincos(gb):
        g8 = slice(gb * GB, gb * GB + GB)
        sEim, nsEre = sE[:, 0, :, :], sE[:, 1, :, :]
        EreB, EimB = Ere[:, g8, :], Eim[:, g8, :]
        UBK = ["Ub_%d" % g for g in range(GB)] + ["Ub_a%d" % g for g in range(GB)] + ["Ub_b%d" % g for g in range(GB)]
        GSK = ["Gs_%d" % g for g in range(GB)]
        S.add("dve", lambda e, g8=g8: e.tensor_tensor(out=sc[0][:, 0:GB * 259].rearrange("p (g k) -> p g k", k=259),
                                                       in0=pp[:, 3, g8].unsqueeze(2).to_broadcast([128, GB, 259]),
                                                       in1=krow[:, 0:259].unsqueeze(1).to_broadcast([128, GB, 259]), op=ALU.mult),
              [PPK(3), "krow", "sc0"], ["sc0"])
        sincos(sc[0][:, 0:GB * 259], cosT.rearrange("p g k -> p (g k)"), sinT.rearrange("p g k -> p (g k)"), GB * 259, ["sc0"], ["cosT"], ["sinT"])
        S.add("act", lambda e, gb=gb: e.copy(out=C258[:, gb * GB:gb * GB + GB], in_=cosT[:, :, 258]), ["cosT"], ["C258"])
        S.add("act", lambda e, gb=gb: e.copy(out=S258[:, gb * GB:gb * GB + GB], in_=sinT[:, :, 258]), ["sinT"], ["S258"])

    def ssm_frontB(gb):
        g8 = slice(gb * GB, gb * GB + GB)
        sEim, nsEre = sE[:, 0, :, :], sE[:, 1, :, :]
        EreB, EimB = Ere[:, g8, :], Eim[:, g8, :]
        UBK = ["Ub_%d" % g for g in range(GB)] + ["Ub_a%d" % g for g in range(GB)] + ["Ub_b%d" % g for g in range(GB)]
        GSK = ["Gs_%d" % g for g in range(GB)]
        gen_mat(R32, nsEre[:, :, 8:16], cself[:, g8, :], EimB[:, :, 8:16], cpart[:, g8, :], ALU.subtract, ["Eim", "sE1", "cself", "cpart"], ["R32"], mul_eng="dve")
        gen_mat(Mab[:, 0, :, :], nsEre[:, :, 16:24], cself[:, g8, :], EimB[:, :, 16:24], cpart[:, g8, :], ALU.subtract, ["Eim", "sE1", "cself", "cpart"], ["Mab0"], mul_eng="dve")
        gen_mat(Mab[:, 1, :, :], sEim[:, :, 16:24], cself[:, g8, :], EreB[:, :, 16:24], cpart[:, g8, :], ALU.subtract, ["Ere", "sE0", "cself", "cpart"], ["Mab1"], mul_eng="dve")
        for g0 in range(0, GB, 2):
            bk = 6 + (g0 // 2) % 2
            kb = "PS%d" % bk
            pv2 = PSB[bk][:, 30:30 + 640].rearrange("p (g t) -> p g t", t=320)
            for jj in range(2):
                gg = gb * GB + g0 + jj
                S.add("pe", lambda e, jj=jj, gg=gg, pv2=pv2: e.transpose(out=pv2[:, jj, 2:130], in_=u8p2[:, gg, :, :].rearrange("p s h -> p (s h)"), identity=identb[:]),
                      U8K_P2 + ["identb"], [kb])
                S.add("pe", lambda e, jj=jj, gg=gg, pv2=pv2: e.transpose(out=pv2[:, jj, 130:258], in_=u8own[:, gg, :, :].rearrange("p s h -> p (s h)"), identity=identb[:]),
                      U8K_OWN + ["identb"], [kb])
                S.add("act", lambda e, jj=jj, gg=gg, g0=g0, pv2=pv2: e.copy(out=Ub[:, g0 + jj, 2:258], in_=pv2[:, jj, 2:258]), [kb], ["Ub_%d" % (g0 + jj)])
                S.add("act", lambda e, jj=jj, gg=gg, g0=g0: e.copy(out=Ub[:, g0 + jj, 0:2], in_=Um[:, gg, 0:2]), UMK, ["Ub_a%d" % (g0 + jj)])
                S.add("act", lambda e, jj=jj, gg=gg, g0=g0: e.copy(out=Ub[:, g0 + jj, 258:274], in_=Um[:, gg, 2:18]), UMK, ["Ub_b%d" % (g0 + jj)])
        UBK = ["Ub_%d" % g for g in range(GB)] + ["Ub_a%d" % g for g in range(GB)] + ["Ub_b%d" % g for g in range(GB)]
        dma("sp", st_s[0:NS], din["st_self"][:, g8, :], [], ["st_s"], "st_s")
        dma("sp", st_p[0:NS], din["st_part"][:, g8, :], [], ["st_p"], "st_p")
        for g in range(GB):
            S.add("pe", lambda e, g=g: e.transpose(out=PS[2][:, g * NS:(g + 1) * NS], in_=st_s[0:NS, g, :], identity=identf[0:NS, 0:NS]), ["st_s", "identf"], ["PS2"])
            S.add("pe", lambda e, g=g: e.transpose(out=PS[2][:, 128 + g * NS:128 + (g + 1) * NS], in_=st_p[0:NS, g, :], identity=identf[0:NS, 0:NS]), ["st_p", "identf"], ["PS2"])
        h0v = PS[2][:, 0:GB * NS].rearrange("p (g b) -> p g b", b=NS)
        h0s = PS[2][:, 128:128 + GB * NS].rearrange("p (g b) -> p g b", b=NS)
        S.add("dve", lambda e, g8=g8: e.tensor_tensor(out=hs1, in0=h0v, in1=Ere[:, g8, 16:17].to_broadcast([128, GB, NS]), op=ALU.mult), ["PS2", "Ere"], ["hs1"])
        S.add("dve", lambda e: e.tensor_tensor(out=hs2, in0=h0s, in1=sE[:, 0, :, 16:17].to_broadcast([128, GB, NS]), op=ALU.mult), ["PS2", "sE0"], ["hs2"])
        S.add("pool", lambda e: e.tensor_tensor(out=hs1, in0=hs1, in1=hs2, op=ALU.add), ["hs1", "hs2"], ["hs1"])
        S.add("dve", lambda e, g8=g8: e.tensor_tensor(out=hn1, in0=h0v, in1=Ere[:, g8, 8:9].to_broadcast([128, GB, NS]), op=ALU.mult), ["PS2", "Ere"], ["hn1k"])
        S.add("dve", lambda e: e.tensor_tensor(out=hn2, in0=h0s, in1=sE[:, 0, :, 8:9].to_broadcast([128, GB, NS]), op=ALU.mult), ["PS2", "sE0"], ["hn2k"])
        S.add("pool", lambda e: e.tensor_tensor(out=Hins, in0=hn1, in1=hn2, op=ALU.add), ["hn1k", "hn2k"], ["Hinsk"])

    def ssm_loop(gb):
        g8 = slice(gb * GB, gb * GB + GB)
        sEim, nsEre = sE[:, 0, :, :], sE[:, 1, :, :]
        EreB, EimB = Ere[:, g8, :], Eim[:, g8, :]
        UBK = ["Ub_%d" % g for g in range(GB)] + ["Ub_a%d" % g for g in range(GB)] + ["Ub_b%d" % g for g in range(GB)]
        GSK = ["Gs_%d" % g for g in range(GB)]
        for g in range(GB):
            gg = gb * GB + g
            pa, pb_ = (0, 1) if g % 2 == 0 else (6, 7)
            ka, kb_ = "PS%d" % pa, "PS%d" % pb_
            S.add("pe", lambda e, g=g, pa=pa: e.matmul(PS[pa][:, 0:NU], lhsT=Msb[:, 0, g, :], rhs=Ub[:, g, :], start=True, stop=True), ["Msb"] + UBK, [ka])
            S.add("pe", lambda e, g=g, pb_=pb_: e.matmul(PS[pb_][:, 0:NCH], lhsT=Msb[:, 1, g, :], rhs=Ub[:, g, 0:NCH], start=True, stop=True), ["Msb"] + UBK, [kb_])
            S.add("dve", lambda e, g=g, pa=pa: e.tensor_tensor(out=Xt[:, g, :], in0=PS[pa][:, 0:NCH], in1=cosT[:, g, 1:259], op=ALU.mult), [ka, "cosT"], ["Xt_%d" % g])
            S.add("act", lambda e, g=g, pa=pa: e.copy(out=Xsm[:, g, :], in_=PS[pa][:, NCH:NU]), [ka], ["Xsm_%d" % g])
            x2 = Xt2[g % 2]
            S.add("dve", lambda e, g=g, pb_=pb_, x2=x2: e.tensor_tensor(out=x2, in0=PS[pb_][:, 0:NCH], in1=sinT[:, g, 1:259], op=ALU.mult), [kb_, "sinT"], ["Xt2_%d" % (g % 2)])
            S.add("dve", lambda e, g=g, x2=x2: e.tensor_tensor(out=Xt[:, g, :], in0=Xt[:, g, :], in1=x2, op=ALU.add), ["Xt_%d" % g, "Xt2_%d" % (g % 2)], ["Xt_%d" % g])
            S.add("dve", lambda e, g=g, gg=gg: e.tensor_tensor_scan(out=Gs[:, g, 1:259], data0=mumag[:, gg:gg + 1].to_broadcast([128, NCH]), data1=Xt[:, g, :],
                                                                    initial=0.0, op0=ALU.mult, op1=ALU.add), ["Xt_%d" % g, "mumag", "Gs"], ["Gs_%d" % g])
            S.add("act", lambda e, g=g, gg=gg: e.copy(out=G258[:, gg:gg + 1], in_=Gs[:, g, 258:259]), ["Gs_%d" % g], ["G258"])
        S.add("dve", lambda e: e.tensor_tensor(out=hs1, in0=hs1, in1=Xsm, op=ALU.add), ["hs1"] + ["Xsm_%d" % g for g in range(GB)], ["hs1"])

    def ssm_y1(gb):
        g8 = slice(gb * GB, gb * GB + GB)
        sEim, nsEre = sE[:, 0, :, :], sE[:, 1, :, :]
        EreB, EimB = Ere[:, g8, :], Eim[:, g8, :]
        UBK = ["Ub_%d" % g for g in range(GB)] + ["Ub_a%d" % g for g in range(GB)] + ["Ub_b%d" % g for g in range(GB)]
        GSK = ["Gs_%d" % g for g in range(GB)]
        GSK = ["Gs_%d" % g for g in range(GB)]
        for g in range(GB):
            S.add("pe", lambda e, g=g: e.matmul(PS[5][:, g * 128:(g + 1) * 128], lhsT=A32[:, g, :], rhs=R32[:, g, :], start=True, stop=True), ["A32", "R32"], ["PS5"])
        S.add("dve", lambda e: e.tensor_tensor(out=Mib, in0=PS[5][:, 0:GB * 128].rearrange("p (g q) -> p g q", q=128),
                                               in1=blockmask[:].unsqueeze(1).to_broadcast([128, GB, 128]), op=ALU.mult), ["PS5", "blockmask"], ["Mib"])

    def ssm_y2a(gb):
        g8 = slice(gb * GB, gb * GB + GB)
        sEim, nsEre = sE[:, 0, :, :], sE[:, 1, :, :]
        EreB, EimB = Ere[:, g8, :], Eim[:, g8, :]
        UBK = ["Ub_%d" % g for g in range(GB)] + ["Ub_a%d" % g for g in range(GB)] + ["Ub_b%d" % g for g in range(GB)]
        GSK = ["Gs_%d" % g for g in range(GB)]
        S.add("dve", lambda e: e.tensor_tensor(out=Pcs[:, 0, :, :], in0=cosT[:, :, 130:258], in1=Gs[:, :, 130:258], op=ALU.mult), ["cosT"] + GSK, ["Pc0"])
        S.add("dve", lambda e: e.tensor_tensor(out=Pcs[:, 1, :, :], in0=sinT[:, :, 130:258], in1=Gs[:, :, 130:258], op=ALU.mult), ["sinT"] + GSK, ["Pc1"])
        for g in range(GB):
            osl = PS[5][:, g * 128:(g + 1) * 128]
            S.add("pe", lambda e, g=g, osl=osl: e.matmul(osl, lhsT=Mib[:, g, :], rhs=Ub[:, g, 130:258], start=True, stop=False), ["Mib"] + UBK, ["PS5"])
            S.add("pe", lambda e, g=g, osl=osl: e.matmul(osl, lhsT=Mab[:, 0, g, :], rhs=Pcs[:, 0, g, :], start=False, stop=False), ["Mab0", "Pc0"], ["PS5"])
            S.add("pe", lambda e, g=g, osl=osl: e.matmul(osl, lhsT=Mab[:, 1, g, :], rhs=Pcs[:, 1, g, :], start=False, stop=True), ["Mab1", "Pc1"], ["PS5"])
            oss = PS[4][:, g * NS:(g + 1) * NS]
            S.add("pe", lambda e, g=g, oss=oss: e.matmul(oss, lhsT=Mib[:, g, :], rhs=Ub[:, g, 258:274], start=True, stop=False), ["Mib"] + UBK, ["PS4"])
            S.add("pe", lambda e, g=g, oss=oss: e.matmul(oss, lhsT=Mab[:, 0, g, :], rhs=Hins[:, g, :], start=False, stop=True), ["Mab0", "Hinsk"], ["PS4"])


    def ssm_y2b(gb):
        g8 = slice(gb * GB, gb * GB + GB)
        sEim, nsEre = sE[:, 0, :, :], sE[:, 1, :, :]
        EreB, EimB = Ere[:, g8, :], Eim[:, g8, :]
        UBK = ["Ub_%d" % g for g in range(GB)] + ["Ub_a%d" % g for g in range(GB)] + ["Ub_b%d" % g for g in range(GB)]
        GSK = ["Gs_%d" % g for g in range(GB)]
        def gelu_chain(ps_view, u_view, ya, yb, yout, W, kin, kout, tag):
            kk = "ypo" if tag == "o" else "yps"
            dsk_b = dsk[:, g8].unsqueeze(2).to_broadcast([128, GB, W])
            S.add("dve", lambda e: e.tensor_tensor(out=ya, in0=u_view, in1=dsk_b, op=ALU.mult), UBK + ["dsk", kk], [kk])
            S.add("dve", lambda e: e.tensor_tensor(out=ya, in0=ya, in1=ps_view, op=ALU.add), [kk] + kin, [kk])
            S.add("act", lambda e: e.activation(out=yb, in_=ya, func=AF.Square), [kk], [kk])
            S.add("dve", lambda e: e.scalar_tensor_tensor(out=yb, in0=yb, scalar=0.044715, in1=ya, op0=ALU.mult, op1=ALU.mult), [kk], [kk])
            S.add("dve", lambda e: e.tensor_tensor(out=yb, in0=yb, in1=ya, op=ALU.add), [kk], [kk])
            S.add("act", lambda e: e.activation(out=yb, in_=yb, func=AF.Tanh, scale=0.7978845608028654), [kk], [kk])
            S.add("dve", lambda e: e.scalar_tensor_tensor(out=yout, in0=yb, scalar=1.0, in1=ya, op0=ALU.add, op1=ALU.mult), [kk] + kout, kout)
        gelu_chain(PS[5][:, 0:GB * 128].rearrange("p (g q) -> p g q", q=128), Ub[:, :, 130:258], yA, yB, ygb, 128, ["PS5"], ["ygbk"], "o")
        gelu_chain(PS[4][:, 0:GB * NS].rearrange("p (g b) -> p g b", b=NS), Ub[:, :, 258:274], ysA, ysB, ygs, NS, ["PS4"], ["ygsk"], "s")
        goff = (gb % 2) * GB
        pvy = PSB[6][:, 0:GB * 128].rearrange("p (g q) -> p g q", q=128)
        for g in range(GB):
            S.add("pe", lambda e, g=g: e.transpose(out=pvy[:, g, :], in_=ygb[:, g, :], identity=identb[:]), ["ygbk", "identb"], ["PS6"])
        S.add("act", lambda e, goff=goff: e.activation(out=y8.rearrange("p t (g h) -> p g t h", h=16)[:, goff:goff + GB, :, :],
                                                       in_=pvy.rearrange("p g (t h) -> p g t h", h=16), func=AF.Identity, scale=0.5), ["PS6"], ["y8_%d" % (gb % 2)])
        pvs = PSB[7][:, 0:GB * 128].rearrange("p (g q) -> p g q", q=128)
        for g in range(GB):
            S.add("pe", lambda e, g=g: e.transpose(out=pvs[0:NS, g, :], in_=ygs[:, g, :], identity=identb[:]), ["ygsk", "identb"], ["PS7"])
        S.add("act", lambda e, goff=goff: e.activation(out=y8s[0:NS, goff * 16:(goff + GB) * 16].rearrange("p (g h) -> p g h", h=16), in_=pvs[0:NS, :, 112:128],
                                                       func=AF.Identity, scale=0.5), ["PS7"], ["y8s_%d" % (gb % 2)])
        if gb % 2 == 1:
            fc = gb // 2
            pvt = PSB[0][:, 0:1024].rearrange("p (t c) -> p t c", c=128)
            for t in range(8):
                S.add("pe", lambda e, t=t: e.transpose(out=pvt[:, t, :], in_=y8[:, t, :], identity=identb[:]), ["y8_0", "y8_1", "identb"], ["PS0"])
            S.add("act", lambda e, fc=fc: e.copy(out=ST[:, fc, 0:NT].rearrange("p (c t) -> p t c", t=8), in_=pvt), ["PS0"], ["ST_%d" % fc])
            S.add("pe", lambda e: e.transpose(out=PSB[1][:, 0:NS], in_=y8s[0:NS, :], identity=identb[0:NS, 0:NS]), ["y8s_0", "y8s_1", "identb"], ["PS1"])
            S.add("dve", lambda e, fc=fc: e.tensor_copy(out=ST[:, fc, NT:NTS], in_=PSB[1][:, 0:NS]), ["PS1"], ["STs_%d" % fc])
        for g in range(GB):
            S.add("pe", lambda e, g=g: e.transpose(out=PS[3][0:NS, g * 128:(g + 1) * 128], in_=hs1[:, g, :], identity=identf[:]), ["hs1", "identf"], ["PS3"])
        S.add("act", lambda e: e.copy(out=hso[0:NS], in_=PS[3][0:NS, 0:GB * 128].rearrange("p (g q) -> p g q", q=128)), ["PS3"], ["hso"])
        dma("sp", dout["sssm"][:, g8, :], hso[0:NS], ["hso"], [], "hso")

    NBATCH = 64 // GB
    ssm_sincos(0)
    for gb in range(NBATCH):
        ssm_frontA(gb)
        ssm_frontB(gb)
        ssm_loop(gb)
        ssm_y1(gb)
        ssm_y2a(gb)
        if gb + 1 < NBATCH:
            ssm_sincos(gb + 1)
        ssm_y2b(gb)
        S.mark('batch%d' % gb)

    S.add("pe", lambda e: e.matmul(PS[2][:, 0:64], lhsT=swapm[:], rhs=G258[:], start=True, stop=True), ["swapm", "G258"], ["PS2"])
    S.add("dve", lambda e: e.tensor_tensor(out=pp[:, 11, :], in0=C258[:], in1=G258[:], op=ALU.mult), ["C258", "G258"], [PPK(11)])
    S.add("dve", lambda e: e.tensor_scalar(out=pp[:, 12, :], in0=S258[:], scalar1=sgn[:, 0:1], scalar2=1.0, op0=ALU.mult, op1=ALU.mult), ["S258"] + SGN, [PPK(12)])
    S.add("dve", lambda e: e.tensor_tensor(out=pp[:, 12, :], in0=pp[:, 12, :], in1=PS[2][:, 0:64], op=ALU.mult), [PPK(12), "PS2"], [PPK(12)])
    S.add("dve", lambda e: e.tensor_tensor(out=pp[:, 11, :], in0=pp[:, 11, :], in1=pp[:, 12, :], op=ALU.add), [PPK(11), PPK(12)], [PPK(11)])
    S.add("pe", lambda e: e.transpose(out=PS[3][0:64, 0:128], in_=pp[:, 11, :], identity=identf[:]), [PPK(11), "identf"], ["PS3"])
    S.add("act", lambda e: e.copy(out=hfo[0:64, :], in_=PS[3][0:64, 0:128]), ["PS3"], ["hfo"])
    dma("sp", dout["pssm"], hfo[0:64, :], ["hfo"], [], "hfo")

    S.mark('p2')
    S.barrier(mk_bar)
    NKC = 1168
    qT = carve(0, 8 * NTS, BF16).rearrange("p (h t) -> p h t", t=NTS)
    kT = carve(17 * KB, 4 * NKC, BF16).rearrange("p (g t) -> p g t", t=NKC)
    PT = [[carve(27 * KB + (3 * j + i) * KB, 512, BF16) for i in range(3)] for j in range(2)]
    AT = carve(51 * KB, 8 * NTS, BF16).rearrange("p (c t) -> p c t", t=NTS)
    Vaug = carve(100 * KB, 10 * 4 * 128, BF16).rearrange("p (b g q) -> p b g q", b=10, q=128)
    o3 = 100 * KB + 10 * KB
    tmpq = [carve(o3 + 4 * KB * i, 1024, F32) for i in range(3)]; o3 += 12 * KB
    sqj = carve(o3 - 8 * KB, D, F32)
    xbuf_off3 = o3
    xbuf = [carve(o3, D, F32)] * 2; o3 += 8 * KB
    xnb = [carve(o3, D, BF16)] * 2; o3 += 4 * KB
    xnTt = [carve(o3, ND * 128, BF16).rearrange("p (c t) -> p c t", c=ND)] * 2; o3 += 4 * KB
    kf = [carve(o3 + i * KB, 256, F32) for i in range(2)]; o3 += 2 * KB
    vf = [carve(o3 + i * KB, 256, F32) for i in range(2)]; o3 += 2 * KB
    kb16 = carve(o3, 256, BF16); o3 += 512
    qf = carve(o3, 512, F32); o3 += 2 * KB
    qb16 = carve(o3, 512, BF16); o3 += KB
    otmp = [carve(o3 + i * KB, 512, BF16) for i in range(2)]; o3 += 2 * KB
    dtmp2 = [carve(xbuf_off3 + 2 * KB * i, 512, F32) for i in range(2)]
    assert o3 <= ARENA, o3
    gain_bc = carve(0, D, F32)
    cnt["x"] = 0
    dma("sp", gain_bc, din["norm_gain"].partition_broadcast(128), [], ["gain_bc"], "gain_bc")
    dma("pool", masks[:, 0, :], din["maskc"], [], ["masks0"], "masks0")
    dma("pool", masks[:, 1, :], din["maskp"], [], ["masks1"], "masks1")
    dma("pool", masks[:, 2, :], din["maskp0"], [], ["masks2"], "masks2")
    dma("sp", expsink[:], din["sinks"].partition_broadcast(128), [], ["expsink"], "expsink")
    S.add("act", lambda e: e.activation(out=expsink[:], in_=expsink[:], func=AF.Exp), ["expsink"], ["expsink"])
    S.add("pool", lambda e: e.memset(onesb[:], 1.0), [], ["onesb"])
    S.add("pool", lambda e: e.memset(Vaug[:, :, :, 64:128], 1.0), [], ["Vones"])

    load_w(0, din["w_in"], 1024, 512, ND)

    def kv_proj(lhs_fn, lhs_keys, n):
        b = cnt["kv"] % 2
        cnt["kv"] += 1
        kb = proj_tok(lhs_fn, lhs_keys, n, 0, 512, b)
        return b, kb

    def kv_post(b, kb, n, ti, kcol0, vblk, out_k=None, out_v=None):
        kfj, vfj = kf[b], vf[b]
        S.add("act", lambda e: e.copy(out=vfj[0:n, :], in_=PS[b][0:n, 256:512]), [kb], ["vf%d" % b])
        headnorm_rope([(PS[b][0:n, 0:256], 256)], [kb], n, 4, ti, 64, 1.0, kfj[0:n, :], ["kf%d" % b])
        if out_k is not None:
            dma("sp", out_k, kfj[0:n, :], ["kf%d" % b], ["swk_new"], "kf%d" % b)
        if out_v is not None:
            dma("sp", out_v, vfj[0:n, :], ["vf%d" % b], ["swv_new"], "vf%d" % b)
        if vblk is not None:
            S.add("pool", lambda e: e.tensor_copy(out=Vaug[0:n, vblk, :, 0:64], in_=vfj[0:n, :].rearrange("p (g d) -> p g d", d=64)),
                  ["vf%d" % b, "Vones"], ["Vaug_%d" % vblk])
            S.add("act", lambda e: e.copy(out=kb16[0:n, :], in_=kfj[0:n, :]), ["kf%d" % b], ["kb16"])
            pvk = PSB[6][:, 0:512].rearrange("p (g t) -> p g t", t=128)
            for g in range(4):
                S.add("pe", lambda e, g=g: e.transpose(out=pvk[0:64, g, 0:n], in_=kb16[0:n, g * 64:(g + 1) * 64], identity=identb[0:n, 0:n]),
                      ["kb16", "identb"], ["PS6"])
            S.add("act", lambda e: e.copy(out=kT[0:64, :, kcol0:kcol0 + n], in_=pvk[0:64, :, 0:n]), ["PS6"], ["kT_%d" % vblk])

    def kv_tile(lhs_fn, lhs_keys, n, ti, kcol0, vblk, out_k=None, out_v=None):
        b, kb = kv_proj(lhs_fn, lhs_keys, n)
        kv_post(b, kb, n, ti, kcol0, vblk, out_k, out_v)

    for (r0, n, ti, kc0, vb, ok, ov) in ((0, 128, 8, 1024, 8, None, None), (128, 16, 9, 1152, 9, dout["pmk"], dout["pmv"])):
        tt = xnTt[0]
        norm_tile(din["x_kvx"][r0:r0 + n, :], n, lambda dc0, tt=tt, n=n: tt[:, dc0:dc0 + 8, 0:n], ["xnTt0"])
        kv_tile(lambda dc, tt=tt, n=n: tt[:, dc, 0:n], ["xnTt0"], n, ti, kc0, vb, ok, ov)
    S.add("pool", lambda e: e.memset(qT, 0.0), ["gain_bc"], ["qT", "gain_bc"])
    dma("sp", dout["swk"][:, 0:127, :], din["cwk"][:, 1:128, :], [], [], "swk")
    dma("sp", dout["swv"][:, 0:127, :], din["cwv"][:, 1:128, :], [], [], "swv")
    kspecs = [(lambda dc, i=i: xnT[:, dc, i * 128:(i + 1) * 128], ["xnT_%d" % i], 128, i, i * 128, i,
               dout["pwk"] if i == 7 else None, dout["pwv"] if i == 7 else None) for i in range(8)]
    kspecs.append((lambda dc: xnT[:, dc, NT:NTS], ["xnT_s"], NS, 10, 0, None, dout["swk"][:, 127, :], dout["swv"][:, 127, :]))
    pendk = kv_proj(kspecs[0][0], kspecs[0][1], kspecs[0][2])
    for ki, (lf, lk, n_, ti_, kc_, vb_, ok_, ov_) in enumerate(kspecs):
        curk = pendk
        if ki + 1 < len(kspecs):
            pendk = kv_proj(kspecs[ki + 1][0], kspecs[ki + 1][1], kspecs[ki + 1][2])
        kv_post(curk[0], curk[1], n_, ti_, kc_, vb_, ok_, ov_)
    KTK = ["kT_%d" % i for i in range(10)]
    VK = ["Vaug_%d" % i for i in range(10)] + ["Vones"]

    def q_proj(lhs_fn, lhs_keys, n, slot):
        b = cnt["kv"] % 2
        cnt["kv"] += 1
        kb = proj_tok(lhs_fn, lhs_keys, n, slot, 512, b)
        return b, kb

    def q_post(b, kb, n, ti, tokc0, smp_hoff):
        headnorm_rope([(PS[b][0:n, 0:512], 512)], [kb], n, 8, ti, 0, 0.125, qf[0:n, :], ["qf"])
        if smp_hoff is not None:
            S.add("act", lambda e: e.copy(out=qs16[0:n, smp_hoff * 64:(smp_hoff + 8) * 64], in_=qf[0:n, :]), ["qf"], ["qs16_%d" % smp_hoff])
            return
        S.add("act", lambda e: e.copy(out=qb16[0:n, :], in_=qf[0:n, :]), ["qf"], ["qb16"])
        pvq = PSB[7][:, 0:1024].rearrange("p (h t) -> p h t", t=128)
        for h in range(8):
            S.add("pe", lambda e, h=h: e.transpose(out=pvq[0:64, h, 0:n], in_=qb16[0:n, h * 64:(h + 1) * 64], identity=identb[0:n, 0:n]),
                  ["qb16", "identb"], ["PS7"])
        S.add("act", lambda e: e.copy(out=qT[0:64, :, tokc0:tokc0 + n], in_=pvq[0:64, :, 0:n]), ["PS7", "qT"], ["qT_%d" % (tokc0 // 128)])

    def att_setup(nb, g, hl0, pb):
        d_ = dict(nb=nb, g=g, hl0=hl0, pb=pb)
        d_["banks"] = (2, 3, 4, 5) if pb == 0 else (6, 7, 0, 1)
        d_["prev_cols"] = slice(1024, 1152) if nb == 0 else slice((nb - 1) * 128, nb * 128)
        d_["prev_blk"] = 8 if nb == 0 else nb - 1
        d_["mprev"] = 2 if nb == 0 else 1
        return d_

    def att_A(d_):
        nb, g, hl0, pb = d_["nb"], d_["g"], d_["hl0"], d_["pb"]
        PSm, PSp, PSc, PSo = d_["banks"]
        PTm, PTp, PTc = PT[pb]
        qk = ["qT_%d" % nb]
        cur_cols = slice(nb * 128, (nb + 1) * 128)
        prev_cols, mprev = d_["prev_cols"], d_["mprev"]
        for r in range(4):
            rs = slice(r * 128, (r + 1) * 128)
            qa = qT[0:64, hl0 + r, nb * 128:(nb + 1) * 128]
            S.add("pe", lambda e, rs=rs, qa=qa: e.matmul(PS[PSm][0:16, rs], lhsT=kT[0:64, g, 1152:1168], rhs=qa, start=True, stop=True), KTK + qk, ["PS%d" % PSm])
            S.add("pe", lambda e, rs=rs: e.matmul(PS[PSp][:, rs], lhsT=identb[:], rhs=masks[:, mprev, :], start=True, stop=False), ["identb", "masks1", "masks2"], ["PS%d" % PSp])
            S.add("pe", lambda e, rs=rs, qa=qa: e.matmul(PS[PSp][:, rs], lhsT=kT[0:64, g, prev_cols], rhs=qa, start=False, stop=True), KTK + qk, ["PS%d" % PSp])
            S.add("pe", lambda e, rs=rs: e.matmul(PS[PSc][:, rs], lhsT=identb[:], rhs=masks[:, 0, :], start=True, stop=False), ["identb", "masks0"], ["PS%d" % PSc])
            S.add("pe", lambda e, rs=rs, qa=qa: e.matmul(PS[PSc][:, rs], lhsT=kT[0:64, g, cur_cols], rhs=qa, start=False, stop=True), KTK + qk, ["PS%d" % PSc])
        km, kp, kc = "PTm%d" % pb, "PTp%d" % pb, "PTc%d" % pb
        S.add("act", lambda e: e.activation(out=PTm[0:16, :], in_=PS[PSm][0:16, :], func=AF.Exp), ["PS%d" % PSm], [km])
        S.add("act", lambda e: e.activation(out=PTp, in_=PS[PSp][:, :], func=AF.Exp), ["PS%d" % PSp], [kp])
        S.add("act", lambda e: e.activation(out=PTc, in_=PS[PSc][:, :], func=AF.Exp), ["PS%d" % PSc], [kc])

    def att_B(d_):
        nb, g, pb = d_["nb"], d_["g"], d_["pb"]
        PSm, PSp, PSc, PSo = d_["banks"]
        PTm, PTp, PTc = PT[pb]
        km, kp, kc = "PTm%d" % pb, "PTp%d" % pb, "PTc%d" % pb
        prev_blk = d_["prev_blk"]
        for r in range(4):
            rs = slice(r * 128, (r + 1) * 128)
            S.add("pe", lambda e, rs=rs: e.matmul(PS[PSo][:, rs], lhsT=Vaug[0:16, 9, g, :], rhs=PTm[0:16, rs], start=True, stop=False), VK + [km], ["PS%d" % PSo])
            S.add("pe", lambda e, rs=rs: e.matmul(PS[PSo][:, rs], lhsT=Vaug[:, prev_blk, g, :], rhs=PTp[:, rs], start=False, stop=False), VK + [kp], ["PS%d" % PSo])
            S.add("pe", lambda e, rs=rs: e.matmul(PS[PSo][:, rs], lhsT=Vaug[:, nb, g, :], rhs=PTc[:, rs], start=False, stop=True), VK + [kc], ["PS%d" % PSo])

    def att_C(d_):
        nb, g, pb = d_["nb"], d_["g"], d_["pb"]
        PSo = d_["banks"][3]
        dtm = dtmp2[pb]
        dv = dtm[64:128, :].rearrange("p (r q) -> p r q", q=128)
        kd = "dtmp%d" % pb
        S.add("dve", lambda e: e.tensor_tensor(out=dv, in0=PS[PSo][64:128, :].rearrange("p (r q) -> p r q", q=128),
                                               in1=expsink[64:128, 4 * g:4 * g + 4].unsqueeze(2).to_broadcast([64, 4, 128]), op=ALU.add), ["PS%d" % PSo, "expsink"], [kd, "xbuf0"])
        S.add("act", lambda e: e.activation(out=dtm[64:128, :], in_=dtm[64:128, :], func=AF.Ln), [kd], [kd])
        S.add("act", lambda e: e.activation(out=dtm[64:128, :], in_=dtm[64:128, :], func=AF.Exp, scale=-1.0), [kd], [kd])
        ot = otmp[pb]
        S.add("dve", lambda e: e.tensor_tensor(out=ot[0:64, :], in0=PS[PSo][0:64, :], in1=dtm[64:128, :], op=ALU.mult), ["PS%d" % PSo, kd], ["otmp%d" % pb])
        o3v = ot[0:64, :].rearrange("p (r q) -> p r q", q=128)
        dma("sp", AT[0:64, 2 * g:2 * g + 2, nb * 128:(nb + 1) * 128], o3v[:, 0:4:2, :], ["otmp%d" % pb], ["AT_%d_%d" % (g, nb)], "otmp%d" % pb)
        dma("sp", AT[64:128, 2 * g:2 * g + 2, nb * 128:(nb + 1) * 128], o3v[:, 1:4:2, :], ["otmp%d" % pb], ["ATb_%d_%d" % (g, nb)], "otmp%d" % pb)

    acnt = 0
    for qblk in range(2):
        load_w(1, din["w_in"], qblk * 512, 512, ND)
        qspecs = [(lambda dc, i=i: xnT[:, dc, i * 128:(i + 1) * 128], ["xnT_%d" % i], 128, i, i * 128, None) for i in range(8)]
        qspecs.append((lambda dc: xnT[:, dc, NT:NTS], ["xnT_s"], NS, 10, 0, qblk * 8))
        pend = q_proj(qspecs[0][0], qspecs[0][1], qspecs[0][2], 1)
        for qi, (lf, lk, n_, ti_, tc_, sh_) in enumerate(qspecs):
            cur = pend
            if qi + 1 < len(qspecs):
                pend = q_proj(qspecs[qi + 1][0], qspecs[qi + 1][1], qspecs[qi + 1][2], 1)
            q_post(cur[0], cur[1], n_, ti_, tc_, sh_)
        blocks = []
        for nb in range(8):
            for gl in range(2):
                blocks.append(att_setup(nb, qblk * 2 + gl, gl * 4, acnt % 2))
                acnt += 1
        att_A(blocks[0])
        for bi in range(len(blocks)):
            if bi + 1 < len(blocks):
                att_A(blocks[bi + 1])
            att_B(blocks[bi])
            att_C(blocks[bi])
    ATK = ["AT_%d_%d" % (g, nb) for g in range(4) for nb in range(8)] + ["ATb_%d_%d" % (g, nb) for g in range(4) for nb in range(8)]
    S.mark('p3a')

    S.barrier(mk_bar)
    Kc = carve(0, NS * 256, BF16).rearrange("p (b e) -> p b e", e=256)
    Vc = carve(8 * KB, NS * 256, BF16).rearrange("p (b e) -> p b e", e=256)
    Kmc = carve(16 * KB, NS * 256, BF16).rearrange("p (b e) -> p b e", e=256)
    Vmc = carve(24 * KB, NS * 256, BF16).rearrange("p (b e) -> p b e", e=256)
    KsT = carve(100 * KB, NS * 4 * 128, BF16).rearrange("p (b g t) -> p b g t", b=NS, t=128)
    KsT2 = carve(116 * KB, NS * 4 * 32, BF16).rearrange("p (b g t) -> p b g t", b=NS, t=32)
    o4 = 120 * KB
    qsT = carve(o4, 16 * NS, BF16).rearrange("p (h b) -> p h b", b=NS); o4 += 512
    PTs = carve(o4, 256, BF16); o4 += 512
    PTs2 = carve(o4, 256, BF16); o4 += 512
    dts = carve(o4, 256, F32); o4 += KB
    osb = carve(o4, 256, BF16).rearrange("p (h b) -> p h b", b=NS); o4 += 512
    load_w(0, din["w_in"], 1536, 512, ND)
    load_w(1, din["w_in"], 1536 + 512, 512, ND)
    dma("pool", Kc, din["cwk"].rearrange("b j e -> j b e"), [], ["Kc"], "Kc")
    dma("pool", Vc, din["cwv"].rearrange("b j e -> j b e"), [], ["Vc"], "Vc")
    dma("pool", Kmc[0:16], din["cmk"].rearrange("b j e -> j b e"), [], ["Kmc_a"], "Kmc")
    dma("pool", Vmc[0:16], din["cmv"].rearrange("b j e -> j b e"), [], ["Vmc_a"], "Vmc")
    dma("pool", Kmc[16:17], dout["swk"][:, 127:128, :].rearrange("b o e -> o b e"), ["swk_new"], ["Kmc_b"], "Kmc")
    dma("pool", Vmc[16:17], dout["swv"][:, 127:128, :].rearrange("b o e -> o b e"), ["swv_new"], ["Vmc_b"], "Vmc")
    pvq2 = PSB[7][:, 0:256].rearrange("p (h b) -> p h b", b=NS)
    for h in range(16):
        S.add("pe", lambda e, h=h: e.transpose(out=pvq2[0:64, h, :], in_=qs16[0:NS, h * 64:(h + 1) * 64], identity=identb[0:NS, 0:NS]),
              ["qs16_0", "qs16_8", "identb"], ["PS7"])
    S.add("act", lambda e: e.copy(out=qsT[0:64], in_=pvq2[0:64]), ["PS7"], ["qsT"])
    for b0 in range(0, NS, 2):
        bk = 4 + (b0 // 2) % 2
        pvk2 = PSB[bk][:, 0:1024].rearrange("p (b g t) -> p b g t", b=2, t=128)
        for bb in range(2):
            for g in range(4):
                S.add("pe", lambda e, bb=bb, g=g, b0=b0, pvk2=pvk2: e.transpose(out=pvk2[0:64, bb, g, :], in_=Kc[:, b0 + bb, g * 64:(g + 1) * 64], identity=identb[:]),
                      ["Kc", "identb"], ["PS%d" % bk])
        S.add("act" if (b0 // 2) % 2 == 0 else "dve",
              (lambda e, b0=b0, pvk2=pvk2: e.copy(out=KsT[0:64, b0:b0 + 2], in_=pvk2[0:64])) if (b0 // 2) % 2 == 0 else
              (lambda e, b0=b0, pvk2=pvk2: e.tensor_copy(out=KsT[0:64, b0:b0 + 2], in_=pvk2[0:64])), ["PS%d" % bk], ["KsT_%d" % b0])
    KSK = ["KsT_%d" % b0 for b0 in range(0, NS, 2)]
    for b0 in range(0, NS, 8):
        bk = 6 + (b0 // 8) % 2
        pvk3 = PSB[bk][:, 0:1024].rearrange("p (b g t) -> p b g t", b=8, t=32)
        for bb in range(8):
            for g in range(4):
                S.add("pe", lambda e, bb=bb, g=g, b0=b0, pvk3=pvk3: e.transpose(out=pvk3[0:64, bb, g, 0:17], in_=Kmc[0:17, b0 + bb, g * 64:(g + 1) * 64],
                                                                              identity=identb[0:17, 0:17]), ["Kmc_a", "Kmc_b", "identb"], ["PS%d" % bk])
        S.add("act", lambda e, b0=b0, pvk3=pvk3: e.copy(out=KsT2[0:64, b0:b0 + 8, :, 0:17], in_=pvk3[0:64, :, :, 0:17]), ["PS%d" % bk], ["KsT2_%d" % b0])
    KS2K = ["KsT2_0", "KsT2_8"]
    for b in range(NS):
        for g in range(4):
            cs = slice(b * 16 + g * 4, b * 16 + g * 4 + 4)
            S.add("pe", lambda e, b=b, g=g, cs=cs: e.matmul(PS[0][:, cs], lhsT=KsT[0:64, b, g, :], rhs=qsT[0:64, 4 * g:4 * g + 4, b], start=True, stop=True),
                  KSK + ["qsT"], ["PS0"])
            S.add("pe", lambda e, b=b, g=g, cs=cs: e.matmul(PS[1][0:17, cs], lhsT=KsT2[0:64, b, g, 0:17], rhs=qsT[0:64, 4 * g:4 * g + 4, b], start=True, stop=True),
                  KS2K + ["qsT"], ["PS1"])
    S.add("act", lambda e: e.activation(out=PTs, in_=PS[0][:, 0:256], func=AF.Exp), ["PS0"], ["PTs"])
    S.add("act", lambda e: e.activation(out=PTs2[0:17], in_=PS[1][0:17, 0:256], func=AF.Exp), ["PS1"], ["PTs2"])
    S.add("pool", lambda e: e.memset(PTs[0:1, :], 0.0), ["PTs"], ["PTs"])
    S.add("pe", lambda e: e.matmul(PS[2][0:64, 0:256], lhsT=onesb[:, 0:64], rhs=PTs, start=True, stop=False), ["onesb", "PTs"], ["PS2"])
    S.add("pe", lambda e: e.matmul(PS[2][0:64, 0:256], lhsT=onesb[0:17, 0:64], rhs=PTs2[0:17], start=False, stop=True), ["onesb", "PTs2"], ["PS2"])
    for b in range(NS):
        for g in range(4):
            cs = slice(b * 16 + g * 4, b * 16 + g * 4 + 4)
            S.add("pe", lambda e, b=b, g=g, cs=cs: e.matmul(PS[3][0:64, cs], lhsT=Vc[:, b, g * 64:(g + 1) * 64], rhs=PTs[:, cs], start=True, stop=False), ["Vc", "PTs"], ["PS3"])
            S.add("pe", lambda e, b=b, g=g, cs=cs: e.matmul(PS[3][0:64, cs], lhsT=Vmc[0:17, b, g * 64:(g + 1) * 64], rhs=PTs2[0:17, cs], start=False, stop=True),
                  ["Vmc_a", "Vmc_b", "PTs2"], ["PS3"])
    dv3 = dts[0:64].rearrange("p (b h) -> p b h", h=16)
    S.add("dve", lambda e: e.tensor_tensor(out=dv3, in0=PS[2][0:64, 0:256].rearrange("p (b h) -> p b h", h=16),
                                           in1=expsink[0:64, :].unsqueeze(1).to_broadcast([64, NS, 16]), op=ALU.add), ["PS2", "expsink"], ["dts"])
    S.add("dve", lambda e: e.reciprocal(out=dts[0:64], in_=dts[0:64]), ["dts"], ["dts"])
    S.add("dve", lambda e: e.tensor_tensor(out=osb[0:64].rearrange("p h b -> p b h"), in0=PS[3][0:64, 0:256].rearrange("p (b h) -> p b h", h=16), in1=dv3, op=ALU.mult),
          ["PS3", "dts"], ["osb"])
    dma("sp", AT[0:64, :, NT:NTS], osb[0:64, 0:16:2, :], ["osb"], ["AT_s0"], "osb")
    dma("sp", AT[64:128, :, NT:NTS], osb[0:64, 1:16:2, :], ["osb"], ["AT_s1"], "osb")
    ATK = ATK + ["AT_s0", "AT_s1"]
    S.mark('p3b')

    S.barrier(mk_bar)
    mT = carve(0, 16 * NTS, BF16).rearrange("p (c t) -> p c t", t=NTS)
    ST2 = carve(100 * KB, 8 * NTS, BF16).rearrange("p (c t) -> p c t", t=NTS)
    o5 = 117 * KB
    slt = [carve(o5 + 2 * KB * i, 512, F32) for i in range(2)]; o5 += 4 * KB
    bra = carve(o5, 2 * NTS, F32).rearrange("p (c t) -> p c t", t=NTS); o5 += 2 * NTS * 4
    m1 = carve(o5, 2 * NTS, F32).rearrange("p (c t) -> p c t", t=NTS); o5 += 2 * NTS * 4
    o5 = (o5 + 3) // 4 * 4
    xin = [carve(o5 + 2 * KB * i, 512, F32) for i in range(3)]; o5 += 6 * KB
    oti = [carve(o5 + 2 * KB * i, 512, F32) for i in range(3)]; o5 += 6 * KB
    assert o5 <= ARENA, o5
    dma("sp", b_glu[:], din["b_glu"], [], ["b_glu"], "b_glu")
    TB = ((0, 512), (512, 512), (NT, NS))
    XK = ["xnT_%d" % i for i in range(8)] + ["xnT_s"]
    STK = ["ST_%d" % i for i in range(8)] + ["STs_%d" % i for i in range(8)]
    c4 = {"ps": 0, "sl": 0}

    def feat_mm(slot, wc0, nk, rhs_fn, rk):
        for (t0, N) in TB:
            b = c4["ps"] % 4
            c4["ps"] += 1
            kb = "PS%d" % b
            for kc in range(nk):
                S.add("pe", lambda e, kc=kc, b=b, t0=t0, N=N: e.matmul(PS[b][:, 0:N], lhsT=wbuf[slot][:, kc, wc0:wc0 + 128], rhs=rhs_fn(kc, t0, N),
                                                                       start=(kc == 0), stop=(kc == nk - 1)), rk + ["wbuf%d" % slot], [kb])
            yield b, kb, t0, N

    def act_tmp(b, kb, N, func, bias=None):
        j = c4["sl"] % 2
        c4["sl"] += 1
        t = slt[j]
        if bias is None:
            S.add("act", lambda e: e.activation(out=t[:, 0:N], in_=PS[b][:, 0:N], func=func), [kb], ["slt%d" % j])
        else:
            S.add("act", lambda e: e.activation(out=t[:, 0:N], in_=PS[b][:, 0:N], func=func, bias=bias), [kb, "b_glu"], ["slt%d" % j])
        return t, "slt%d" % j

    xrhs = lambda kc, t0, N: xnT[:, kc, t0:t0 + N]
    for blk in range(2):
        sl_ = blk % 2
        for c in range(4):
            jc = blk * 4 + c
            for b, kb, t0, N in feat_mm(sl_, c * 128, ND, xrhs, XK):
                t, tk = act_tmp(b, kb, N, AF.Silu)
                S.add("dve", lambda e, t=t, jc=jc, t0=t0, N=N: e.tensor_tensor(out=AT[:, jc, t0:t0 + N], in0=AT[:, jc, t0:t0 + N], in1=t[:, 0:N], op=ALU.mult),
                      [tk] + ATK, ["A_%d_%d" % (jc, t0)])
    AK = ["A_%d_%d" % (jc, t0) for jc in range(8) for (t0, _) in TB]
    strhs = lambda kc, t0, N: ST[:, kc, t0:t0 + N]
    for blk in range(2):
        sl_ = blk % 2
        load_w(sl_, din["w_glu"], blk * 512, 512, 8)
        for c in range(4):
            jc = blk * 4 + c
            for b, kb, t0, N in feat_mm(sl_, c * 128, 8, strhs, STK):
                t, tk = act_tmp(b, kb, N, AF.Sigmoid, bias=b_glu[:, jc:jc + 1])
                S.add("dve", lambda e, t=t, jc=jc, t0=t0, N=N: e.tensor_tensor(out=ST2[:, jc, t0:t0 + N], in0=ST[:, jc, t0:t0 + N], in1=t[:, 0:N], op=ALU.mult),
                      [tk] + STK, ["S2_%d_%d" % (jc, t0)])
    for blk in range(2):
        sl_ = blk % 2
        load_w(sl_, din["w_in"], 3584 + blk * 512, 512, ND)
        for c in range(4):
            jc = blk * 4 + c
            for b, kb, t0, N in feat_mm(sl_, c * 128, ND, xrhs, XK):
                t, tk = act_tmp(b, kb, N, AF.Silu)
                S.add("dve", lambda e, t=t, jc=jc, t0=t0, N=N: e.tensor_tensor(out=ST2[:, jc, t0:t0 + N], in0=ST2[:, jc, t0:t0 + N], in1=t[:, 0:N], op=ALU.mult),
                      [tk, "S2_%d_%d" % (jc, t0)], ["S2_%d_%d" % (jc, t0)])
    S2K = ["S2_%d_%d" % (jc, t0) for jc in range(8) for (t0, _) in TB]
    arhs = lambda kc, t0, N: AT[:, kc, t0:t0 + N]
    s2rhs = lambda kc, t0, N: ST2[:, kc, t0:t0 + N]
    wq = [carve((68 + 8 * i) * KB, ND * 256, BF16).rearrange("p (c n) -> p c n", c=ND) for i in range(4)]

    def load_wq(slot, src, col0, nk):
        key = "wq%d" % slot
        v = src.rearrange("(c p) n -> p c n", p=128)
        for c0 in range(0, nk, 8):
            dma("pool", wq[slot][:, c0:c0 + 8, :], v[:, c0:c0 + 8, col0:col0 + 256], [], [key], key)

    def feat_mm_q(slot, wc0, nk, rhs_fn, rk):
        for (t0, N) in TB:
            b = c4["ps"] % 4
            c4["ps"] += 1
            kb = "PS%d" % b
            for kc in range(nk):
                S.add("pe", lambda e, kc=kc, b=b, t0=t0, N=N: e.matmul(PS[b][:, 0:N], lhsT=wq[slot][:, kc, wc0:wc0 + 128], rhs=rhs_fn(kc, t0, N),
                                                                       start=(kc == 0), stop=(kc == nk - 1)), rk + ["wq%d" % slot], [kb])
            yield b, kb, t0, N

    stages = ((0, din["w_ao"], 0, 8), (1, din["w_in"], 4608, ND), (2, din["w_so"], 0, 8), (3, din["w_in"], 6656, ND))
    S.add("pool", lambda e: e.memset(dummy[0:1, 6:7], 0.0), [], ["wbuf0", "wbuf1", "wq0", "wq1", "wq2", "wq3"])
    for (sl_, src, c0_, nk_) in stages:
        load_wq(sl_, src, c0_, nk_)
    for gq in range(8):
        for half_ in range(2):
            rhs_fn, rk, last = ((arhs, AK, False), (s2rhs, S2K, True))[half_]
            sa, sg_ = 2 * half_, 2 * half_ + 1
            for c in range(2):
                for b, kb, t0, N in feat_mm_q(sa, c * 128, 8, rhs_fn, rk):
                    S.add("act", lambda e, b=b, c=c, t0=t0, N=N: e.copy(out=bra[:, c, t0:t0 + N], in_=PS[b][:, 0:N]), [kb], ["bra_%d_%d" % (c, t0)])
            if gq < 7:
                load_wq(sa, stages[sa][1], stages[sa][2] + (gq + 1) * 256, stages[sa][3])
            for c in range(2):
                fc = gq * 2 + c
                for b, kb, t0, N in feat_mm_q(sg_, c * 128, ND, xrhs, XK):
                    t, tk = act_tmp(b, kb, N, AF.Sigmoid)
                    if not last:
                        S.add("dve", lambda e, t=t, c=c, t0=t0, N=N: e.tensor_tensor(out=m1[:, c, t0:t0 + N], in0=t[:, 0:N], in1=bra[:, c, t0:t0 + N], op=ALU.mult),
                              [tk, "bra_%d_%d" % (c, t0)], ["m1_%d_%d" % (c, t0)])
                    else:
                        S.add("dve", lambda e, t=t, c=c, t0=t0, N=N: e.tensor_tensor(out=t[:, 0:N], in0=t[:, 0:N], in1=bra[:, c, t0:t0 + N], op=ALU.mult),
                              [tk, "bra_%d_%d" % (c, t0)], [tk])
                        S.add("dve", lambda e, t=t, c=c, fc=fc, t0=t0, N=N: e.tensor_tensor(out=mT[:, fc, t0:t0 + N], in0=t[:, 0:N], in1=m1[:, c, t0:t0 + N], op=ALU.add),
                              [tk, "m1_%d_%d" % (c, t0)], ["mT_%d_%d" % (fc, t0)])
            if gq < 7:
                load_wq(sg_, stages[sg_][1], stages[sg_][2] + (gq + 1) * 256, stages[sg_][3])
    MK = ["mT_%d_%d" % (fc, t0) for fc in range(16) for (t0, _) in TB]
    S.mark('p4')
    S.add("pool", lambda e: e.memset(dummy[0:1, 5:6], 0.0), [], ["wbuf0", "wbuf1", "wq0", "wq1", "wq2", "wq3"])
    tiles5 = []
    for cb in range(4):
        for i in range(9):
            tiles5.append((cb, i))

    def p5_load(k):
        cb, i = tiles5[k]
        n = 128 if i < 8 else NS
        xsrc = din["x_own"][i * 128:(i + 1) * 128, cb * 512:(cb + 1) * 512] if i < 8 else din["x_smp"][:, cb * 512:(cb + 1) * 512]
        j = k % 3
        dma("sp", xin[j][0:n, :], xsrc, [], ["xin%d" % j], "xin%d" % j)

    p5_load(0)
    p5_load(1)
    for k, (cb, i) in enumerate(tiles5):
        sl_ = cb % 2
        if i == 0:
            load_w(sl_, din["w_out"], cb * 512, 512, ND)
        n = 128 if i < 8 else NS
        tc0 = i * 128 if i < 8 else NT
        ydst = dout["y_own"][i * 128:(i + 1) * 128, cb * 512:(cb + 1) * 512] if i < 8 else dout["y_smp"][:, cb * 512:(cb + 1) * 512]
        j = k % 3
        b = c4["ps"] % 4
        c4["ps"] += 1
        kb = "PS%d" % b
        if k + 2 < len(tiles5):
            p5_load(k + 2)
        for fc in range(16):
            S.add("pe", lambda e, fc=fc, b=b, n=n, tc0=tc0, sl_=sl_: e.matmul(PS[b][0:n, 0:512], lhsT=mT[:, fc, tc0:tc0 + n], rhs=wbuf[sl_][:, fc, 0:512],
                                                                              start=(fc == 0), stop=(fc == 15)), MK + ["wbuf%d" % sl_], [kb])
        S.add("dve", lambda e, j=j, b=b, n=n: e.tensor_tensor(out=oti[j][0:n, :], in0=PS[b][0:n, 0:512], in1=xin[j][0:n, :], op=ALU.add),
              [kb, "xin%d" % j], ["oti%d" % j])
        dma("sp", ydst, oti[j][0:n, :], ["oti%d" % j], [], "oti%d" % j)
    S.mark('p5')

    S.emit()
    for cm in reversed(ctxs):
        cm.__exit__(None, None, None)
    return nc


def _rope_tables(pos):
    half = 32
    inv_freq = (10000.0 ** (-np.arange(half, dtype=np.float32) / half)).astype(np.float32)
    ang = pos.astype(np.float32)[:, None] * inv_freq[None, :]
    c = np.cos(ang.astype(np.float64)).astype(np.float32)
    s = np.sin(ang.astype(np.float64)).astype(np.float32)
    return np.concatenate([c, c], 1), np.concatenate([-s, s], 1)


def kernel(**inp):
    f32 = np.float32
    x_prompt = np.asarray(inp["x_prompt"], f32)
    x_sample = np.asarray(inp["x_sample"], f32)
    meta = np.asarray(inp["meta_tokens"], f32)
    shared = {
        "w_in": np.ascontiguousarray(inp["w_in"][0], f32),
        "w_glu": np.ascontiguousarray(inp["w_glu"][0], f32),
        "w_ao": np.ascontiguousarray(inp["w_attn_out"][0], f32),
        "w_so": np.ascontiguousarray(inp["w_ssm_out"][0], f32),
        "w_out": np.ascontiguousarray(inp["w_out"][0], f32),
        "norm_gain": np.ascontiguousarray(inp["norm_gain"], f32).reshape(1, D),
        "qk_gain": np.concatenate([np.asarray(inp["q_norm_gain"], f32).reshape(1, 64),
                                   np.asarray(inp["k_norm_gain"], f32).reshape(1, 64)], 1),
        "sinks": np.asarray(inp["sinks"], f32).reshape(1, 16),
        "b_glu": np.ascontiguousarray(np.asarray(inp["b_glu"], f32).reshape(8, 128).T),
        "dsk": np.ascontiguousarray(np.tile(np.asarray(inp["d_skip"], f32).reshape(64, 16).T, (8, 1))),
        "ident": np.eye(128, dtype=f32),
        "a_re2": np.ascontiguousarray(np.tile(np.asarray(inp["a_re"][0], f32).T, (2, 1))),
        "a_im2": np.ascontiguousarray(np.tile(np.asarray(inp["a_im"][0], f32).T, (2, 1))),
        "logdt": np.asarray(inp["log_dt"], f32).reshape(1, 64),
        "kvals": np.asarray([7, 6, 5, 4, 3, 2, 1, 0, -7, -6, -5, -4, -3, -2, -1, 0, 1, 2, 3, 4, 5, 6, 7, 8], f32).reshape(1, 24),
        "krow": np.arange(260, dtype=f32).reshape(1, 260),
        "blockmask": np.kron(np.triu(np.ones((8, 8), f32)), np.ones((16, 16), f32)).astype(f32),
        "swapm": np.roll(np.eye(128, dtype=f32), 64, axis=0),
    }
    kk_, qq_ = np.meshgrid(np.arange(128), np.arange(128), indexing="ij")
    NEG = np.float32(-30000.0)
    shared["maskc"] = np.where(kk_ <= qq_, np.float32(0), NEG).astype(f32)
    shared["maskp"] = np.where(kk_ > qq_, np.float32(0), NEG).astype(f32)
    b_re = np.asarray(inp["b_re"][0], f32).transpose(1, 0, 2)
    b_im = np.asarray(inp["b_im"][0], f32).transpose(1, 0, 2)
    c_re = np.asarray(inp["c_re"][0], f32).transpose(2, 0, 1)
    c_im = np.asarray(inp["c_im"][0], f32).transpose(2, 0, 1)
    shared["b_self"] = np.ascontiguousarray(np.concatenate([b_re, b_im], 0))
    shared["b_part"] = np.ascontiguousarray(np.concatenate([b_im, b_re], 0))
    shared["c_self"] = np.ascontiguousarray(np.concatenate([c_re, c_im], 0))
    shared["c_part"] = np.ascontiguousarray(np.concatenate([c_im, c_re], 0))
    in_maps = []
    for core in range(N_CORES):
        b, half = core // 2, core % 2
        m = dict(shared)
        m["x_own"] = np.ascontiguousarray(x_prompt[b, half * NT:(half + 1) * NT])
        if half == 1:
            m["x_pre2"] = np.ascontiguousarray(x_prompt[b, 0:NT])
            m["x_pre1"] = meta.copy()
            halo = x_prompt[b, NT - 128:NT]
        else:
            m["x_pre2"] = np.concatenate([np.zeros((NT - 16, D), f32), meta], 0)
            m["x_pre1"] = np.zeros((16, D), f32)
            halo = np.zeros((128, D), f32)
        m["x_kvx"] = np.concatenate([halo, meta], 0)
        m["maskp0"] = shared["maskp"] if half == 1 else np.full((128, 128), NEG, f32)
        m["x_smp"] = np.ascontiguousarray(x_sample[core * NS:(core + 1) * NS, 0])
        base = 16 + half * NT
        cc = np.zeros((11, 128, 64), f32)
        ss = np.zeros((11, 128, 64), f32)
        for t in range(8):
            cc[t], ss[t] = _rope_tables(base + t * 128 + np.arange(128))
        cc[8], ss[8] = _rope_tables(np.maximum(base - 128 + np.arange(128), 0))
        cc[9, :16], ss[9, :16] = _rope_tables(np.arange(16))
        cc[10, :16], ss[10, :16] = _rope_tables(np.full(16, 8192))
        m["rope_cc"] = np.ascontiguousarray(cc.transpose(1, 0, 2))
        m["rope_ss"] = np.ascontiguousarray(ss.transpose(1, 0, 2))
        sl = slice(core * NS, (core + 1) * NS)
        m["cwk"] = np.ascontiguousarray(inp["cache_win_k"][0, sl], f32).reshape(NS, 128, 256)
        m["cwv"] = np.ascontiguousarray(inp["cache_win_v"][0, sl], f32).reshape(NS, 128, 256)
        m["cmk"] = np.ascontiguousarray(inp["cache_meta_k"][0, sl], f32).reshape(NS, 16, 256)
        m["cmv"] = np.ascontiguousarray(inp["cache_meta_v"][0, sl], f32).reshape(NS, 16, 256)
        sre = np.asarray(inp["state_ssm_re"][0, sl], f32)
        sim = np.asarray(inp["state_ssm_im"][0, sl], f32)
        m["st_self"] = np.ascontiguousarray(np.concatenate([sre, sim], 2))
        m["st_part"] = np.ascontiguousarray(np.concatenate([sim, sre], 2))
        in_maps.append(m)

    nc = build_program()
    res = run_bass_kernel_spmd(nc, in_maps, core_ids=list(range(N_CORES)))
    R = res.results

    y_prompt = np.zeros((4, 2048, D), f32)
    y_sample = np.zeros((128, 1, D), f32)
    p_win_k = np.zeros((1, 4, 128, 4, 64), f32)
    p_win_v = np.zeros((1, 4, 128, 4, 64), f32)
    p_meta_k = np.zeros((1, 4, 16, 4, 64), f32)
    p_meta_v = np.zeros((1, 4, 16, 4, 64), f32)
    p_re = np.zeros((1, 4, 64, 64), f32)
    p_im = np.zeros((1, 4, 64, 64), f32)
    s_win_k = np.zeros((1, 128, 128, 4, 64), f32)
    s_win_v = np.zeros((1, 128, 128, 4, 64), f32)
    s_re = np.zeros((1, 128, 64, 64), f32)
    s_im = np.zeros((1, 128, 64, 64), f32)
    for core in range(N_CORES):
        b, half = core // 2, core % 2
        r = R[core]
        y_prompt[b, half * NT:(half + 1) * NT] = r["y_own"]
        sl = slice(core * NS, (core + 1) * NS)
        y_sample[sl, 0] = r["y_smp"]
        if half == 1:
            p_win_k[0, b] = r["pwk"].reshape(128, 4, 64)
            p_win_v[0, b] = r["pwv"].reshape(128, 4, 64)
            p_re[0, b] = r["pssm"][:, 0:64]
            p_im[0, b] = r["pssm"][:, 64:128]
        else:
            p_meta_k[0, b] = r["pmk"].reshape(16, 4, 64)
            p_meta_v[0, b] = r["pmv"].reshape(16, 4, 64)
        s_win_k[0, sl] = r["swk"].reshape(NS, 128, 4, 64)
        s_win_v[0, sl] = r["swv"].reshape(NS, 128, 4, 64)
        s_re[0, sl] = r["sssm"][:, :, 0:64]
        s_im[0, sl] = r["sssm"][:, :, 64:128]
    return (y_prompt, y_sample, p_win_k, p_win_v, p_meta_k, p_meta_v, p_re, p_im, s_win_k, s_win_v, s_re, s_im)
```
